# Optimizing a Trainium2 kernel written in Bass

```python
import math
import jax, jax.numpy as jnp
from jax import lax
import numpy as np

D_MODEL = 1024
BATCH = 8
SEQ = 4096
DEPTH = 2
DEC_BATCH = 128
DEC_SEQ = 8
PAST_LEN = 16384
PAGE_SIZE = 128

N_MIXERS = 2
N_SSM_LAYERS = (DEPTH + 1) // 2
N_SWA_LAYERS = DEPTH // 2

SSM_EXPAND = 2
SSM_D_INNER = SSM_EXPAND * D_MODEL
SSM_HEAD_DIM = 64
SSM_HEADS = SSM_D_INNER // SSM_HEAD_DIM
SSM_GROUPS = 4
SSM_HPG = SSM_HEADS // SSM_GROUPS
SSM_STATE = 128
SSM_CONV = 4
SSM_CHUNK = 128
SSM_CONV_DIM = SSM_D_INNER + 2 * SSM_GROUPS * SSM_STATE
SSM_IN_DIM = SSM_D_INNER + SSM_CONV_DIM + SSM_HEADS

ATTN_HEAD_DIM = 64
ATTN_HEADS = D_MODEL // ATTN_HEAD_DIM
ATTN_KV_HEADS = 4
ATTN_REP = ATTN_HEADS // ATTN_KV_HEADS
WINDOW = 128
SWA_BLOCK = WINDOW
QKV_DIM = (ATTN_HEADS + 2 * ATTN_KV_HEADS) * ATTN_HEAD_DIM

REL_BUCKETS = 32
REL_MAX_DIST = WINDOW

D_FF = 2816
FFN_RES = 0.5
N_SUB = 3
RMS_EPS = 1e-6

kernel_name = 'hybrid_ssd_swa_macaron_step'


def rms_norm(x, g):
    xf = x.astype(jnp.float32)
    y = xf * lax.rsqrt(jnp.mean(xf * xf, axis=-1, keepdims=True) + RMS_EPS)
    return (y * g.astype(jnp.float32)).astype(x.dtype)


def modulated_norm(x, g, m):
    return rms_norm(x, g) * (1 + m[:, :, 1]) + m[:, :, 0]


def swiglu(h, w_in, w_out):
    g, u = jnp.split(h @ w_in, 2, axis=-1)
    return (jax.nn.silu(g) * u) @ w_out


def t5_bucket(dist):
    exact = REL_BUCKETS // 2
    d = jnp.maximum(dist, 0)
    df = jnp.maximum(d, 1).astype(jnp.float32)
    large = exact + (jnp.log(df / exact) / math.log(REL_MAX_DIST / exact) * (REL_BUCKETS - exact)).astype(jnp.int32)
    large = jnp.minimum(large, REL_BUCKETS - 1)
    return jnp.where(d < exact, d, large)


def causal_dwconv(xpad, w, b, l):
    return sum(xpad[:, k:k + l] * w[k] for k in range(SSM_CONV)) + b


def ssd_scan(xdt, dA, Bm, Cm, h0):
    bsz, l = xdt.shape[:2]
    q = min(SSM_CHUNK, l)
    nc = -(-l // q)
    pad = nc * q - l
    if pad:
        padw = lambda a: jnp.pad(a, [(0, 0), (0, pad)] + [(0, 0)] * (a.ndim - 2))
        xdt, dA, Bm, Cm = padw(xdt), padw(dA), padw(Bm), padw(Cm)
    xc = xdt.reshape(bsz, nc, q, SSM_GROUPS, SSM_HPG, SSM_HEAD_DIM)
    Bc = Bm.reshape(bsz, nc, q, SSM_GROUPS, SSM_STATE)
    Cc = Cm.reshape(bsz, nc, q, SSM_GROUPS, SSM_STATE)
    a_cs = jnp.cumsum(dA.reshape(bsz, nc, q, SSM_GROUPS, SSM_HPG), axis=2)
    causal = jnp.tril(jnp.ones((q, q), bool))[None, None, :, :, None, None]
    seg = a_cs[:, :, :, None] - a_cs[:, :, None]
    decay = jnp.exp(jnp.where(causal, seg, -jnp.inf))
    cb = jnp.einsum('bclgn,bcsgn->bclsg', Cc, Bc)
    w = (decay * cb[..., None]).astype(xc.dtype)
    y_diag = jnp.einsum('bclsgr,bcsgrp->bclgrp', w, xc)
    decay_to_end = jnp.exp(a_cs[:, :, -1:] - a_cs)
    chunk_states = jnp.einsum('bclgn,bclgrp->bcgrpn', Bc, xc * decay_to_end[..., None])
    chunk_decay = jnp.exp(a_cs[:, :, -1])

    def step(hc, inp):
        s, a = inp
        return hc * a[..., None, None] + s, hc

    h_T, h_in = lax.scan(step, h0.astype(jnp.float32),
                         (jnp.moveaxis(chunk_states, 1, 0).astype(jnp.float32), jnp.moveaxis(chunk_decay, 1, 0)))
    y_off = jnp.einsum('bclgn,cbgrpn->bclgrp', Cc, h_in) * jnp.exp(a_cs)[..., None]
    y = (y_diag + y_off).reshape(bsz, nc * q, SSM_GROUPS, SSM_HPG, SSM_HEAD_DIM)[:, :l]
    return y.astype(xdt.dtype), h_T


def ssd_mixer(h, conv_buf, h0, in_w, conv_w, conv_b, dt_bias, a_log, d_skip, norm_w, out_w):
    bsz, l, _ = h.shape
    zxbcdt = h @ in_w
    z = zxbcdt[..., :SSM_D_INNER]
    xbc = zxbcdt[..., SSM_D_INNER:SSM_D_INNER + SSM_CONV_DIM]
    dt = zxbcdt[..., SSM_D_INNER + SSM_CONV_DIM:]
    xpad = jnp.concatenate([conv_buf.astype(xbc.dtype), xbc], axis=1)
    new_conv = xpad[:, -(SSM_CONV - 1):]
    xbc = jax.nn.silu(causal_dwconv(xpad, conv_w, conv_b, l))
    gn = SSM_GROUPS * SSM_STATE
    xs = xbc[..., :SSM_D_INNER].reshape(bsz, l, SSM_GROUPS, SSM_HPG, SSM_HEAD_DIM)
    Bm = xbc[..., SSM_D_INNER:SSM_D_INNER + gn].reshape(bsz, l, SSM_GROUPS, SSM_STATE)
    Cm = xbc[..., SSM_D_INNER + gn:].reshape(bsz, l, SSM_GROUPS, SSM_STATE)
    dt = jax.nn.softplus((dt + dt_bias).astype(jnp.float32)).reshape(bsz, l, SSM_GROUPS, SSM_HPG)
    A = -jnp.exp(a_log.astype(jnp.float32)).reshape(SSM_GROUPS, SSM_HPG)
    xdt = xs * dt[..., None].astype(xs.dtype)
    y, h_T = ssd_scan(xdt, dt * A, Bm, Cm, h0.reshape(bsz, SSM_GROUPS, SSM_HPG, SSM_HEAD_DIM, SSM_STATE))
    y = y + xs * d_skip.reshape(SSM_GROUPS, SSM_HPG, 1)
    yg = (y.reshape(bsz, l, SSM_D_INNER) * jax.nn.silu(z)).astype(jnp.float32).reshape(bsz, l, SSM_GROUPS, -1)
    yg = yg * lax.rsqrt(jnp.mean(yg * yg, axis=-1, keepdims=True) + RMS_EPS)
    yn = (yg.reshape(bsz, l, SSM_D_INNER) * norm_w.astype(jnp.float32)).astype(h.dtype)
    new_h = h_T.reshape(bsz, SSM_HEADS, SSM_HEAD_DIM, SSM_STATE).astype(h.dtype)
    return yn @ out_w, new_conv, new_h


def window_attention(q, k, v, q_pos, k_pos, sinks, rel_bias):
    logits = jnp.einsum('...qhrd,...khd->...hrqk', q, k).astype(jnp.float32) * (ATTN_HEAD_DIM ** -0.5)
    dist = q_pos[..., :, None] - k_pos[..., None, :]
    valid = (dist >= 0) & (dist <= WINDOW) & (k_pos[..., None, :] >= 0)
    bias = rel_bias[t5_bucket(dist)].astype(jnp.float32)
    bias = jnp.moveaxis(bias.reshape(bias.shape[:-1] + (ATTN_KV_HEADS, ATTN_REP)), (-2, -1), (-4, -3))
    logits = jnp.where(valid[..., None, None, :, :], logits + bias, -jnp.inf)
    s = sinks.astype(jnp.float32).reshape(ATTN_KV_HEADS, ATTN_REP, 1, 1)
    m = jnp.maximum(jnp.max(logits, axis=-1, keepdims=True), s)
    e = jnp.exp(logits - m)
    p = e / (jnp.sum(e, axis=-1, keepdims=True) + jnp.exp(s - m))
    return jnp.einsum('...hrqk,...khd->...qhrd', p.astype(v.dtype), v)


def swa_mixer(h, k_buf, v_buf, start, qkv_w, qkv_b, sinks, o_w, o_b, rel_bias):
    bsz, l, _ = h.shape
    qkv = h @ qkv_w + qkv_b
    nq = ATTN_HEADS * ATTN_HEAD_DIM
    nkv = ATTN_KV_HEADS * ATTN_HEAD_DIM
    q = qkv[..., :nq].reshape(bsz, l, ATTN_KV_HEADS, ATTN_REP, ATTN_HEAD_DIM)
    k = qkv[..., nq:nq + nkv].reshape(bsz, l, ATTN_KV_HEADS, ATTN_HEAD_DIM)
    v = qkv[..., nq + nkv:].reshape(bsz, l, ATTN_KV_HEADS, ATTN_HEAD_DIM)
    if k_buf is None:
        nb = l // SWA_BLOCK
        qb = q.reshape(bsz, nb, SWA_BLOCK, ATTN_KV_HEADS, ATTN_REP, ATTN_HEAD_DIM)
        kb = k.reshape(bsz, nb, SWA_BLOCK, ATTN_KV_HEADS, ATTN_HEAD_DIM)
        vb = v.reshape(bsz, nb, SWA_BLOCK, ATTN_KV_HEADS, ATTN_HEAD_DIM)
        zk = jnp.zeros_like(kb[:, :1])
        kk = jnp.concatenate([jnp.concatenate([zk, kb[:, :-1]], axis=1), kb], axis=2)
        vv = jnp.concatenate([jnp.concatenate([zk, vb[:, :-1]], axis=1), vb], axis=2)
        starts = start + jnp.arange(nb)[:, None] * SWA_BLOCK
        q_pos = starts + jnp.arange(SWA_BLOCK)
        k_pos = starts - SWA_BLOCK + jnp.arange(2 * SWA_BLOCK)
        out = window_attention(qb, kk, vv, q_pos, k_pos, sinks, rel_bias)
        nbuf = min(WINDOW, l)
        new_k, new_v = k[:, -nbuf:], v[:, -nbuf:]
    else:
        nbuf = k_buf.shape[1]
        kk = jnp.concatenate([k_buf.astype(k.dtype), k], axis=1)
        vv = jnp.concatenate([v_buf.astype(v.dtype), v], axis=1)
        q_pos = start + jnp.arange(l)
        k_pos = start - nbuf + jnp.arange(nbuf + l)
        out = window_attention(q, kk, vv, q_pos, k_pos, sinks, rel_bias)
        new_k, new_v = kk[:, -nbuf:], vv[:, -nbuf:]
    out = out.reshape(bsz, l, nq) @ o_w + o_b
    return out, new_k, new_v


def trunk(x, c, ssm_h0s, conv_bufs, k_bufs, v_bufs, start,
          ada_w, ada_b, norm_pre, norm_post, ffn_w_in, ffn_w_out,
          ssm_in_w, ssm_conv_w, ssm_conv_b, ssm_dt_bias, ssm_a_log, ssm_d, ssm_norm_w, ssm_out_w,
          attn_qkv_w, attn_qkv_b, attn_sinks, attn_o_w, attn_o_b, rel_bias):
    bsz = x.shape[0]
    new_ssm, new_conv, new_k, new_v = [], [], [], []
    cs = jax.nn.silu(c)
    for i in range(DEPTH):
        mod = (cs @ ada_w[i] + ada_b[i]).reshape(bsz, 1, N_SUB, 3, D_MODEL)
        hin = modulated_norm(x, norm_pre[i, 0], mod[:, :, 0])
        f = swiglu(hin, ffn_w_in[i, 0], ffn_w_out[i, 0])
        x = x + FFN_RES * mod[:, :, 0, 2] * rms_norm(f, norm_post[i, 0])
        hin = modulated_norm(x, norm_pre[i, 1], mod[:, :, 1])
        j = i // N_MIXERS
        if i % N_MIXERS == 0:
            o, conv_j, ssm_j = ssd_mixer(hin, conv_bufs[j], ssm_h0s[j], ssm_in_w[j], ssm_conv_w[j], ssm_conv_b[j],
                                         ssm_dt_bias[j], ssm_a_log[j], ssm_d[j], ssm_norm_w[j], ssm_out_w[j])
            new_conv.append(conv_j)
            new_ssm.append(ssm_j)
        else:
            kb = None if k_bufs is None else k_bufs[j]
            vb = None if v_bufs is None else v_bufs[j]
            o, k_j, v_j = swa_mixer(hin, kb, vb, start, attn_qkv_w[j], attn_qkv_b[j], attn_sinks[j],
                                    attn_o_w[j], attn_o_b[j], rel_bias)
            new_k.append(k_j)
            new_v.append(v_j)
        x = x + mod[:, :, 1, 2] * rms_norm(o, norm_post[i, 1])
        hin = modulated_norm(x, norm_pre[i, 2], mod[:, :, 2])
        f = swiglu(hin, ffn_w_in[i, 1], ffn_w_out[i, 1])
        x = x + FFN_RES * mod[:, :, 2, 2] * rms_norm(f, norm_post[i, 2])
    return x, jnp.stack(new_ssm), jnp.stack(new_conv), jnp.stack(new_k), jnp.stack(new_v)


def setup_inputs(seed: int = 0) -> dict:
    key = jax.random.key(seed)
    ks = jax.random.split(key, 32)
    nrm = lambda k, shape, s: jax.random.normal(k, shape, jnp.float32) * s
    nbuf = min(WINDOW, PAST_LEN)
    dt0 = jnp.exp(jax.random.uniform(ks[14], (N_SSM_LAYERS, SSM_HEADS), jnp.float32, math.log(1e-3), math.log(1e-1)))
    return {
        'x_prompt': nrm(ks[0], (BATCH, SEQ, D_MODEL), 1.0),
        'x_sample': nrm(ks[1], (DEC_BATCH, DEC_SEQ, D_MODEL), 1.0),
        'state_ssm': nrm(ks[2], (N_SSM_LAYERS, DEC_BATCH, SSM_HEADS, SSM_HEAD_DIM, SSM_STATE), 0.1),
        'state_conv': nrm(ks[3], (N_SSM_LAYERS, DEC_BATCH, SSM_CONV - 1, SSM_CONV_DIM), 1.0),
        'cache_k': nrm(ks[4], (N_SWA_LAYERS, DEC_BATCH, nbuf, ATTN_KV_HEADS, ATTN_HEAD_DIM), 1.0),
        'cache_v': nrm(ks[5], (N_SWA_LAYERS, DEC_BATCH, nbuf, ATTN_KV_HEADS, ATTN_HEAD_DIM), 1.0),
        'c_prompt': nrm(ks[6], (BATCH, D_MODEL), 1.0),
        'c_sample': nrm(ks[7], (DEC_BATCH, D_MODEL), 1.0),
        'ada_w': nrm(ks[8], (DEPTH, D_MODEL, N_SUB * 3 * D_MODEL), 0.5 * D_MODEL ** -0.5),
        'ada_b': nrm(ks[9], (DEPTH, N_SUB * 3 * D_MODEL), 0.02),
        'norm_pre': 1.0 + nrm(ks[10], (DEPTH, N_SUB, D_MODEL), 0.05),
        'norm_post': 1.0 + nrm(ks[11], (DEPTH, N_SUB, D_MODEL), 0.05),
        'ffn_w_in': nrm(ks[12], (DEPTH, 2, D_MODEL, 2 * D_FF), D_MODEL ** -0.5),
        'ffn_w_out': nrm(ks[13], (DEPTH, 2, D_FF, D_MODEL), D_FF ** -0.5),
        'ssm_in_w': nrm(ks[15], (N_SSM_LAYERS, D_MODEL, SSM_IN_DIM), D_MODEL ** -0.5),
        'ssm_conv_w': nrm(ks[16], (N_SSM_LAYERS, SSM_CONV, SSM_CONV_DIM), SSM_CONV ** -0.5),
        'ssm_conv_b': nrm(ks[17], (N_SSM_LAYERS, SSM_CONV_DIM), 0.02),
        'ssm_dt_bias': dt0 + jnp.log(-jnp.expm1(-dt0)),
        'ssm_a_log': jnp.log(jax.random.uniform(ks[18], (N_SSM_LAYERS, SSM_HEADS), jnp.float32, 1.0, 16.0)),
        'ssm_d': 1.0 + nrm(ks[19], (N_SSM_LAYERS, SSM_HEADS), 0.1),
        'ssm_norm_w': 1.0 + nrm(ks[20], (N_SSM_LAYERS, SSM_D_INNER), 0.05),
        'ssm_out_w': nrm(ks[21], (N_SSM_LAYERS, SSM_D_INNER, D_MODEL), SSM_D_INNER ** -0.5),
        'attn_qkv_w': nrm(ks[22], (N_SWA_LAYERS, D_MODEL, QKV_DIM), D_MODEL ** -0.5),
        'attn_qkv_b': nrm(ks[23], (N_SWA_LAYERS, QKV_DIM), 0.02),
        'attn_sinks': nrm(ks[24], (N_SWA_LAYERS, ATTN_HEADS), 1.0),
        'attn_o_w': nrm(ks[25], (N_SWA_LAYERS, ATTN_HEADS * ATTN_HEAD_DIM, D_MODEL), (ATTN_HEADS * ATTN_HEAD_DIM) ** -0.5),
        'attn_o_b': nrm(ks[26], (N_SWA_LAYERS, D_MODEL), 0.02),
        'rel_bias': nrm(ks[27], (REL_BUCKETS, ATTN_HEADS), 0.5),
    }


def reference(x_prompt, x_sample, state_ssm, state_conv, cache_k, cache_v, c_prompt, c_sample,
              ada_w, ada_b, norm_pre, norm_post, ffn_w_in, ffn_w_out,
              ssm_in_w, ssm_conv_w, ssm_conv_b, ssm_dt_bias, ssm_a_log, ssm_d, ssm_norm_w, ssm_out_w,
              attn_qkv_w, attn_qkv_b, attn_sinks, attn_o_w, attn_o_b, rel_bias):
    weights = (ada_w, ada_b, norm_pre, norm_post, ffn_w_in, ffn_w_out,
               ssm_in_w, ssm_conv_w, ssm_conv_b, ssm_dt_bias, ssm_a_log, ssm_d, ssm_norm_w, ssm_out_w,
               attn_qkv_w, attn_qkv_b, attn_sinks, attn_o_w, attn_o_b, rel_bias)
    bsz = x_prompt.shape[0]
    zero_ssm = jnp.zeros((N_SSM_LAYERS, bsz, SSM_HEADS, SSM_HEAD_DIM, SSM_STATE), x_prompt.dtype)
    zero_conv = jnp.zeros((N_SSM_LAYERS, bsz, SSM_CONV - 1, SSM_CONV_DIM), x_prompt.dtype)
    y_prompt, ssm_p, conv_p, k_p, v_p = trunk(x_prompt, c_prompt, zero_ssm, zero_conv, None, None, 0, *weights)
    y_sample, ssm_s, conv_s, k_s, v_s = trunk(x_sample, c_sample, state_ssm, state_conv, cache_k, cache_v, PAST_LEN, *weights)
    return (y_prompt, y_sample, ssm_p, conv_p, k_p, v_p, ssm_s, conv_s, k_s, v_s)
```

```python
import contextlib
import math
import numpy as np
import concourse.bass as bass
import concourse.mybir as mybir
from concourse.bass_utils import run_bass_kernel_spmd

F32 = mybir.dt.float32
BF16 = mybir.dt.bfloat16
AF = mybir.ActivationFunctionType
ALU = mybir.AluOpType
AX = mybir.AxisListType

D = 1024
SEQ = 4096
NSEQ_S = 16
TPS_S = 8
DFF = 2816
NJ = DFF // 128
DIN = 2048
NEG = -30000.0
EPS = 1e-6

C_ID, C_TRIP, C_STRIP, C_TRIS, C_STRIS, C_J, C_JREP = 0, 128, 256, 384, 512, 640, 768
C_SEL = 896
C_OH = 912
C_ONES = 1296
NCST = 1424


def _bucket(d):
    d = np.asarray(d)
    df = np.maximum(d, 1).astype(np.float32)
    large = 16 + (np.log(df / np.float32(16)) / np.float32(math.log(128 / 16)) * np.float32(16)).astype(np.int32)
    large = np.minimum(large, 31)
    return np.where(d < 16, d, large)


def _make_consts():
    c = np.zeros((128, NCST), np.float32)
    i = np.arange(128)
    same = (i[:, None] // 8) == (i[None, :] // 8)
    c[:, C_ID:C_ID + 128] = np.eye(128)
    c[:, C_TRIP:C_TRIP + 128] = (i[:, None] <= i[None, :])
    c[:, C_STRIP:C_STRIP + 128] = (i[:, None] > i[None, :])
    c[:, C_TRIS:C_TRIS + 128] = (i[:, None] <= i[None, :]) & same
    c[:, C_STRIS:C_STRIS + 128] = (i[:, None] > i[None, :]) & same
    c[:, C_J:C_J + 128] = (i[:, None] == 127 - i[None, :])
    c[:, C_JREP:C_JREP + 128] = (i[:, None] == 127 - (i[None, :] % 8))
    c[:, C_SEL:C_SEL + 16] = (i[:, None] // 8) == np.arange(16)[None, :]
    oh = np.zeros((33, 384), np.float32)
    for ii in range(384):
        dist = 255 - ii
        if 0 <= dist <= 128:
            oh[int(_bucket(dist)), ii] = 1.0
        else:
            oh[32, ii] = 1.0
    c[0:33, C_OH:C_OH + 384] = oh
    c[:, C_ONES:C_ONES + 128] = 1.0
    return c


class Buf:
    __slots__ = ("t", "w", "r", "dsem")

    def __init__(self, t, pend=None):
        self.t = t
        self.w = None
        self.r = dict(pend) if pend else {}
        self.dsem = None


class Eng:
    def __init__(self, name, h):
        self.name = name
        self.h = h
        self.cnt = 0
        self.seen = {}


class KB:
    def __init__(self, nc, es):
        self.nc = nc
        self.es = es
        self.sems = {}
        self.cnts = {}
        self.pe = self._eng("pe", nc.tensor)
        self.act = self._eng("act", nc.scalar)
        self.dve = self._eng("dve", nc.vector)
        self.pool = self._eng("pool", nc.gpsimd)
        self.sp = self._eng("sp", nc.sync)
        self.engs = [self.pe, self.act, self.dve, self.pool, self.sp]
        self.banks = []
        for i in range(8):
            t = es.enter_context(nc.psum_tensor(f"psb{i}", [128, 512], F32))
            self.banks.append(Buf(t))
        self.bank_i = 0
        self.held = set()
        self.pending = {}
        self.arena_bufs = []
        self.arena_es = None
        self.out_events = {}
        self.wslots = []
        self.wslot_i = 0
        self.dq = 0

    def _eng(self, name, h):
        self.sems[name] = self.es.enter_context(self.nc.semaphore(name))
        self.cnts[name] = 0
        return Eng(name, h)

    def dsem(self, name):
        self.sems[name] = self.es.enter_context(self.nc.semaphore(name))
        self.cnts[name] = 0
        return name

    def pbuf(self, name, shape, dt):
        t = self.es.enter_context(self.nc.sbuf_tensor(name, list(shape), dt))
        return Buf(t)

    def arena_open(self):
        self.arena_es = contextlib.ExitStack()
        self.arena_bufs = []

    def abuf(self, name, shape, dt):
        self.uid = getattr(self, "uid", 0) + 1
        t = self.arena_es.enter_context(self.nc.sbuf_tensor(f"{name}_{self.uid}", list(shape), dt))
        b = Buf(t, self.pending)
        self.arena_bufs.append(b)
        return b

    def arena_close(self):
        pend = dict(self.pending)
        for b in self.arena_bufs:
            if b.w and pend.get(b.w[0], 0) < b.w[1]:
                pend[b.w[0]] = b.w[1]
            for s, v in b.r.items():
                if pend.get(s, 0) < v:
                    pend[s] = v
        self.pending = pend
        self.arena_es.close()
        self.arena_es = None
        self.arena_bufs = []

    def ps(self):
        for _ in range(16):
            i = self.bank_i
            self.bank_i = (self.bank_i + 1) % 8
            if i not in self.held:
                return self.banks[i]
        raise RuntimeError("no psum bank")

    def ps_hold(self):
        b = self.ps()
        self.held.add(self.banks.index(b))
        return b

    def ps_release(self, b):
        self.held.discard(self.banks.index(b))

    def _need(self, E, reads, writes, acc, skipname):
        need = {}

        def add(s, v):
            if need.get(s, 0) < v:
                need[s] = v
        for b in reads:
            if b.w:
                add(*b.w)
        for b in writes:
            if b.w and not (acc and b.w[0] == skipname):
                add(*b.w)
            for s, v in b.r.items():
                if s != E.name:
                    add(s, v)
        for s, v in need.items():
            if E.seen.get(s, 0) < v:
                E.h.wait_ge(self.sems[s], v)
                E.seen[s] = v

    def op(self, E, fn, reads=(), writes=(), acc=False):
        self._need(E, reads, writes, acc, E.name)
        ins = fn()
        E.cnt += 1
        ins.then_inc(self.sems[E.name], 1)
        for b in reads:
            if b.r.get(E.name, 0) < E.cnt:
                b.r[E.name] = E.cnt
        for b in writes:
            b.w = (E.name, E.cnt)
            b.r = {}
        return ins

    def dma(self, Q, out, in_, reads=(), writes=(), dsem=None, join=False, **kw):
        self._need(Q, reads, writes, join, dsem)
        self.cnts[dsem] += 16
        v = self.cnts[dsem]
        Q.h.dma_start(out=out, in_=in_, **kw).then_inc(self.sems[dsem], 16)
        for b in reads:
            if b.r.get(dsem, 0) < v:
                b.r[dsem] = v
        for b in writes:
            b.w = (dsem, v)
            b.r = {}
        return (dsem, v)

    def wslot(self):
        s = self.wslots[self.wslot_i]
        self.wslot_i = (self.wslot_i + 1) % len(self.wslots)
        return s


class WStream:
    def __init__(self, kb, loads, depth=2):
        self.kb = kb
        self.loads = loads
        self.depth = depth
        self.issued = 0
        self.slots = []

    def get(self, k):
        kb = self.kb
        while self.issued < min(len(self.loads), k + 1 + self.depth):
            sl = kb.wslot()
            for i, (d, s) in enumerate(self.loads[self.issued](sl.t)):
                kb.dma(kb.pool, d, s, writes=[sl], dsem=sl.dsem, join=(i > 0))
            self.slots.append(sl)
            self.issued += 1
        return self.slots[k]


def build_program(flags=None):
    flags = flags or {}
    nc = bass.Bass("TRN2", target_bir_lowering=False)

    def din(name, shape, dt=F32):
        return nc.dram_tensor(name, list(shape), dt, kind="ExternalInput").ap()

    def dout(name, shape):
        return nc.dram_tensor(name, list(shape), F32, kind="ExternalOutput").ap()

    xp = din("xp", [SEQ, D]); xs = din("xs", [128, D])
    st_ssm = din("st_ssm", [NSEQ_S, DIN, 128]); st_conv = din("st_conv", [48, 3072])
    ck = din("ck", [NSEQ_S, 128, 256]); cv = din("cv", [NSEQ_S, 128, 256])
    cvec = din("cvec", [17, D])
    ada_w = din("ada_w", [2, D, 9216]); ada_b = din("ada_b", [2, 9216])
    norm_pre = din("norm_pre", [2, 3, D]); norm_post = din("norm_post", [2, 3, D])
    ffn_w_in = din("ffn_w_in", [2, 2, D, 2 * DFF]); ffn_w_out = din("ffn_w_out", [2, 2, DFF, D])
    ssm_in_w = din("ssm_in_w", [D, 5152]); conv_w = din("conv_w", [4, 3072]); conv_b = din("conv_b", [1, 3072])
    dt_bias = din("dt_bias", [1, 32]); a_log = din("a_log", [1, 32]); ssm_d = din("ssm_d", [1, 32])
    ssm_norm_w = din("ssm_norm_w", [1, DIN]); ssm_out_w = din("ssm_out_w", [DIN, D])
    qkv_w = din("qkv_w", [D, 1536]); qkv_b = din("qkv_b", [1, 1536]); sinks = din("sinks", [1, 16])
    o_w = din("o_w", [D, D]); o_b = din("o_b", [1, D]); rel_bias = din("rel_bias", [32, 16])
    cst = din("cst", [128, NCST])

    yp = dout("yp", [SEQ, D]); ys = dout("ys", [128, D])
    ssm_p = dout("ssm_p", [DIN, 128]); conv_p = dout("conv_p", [3, 3072])
    kp = dout("kp", [128, 256]); vp = dout("vp", [128, 256])
    ssm_s = dout("ssm_s", [NSEQ_S, DIN, 128]); conv_s = dout("conv_s", [48, 3072])
    ks = dout("ks", [NSEQ_S, 128, 256]); vs = dout("vs", [NSEQ_S, 128, 256])
    uscr = nc.dram_tensor("uscr", [16, 384], F32, kind="Internal")

    es = contextlib.ExitStack()
    with es:
        kb = KB(nc, es)
        pe, act, dve, pool, sp = kb.pe, kb.act, kb.dve, kb.pool, kb.sp
        NW = 4
        for i in range(NW):
            b = kb.pbuf(f"wslot{i}", [128, 4096], BF16)
            b.dsem = kb.dsem(f"dw{i}")
            kb.wslots.append(b)
        dmisc = [kb.dsem(f"dm{i}") for i in range(8)]
        dout_sems = [kb.dsem(f"do{i}") for i in range(4)]
        mi = [0]

        def msem():
            mi[0] = (mi[0] + 1) % len(dmisc)
            return dmisc[mi[0]]
        oi = [0]

        def osem():
            oi[0] = (oi[0] + 1) % len(dout_sems)
            return dout_sems[oi[0]]

        def out_dma(Q, out, in_, reads, **kw):
            s = osem()
            kb.dma(Q, out, in_, reads=reads, dsem=s, **kw)

        cstb = kb.pbuf("cstb", [128, NCST], F32)
        cbf = kb.pbuf("cbf", [128, 256], BF16)
        epsb = kb.pbuf("epsb", [128, 2], F32)
        x_fm = kb.pbuf("x_fm", [128, 8, 512], F32)
        hin = kb.pbuf("hin", [128, 8, 512], BF16)
        rstd = kb.pbuf("rstd", [128, 512], F32)
        tmpA = [kb.pbuf(f"tmpA{i}", [128, 512], F32) for i in range(3)]
        PRE = kb.pbuf("PRE", [128, 18, 8, 17], F32)
        hT = kb.pbuf("hT", [128, DIN], F32)
        hT_bf = kb.pbuf("hT_bf", [128, DIN], BF16)
        tailP = kb.pbuf("tailP", [128, 24, 3], F32)
        convw = kb.pbuf("convw", [128, 24, 4], F32)
        convb = kb.pbuf("convb", [128, 24], F32)
        vec32 = kb.pbuf("vec32", [128, 4, 32], F32)
        normwT = kb.pbuf("normwT", [128, 16], F32)
        wdt = kb.pbuf("wdt", [128, 8, 32], BF16)
        qkb = kb.pbuf("qkb", [128, 10], F32)
        kvb = kb.pbuf("kvb", [128, 512], F32)
        ob = kb.pbuf("ob", [128, 8], F32)
        sinkb = kb.pbuf("sinkb", [128, 16], F32)
        kT = kb.pbuf("kT", [128, 2, 128 + 512], BF16)
        vtok = kb.pbuf("vtok", [128, 5, 256], BF16)
        tmi = [0]

        def tmp():
            tmi[0] = (tmi[0] + 1) % 3
            return tmpA[tmi[0]]

        ident = lambda: cstb.t[:, C_ID:C_ID + 128]
        ident_bf = lambda: cbf.t[:, 0:128]
        ones_bf = lambda: cbf.t[:, 128:256]
        ones_f = lambda: cstb.t[:, C_ONES:C_ONES + 128]

        kb.dma(sp, cstb.t[:], cst, writes=[cstb], dsem=msem())
        kb.op(dve, lambda: nc.vector.tensor_copy(out=cbf.t[:, 0:128], in_=cstb.t[:, C_ID:C_ID + 128]), [cstb], [cbf])
        kb.op(dve, lambda: nc.vector.tensor_copy(out=cbf.t[:, 128:256], in_=cstb.t[:, C_ONES:C_ONES + 128]), [cbf, cstb], [cbf])
        kb.op(dve, lambda: nc.vector.memset(epsb.t[:, 0:1], EPS), [], [epsb])
        kb.op(dve, lambda: nc.vector.memset(epsb.t[:, 1:2], 1.0), [epsb], [epsb])
        kb.op(dve, lambda: nc.vector.memset(tailP.t[:], 0.0), [], [tailP])
        kb.op(dve, lambda: nc.vector.memset(hT.t[:], 0.0), [], [hT])
        kb.op(dve, lambda: nc.vector.memset(hT_bf.t[:], 0.0), [], [hT_bf])
        kb.op(dve, lambda: nc.vector.memset(kT.t[:], 0.0), [], [kT])
        kb.op(dve, lambda: nc.vector.memset(vtok.t[:], 0.0), [], [vtok])

        with nc.allow_non_contiguous_dma(reason="small param loads"):
            for k in range(4):
                kb.dma(sp, convw.t[:, :, k], conv_w[k].rearrange("(c p) -> p c", p=128), writes=[convw], dsem=dmisc[3], join=True)
            kb.dma(sp, convb.t[:], conv_b.rearrange("o (c p) -> p (o c)", p=128), writes=[convb], dsem=msem())
            kb.dma(sp, ob.t[:], o_b.rearrange("o (c p) -> p (o c)", p=128), writes=[ob], dsem=msem())
            for c in range(8):
                A = c if c < 4 else c + 4
                for half, hh in ((0, A), (1, A + 4)):
                    kb.dma(sp, qkb.t[half * 64:(half + 1) * 64, c:c + 1],
                           qkv_b[0:1, hh * 64:(hh + 1) * 64].rearrange("o d -> d o"), writes=[qkb], dsem=dmisc[0], join=True)
            kb.dma(sp, qkb.t[:, 8:10], qkv_b[0:1, 1024:1280].rearrange("o (c p) -> p (o c)", p=128), writes=[qkb], dsem=dmisc[0], join=True)
        kb.dma(sp, vec32.t[:, 0, :], dt_bias.partition_broadcast(128), writes=[vec32], dsem=dmisc[1])
        kb.dma(sp, vec32.t[:, 1, :], a_log.partition_broadcast(128), writes=[vec32], dsem=dmisc[1], join=True)
        kb.dma(sp, vec32.t[:, 2, :], ssm_d.partition_broadcast(128), writes=[vec32], dsem=dmisc[1], join=True)
        with nc.allow_non_contiguous_dma(reason="small param loads"):
            kb.dma(sp, normwT.t[:], ssm_norm_w.rearrange("o (c p) -> p (o c)", p=128), writes=[normwT], dsem=msem())
        kb.dma(sp, sinkb.t[:], sinks.partition_broadcast(128), writes=[sinkb], dsem=msem())
        kb.dma(pool, wdt.t[:], ssm_in_w.rearrange("(c p) n -> p c n", p=128)[:, :, 5120:5152], writes=[wdt], dsem=msem())
        kb.dma(sp, kvb.t[:], qkv_b[0:1, 1024:1536].partition_broadcast(128), writes=[kvb], dsem=msem())
        kb.op(act, lambda: nc.scalar.activation(out=vec32.t[:, 1, :], in_=vec32.t[:, 1, :], func=AF.Exp), [vec32], [vec32])
        kb.op(dve, lambda: nc.vector.tensor_scalar(out=vec32.t[:, 1, :], in0=vec32.t[:, 1, :], scalar1=-1.0, scalar2=None, op0=ALU.mult), [vec32], [vec32])
        kb.op(dve, lambda: nc.vector.tensor_scalar(out=qkb.t[:, 0:8], in0=qkb.t[:, 0:8], scalar1=0.125, scalar2=None, op0=ALU.mult), [qkb], [qkb])

        kb.arena_open()
        cT = kb.abuf("cT", [128, 8, 17], F32)
        csT = kb.abuf("csT", [128, 8, 17], BF16)
        adab = kb.abuf("adab", [128, 2, 72], F32)
        npre = kb.abuf("npre", [128, 6, 8], F32)
        npost = kb.abuf("npost", [128, 6, 8], F32)
        modT = kb.abuf("modT", [128, 2, 72, 17], F32)
        ctok = kb.abuf("ctok", [17, D], F32)
        kb.dma(sp, ctok.t[:], cvec, writes=[ctok], dsem=msem())
        for c in range(8):
            pb = kb.ps()
            kb.op(pe, lambda: nc.tensor.transpose(out=pb.t[:, 0:17], in_=ctok.t[:, c * 128:(c + 1) * 128], identity=cstb.t[0:17, C_ID:C_ID + 17]), [ctok, cstb], [pb])
            kb.op(dve, lambda: nc.vector.tensor_copy(out=cT.t[:, c, :], in_=pb.t[:, 0:17]), [pb], [cT])
        kb.op(act, lambda: nc.scalar.activation(out=csT.t[:], in_=cT.t[:], func=AF.Silu), [cT], [csT])
        with nc.allow_non_contiguous_dma(reason="small param loads"):
            for i in range(2):
                kb.dma(sp, adab.t[:, i, :], ada_b[i].rearrange("(c p) -> p c", p=128), writes=[adab], dsem=dmisc[4], join=True)
                for sub in range(3):
                    kb.dma(sp, npre.t[:, i * 3 + sub, :], norm_pre[i, sub].rearrange("(c p) -> p c", p=128), writes=[npre], dsem=dmisc[5], join=True)
                    kb.dma(sp, npost.t[:, i * 3 + sub, :], norm_post[i, sub].rearrange("(c p) -> p c", p=128), writes=[npost], dsem=dmisc[6], join=True)
        for i in range(2):
            aw = ada_w[i].rearrange("(c p) n -> p c n", p=128)
            loads = []
            for nb in range(18):
                loads.append(lambda t, nb=nb: [(t[:, :].rearrange("p (c n) -> p c n", c=8), aw[:, :, nb * 512:(nb + 1) * 512])])
            wsm = WStream(kb, loads)
            for nb in range(18):
                sl = wsm.get(nb)
                wv = sl.t[:, :].rearrange("p (c n) -> p c n", c=8)
                pb = kb.ps()
                for m in range(4):
                    for kc in range(8):
                        kb.op(pe, lambda: nc.tensor.matmul(pb.t[:, m * 17:(m + 1) * 17], lhsT=wv[:, kc, m * 128:(m + 1) * 128], rhs=csT.t[:, kc, :], start=(kc == 0), stop=(kc == 7)), [sl, csT], [pb], acc=True)
                kb.op(dve, lambda: nc.vector.tensor_tensor(out=modT.t[:, i, nb * 4:(nb + 1) * 4, :], in0=pb.t[:, 0:68].rearrange("p (m s) -> p m s", m=4),
                                                           in1=adab.t[:, i, nb * 4:(nb + 1) * 4].unsqueeze(2).to_broadcast([128, 4, 17]), op=ALU.add), [pb, adab], [modT])
        for i in range(2):
            for sub in range(3):
                base = (i * 3 + sub) * 3
                sh = modT.t[:, i, (sub * 3 + 0) * 8:(sub * 3 + 0) * 8 + 8, :]
                sc = modT.t[:, i, (sub * 3 + 1) * 8:(sub * 3 + 1) * 8 + 8, :]
                gt = modT.t[:, i, (sub * 3 + 2) * 8:(sub * 3 + 2) * 8 + 8, :]
                npb = npre.t[:, i * 3 + sub, :].unsqueeze(2).to_broadcast([128, 8, 17])
                npo = npost.t[:, i * 3 + sub, :].unsqueeze(2).to_broadcast([128, 8, 17])
                kb.op(dve, lambda: nc.vector.scalar_tensor_tensor(out=PRE.t[:, base + 0, :, :], in0=sc, scalar=1.0, in1=npb, op0=ALU.add, op1=ALU.mult), [modT, npre], [PRE])
                kb.op(dve, lambda: nc.vector.tensor_copy(out=PRE.t[:, base + 1, :, :], in_=sh), [modT, PRE], [PRE])
                res = 1.0 if sub == 1 else 0.5
                kb.op(dve, lambda: nc.vector.scalar_tensor_tensor(out=PRE.t[:, base + 2, :, :], in0=gt, scalar=res, in1=npo, op0=ALU.mult, op1=ALU.mult), [modT, npost, PRE], [PRE])

        rb = kb.abuf("rb", [33, 16], F32)
        usb = kb.abuf("usb", [16, 384], F32)
        kb.op(dve, lambda: nc.vector.memset(rb.t[:], NEG), [], [rb])
        kb.dma(sp, rb.t[0:32, :], rel_bias, reads=[], writes=[rb], dsem=msem())
        pb = kb.ps()
        kb.op(pe, lambda: nc.tensor.matmul(pb.t[0:16, 0:384], lhsT=rb.t[:, :], rhs=cstb.t[0:33, C_OH:C_OH + 384], start=True, stop=True), [rb, cstb], [pb])
        kb.op(dve, lambda: nc.vector.tensor_copy(out=usb.t[:], in_=pb.t[0:16, 0:384]), [pb], [usb])
        uev = Buf(None)
        kb.dma(sp, uscr.ap(), usb.t[:], reads=[usb], writes=[uev], dsem=msem())
        kb.arena_close()

        def geom(kind):
            if kind == "P":
                return 512, 4, 1, 512
            return 128, 1, 16, 8

        def sumsq_rstd(src, sv, T):
            kb.op(act, lambda: nc.scalar.activation(out=hin.t[:, :, 0:T], in_=sv, func=AF.Square), [src], [hin])
            pb = kb.ps()
            for c in range(8):
                kb.op(pe, lambda: nc.tensor.matmul(pb.t[:, 0:T], lhsT=ones_bf(), rhs=hin.t[:, c, 0:T], start=(c == 0), stop=(c == 7)), [hin, cbf], [pb], acc=True)
            kb.op(act, lambda: nc.scalar.activation(out=rstd.t[:, 0:T], in_=pb.t[:, 0:T], func=AF.Ln, bias=epsb.t[:, 0:1], scale=1.0 / D), [pb, epsb], [rstd])
            kb.op(act, lambda: nc.scalar.activation(out=rstd.t[:, 0:T], in_=rstd.t[:, 0:T], func=AF.Exp, scale=-0.5), [rstd], [rstd])

        def norm_mod(i, sub, kind):
            T, NCH, nseq, tps = geom(kind)
            base = (i * 3 + sub) * 3
            sumsq_rstd(x_fm, x_fm.t[:, :, 0:T], T)
            for c in range(8):
                t1 = tmp()
                kb.op(dve, lambda: nc.vector.tensor_tensor(out=t1.t[:, 0:T], in0=x_fm.t[:, c, 0:T], in1=rstd.t[:, 0:T], op=ALU.mult), [x_fm, rstd], [t1])
                if kind == "P":
                    kb.op(act, lambda: nc.scalar.activation(out=hin.t[:, c, 0:T], in_=t1.t[:, 0:T], func=AF.Identity,
                                                            bias=PRE.t[:, base + 1, c, 0:1], scale=PRE.t[:, base + 0, c, 0:1]), [t1, PRE], [hin])
                else:
                    v3 = lambda ap: ap.rearrange("p (s t) -> p s t", s=16)
                    kb.op(dve, lambda: nc.vector.tensor_tensor(out=v3(t1.t[:, 0:T]), in0=v3(t1.t[:, 0:T]), in1=PRE.t[:, base + 0, c, 1:17].unsqueeze(2).to_broadcast([128, 16, 8]), op=ALU.mult), [t1, PRE], [t1])
                    kb.op(dve, lambda: nc.vector.tensor_tensor(out=v3(hin.t[:, c, 0:T]), in0=v3(t1.t[:, 0:T]), in1=PRE.t[:, base + 1, c, 1:17].unsqueeze(2).to_broadcast([128, 16, 8]), op=ALU.add), [t1, PRE], [hin])

        def post(i, sub, kind, f_fm, fv):
            T, NCH, nseq, tps = geom(kind)
            base = (i * 3 + sub) * 3
            sumsq_rstd(f_fm, fv, T)
            for c in range(8):
                t1 = tmp()
                kb.op(dve, lambda: nc.vector.tensor_tensor(out=t1.t[:, 0:T], in0=fv[:, c, :], in1=rstd.t[:, 0:T], op=ALU.mult), [f_fm, rstd], [t1])
                if kind == "P":
                    kb.op(dve, lambda: nc.vector.scalar_tensor_tensor(out=x_fm.t[:, c, 0:T], in0=t1.t[:, 0:T], scalar=PRE.t[:, base + 2, c, 0:1], in1=x_fm.t[:, c, 0:T], op0=ALU.mult, op1=ALU.add), [t1, PRE, x_fm], [x_fm])
                else:
                    v3 = lambda ap: ap.rearrange("p (s t) -> p s t", s=16)
                    kb.op(dve, lambda: nc.vector.tensor_tensor(out=v3(t1.t[:, 0:T]), in0=v3(t1.t[:, 0:T]), in1=PRE.t[:, base + 2, c, 1:17].unsqueeze(2).to_broadcast([128, 16, 8]), op=ALU.mult), [t1, PRE], [t1])
                    kb.op(dve, lambda: nc.vector.tensor_tensor(out=x_fm.t[:, c, 0:T], in0=x_fm.t[:, c, 0:T], in1=t1.t[:, 0:T], op=ALU.add), [t1, x_fm], [x_fm])

        def ffn(i, which, kind):
            T, NCH, nseq, tps = geom(kind)
            sub = 0 if which == 0 else 2
            norm_mod(i, sub, kind)
            win = ffn_w_in[i, which].rearrange("(c p) n -> p c n", p=128)
            wout = ffn_w_out[i, which].rearrange("(j p) n -> p j n", p=128)
            loads = []
            for j in range(NJ):
                loads.append(lambda t, j=j: [
                    (t[:, 0:2048].rearrange("p (c n) -> p c n", c=8)[:, :, 0:128], win[:, :, j * 128:(j + 1) * 128]),
                    (t[:, 0:2048].rearrange("p (c n) -> p c n", c=8)[:, :, 128:256], win[:, :, DFF + j * 128:DFF + (j + 1) * 128])])
            for m in range(8):
                loads.append(lambda t, m=m: [(t[:, 0:NJ * 128].rearrange("p (j n) -> p j n", j=NJ), wout[:, :, m * 128:(m + 1) * 128])])
            wsm = WStream(kb, loads)
            kb.arena_open()
            actb = [kb.abuf(f"act{j}", [128, 512], BF16) for j in range(NJ)]
            sg = [kb.abuf(f"sg{j}", [128, 512], F32) for j in range(2)]
            f_fm = kb.abuf("f_fm", [128, 8, T], F32)
            for j in range(NJ):
                sl = wsm.get(j)
                wv = sl.t[:, 0:2048].rearrange("p (c n) -> p c n", c=8)
                pg = kb.ps(); pu = kb.ps()
                for kc in range(8):
                    kb.op(pe, lambda: nc.tensor.matmul(pg.t[:, 0:T], lhsT=wv[:, kc, 0:128], rhs=hin.t[:, kc, 0:T], start=(kc == 0), stop=(kc == 7)), [sl, hin], [pg], acc=True)
                for kc in range(8):
                    kb.op(pe, lambda: nc.tensor.matmul(pu.t[:, 0:T], lhsT=wv[:, kc, 128:256], rhs=hin.t[:, kc, 0:T], start=(kc == 0), stop=(kc == 7)), [sl, hin], [pu], acc=True)
                s = sg[j % 2]
                kb.op(act, lambda: nc.scalar.activation(out=s.t[:, 0:T], in_=pg.t[:, 0:T], func=AF.Silu), [pg], [s])
                kb.op(dve, lambda: nc.vector.tensor_tensor(out=actb[j].t[:, 0:T], in0=s.t[:, 0:T], in1=pu.t[:, 0:T], op=ALU.mult), [s, pu], [actb[j]])
            for m in range(8):
                sl = wsm.get(NJ + m)
                wv = sl.t[:, 0:NJ * 128].rearrange("p (j n) -> p j n", j=NJ)
                pb = kb.ps()
                for j in range(NJ):
                    kb.op(pe, lambda: nc.tensor.matmul(pb.t[:, 0:T], lhsT=wv[:, j, :], rhs=actb[j].t[:, 0:T], start=(j == 0), stop=(j == NJ - 1)), [sl, actb[j]], [pb], acc=True)
                kb.op(act, lambda: nc.scalar.copy(out=f_fm.t[:, m, 0:T], in_=pb.t[:, 0:T]), [pb], [f_fm])
            post(i, sub, kind, f_fm, f_fm.t[:, :, :])
            kb.arena_close()

        def ssd(kind, last):
            T, NCH, nseq, tps = geom(kind)
            P = (kind == "P")
            norm_mod(0, 1, kind)
            inw = ssm_in_w.rearrange("(c p) n -> p c n", p=128)
            outw = ssm_out_w.rearrange("(c p) n -> p c n", p=128)
            loads = []
            for q in range(6):
                loads.append(lambda t, q=q: [(t[:, :].rearrange("p (c n) -> p c n", c=8), inw[:, :, 2048 + q * 512:2048 + (q + 1) * 512])])
            for zb in range(4):
                loads.append(lambda t, zb=zb: [(t[:, :].rearrange("p (c n) -> p c n", c=8), inw[:, :, zb * 512:(zb + 1) * 512])])
            for mm in range(4):
                loads.append(lambda t, mm=mm: [(t[:, :].rearrange("p (c n) -> p c n", c=16), outw[:, :, mm * 256:(mm + 1) * 256])])
            wsm = WStream(kb, loads)
            tri_o, stri_o = (C_TRIP, C_STRIP) if P else (C_TRIS, C_STRIS)
            tri = lambda: cstb.t[:, tri_o:tri_o + 128]
            stri = lambda: cstb.t[:, stri_o:stri_o + 128]

            kb.arena_open()
            xsT = kb.abuf("xsT", [128, 16, T], BF16)
            BT = kb.abuf("BT", [128, 4, T], BF16)
            CT = kb.abuf("CT", [128, 4, T], BF16)
            raw = [kb.abuf(f"raw{k}", [128, nseq, 3 + tps], F32) for k in range(2)]
            cacc = [kb.abuf(f"cacc{k}", [128, nseq, tps], F32) for k in range(2)]
            xs_tok = kb.abuf("xs_tok", [128, NCH * DIN], BF16)
            xsv = xs_tok.t[:, :].rearrange("p (c n) -> p c n", c=NCH)
            tail = tailP if P else kb.abuf("tailS", [128, 24, 48], F32)
            B_tok = kb.abuf("B_tok", [128, NCH, 512], BF16)
            sz = kb.abuf("sz", [128, NCH, DIN], BF16)
            ynT = xsT
            dA = kb.abuf("dA", [128, NCH, 32], F32)
            dtv = kb.abuf("dtv", [128, NCH, 32], F32)
            sp1 = kb.abuf("sp1", [128, 32], F32)
            sp2 = kb.abuf("sp2", [128, 32], F32)

            if not P:
                hist = kb.abuf("hist", [48, 3072], F32)
                kb.dma(sp, hist.t[:], st_conv, writes=[hist], dsem=msem())
                for cc in range(24):
                    pb = kb.ps()
                    kb.op(pe, lambda: nc.tensor.transpose(out=pb.t[:, 0:48], in_=hist.t[:, cc * 128:(cc + 1) * 128], identity=cstb.t[0:48, C_ID:C_ID + 48]), [hist, cstb], [pb])
                    kb.op(dve, lambda: nc.vector.tensor_copy(out=tail.t[:, cc, :], in_=pb.t[:, 0:48]), [pb], [tail])

            for ch in range(NCH):
                pb = kb.ps()
                for kc in range(8):
                    kb.op(pe, lambda: nc.tensor.matmul(pb.t[:, 0:32], lhsT=hin.t[:, kc, ch * 128:(ch + 1) * 128], rhs=wdt.t[:, kc, :], start=(kc == 0), stop=(kc == 7)), [hin, wdt], [pb], acc=True)
                kb.op(dve, lambda: nc.vector.tensor_tensor(out=sp1.t[:], in0=pb.t[:, 0:32], in1=vec32.t[:, 0, :], op=ALU.add), [pb, vec32], [sp1])
                kb.op(dve, lambda: nc.vector.tensor_scalar(out=sp2.t[:], in0=sp1.t[:], scalar1=-1.0, scalar2=None, op0=ALU.mult), [sp1], [sp2])
                kb.op(dve, lambda: nc.vector.tensor_tensor(out=sp2.t[:], in0=sp2.t[:], in1=sp1.t[:], op=ALU.max), [sp1, sp2], [sp2])
                kb.op(act, lambda: nc.scalar.activation(out=sp2.t[:], in_=sp2.t[:], func=AF.Exp, scale=-1.0), [sp2], [sp2])
                kb.op(act, lambda: nc.scalar.activation(out=sp2.t[:], in_=sp2.t[:], func=AF.Ln, bias=epsb.t[:, 1:2], scale=1.0), [sp2, epsb], [sp2])
                kb.op(dve, lambda: nc.vector.tensor_scalar(out=sp1.t[:], in0=sp1.t[:], scalar1=0.0, scalar2=None, op0=ALU.max), [sp1], [sp1])
                kb.op(dve, lambda: nc.vector.tensor_tensor(out=dtv.t[:, ch, :], in0=sp1.t[:], in1=sp2.t[:], op=ALU.add), [sp1, sp2], [dtv])
                kb.op(dve, lambda: nc.vector.tensor_tensor(out=dA.t[:, ch, :], in0=dtv.t[:, ch, :], in1=vec32.t[:, 1, :], op=ALU.mult), [dtv, vec32], [dA])

            for cc in range(24):
                sl = wsm.get(cc // 4)
                wv = sl.t[:, :].rearrange("p (c n) -> p c n", c=8)
                pb = kb.ps()
                for kc in range(8):
                    kb.op(pe, lambda: nc.tensor.matmul(pb.t[:, 0:T], lhsT=wv[:, kc, (cc % 4) * 128:(cc % 4 + 1) * 128], rhs=hin.t[:, kc, 0:T], start=(kc == 0), stop=(kc == 7)), [sl, hin], [pb], acc=True)
                rw = raw[cc % 2]; ca = cacc[cc % 2]
                kb.op(dve, lambda: nc.vector.tensor_copy(out=rw.t[:, :, 0:3], in_=tail.t[:, cc, :].rearrange("p (s j) -> p s j", j=3)), [tail], [rw])
                kb.op(act, lambda: nc.scalar.copy(out=rw.t[:, :, 3:3 + tps], in_=pb.t[:, 0:T].rearrange("p (s t) -> p s t", s=nseq)), [pb, rw], [rw])
                kb.op(dve, lambda: nc.vector.tensor_copy(out=tail.t[:, cc, :].rearrange("p (s j) -> p s j", j=3), in_=rw.t[:, :, tps:tps + 3]), [rw, tail], [tail])
                kb.op(act, lambda: nc.scalar.activation(out=ca.t[:], in_=rw.t[:, :, 0:tps], func=AF.Identity, bias=convb.t[:, cc:cc + 1], scale=convw.t[:, cc, 0:1]), [rw, convw, convb], [ca])
                for k in range(1, 4):
                    kb.op(dve, lambda: nc.vector.scalar_tensor_tensor(out=ca.t[:], in0=rw.t[:, :, k:k + tps], scalar=convw.t[:, cc, k:k + 1], in1=ca.t[:], op0=ALU.mult, op1=ALU.add), [rw, convw, ca], [ca])
                if cc < 16:
                    dstb, dst = xsT, xsT.t[:, cc, :]
                elif cc < 20:
                    dstb, dst = BT, BT.t[:, cc - 16, :]
                else:
                    dstb, dst = CT, CT.t[:, cc - 20, :]
                kb.op(act, lambda: nc.scalar.activation(out=dst.rearrange("p (s t) -> p s t", s=nseq), in_=ca.t[:], func=AF.Silu), [ca], [dstb])

            if last and P:
                with nc.allow_non_contiguous_dma(reason="small state out"):
                    for j3 in range(3):
                        out_dma(sp, conv_p[j3].rearrange("(c p) -> p c", p=128), tail.t[:, :, j3], [tail])
            if last and not P:
                nr = nseq * 3
                cso = hist
                for cc in range(24):
                    pb = kb.ps()
                    kb.op(pe, lambda: nc.tensor.transpose(out=pb.t[0:nr, 0:128], in_=tail.t[:, cc, :], identity=ident()), [tail, cstb], [pb])
                    kb.op(dve, lambda: nc.vector.tensor_copy(out=cso.t[:, cc * 128:(cc + 1) * 128], in_=pb.t[0:nr, 0:128]), [pb], [cso])
                out_dma(sp, conv_p if P else conv_s, cso.t[:], [cso])

            for ch in range(NCH):
                for q in range(4):
                    pb = kb.ps()
                    pbv = pb.t[:].bitcast(BF16)
                    for k in range(4):
                        cc = q * 4 + k
                        kb.op(pe, lambda: nc.tensor.transpose(out=pbv[:, k * 128:(k + 1) * 128], in_=xsT.t[:, cc, ch * 128:(ch + 1) * 128], identity=ident_bf()), [xsT, cbf], [pb], acc=True)
                    kb.op(act, lambda: nc.scalar.copy(out=xsv[:, ch, q * 512:(q + 1) * 512], in_=pbv[:, 0:512]), [pb], [xs_tok])
                pb = kb.ps()
                pbv = pb.t[:].bitcast(BF16)
                for g in range(4):
                    kb.op(pe, lambda: nc.tensor.transpose(out=pbv[:, g * 128:(g + 1) * 128], in_=BT.t[:, g, ch * 128:(ch + 1) * 128], identity=ident_bf()), [BT, cbf], [pb], acc=True)
                kb.op(act, lambda: nc.scalar.copy(out=B_tok.t[:, ch, :], in_=pbv[:, 0:512]), [pb], [B_tok])

            for zb in range(4):
                sl = wsm.get(6 + zb)
                wv = sl.t[:, :].rearrange("p (c n) -> p c n", c=8)
                for ch in range(NCH):
                    pb = kb.ps()
                    for kc in range(8):
                        kb.op(pe, lambda: nc.tensor.matmul(pb.t[:, :], lhsT=hin.t[:, kc, ch * 128:(ch + 1) * 128], rhs=wv[:, kc, :], start=(kc == 0), stop=(kc == 7)), [sl, hin], [pb], acc=True)
                    kb.op(act, lambda: nc.scalar.activation(out=sz.t[:, ch, zb * 512:(zb + 1) * 512], in_=pb.t[:, :], func=AF.Silu), [pb], [sz])

            R1 = kb.abuf("R1", [128, 16, 128], F32)
            Lsb = [kb.abuf(f"Lsb{k}", [128, 512], F32) for k in range(2)]
            wTg = [kb.abuf(f"wT{k}", [128, 8, 128], BF16) for k in range(2)]
            cbm = kb.abuf("cbm", [128, 4, 128], F32)
            xdt = kb.abuf("xdt", [128, DIN], BF16)
            xdec = xdt if P else kb.abuf("xdec", [128, DIN], BF16)
            sm = kb.abuf("sm", [128, 4, 32], F32)
            ygb = [kb.abuf(f"yg{k}", [128, 512], F32) for k in range(2)]
            ynb = [kb.abuf(f"yn{k}", [128, 512], BF16) for k in range(2)]
            ssq = kb.abuf("ssq", [128, 8], F32)
            junk = kb.abuf("junk", [128, 512], BF16)
            v32 = lambda ap: ap.rearrange("p (h d) -> p h d", h=32)
            if not P:
                dAexp = R1
                dAv = R1.t[:, :, :].rearrange("p h l -> p (h l)")
                decS = kb.abuf("decS", [128, 16, 16], F32)
                CTmj = [kb.abuf(f"CTmj{k}", [128, 4, 128], BF16) for k in range(2)]
                h0b = [kb.abuf(f"h0b{k}", [128, 16, 128], BF16) for k in range(2)]
                h0f = kb.abuf("h0f", [128, 16, 128], F32)
                hnv = hT.t[:, :].rearrange("p (c n) -> p c n", c=16)
                Bm = [kb.abuf(f"Bm{k}", [128, 512], BF16) for k in range(2)]
                mJL = kb.abuf("mJL", [128, 16, 128], BF16)
                kb.op(dve, lambda: nc.vector.memset(mJL.t[:], 0.0), [], [mJL])
                for j in range(16):
                    kb.op(dve, lambda: nc.vector.memset(mJL.t[:, j, j * 8:(j + 1) * 8], 1.0), [mJL], [mJL])

            for ch in range(NCH):
                csl = slice(ch * 128, (ch + 1) * 128)
                pv = kb.ps()
                kb.op(pe, lambda: nc.tensor.matmul(pv.t[:, 0:32], lhsT=tri(), rhs=dA.t[:, ch, :], start=True, stop=True), [dA, cstb], [pv])
                kb.op(pe, lambda: nc.tensor.matmul(pv.t[:, 32:64], lhsT=stri(), rhs=dA.t[:, ch, :], start=True, stop=True), [dA, cstb], [pv], acc=True)
                kb.op(pe, lambda: nc.tensor.matmul(pv.t[:, 64:96], lhsT=ones_f(), rhs=dA.t[:, ch, :], start=True, stop=True), [dA, cstb], [pv], acc=True)
                kb.op(act, lambda: nc.scalar.activation(out=sm.t[:, 0:3, :], in_=pv.t[:, 0:96].rearrange("p (a h) -> p a h", a=3), func=AF.Exp), [pv], [sm])
                pc = kb.ps()
                for g in range(4):
                    kb.op(pe, lambda: nc.tensor.matmul(pc.t[:, g * 128:(g + 1) * 128], lhsT=BT.t[:, g, csl], rhs=CT.t[:, g, csl], start=True, stop=True), [BT, CT], [pc], acc=True)
                kb.op(dve, lambda: nc.vector.tensor_tensor(out=cbm.t[:], in0=pc.t[:, :].rearrange("p (g l) -> p g l", g=4), in1=tri().unsqueeze(1).to_broadcast([128, 4, 128]), op=ALU.mult), [pc, cstb], [cbm])
                kb.op(dve, lambda: nc.vector.tensor_tensor(out=v32(xdt.t[:]), in0=v32(xsv[:, ch, :]), in1=dtv.t[:, ch, :].unsqueeze(2).to_broadcast([128, 32, 64]), op=ALU.mult), [xs_tok, dtv], [xdt])
                if not P:
                    kb.op(dve, lambda: nc.vector.tensor_tensor(out=v32(xdec.t[:]), in0=v32(xdt.t[:]), in1=sm.t[:, 1, :].unsqueeze(2).to_broadcast([128, 32, 64]), op=ALU.mult), [xdt, sm], [xdec])
                    kb.op(dve, lambda: nc.vector.tensor_copy(out=v32(dAv), in_=dA.t[:, 0, :].unsqueeze(2).to_broadcast([128, 32, 64])), [dA], [dAexp])
                    pd = kb.ps()
                    for c in range(16):
                        kb.op(pe, lambda: nc.tensor.matmul(pd.t[:, c * 16:(c + 1) * 16], lhsT=dAv[:, c * 128:(c + 1) * 128], rhs=cstb.t[:, C_SEL:C_SEL + 16], start=True, stop=True), [dAexp, cstb], [pd], acc=True)
                    kb.op(act, lambda: nc.scalar.activation(out=decS.t[:], in_=pd.t[:, 0:256].rearrange("p (c j) -> p c j", c=16), func=AF.Exp), [pd], [decS])
                yo = []
                if P:
                    for g in range(4):
                        pb = kb.ps_hold()
                        kb.op(pe, lambda: nc.tensor.matmul(pb.t[:, :], lhsT=CT.t[:, g, csl], rhs=hT_bf.t[:, g * 512:(g + 1) * 512], start=True, stop=True), [CT, hT_bf], [pb])
                        yo.append(pb)
                else:
                    yo = [kb.ps_hold() for g in range(4)]
                    for j in range(NSEQ_S):
                        hb = h0b[j % 2]; hf = h0f; ht = hT_bf; hn = hT; bm = Bm[j % 2]; CTm = CTmj[j % 2]
                        kb.op(dve, lambda: nc.vector.tensor_tensor(out=CTm.t[:], in0=CT.t[:, :, :], in1=mJL.t[:, j, :].unsqueeze(1).to_broadcast([128, 4, 128]), op=ALU.mult), [CT, mJL], [CTm])
                        kb.dma(pool, hb.t[:], st_ssm[j].rearrange("(c p) n -> p c n", p=128), writes=[hb], dsem=msem())
                        kb.dma(sp, hf.t[:], st_ssm[j].rearrange("(c p) n -> p c n", p=128), writes=[hf], dsem=msem())
                        for q in range(4):
                            pb = kb.ps()
                            pbv = pb.t[:].bitcast(BF16)
                            for k in range(4):
                                c = q * 4 + k
                                kb.op(pe, lambda: nc.tensor.transpose(out=pbv[:, k * 128:(k + 1) * 128], in_=hb.t[:, c, :], identity=ident_bf()), [hb, cbf], [pb], acc=True)
                            kb.op(act, lambda: nc.scalar.copy(out=ht.t[:, q * 512:(q + 1) * 512], in_=pbv[:, 0:512]), [pb], [ht])
                        for g in range(4):
                            kb.op(pe, lambda: nc.tensor.matmul(yo[g].t[:, :], lhsT=CTm.t[:, g, :], rhs=ht.t[:, g * 512:(g + 1) * 512], start=(j == 0), stop=(j == NSEQ_S - 1)), [CTm, ht], [yo[g]], acc=True)
                        kb.op(dve, lambda: nc.vector.tensor_scalar(out=bm.t[:], in0=B_tok.t[:, 0, :], scalar1=cstb.t[:, C_SEL + j:C_SEL + j + 1], scalar2=None, op0=ALU.mult), [B_tok, cstb], [bm])
                        for q in range(4):
                            pb = kb.ps()
                            for k in range(4):
                                c = q * 4 + k
                                kb.op(pe, lambda: nc.tensor.matmul(pb.t[:, k * 128:(k + 1) * 128], lhsT=xdec.t[:, c * 128:(c + 1) * 128], rhs=bm.t[:, (c // 4) * 128:(c // 4 + 1) * 128], start=True, stop=True), [xdec, bm], [pb], acc=True)
                            for k in range(4):
                                c = q * 4 + k
                                kb.op(dve, lambda: nc.vector.scalar_tensor_tensor(out=hnv[:, c, :], in0=hf.t[:, c, :], scalar=decS.t[:, c, j:j + 1], in1=pb.t[:, k * 128:(k + 1) * 128], op0=ALU.mult, op1=ALU.add), [hf, decS, pb], [hn])
                        out_dma(sp, ssm_s[j].rearrange("(c p) n -> p c n", p=128), hnv, [hn])
                kb.op(dve, lambda: nc.vector.memset(ssq.t[:], 0.0), [ssq], [ssq])
                for g in range(4):
                    if g % 2 == 0:
                        kb.op(dve, lambda: nc.vector.tensor_tensor(out=R1.t[:], in0=dA.t[:, ch, g * 8:g * 8 + 16].unsqueeze(2).to_broadcast([128, 16, 128]),
                                                                   in1=tri().unsqueeze(1).to_broadcast([128, 16, 128]), op=ALU.mult), [dA, cstb], [R1])
                    wT = wTg[g % 2]
                    for b in range(2):
                        pb = kb.ps()
                        r0 = (g % 2) * 8 + b * 4
                        kb.op(pe, lambda: nc.tensor.matmul(pb.t[:, :], lhsT=stri(), rhs=R1.t[:, :, :].rearrange("p h l -> p (h l)")[:, r0 * 128:(r0 + 4) * 128], start=True, stop=True), [R1, cstb], [pb])
                        L = Lsb[b]
                        kb.op(act, lambda: nc.scalar.activation(out=L.t[:], in_=pb.t[:, :], func=AF.Exp), [pb], [L])
                        kb.op(dve, lambda: nc.vector.tensor_tensor(out=wT.t[:, b * 4:(b + 1) * 4, :], in0=L.t[:].rearrange("p (h l) -> p h l", h=4),
                                                                   in1=cbm.t[:, g, :].unsqueeze(1).to_broadcast([128, 4, 128]), op=ALU.mult), [L, cbm], [wT])
                    pb = kb.ps()
                    for r in range(8):
                        h = g * 8 + r
                        kb.op(pe, lambda: nc.tensor.matmul(pb.t[:, r * 64:(r + 1) * 64], lhsT=wT.t[:, r, :], rhs=xdt.t[:, h * 64:(h + 1) * 64], start=True, stop=True), [wT, xdt], [pb], acc=True)
                    gs = slice(g * 512, (g + 1) * 512)
                    v3 = lambda ap: ap.rearrange("p (h d) -> p h d", h=8)
                    eab = sm.t[:, 0, g * 8:(g + 1) * 8].unsqueeze(2).to_broadcast([128, 8, 64])
                    Db = vec32.t[:, 2, g * 8:(g + 1) * 8].unsqueeze(2).to_broadcast([128, 8, 64])
                    t1 = tmp(); t2 = tmp()
                    yg = ygb[g % 2]; yn = ynb[g % 2]
                    kb.op(dve, lambda: nc.vector.tensor_tensor(out=v3(t1.t[:]), in0=v3(yo[g].t[:, :]), in1=eab, op=ALU.mult), [yo[g], sm], [t1])
                    kb.ps_release(yo[g])
                    kb.op(dve, lambda: nc.vector.tensor_tensor(out=t1.t[:], in0=t1.t[:], in1=pb.t[:, :], op=ALU.add), [t1, pb], [t1])
                    kb.op(dve, lambda: nc.vector.tensor_tensor(out=v3(t2.t[:]), in0=v3(xsv[:, ch, gs]), in1=Db, op=ALU.mult), [xs_tok, vec32], [t2])
                    kb.op(dve, lambda: nc.vector.tensor_tensor(out=t1.t[:], in0=t1.t[:], in1=t2.t[:], op=ALU.add), [t1, t2], [t1])
                    kb.op(dve, lambda: nc.vector.tensor_tensor(out=yg.t[:], in0=t1.t[:], in1=sz.t[:, ch, gs], op=ALU.mult), [t1, sz], [yg])
                    kb.op(act, lambda: nc.scalar.activation(out=junk.t[:], in_=yg.t[:], func=AF.Square, accum_out=ssq.t[:, g:g + 1]), [yg, ssq], [junk, ssq])
                    kb.op(act, lambda: nc.scalar.activation(out=ssq.t[:, 4 + g:5 + g], in_=ssq.t[:, g:g + 1], func=AF.Ln, bias=epsb.t[:, 0:1], scale=1.0 / 512), [ssq, epsb], [ssq])
                    kb.op(act, lambda: nc.scalar.activation(out=ssq.t[:, 4 + g:5 + g], in_=ssq.t[:, 4 + g:5 + g], func=AF.Exp, scale=-0.5), [ssq], [ssq])
                    kb.op(dve, lambda: nc.vector.tensor_scalar(out=yn.t[:], in0=yg.t[:], scalar1=ssq.t[:, 4 + g:5 + g], scalar2=None, op0=ALU.mult), [yg, ssq], [yn])
                    pq = kb.ps()
                    pqv = pq.t[:].bitcast(BF16)
                    for k in range(4):
                        kb.op(pe, lambda: nc.tensor.transpose(out=pqv[:, k * 128:(k + 1) * 128], in_=yn.t[:, k * 128:(k + 1) * 128], identity=ident_bf()), [yn, cbf], [pq], acc=True)
                    kb.op(dve, lambda: nc.vector.tensor_tensor(out=ynT.t[:, g * 4:(g + 1) * 4, csl], in0=pqv[:, 0:512].rearrange("p (k t) -> p k t", k=4),
                                                               in1=normwT.t[:, g * 4:(g + 1) * 4].unsqueeze(2).to_broadcast([128, 4, 128]), op=ALU.mult), [pq, normwT], [ynT])
                if P:
                    kb.op(dve, lambda: nc.vector.tensor_tensor(out=v32(xdt.t[:]), in0=v32(xdt.t[:]), in1=sm.t[:, 1, :].unsqueeze(2).to_broadcast([128, 32, 64]), op=ALU.mult), [xdt, sm], [xdt])
                    for g in range(4):
                        gs = slice(g * 512, (g + 1) * 512)
                        pb = kb.ps()
                        kb.op(pe, lambda: nc.tensor.matmul(pb.t[:, :], lhsT=B_tok.t[:, ch, g * 128:(g + 1) * 128], rhs=xdt.t[:, gs], start=True, stop=True), [B_tok, xdt], [pb])
                        v3 = lambda ap: ap.rearrange("p (h d) -> p h d", h=8)
                        kb.op(dve, lambda: nc.vector.tensor_tensor(out=v3(hT.t[:, gs]), in0=v3(hT.t[:, gs]), in1=sm.t[:, 2, g * 8:(g + 1) * 8].unsqueeze(2).to_broadcast([128, 8, 64]), op=ALU.mult), [hT, sm], [hT])
                        kb.op(dve, lambda: nc.vector.tensor_tensor(out=hT.t[:, gs], in0=hT.t[:, gs], in1=pb.t[:, :], op=ALU.add), [hT, pb], [hT])
                    kb.op(act, lambda: nc.scalar.copy(out=hT_bf.t[:], in_=hT.t[:]), [hT], [hT_bf])

            if P and last:
                houtv = sz.t[:, :, :].rearrange("p c n -> p (c n)").bitcast(F32)[:, 0:2048].rearrange("p (c n) -> p c n", c=16)
                hout = sz
                for c in range(16):
                    pb = kb.ps()
                    kb.op(pe, lambda: nc.tensor.transpose(out=pb.t[:, 0:128], in_=hT.t[:, c * 128:(c + 1) * 128], identity=ident()), [hT, cstb], [pb])
                    kb.op(dve, lambda: nc.vector.tensor_copy(out=houtv[:, c, :], in_=pb.t[:, 0:128]), [pb], [hout])
                out_dma(sp, ssm_p.rearrange("(c p) n -> p c n", p=128), houtv, [hout])

            fv = xs_tok.t[:, :].bitcast(F32).rearrange("p (m t) -> p m t", m=8)
            for m in range(8):
                sl = wsm.get(10 + m // 2)
                wv = sl.t[:, :].rearrange("p (c n) -> p c n", c=16)
                pb = kb.ps()
                for kc in range(16):
                    kb.op(pe, lambda: nc.tensor.matmul(pb.t[:, 0:T], lhsT=wv[:, kc, (m % 2) * 128:(m % 2 + 1) * 128], rhs=ynT.t[:, kc, 0:T], start=(kc == 0), stop=(kc == 15)), [sl, ynT], [pb], acc=True)
                kb.op(act, lambda: nc.scalar.copy(out=fv[:, m, :], in_=pb.t[:, 0:T]), [pb], [xs_tok])
            post(0, 1, kind, xs_tok, fv)
            kb.arena_close()

        def swa(kind, first, last):
            T, NCH, nseq, tps = geom(kind)
            P = (kind == "P")
            norm_mod(1, 1, kind)
            qw = qkv_w.rearrange("(c p) n -> p c n", p=128)
            ow = o_w.rearrange("(c p) n -> p c n", p=128)
            loads = []
            for half in range(2):
                loads.append(lambda t, half=half: [(t[:, :].rearrange("p (c n) -> p c n", c=8), qw[:, :, half * 512:(half + 1) * 512])])
            loads.append(lambda t: [(t[:, :].rearrange("p (c n) -> p c n", c=8), qw[:, :, 1024:1536])])
            for mm in range(2):
                loads.append(lambda t, mm=mm: [(t[:, :].rearrange("p (c n) -> p c n", c=8), ow[:, :, mm * 512:(mm + 1) * 512])])
            wsm = WStream(kb, loads)

            kb.arena_open()
            qT = kb.abuf("qT", [128, 8, T], BF16)
            kv_tok = kb.abuf("kv_tok", [128, NCH, 512], F32)
            biasT = kb.abuf("biasT", [128, 16, 256], F32)
            tq = [kb.abuf(f"tq{k}", [128, 256], F32) for k in range(2)]
            sS = [kb.abuf(f"sS{k}", [128, 256], F32) for k in range(2)]
            eS = [kb.abuf(f"eS{k}", [128, 256], BF16) for k in range(2)]
            en = [kb.abuf(f"en{k}", [128, 256], BF16) for k in range(2)]
            pT = [kb.abuf(f"pT{k}", [128, 2, 128], BF16) for k in range(2)]
            st = [kb.abuf(f"st{k}", [128, 8], F32) for k in range(2)]
            o_tok = kb.abuf("o_tok", [128, D], BF16)
            attnT = kb.abuf("attnT", [128, 8, T], BF16)
            f_fm = kb.abuf("f_fm", [128, 8, T], F32)

            with nc.allow_non_contiguous_dma(reason="toeplitz"):
                for h in range(16):
                    t = tq[h % 2]
                    src = bass.AP(uscr, h * 384, [[1, 128], [1, 256]])
                    kb.dma(sp, t.t[:], src, reads=[uev], writes=[t], dsem=msem())
                    pb = kb.ps()
                    jo = C_J if P else C_JREP
                    kb.op(pe, lambda: nc.tensor.matmul(pb.t[:, 0:256], lhsT=cstb.t[:, jo:jo + 128], rhs=t.t[:], start=True, stop=True), [t, cstb], [pb])
                    kb.op(dve, lambda: nc.vector.tensor_copy(out=biasT.t[:, h, :], in_=pb.t[:, 0:256]), [pb], [biasT])

            if flags.get("swa_stop") == 1:
                kb.arena_close(); return
            wqp = [kb.abuf(f"wqp{k}", [128, 8, 4, 2, 64], BF16) for k in range(2)]
            for half in range(2):
                sl = wsm.get(half)
                nat = sl.t[:, :].rearrange("p (c n) -> p c n", c=8)
                for a in range(4):
                    for b in range(2):
                        kb.op(dve, lambda: nc.vector.tensor_copy(out=wqp[half].t[:, :, a, b, :], in_=nat[:, :, b * 256 + a * 64:b * 256 + (a + 1) * 64]), [sl], [wqp[half]])
            for c in range(8):
                wb = wqp[c // 4]
                wv = wb.t[:, :, :, :, :].rearrange("p c a b d -> p c a (b d)")
                pb = kb.ps()
                for kc in range(8):
                    kb.op(pe, lambda: nc.tensor.matmul(pb.t[:, 0:T], lhsT=wv[:, kc, c % 4, :], rhs=hin.t[:, kc, 0:T], start=(kc == 0), stop=(kc == 7)), [wb, hin], [pb], acc=True)
                kb.op(act, lambda: nc.scalar.activation(out=qT.t[:, c, :], in_=pb.t[:, 0:T], func=AF.Identity, bias=qkb.t[:, c:c + 1], scale=0.125), [pb, qkb], [qT])
            sl = wsm.get(2)
            wv = sl.t[:, :].rearrange("p (c n) -> p c n", c=8)
            for c2 in range(2):
                pb = kb.ps()
                for kc in range(8):
                    kb.op(pe, lambda: nc.tensor.matmul(pb.t[:, 0:T], lhsT=wv[:, kc, c2 * 128:(c2 + 1) * 128], rhs=hin.t[:, kc, 0:T], start=(kc == 0), stop=(kc == 7)), [sl, hin], [pb], acc=True)
                kb.op(act, lambda: nc.scalar.activation(out=kT.t[:, c2, 128:128 + T], in_=pb.t[:, 0:T], func=AF.Identity, bias=qkb.t[:, 8 + c2:9 + c2], scale=1.0), [pb, qkb], [kT])
            for ch in range(NCH):
                pb = kb.ps()
                for kc in range(8):
                    kb.op(pe, lambda: nc.tensor.matmul(pb.t[:, :], lhsT=hin.t[:, kc, ch * 128:(ch + 1) * 128], rhs=wv[:, kc, :], start=(kc == 0), stop=(kc == 7)), [sl, hin], [pb], acc=True)
                kb.op(dve, lambda: nc.vector.tensor_tensor(out=kv_tok.t[:, ch, :], in0=pb.t[:, :], in1=kvb.t[:, :], op=ALU.add), [pb, kvb], [kv_tok])
                kb.op(act, lambda: nc.scalar.copy(out=vtok.t[:, 1 + ch, :], in_=kv_tok.t[:, ch, 256:512]), [kv_tok], [vtok])

            if flags.get("swa_stop") == 2:
                kb.arena_close(); return

            def softmax_head(h, lg, nk, bias_ap, selcol):
                k2 = h % 2
                s_ = sS[k2]; e_ = eS[k2]; n_ = en[k2]; t_ = st[k2]
                kb.op(dve, lambda: nc.vector.memset(t_.t[:], 0.0), [t_], [t_])
                kb.op(dve, lambda: nc.vector.tensor_tensor(out=s_.t[:, 0:nk], in0=lg.t[:, 0:nk], in1=bias_ap, op=ALU.add), [lg, biasT], [s_])
                kb.op(dve, lambda: nc.vector.tensor_reduce(out=t_.t[:, 0:1], in_=s_.t[:, 0:nk], axis=AX.X, op=ALU.max), [s_], [t_])
                kb.op(dve, lambda: nc.vector.tensor_scalar(out=t_.t[:, 1:2], in0=t_.t[:, 0:1], scalar1=sinkb.t[:, h:h + 1], scalar2=-1.0, op0=ALU.max, op1=ALU.mult), [t_, sinkb], [t_])
                kb.op(act, lambda: nc.scalar.activation(out=e_.t[:, 0:nk], in_=s_.t[:, 0:nk], func=AF.Exp, bias=t_.t[:, 1:2], scale=1.0, accum_out=t_.t[:, 2:3]), [s_, t_], [e_, t_])
                kb.op(act, lambda: nc.scalar.activation(out=t_.t[:, 3:4], in_=sinkb.t[:, h:h + 1], func=AF.Exp, bias=t_.t[:, 1:2], scale=1.0), [sinkb, t_], [t_])
                kb.op(dve, lambda: nc.vector.tensor_tensor(out=t_.t[:, 4:5], in0=t_.t[:, 2:3], in1=t_.t[:, 3:4], op=ALU.add), [t_], [t_])
                kb.op(dve, lambda: nc.vector.reciprocal(out=t_.t[:, 5:6], in_=t_.t[:, 4:5]), [t_], [t_])
                if selcol is not None:
                    kb.op(dve, lambda: nc.vector.tensor_tensor(out=t_.t[:, 5:6], in0=t_.t[:, 5:6], in1=selcol, op=ALU.mult), [t_, cstb], [t_])
                kb.op(dve, lambda: nc.vector.tensor_scalar(out=n_.t[:, 0:nk], in0=e_.t[:, 0:nk], scalar1=t_.t[:, 5:6], scalar2=None, op0=ALU.mult), [e_, t_], [n_])
                return n_

            def head_rows(h):
                g = h // 4
                if h < 8:
                    c = h % 4; half = h // 4
                else:
                    c = 4 + (h - 8) % 4; half = (h - 8) // 4
                return c, half, g

            if P:
                for ch in range(NCH):
                    csl = slice(ch * 128, (ch + 1) * 128)
                    blk0 = first and ch == 0
                    po = [kb.ps_hold() for _ in range(2)]
                    for h in range(16):
                        c, half, g = head_rows(h)
                        rs = slice(half * 64, (half + 1) * 64)
                        lg = kb.ps()
                        if blk0:
                            nk = 128
                            kb.op(pe, lambda: nc.tensor.matmul(lg.t[:, 0:128], lhsT=qT.t[rs, c, csl], rhs=kT.t[rs, g // 2, 128:256], start=True, stop=True), [qT, kT], [lg])
                            bias_ap = biasT.t[:, h, 128:256]
                        else:
                            nk = 256
                            kb.op(pe, lambda: nc.tensor.matmul(lg.t[:, 0:256], lhsT=qT.t[rs, c, csl], rhs=kT.t[rs, g // 2, ch * 128:ch * 128 + 256], start=True, stop=True), [qT, kT], [lg])
                            bias_ap = biasT.t[:, h, 0:256]
                        n_ = softmax_head(h, lg, nk, bias_ap, None)
                        nb = nk // 128
                        pt = kb.ps()
                        ptv = pt.t[:].bitcast(BF16)
                        for b in range(nb):
                            kb.op(pe, lambda: nc.tensor.transpose(out=ptv[:, b * 128:(b + 1) * 128], in_=n_.t[:, b * 128:(b + 1) * 128], identity=ident_bf()), [n_, cbf], [pt], acc=True)
                        p_ = pT[h % 2]
                        kb.op(act, lambda: nc.scalar.copy(out=p_.t[:, 0:nb, :], in_=ptv[:, 0:nb * 128].rearrange("p (b q) -> p b q", b=nb)), [pt], [p_])
                        ob_ = po[h // 8]
                        oc = slice((h % 8) * 64, (h % 8 + 1) * 64)
                        for b in range(nb):
                            vb = (ch + b) if not blk0 else 1
                            kb.op(pe, lambda: nc.tensor.matmul(ob_.t[:, oc], lhsT=p_.t[:, b, :], rhs=vtok.t[:, vb, g * 64:(g + 1) * 64], start=(b == 0), stop=(b == nb - 1)), [p_, vtok], [ob_], acc=True)
                    for k in range(2):
                        kb.op(act, lambda: nc.scalar.copy(out=o_tok.t[:, k * 512:(k + 1) * 512], in_=po[k].t[:, :]), [po[k]], [o_tok])
                        kb.ps_release(po[k])
                    for q in range(2):
                        pb = kb.ps()
                        pbv = pb.t[:].bitcast(BF16)
                        for k in range(4):
                            c = q * 4 + k
                            kb.op(pe, lambda: nc.tensor.transpose(out=pbv[:, k * 128:(k + 1) * 128], in_=o_tok.t[:, c * 128:(c + 1) * 128], identity=ident_bf()), [o_tok, cbf], [pb], acc=True)
                        kb.op(act, lambda: nc.scalar.copy(out=attnT.t[:, q * 4:(q + 1) * 4, csl], in_=pbv[:, 0:512].rearrange("p (k t) -> p k t", k=4)), [pb], [attnT])
                if last:
                    out_dma(sp, kp, kv_tok.t[:, NCH - 1, 0:256], [kv_tok])
                    out_dma(sp, vp, kv_tok.t[:, NCH - 1, 256:512], [kv_tok])
                kb.op(dve, lambda: nc.vector.tensor_copy(out=kT.t[:, :, 0:128], in_=kT.t[:, :, T:T + 128]), [kT], [kT])
                kb.op(dve, lambda: nc.vector.tensor_copy(out=vtok.t[:, 0, :], in_=vtok.t[:, NCH, :]), [vtok], [vtok])
            else:
                ckf = [kb.abuf(f"ckf{k}", [128, 256], F32) for k in range(2)]
                kTj = [kb.abuf(f"kTj{k}", [128, 2, 136], BF16) for k in range(2)]
                vj = [kb.abuf(f"vj{k}", [128, 256], BF16) for k in range(2)]
                vnew = kb.abuf("vnew", [8, 16, 256], BF16)
                vtb = kb.abuf("vtb", [128, 256], BF16)
                kb.op(dve, lambda: nc.vector.tensor_copy(out=vtb.t[:], in_=vtok.t[:, 1, :]), [vtok], [vtb])
                for j in range(NSEQ_S):
                    kb.dma(sp, vnew.t[:, j, :], vtb.t[j * 8:(j + 1) * 8, :], reads=[vtb], writes=[vnew], dsem=dmisc[2], join=True)
                po = [kb.ps_hold() for _ in range(2)]
                for j in range(NSEQ_S):
                    kf = ckf[j % 2]; ktj = kTj[j % 2]; v_ = vj[j % 2]
                    kb.dma(sp, kf.t[:], ck[j], writes=[kf], dsem=msem())
                    kb.dma(pool, v_.t[:], cv[j], writes=[v_], dsem=msem())
                    out_dma(sp, ks[j, 0:120, :], ck[j, 8:128, :], [])
                    out_dma(sp, vs[j, 0:120, :], cv[j, 8:128, :], [])
                    out_dma(sp, ks[j, 120:128, :], kv_tok.t[j * 8:(j + 1) * 8, 0, 0:256], [kv_tok])
                    out_dma(sp, vs[j, 120:128, :], kv_tok.t[j * 8:(j + 1) * 8, 0, 256:512], [kv_tok])
                    for c2 in range(2):
                        pb = kb.ps()
                        kb.op(pe, lambda: nc.tensor.transpose(out=pb.t[:, 0:128], in_=kf.t[:, c2 * 128:(c2 + 1) * 128], identity=ident()), [kf, cstb], [pb])
                        kb.op(act, lambda: nc.scalar.copy(out=ktj.t[:, c2, 0:128], in_=pb.t[:, 0:128]), [pb], [ktj])
                    kb.op(dve, lambda: nc.vector.tensor_copy(out=ktj.t[:, :, 128:136], in_=kT.t[:, :, 128 + j * 8:128 + (j + 1) * 8]), [kT, ktj], [ktj])
                    for h in range(16):
                        c, half, g = head_rows(h)
                        rs = slice(half * 64, (half + 1) * 64)
                        lg = kb.ps()
                        kb.op(pe, lambda: nc.tensor.matmul(lg.t[:, 0:136], lhsT=qT.t[rs, c, :], rhs=ktj.t[rs, g // 2, :], start=True, stop=True), [qT, ktj], [lg])
                        n_ = softmax_head(h, lg, 136, biasT.t[:, h, 0:136], cstb.t[:, C_SEL + j:C_SEL + j + 1])
                        pt = kb.ps()
                        ptv = pt.t[:].bitcast(BF16)
                        kb.op(pe, lambda: nc.tensor.transpose(out=ptv[:, 0:128], in_=n_.t[:, 0:128], identity=ident_bf()), [n_, cbf], [pt], acc=True)
                        kb.op(pe, lambda: nc.tensor.transpose(out=ptv[0:8, 128:256], in_=n_.t[:, 128:136], identity=ident_bf()), [n_, cbf], [pt], acc=True)
                        p_ = pT[h % 2]
                        kb.op(act, lambda: nc.scalar.copy(out=p_.t[:, 0, :], in_=ptv[:, 0:128]), [pt], [p_])
                        kb.op(act, lambda: nc.scalar.copy(out=p_.t[0:8, 1, :], in_=ptv[0:8, 128:256]), [pt, p_], [p_])
                        ob_ = po[h // 8]
                        oc = slice((h % 8) * 64, (h % 8 + 1) * 64)
                        kb.op(pe, lambda: nc.tensor.matmul(ob_.t[:, oc], lhsT=p_.t[:, 0, :], rhs=v_.t[:, g * 64:(g + 1) * 64], start=(j == 0 and h % 8 == 0), stop=False), [p_, v_], [ob_], acc=True)
                        kb.op(pe, lambda: nc.tensor.matmul(ob_.t[:, oc], lhsT=p_.t[0:8, 1, :], rhs=vnew.t[0:8, j, g * 64:(g + 1) * 64], start=False, stop=(j == NSEQ_S - 1)), [p_, vnew], [ob_], acc=True)
                for k in range(2):
                    kb.op(act, lambda: nc.scalar.copy(out=o_tok.t[:, k * 512:(k + 1) * 512], in_=po[k].t[:, :]), [po[k]], [o_tok])
                    kb.ps_release(po[k])
                for q in range(2):
                    pb = kb.ps()
                    pbv = pb.t[:].bitcast(BF16)
                    for k in range(4):
                        c = q * 4 + k
                        kb.op(pe, lambda: nc.tensor.transpose(out=pbv[:, k * 128:(k + 1) * 128], in_=o_tok.t[:, c * 128:(c + 1) * 128], identity=ident_bf()), [o_tok, cbf], [pb], acc=True)
                    kb.op(act, lambda: nc.scalar.copy(out=attnT.t[:, q * 4:(q + 1) * 4, :], in_=pbv[:, 0:512].rearrange("p (k t) -> p k t", k=4)), [pb], [attnT])

            if flags.get("swa_stop") == 3:
                kb.arena_close(); return
            for m in range(8):
                sl = wsm.get(3 + m // 4)
                wv = sl.t[:, :].rearrange("p (c n) -> p c n", c=8)
                pb = kb.ps()
                for kc in range(8):
                    kb.op(pe, lambda: nc.tensor.matmul(pb.t[:, 0:T], lhsT=wv[:, kc, (m % 4) * 128:(m % 4 + 1) * 128], rhs=attnT.t[:, kc, 0:T], start=(kc == 0), stop=(kc == 7)), [sl, attnT], [pb], acc=True)
                kb.op(act, lambda: nc.scalar.activation(out=f_fm.t[:, m, 0:T], in_=pb.t[:, 0:T], func=AF.Identity, bias=ob.t[:, m:m + 1], scale=1.0), [pb, ob], [f_fm])
            post(1, 1, kind, f_fm, f_fm.t[:, :, :])
            kb.arena_close()

        tiles = [("P", t) for t in range(8)] + [("S", 0)]
        if "tiles" in flags:
            tiles = flags["tiles"]
        xi = [0]
        for kind, ti in tiles:
            T, NCH, nseq, tps = geom(kind)
            src = xp if kind == "P" else xs
            dst = yp if kind == "P" else ys
            first = (ti == 0); last = (ti == 7) or kind == "S"
            if kind == "S":
                pass
            kb.arena_open()
            xtk = [kb.abuf(f"xtk{k}", [128, D], F32) for k in range(2)]
            for ch in range(NCH):
                xt = xtk[xi[0] % 2]; xi[0] += 1
                r0 = ti * 512 + ch * 128
                kb.dma(sp, xt.t[:], src[r0:r0 + 128, :], writes=[xt], dsem=msem())
                for q in range(2):
                    pb = kb.ps()
                    for k in range(4):
                        c = q * 4 + k
                        kb.op(pe, lambda: nc.tensor.transpose(out=pb.t[:, k * 128:(k + 1) * 128], in_=xt.t[:, c * 128:(c + 1) * 128], identity=ident()), [xt, cstb], [pb], acc=True)
                    kb.op(act, lambda: nc.scalar.copy(out=x_fm.t[:, q * 4:(q + 1) * 4, ch * 128:(ch + 1) * 128], in_=pb.t[:, :].rearrange("p (k t) -> p k t", k=4)), [pb], [x_fm])
            kb.arena_close()
            for fl in flags.get("phases", ["f00", "ssd", "f01", "f10", "swa", "f11"]):
                if fl == "f00":
                    ffn(0, 0, kind)
                elif fl == "ssd":
                    ssd(kind, last)
                elif fl == "f01":
                    ffn(0, 1, kind)
                elif fl == "f10":
                    ffn(1, 0, kind)
                elif fl == "swa":
                    swa(kind, first, last)
                elif fl == "f11":
                    ffn(1, 1, kind)
            kb.arena_open()
            xtk = [kb.abuf(f"xtk{k}", [128, D], F32) for k in range(2)]
            for ch in range(NCH):
                xt = xtk[xi[0] % 2]; xi[0] += 1
                r0 = ti * 512 + ch * 128
                for q in range(2):
                    pb = kb.ps()
                    for k in range(4):
                        c = q * 4 + k
                        kb.op(pe, lambda: nc.tensor.transpose(out=pb.t[:, k * 128:(k + 1) * 128], in_=x_fm.t[:, c, ch * 128:(ch + 1) * 128], identity=ident()), [x_fm, cstb], [pb], acc=True)
                    kb.op(act, lambda: nc.scalar.copy(out=xt.t[:, q * 512:(q + 1) * 512], in_=pb.t[:, :]), [pb], [xt])
                out_dma(sp, dst[r0:r0 + 128, :], xt.t[:], [xt])
            kb.arena_close()

        for s in dout_sems:
            if kb.cnts[s] > 0:
                nc.sync.wait_ge(kb.sems[s], kb.cnts[s])
        for E in (pe, act, dve, pool):
            if E.cnt > 0:
                nc.sync.wait_ge(kb.sems[E.name], E.cnt)
        for s in dmisc + [b.dsem for b in kb.wslots]:
            if kb.cnts[s] > 0:
                nc.sync.wait_ge(kb.sems[s], kb.cnts[s])
    return nc


_CACHE = {}


def make_in_maps(inp, cores=range(8)):
    f = lambda a: np.ascontiguousarray(np.asarray(a, dtype=np.float32))
    cstv = _make_consts()
    shared = {
        "ada_w": f(inp["ada_w"]), "ada_b": f(inp["ada_b"]), "norm_pre": f(inp["norm_pre"]), "norm_post": f(inp["norm_post"]),
        "ffn_w_in": f(inp["ffn_w_in"]), "ffn_w_out": f(inp["ffn_w_out"]),
        "ssm_in_w": f(inp["ssm_in_w"][0]), "conv_w": f(inp["ssm_conv_w"][0]), "conv_b": f(inp["ssm_conv_b"]),
        "dt_bias": f(inp["ssm_dt_bias"]), "a_log": f(inp["ssm_a_log"]), "ssm_d": f(inp["ssm_d"]),
        "ssm_norm_w": f(inp["ssm_norm_w"]), "ssm_out_w": f(inp["ssm_out_w"][0]),
        "qkv_w": f(inp["attn_qkv_w"][0]), "qkv_b": f(inp["attn_qkv_b"]), "sinks": f(inp["attn_sinks"]),
        "o_w": f(inp["attn_o_w"][0]), "o_b": f(inp["attn_o_b"]), "rel_bias": f(inp["rel_bias"]), "cst": cstv,
    }
    x_prompt = f(inp["x_prompt"]); x_sample = f(inp["x_sample"])
    state_ssm = f(inp["state_ssm"]); state_conv = f(inp["state_conv"])
    cache_k = f(inp["cache_k"]); cache_v = f(inp["cache_v"])
    c_prompt = f(inp["c_prompt"]); c_sample = f(inp["c_sample"])
    in_maps = []
    for b in cores:
        s0, s1 = b * 16, (b + 1) * 16
        m = dict(shared)
        m["xp"] = x_prompt[b]
        m["xs"] = x_sample[s0:s1].reshape(128, D)
        m["st_ssm"] = state_ssm[0, s0:s1].reshape(16, DIN, 128)
        m["st_conv"] = state_conv[0, s0:s1].reshape(48, 3072)
        m["ck"] = cache_k[0, s0:s1].reshape(16, 128, 256)
        m["cv"] = cache_v[0, s0:s1].reshape(16, 128, 256)
        m["cvec"] = np.concatenate([c_prompt[b:b + 1], c_sample[s0:s1]], axis=0)
        in_maps.append(m)
    return in_maps


def assemble(R):
    n = len(R)
    y_prompt = np.stack([R[b]["yp"] for b in range(n)]).reshape(n, SEQ, D)
    y_sample = np.concatenate([R[b]["ys"].reshape(16, 8, D) for b in range(n)], axis=0)
    ssm_p = np.stack([R[b]["ssm_p"].reshape(32, 64, 128) for b in range(n)])[None]
    conv_p = np.stack([R[b]["conv_p"] for b in range(n)])[None]
    k_p = np.stack([R[b]["kp"].reshape(128, 4, 64) for b in range(n)])[None]
    v_p = np.stack([R[b]["vp"].reshape(128, 4, 64) for b in range(n)])[None]
    ssm_s = np.concatenate([R[b]["ssm_s"].reshape(16, 32, 64, 128) for b in range(n)], axis=0)[None]
    conv_s = np.concatenate([R[b]["conv_s"].reshape(16, 3, 3072) for b in range(n)], axis=0)[None]
    k_s = np.concatenate([R[b]["ks"].reshape(16, 128, 4, 64) for b in range(n)], axis=0)[None]
    v_s = np.concatenate([R[b]["vs"].reshape(16, 128, 4, 64) for b in range(n)], axis=0)[None]
    outs = (y_prompt, y_sample, ssm_p, conv_p, k_p, v_p, ssm_s, conv_s, k_s, v_s)
    return tuple(np.ascontiguousarray(o, dtype=np.float32) for o in outs)


def kernel(**inp):
    nc = _CACHE.get("nc")
    if nc is None:
        nc = build_program()
        _CACHE["nc"] = nc
    in_maps = make_in_maps(inp)
    res = run_bass_kernel_spmd(nc, in_maps, core_ids=list(range(8)))
    return assemble(res.results)
```

```python
import contextlib
import math
import numpy as np
import concourse.bass as bass
import concourse.mybir as mybir
from concourse.bass_utils import run_bass_kernel_spmd

F32 = mybir.dt.float32
BF16 = mybir.dt.bfloat16
AF = mybir.ActivationFunctionType
ALU = mybir.AluOpType
AX = mybir.AxisListType

D = 1024
SEQ = 4096
NSEQ_S = 16
TPS_S = 8
DFF = 2816
NJ = DFF // 128
DIN = 2048
NEG = -30000.0
EPS = 1e-6

C_ID, C_TRIP, C_STRIP, C_TRIS, C_STRIS, C_J, C_JREP = 0, 128, 256, 384, 512, 640, 768
C_SEL = 896
C_OH = 912
C_ONES = 1296
NCST = 1424


def _bucket(d):
    d = np.asarray(d)
    df = np.maximum(d, 1).astype(np.float32)
    large = 16 + (np.log(df / np.float32(16)) / np.float32(math.log(128 / 16)) * np.float32(16)).astype(np.int32)
    large = np.minimum(large, 31)
    return np.where(d < 16, d, large)


def _make_consts():
    c = np.zeros((128, NCST), np.float32)
    i = np.arange(128)
    same = (i[:, None] // 8) == (i[None, :] // 8)
    c[:, C_ID:C_ID + 128] = np.eye(128)
    c[:, C_TRIP:C_TRIP + 128] = (i[:, None] <= i[None, :])
    c[:, C_STRIP:C_STRIP + 128] = (i[:, None] > i[None, :])
    c[:, C_TRIS:C_TRIS + 128] = (i[:, None] <= i[None, :]) & same
    c[:, C_STRIS:C_STRIS + 128] = (i[:, None] > i[None, :]) & same
    c[:, C_J:C_J + 128] = (i[:, None] == 127 - i[None, :])
    c[:, C_JREP:C_JREP + 128] = (i[:, None] == 127 - (i[None, :] % 8))
    c[:, C_SEL:C_SEL + 16] = (i[:, None] // 8) == np.arange(16)[None, :]
    oh = np.zeros((33, 384), np.float32)
    for ii in range(384):
        dist = 255 - ii
        if 0 <= dist <= 128:
            oh[int(_bucket(dist)), ii] = 1.0
        else:
            oh[32, ii] = 1.0
    c[0:33, C_OH:C_OH + 384] = oh
    c[:, C_ONES:C_ONES + 128] = 1.0
    return c


class Buf:
    __slots__ = ("t", "w", "r", "dsem")

    def __init__(self, t, pend=None):
        self.t = t
        self.w = None
        self.r = dict(pend) if pend else {}
        self.dsem = None


class Eng:
    def __init__(self, name, h):
        self.name = name
        self.h = h
        self.cnt = 0
        self.seen = {}


class KB:
    def __init__(self, nc, es):
        self.nc = nc
        self.es = es
        self.sems = {}
        self.cnts = {}
        self.pe = self._eng("pe", nc.tensor)
        self.act = self._eng("act", nc.scalar)
        self.dve = self._eng("dve", nc.vector)
        self.pool = self._eng("pool", nc.gpsimd)
        self.sp = self._eng("sp", nc.sync)
        self.engs = [self.pe, self.act, self.dve, self.pool, self.sp]
        self.banks = []
        for i in range(8):
            t = es.enter_context(nc.psum_tensor(f"psb{i}", [128, 512], F32))
            self.banks.append(Buf(t))
        self.bank_i = 0
        self.held = set()
        self.pending = {}
        self.arena_bufs = []
        self.arena_es = None
        self.out_events = {}
        self.wslots = []
        self.wslot_i = 0
        self.skip_self_waw = True
        self.dq = 0

    def _eng(self, name, h):
        self.sems[name] = self.es.enter_context(self.nc.semaphore(name))
        self.cnts[name] = 0
        return Eng(name, h)

    def dsem(self, name):
        self.sems[name] = self.es.enter_context(self.nc.semaphore(name))
        self.cnts[name] = 0
        return name

    def pbuf(self, name, shape, dt):
        t = self.es.enter_context(self.nc.sbuf_tensor(name, list(shape), dt))
        return Buf(t)

    def arena_open(self):
        self.arena_es = contextlib.ExitStack()
        self.arena_bufs = []

    def abuf(self, name, shape, dt):
        self.uid = getattr(self, "uid", 0) + 1
        t = self.arena_es.enter_context(self.nc.sbuf_tensor(f"{name}_{self.uid}", list(shape), dt))
        b = Buf(t, self.pending)
        self.arena_bufs.append(b)
        return b

    def arena_close(self):
        pend = dict(self.pending)
        for b in self.arena_bufs:
            if b.w and pend.get(b.w[0], 0) < b.w[1]:
                pend[b.w[0]] = b.w[1]
            for s, v in b.r.items():
                if pend.get(s, 0) < v:
                    pend[s] = v
        self.pending = pend
        self.arena_es.close()
        self.arena_es = None
        self.arena_bufs = []

    def ps(self):
        for _ in range(16):
            i = self.bank_i
            self.bank_i = (self.bank_i + 1) % 8
            if i not in self.held:
                return self.banks[i]
        raise RuntimeError("no psum bank")

    def ps_hold(self):
        b = self.ps()
        self.held.add(self.banks.index(b))
        return b

    def ps_release(self, b):
        self.held.discard(self.banks.index(b))

    def _need(self, E, reads, writes, acc, skipname):
        need = {}

        def add(s, v):
            if need.get(s, 0) < v:
                need[s] = v
        for b in reads:
            if b.w:
                add(*b.w)
        for b in writes:
            if b.w and not ((acc or self.skip_self_waw) and b.w[0] == skipname):
                add(*b.w)
            for s, v in b.r.items():
                if s != E.name:
                    add(s, v)
        for s, v in need.items():
            if E.seen.get(s, 0) < v:
                E.h.wait_ge(self.sems[s], v)
                E.seen[s] = v

    def op(self, E, fn, reads=(), writes=(), acc=False):
        self._need(E, reads, writes, acc, E.name)
        ins = fn()
        E.cnt += 1
        ins.then_inc(self.sems[E.name], 1)
        for b in reads:
            if b.r.get(E.name, 0) < E.cnt:
                b.r[E.name] = E.cnt
        for b in writes:
            b.w = (E.name, E.cnt)
            b.r = {}
        return ins

    def dma(self, Q, out, in_, reads=(), writes=(), dsem=None, join=False, **kw):
        self._need(Q, reads, writes, join, dsem)
        self.cnts[dsem] += 16
        v = self.cnts[dsem]
        Q.h.dma_start(out=out, in_=in_, **kw).then_inc(self.sems[dsem], 16)
        for b in reads:
            if b.r.get(dsem, 0) < v:
                b.r[dsem] = v
        for b in writes:
            b.w = (dsem, v)
            b.r = {}
        return (dsem, v)

    def wslot(self):
        s = self.wslots[self.wslot_i]
        self.wslot_i = (self.wslot_i + 1) % len(self.wslots)
        return s


class WStream:
    def __init__(self, kb, loads, depth=2):
        self.kb = kb
        self.loads = loads
        self.depth = depth
        self.issued = 0
        self.slots = []

    def get(self, k):
        kb = self.kb
        while self.issued < min(len(self.loads), k + 1 + self.depth):
            sl = kb.wslot()
            for i, (d, s) in enumerate(self.loads[self.issued](sl.t)):
                kb.dma(kb.pool, d, s, writes=[sl], dsem=sl.dsem, join=(i > 0))
            self.slots.append(sl)
            self.issued += 1
        return self.slots[k]


def build_program(flags=None):
    flags = flags or {}
    nc = bass.Bass("TRN2", target_bir_lowering=False)

    def din(name, shape, dt=F32):
        return nc.dram_tensor(name, list(shape), dt, kind="ExternalInput").ap()

    def dout(name, shape):
        return nc.dram_tensor(name, list(shape), F32, kind="ExternalOutput").ap()

    xp = din("xp", [SEQ, D]); xs = din("xs", [128, D])
    st_ssm = din("st_ssm", [NSEQ_S, DIN, 128]); st_conv = din("st_conv", [48, 3072])
    ck = din("ck", [NSEQ_S, 128, 256]); cv = din("cv", [NSEQ_S, 128, 256])
    cvec = din("cvec", [17, D])
    ada_w = din("ada_w", [2, D, 9216]); ada_b = din("ada_b", [2, 9216])
    norm_pre = din("norm_pre", [2, 3, D]); norm_post = din("norm_post", [2, 3, D])
    ffn_w_in = din("ffn_w_in", [2, 2, D, 2 * DFF]); ffn_w_out = din("ffn_w_out", [2, 2, DFF, D])
    ssm_in_w = din("ssm_in_w", [D, 5152]); conv_w = din("conv_w", [4, 3072]); conv_b = din("conv_b", [1, 3072])
    dt_bias = din("dt_bias", [1, 32]); a_log = din("a_log", [1, 32]); ssm_d = din("ssm_d", [1, 32])
    ssm_norm_w = din("ssm_norm_w", [1, DIN]); ssm_out_w = din("ssm_out_w", [DIN, D])
    qkv_w = din("qkv_w", [D, 1536]); qkv_b = din("qkv_b", [1, 1536]); sinks = din("sinks", [1, 16])
    o_w = din("o_w", [D, D]); o_b = din("o_b", [1, D]); rel_bias = din("rel_bias", [32, 16])
    cst = din("cst", [128, NCST])

    yp = dout("yp", [SEQ, D]); ys = dout("ys", [128, D])
    ssm_p = dout("ssm_p", [DIN, 128]); conv_p = dout("conv_p", [3, 3072])
    kp = dout("kp", [128, 256]); vp = dout("vp", [128, 256])
    ssm_s = dout("ssm_s", [NSEQ_S, DIN, 128]); conv_s = dout("conv_s", [48, 3072])
    ks = dout("ks", [NSEQ_S, 128, 256]); vs = dout("vs", [NSEQ_S, 128, 256])
    uscr = nc.dram_tensor("uscr", [16, 384], F32, kind="Internal")

    es = contextlib.ExitStack()
    with es:
        kb = KB(nc, es)
        pe, act, dve, pool, sp = kb.pe, kb.act, kb.dve, kb.pool, kb.sp
        NW = 4
        for i in range(NW):
            b = kb.pbuf(f"wslot{i}", [128, 4096], BF16)
            b.dsem = kb.dsem(f"dw{i}")
            kb.wslots.append(b)
        dmisc = [kb.dsem(f"dm{i}") for i in range(8)]
        dout_sems = [kb.dsem(f"do{i}") for i in range(4)]
        mi = [0]

        def msem():
            mi[0] = (mi[0] + 1) % len(dmisc)
            return dmisc[mi[0]]
        oi = [0]

        def osem():
            oi[0] = (oi[0] + 1) % len(dout_sems)
            return dout_sems[oi[0]]

        def out_dma(Q, out, in_, reads, **kw):
            s = osem()
            kb.dma(Q, out, in_, reads=reads, dsem=s, **kw)

        cstb = kb.pbuf("cstb", [128, NCST], F32)
        cbf = kb.pbuf("cbf", [128, 256], BF16)
        epsb = kb.pbuf("epsb", [128, 2], F32)
        x_fm = kb.pbuf("x_fm", [128, 8, 512], F32)
        hin = kb.pbuf("hin", [128, 8, 512], BF16)
        rstd = kb.pbuf("rstd", [128, 512], F32)
        tmpA = [kb.pbuf(f"tmpA{i}", [128, 512], F32) for i in range(3)]
        PRE = kb.pbuf("PRE", [128, 18, 8, 17], F32)
        hT = kb.pbuf("hT", [128, DIN], F32)
        hT_bf = kb.pbuf("hT_bf", [128, DIN], BF16)
        tailP = kb.pbuf("tailP", [128, 24, 3], F32)
        convw = kb.pbuf("convw", [128, 24, 4], F32)
        convb = kb.pbuf("convb", [128, 24], F32)
        vec32 = kb.pbuf("vec32", [128, 4, 32], F32)
        normwT = kb.pbuf("normwT", [128, 16], F32)
        wdt = kb.pbuf("wdt", [128, 8, 32], BF16)
        qkb = kb.pbuf("qkb", [128, 10], F32)
        kvb = kb.pbuf("kvb", [128, 512], F32)
        ob = kb.pbuf("ob", [128, 8], F32)
        sinkb = kb.pbuf("sinkb", [128, 16], F32)
        kT = kb.pbuf("kT", [128, 2, 128 + 512], BF16)
        vtok = kb.pbuf("vtok", [128, 5, 256], BF16)
        tmi = [0]

        def tmp():
            tmi[0] = (tmi[0] + 1) % 3
            return tmpA[tmi[0]]

        ident = lambda: cstb.t[:, C_ID:C_ID + 128]
        ident_bf = lambda: cbf.t[:, 0:128]
        ones_bf = lambda: cbf.t[:, 128:256]
        ones_f = lambda: cstb.t[:, C_ONES:C_ONES + 128]

        kb.dma(sp, cstb.t[:], cst, writes=[cstb], dsem=msem())
        kb.op(dve, lambda: nc.vector.tensor_copy(out=cbf.t[:, 0:128], in_=cstb.t[:, C_ID:C_ID + 128]), [cstb], [cbf])
        kb.op(dve, lambda: nc.vector.tensor_copy(out=cbf.t[:, 128:256], in_=cstb.t[:, C_ONES:C_ONES + 128]), [cbf, cstb], [cbf])
        kb.op(dve, lambda: nc.vector.memset(epsb.t[:, 0:1], EPS), [], [epsb])
        kb.op(dve, lambda: nc.vector.memset(epsb.t[:, 1:2], 1.0), [epsb], [epsb])
        kb.op(dve, lambda: nc.vector.memset(tailP.t[:], 0.0), [], [tailP])
        kb.op(dve, lambda: nc.vector.memset(hT.t[:], 0.0), [], [hT])
        kb.op(dve, lambda: nc.vector.memset(hT_bf.t[:], 0.0), [], [hT_bf])
        kb.op(dve, lambda: nc.vector.memset(kT.t[:], 0.0), [], [kT])
        kb.op(dve, lambda: nc.vector.memset(vtok.t[:], 0.0), [], [vtok])

        with nc.allow_non_contiguous_dma(reason="small param loads"):
            for k in range(4):
                kb.dma(sp, convw.t[:, :, k], conv_w[k].rearrange("(c p) -> p c", p=128), writes=[convw], dsem=dmisc[3], join=True)
            kb.dma(sp, convb.t[:], conv_b.rearrange("o (c p) -> p (o c)", p=128), writes=[convb], dsem=msem())
            kb.dma(sp, ob.t[:], o_b.rearrange("o (c p) -> p (o c)", p=128), writes=[ob], dsem=msem())
            for c in range(8):
                A = c if c < 4 else c + 4
                for half, hh in ((0, A), (1, A + 4)):
                    kb.dma(sp, qkb.t[half * 64:(half + 1) * 64, c:c + 1],
                           qkv_b[0:1, hh * 64:(hh + 1) * 64].rearrange("o d -> d o"), writes=[qkb], dsem=dmisc[0], join=True)
            kb.dma(sp, qkb.t[:, 8:10], qkv_b[0:1, 1024:1280].rearrange("o (c p) -> p (o c)", p=128), writes=[qkb], dsem=dmisc[0], join=True)
        kb.dma(sp, vec32.t[:, 0, :], dt_bias.partition_broadcast(128), writes=[vec32], dsem=dmisc[1])
        kb.dma(sp, vec32.t[:, 1, :], a_log.partition_broadcast(128), writes=[vec32], dsem=dmisc[1], join=True)
        kb.dma(sp, vec32.t[:, 2, :], ssm_d.partition_broadcast(128), writes=[vec32], dsem=dmisc[1], join=True)
        with nc.allow_non_contiguous_dma(reason="small param loads"):
            kb.dma(sp, normwT.t[:], ssm_norm_w.rearrange("o (c p) -> p (o c)", p=128), writes=[normwT], dsem=msem())
        kb.dma(sp, sinkb.t[:], sinks.partition_broadcast(128), writes=[sinkb], dsem=msem())
        kb.dma(pool, wdt.t[:], ssm_in_w.rearrange("(c p) n -> p c n", p=128)[:, :, 5120:5152], writes=[wdt], dsem=msem())
        kb.dma(sp, kvb.t[:], qkv_b[0:1, 1024:1536].partition_broadcast(128), writes=[kvb], dsem=msem())
        kb.op(act, lambda: nc.scalar.activation(out=vec32.t[:, 1, :], in_=vec32.t[:, 1, :], func=AF.Exp), [vec32], [vec32])
        kb.op(dve, lambda: nc.vector.tensor_scalar(out=vec32.t[:, 1, :], in0=vec32.t[:, 1, :], scalar1=-1.0, scalar2=None, op0=ALU.mult), [vec32], [vec32])
        kb.op(dve, lambda: nc.vector.tensor_scalar(out=qkb.t[:, 0:8], in0=qkb.t[:, 0:8], scalar1=0.125, scalar2=None, op0=ALU.mult), [qkb], [qkb])

        kb.arena_open()
        cT = kb.abuf("cT", [128, 8, 17], F32)
        csT = kb.abuf("csT", [128, 8, 17], BF16)
        adab = kb.abuf("adab", [128, 2, 72], F32)
        npre = kb.abuf("npre", [128, 6, 8], F32)
        npost = kb.abuf("npost", [128, 6, 8], F32)
        modT = kb.abuf("modT", [128, 2, 72, 17], F32)
        ctok = kb.abuf("ctok", [17, D], F32)
        kb.dma(sp, ctok.t[:], cvec, writes=[ctok], dsem=msem())
        for c in range(8):
            pb = kb.ps()
            kb.op(pe, lambda: nc.tensor.transpose(out=pb.t[:, 0:17], in_=ctok.t[:, c * 128:(c + 1) * 128], identity=cstb.t[0:17, C_ID:C_ID + 17]), [ctok, cstb], [pb])
            kb.op(dve, lambda: nc.vector.tensor_copy(out=cT.t[:, c, :], in_=pb.t[:, 0:17]), [pb], [cT])
        kb.op(act, lambda: nc.scalar.activation(out=csT.t[:], in_=cT.t[:], func=AF.Silu), [cT], [csT])
        with nc.allow_non_contiguous_dma(reason="small param loads"):
            for i in range(2):
                kb.dma(sp, adab.t[:, i, :], ada_b[i].rearrange("(c p) -> p c", p=128), writes=[adab], dsem=dmisc[4], join=True)
                for sub in range(3):
                    kb.dma(sp, npre.t[:, i * 3 + sub, :], norm_pre[i, sub].rearrange("(c p) -> p c", p=128), writes=[npre], dsem=dmisc[5], join=True)
                    kb.dma(sp, npost.t[:, i * 3 + sub, :], norm_post[i, sub].rearrange("(c p) -> p c", p=128), writes=[npost], dsem=dmisc[6], join=True)
        for i in range(2):
            aw = ada_w[i].rearrange("(c p) n -> p c n", p=128)
            loads = []
            for nb in range(18):
                loads.append(lambda t, nb=nb: [(t[:, :].rearrange("p (c n) -> p c n", c=8), aw[:, :, nb * 512:(nb + 1) * 512])])
            wsm = WStream(kb, loads)
            for nb in range(18):
                sl = wsm.get(nb)
                wv = sl.t[:, :].rearrange("p (c n) -> p c n", c=8)
                pb = kb.ps()
                for m in range(4):
                    for kc in range(8):
                        kb.op(pe, lambda: nc.tensor.matmul(pb.t[:, m * 17:(m + 1) * 17], lhsT=wv[:, kc, m * 128:(m + 1) * 128], rhs=csT.t[:, kc, :], start=(kc == 0), stop=(kc == 7)), [sl, csT], [pb], acc=True)
                kb.op(dve, lambda: nc.vector.tensor_tensor(out=modT.t[:, i, nb * 4:(nb + 1) * 4, :], in0=pb.t[:, 0:68].rearrange("p (m s) -> p m s", m=4),
                                                           in1=adab.t[:, i, nb * 4:(nb + 1) * 4].unsqueeze(2).to_broadcast([128, 4, 17]), op=ALU.add), [pb, adab], [modT])
        for i in range(2):
            for sub in range(3):
                base = (i * 3 + sub) * 3
                sh = modT.t[:, i, (sub * 3 + 0) * 8:(sub * 3 + 0) * 8 + 8, :]
                sc = modT.t[:, i, (sub * 3 + 1) * 8:(sub * 3 + 1) * 8 + 8, :]
                gt = modT.t[:, i, (sub * 3 + 2) * 8:(sub * 3 + 2) * 8 + 8, :]
                npb = npre.t[:, i * 3 + sub, :].unsqueeze(2).to_broadcast([128, 8, 17])
                npo = npost.t[:, i * 3 + sub, :].unsqueeze(2).to_broadcast([128, 8, 17])
                kb.op(dve, lambda: nc.vector.scalar_tensor_tensor(out=PRE.t[:, base + 0, :, :], in0=sc, scalar=1.0, in1=npb, op0=ALU.add, op1=ALU.mult), [modT, npre], [PRE])
                kb.op(dve, lambda: nc.vector.tensor_copy(out=PRE.t[:, base + 1, :, :], in_=sh), [modT, PRE], [PRE])
                res = 1.0 if sub == 1 else 0.5
                kb.op(dve, lambda: nc.vector.scalar_tensor_tensor(out=PRE.t[:, base + 2, :, :], in0=gt, scalar=res, in1=npo, op0=ALU.mult, op1=ALU.mult), [modT, npost, PRE], [PRE])

        rb = kb.abuf("rb", [33, 16], F32)
        usb = kb.abuf("usb", [16, 384], F32)
        kb.op(dve, lambda: nc.vector.memset(rb.t[:], NEG), [], [rb])
        kb.dma(sp, rb.t[0:32, :], rel_bias, reads=[], writes=[rb], dsem=msem())
        pb = kb.ps()
        kb.op(pe, lambda: nc.tensor.matmul(pb.t[0:16, 0:384], lhsT=rb.t[:, :], rhs=cstb.t[0:33, C_OH:C_OH + 384], start=True, stop=True), [rb, cstb], [pb])
        kb.op(dve, lambda: nc.vector.tensor_copy(out=usb.t[:], in_=pb.t[0:16, 0:384]), [pb], [usb])
        uev = Buf(None)
        kb.dma(sp, uscr.ap(), usb.t[:], reads=[usb], writes=[uev], dsem=msem())
        kb.arena_close()

        def geom(kind):
            if kind == "P":
                return 512, 4, 1, 512
            return 128, 1, 16, 8

        def sumsq_rstd(src, sv, T):
            kb.op(act, lambda: nc.scalar.activation(out=hin.t[:, :, 0:T], in_=sv, func=AF.Square), [src], [hin])
            pb = kb.ps()
            for c in range(8):
                kb.op(pe, lambda: nc.tensor.matmul(pb.t[:, 0:T], lhsT=ones_bf(), rhs=hin.t[:, c, 0:T], start=(c == 0), stop=(c == 7)), [hin, cbf], [pb], acc=True)
            kb.op(act, lambda: nc.scalar.activation(out=rstd.t[:, 0:T], in_=pb.t[:, 0:T], func=AF.Ln, bias=epsb.t[:, 0:1], scale=1.0 / D), [pb, epsb], [rstd])
            kb.op(act, lambda: nc.scalar.activation(out=rstd.t[:, 0:T], in_=rstd.t[:, 0:T], func=AF.Exp, scale=-0.5), [rstd], [rstd])

        def norm_mod(i, sub, kind):
            T, NCH, nseq, tps = geom(kind)
            base = (i * 3 + sub) * 3
            sumsq_rstd(x_fm, x_fm.t[:, :, 0:T], T)
            for c in range(8):
                t1 = tmp()
                kb.op(dve, lambda: nc.vector.tensor_tensor(out=t1.t[:, 0:T], in0=x_fm.t[:, c, 0:T], in1=rstd.t[:, 0:T], op=ALU.mult), [x_fm, rstd], [t1])
                if kind == "P":
                    kb.op(act, lambda: nc.scalar.activation(out=hin.t[:, c, 0:T], in_=t1.t[:, 0:T], func=AF.Identity,
                                                            bias=PRE.t[:, base + 1, c, 0:1], scale=PRE.t[:, base + 0, c, 0:1]), [t1, PRE], [hin])
                else:
                    v3 = lambda ap: ap.rearrange("p (s t) -> p s t", s=16)
                    kb.op(dve, lambda: nc.vector.tensor_tensor(out=v3(t1.t[:, 0:T]), in0=v3(t1.t[:, 0:T]), in1=PRE.t[:, base + 0, c, 1:17].unsqueeze(2).to_broadcast([128, 16, 8]), op=ALU.mult), [t1, PRE], [t1])
                    kb.op(dve, lambda: nc.vector.tensor_tensor(out=v3(hin.t[:, c, 0:T]), in0=v3(t1.t[:, 0:T]), in1=PRE.t[:, base + 1, c, 1:17].unsqueeze(2).to_broadcast([128, 16, 8]), op=ALU.add), [t1, PRE], [hin])

        def post(i, sub, kind, f_fm, fv):
            T, NCH, nseq, tps = geom(kind)
            base = (i * 3 + sub) * 3
            sumsq_rstd(f_fm, fv, T)
            for c in range(8):
                t1 = tmp()
                kb.op(dve, lambda: nc.vector.tensor_tensor(out=t1.t[:, 0:T], in0=fv[:, c, :], in1=rstd.t[:, 0:T], op=ALU.mult), [f_fm, rstd], [t1])
                if kind == "P":
                    kb.op(dve, lambda: nc.vector.scalar_tensor_tensor(out=x_fm.t[:, c, 0:T], in0=t1.t[:, 0:T], scalar=PRE.t[:, base + 2, c, 0:1], in1=x_fm.t[:, c, 0:T], op0=ALU.mult, op1=ALU.add), [t1, PRE, x_fm], [x_fm])
                else:
                    v3 = lambda ap: ap.rearrange("p (s t) -> p s t", s=16)
                    kb.op(dve, lambda: nc.vector.tensor_tensor(out=v3(t1.t[:, 0:T]), in0=v3(t1.t[:, 0:T]), in1=PRE.t[:, base + 2, c, 1:17].unsqueeze(2).to_broadcast([128, 16, 8]), op=ALU.mult), [t1, PRE], [t1])
                    kb.op(dve, lambda: nc.vector.tensor_tensor(out=x_fm.t[:, c, 0:T], in0=x_fm.t[:, c, 0:T], in1=t1.t[:, 0:T], op=ALU.add), [t1, x_fm], [x_fm])

        def ffn(i, which, kind):
            T, NCH, nseq, tps = geom(kind)
            sub = 0 if which == 0 else 2
            norm_mod(i, sub, kind)
            win = ffn_w_in[i, which].rearrange("(c p) n -> p c n", p=128)
            wout = ffn_w_out[i, which].rearrange("(j p) n -> p j n", p=128)
            loads = []
            for j in range(NJ):
                loads.append(lambda t, j=j: [
                    (t[:, 0:2048].rearrange("p (c n) -> p c n", c=8)[:, :, 0:128], win[:, :, j * 128:(j + 1) * 128]),
                    (t[:, 0:2048].rearrange("p (c n) -> p c n", c=8)[:, :, 128:256], win[:, :, DFF + j * 128:DFF + (j + 1) * 128])])
            for m in range(8):
                loads.append(lambda t, m=m: [(t[:, 0:NJ * 128].rearrange("p (j n) -> p j n", j=NJ), wout[:, :, m * 128:(m + 1) * 128])])
            wsm = WStream(kb, loads)
            kb.arena_open()
            actb = [kb.abuf(f"act{j}", [128, 512], BF16) for j in range(NJ)]
            sg = [kb.abuf(f"sg{j}", [128, 512], F32) for j in range(2)]
            f_fm = kb.abuf("f_fm", [128, 8, T], F32)
            for j in range(NJ):
                sl = wsm.get(j)
                wv = sl.t[:, 0:2048].rearrange("p (c n) -> p c n", c=8)
                pg = kb.ps(); pu = kb.ps()
                for kc in range(8):
                    kb.op(pe, lambda: nc.tensor.matmul(pg.t[:, 0:T], lhsT=wv[:, kc, 0:128], rhs=hin.t[:, kc, 0:T], start=(kc == 0), stop=(kc == 7)), [sl, hin], [pg], acc=True)
                for kc in range(8):
                    kb.op(pe, lambda: nc.tensor.matmul(pu.t[:, 0:T], lhsT=wv[:, kc, 128:256], rhs=hin.t[:, kc, 0:T], start=(kc == 0), stop=(kc == 7)), [sl, hin], [pu], acc=True)
                s = sg[j % 2]
                kb.op(act, lambda: nc.scalar.activation(out=s.t[:, 0:T], in_=pg.t[:, 0:T], func=AF.Silu), [pg], [s])
                kb.op(dve, lambda: nc.vector.tensor_tensor(out=actb[j].t[:, 0:T], in0=s.t[:, 0:T], in1=pu.t[:, 0:T], op=ALU.mult), [s, pu], [actb[j]])
            for m in range(8):
                sl = wsm.get(NJ + m)
                wv = sl.t[:, 0:NJ * 128].rearrange("p (j n) -> p j n", j=NJ)
                pb = kb.ps()
                for j in range(NJ):
                    kb.op(pe, lambda: nc.tensor.matmul(pb.t[:, 0:T], lhsT=wv[:, j, :], rhs=actb[j].t[:, 0:T], start=(j == 0), stop=(j == NJ - 1)), [sl, actb[j]], [pb], acc=True)
                kb.op(act, lambda: nc.scalar.copy(out=f_fm.t[:, m, 0:T], in_=pb.t[:, 0:T]), [pb], [f_fm])
            post(i, sub, kind, f_fm, f_fm.t[:, :, :])
            kb.arena_close()

        def ssd(kind, last):
            T, NCH, nseq, tps = geom(kind)
            P = (kind == "P")
            norm_mod(0, 1, kind)
            inw = ssm_in_w.rearrange("(c p) n -> p c n", p=128)
            outw = ssm_out_w.rearrange("(c p) n -> p c n", p=128)
            loads = []
            for q in range(6):
                loads.append(lambda t, q=q: [(t[:, :].rearrange("p (c n) -> p c n", c=8), inw[:, :, 2048 + q * 512:2048 + (q + 1) * 512])])
            for zb in range(4):
                loads.append(lambda t, zb=zb: [(t[:, :].rearrange("p (c n) -> p c n", c=8), inw[:, :, zb * 512:(zb + 1) * 512])])
            for mm in range(4):
                loads.append(lambda t, mm=mm: [(t[:, :].rearrange("p (c n) -> p c n", c=16), outw[:, :, mm * 256:(mm + 1) * 256])])
            wsm = WStream(kb, loads)
            tri_o, stri_o = (C_TRIP, C_STRIP) if P else (C_TRIS, C_STRIS)
            tri = lambda: cstb.t[:, tri_o:tri_o + 128]
            stri = lambda: cstb.t[:, stri_o:stri_o + 128]

            kb.arena_open()
            xsT = kb.abuf("xsT", [128, 16, T], BF16)
            BT = kb.abuf("BT", [128, 4, T], BF16)
            CT = kb.abuf("CT", [128, 4, T], BF16)
            raw = [kb.abuf(f"raw{k}", [128, nseq, 3 + tps], F32) for k in range(2)]
            cacc = [kb.abuf(f"cacc{k}", [128, nseq, tps], F32) for k in range(3)]
            xs_tok = kb.abuf("xs_tok", [128, NCH * DIN], BF16)
            xsv = xs_tok.t[:, :].rearrange("p (c n) -> p c n", c=NCH)
            tail = tailP if P else kb.abuf("tailS", [128, 24, 48], F32)
            B_tok = kb.abuf("B_tok", [128, NCH, 512], BF16)
            sz = kb.abuf("sz", [128, NCH, DIN], BF16)
            ynT = xsT
            dA = kb.abuf("dA", [128, NCH, 32], F32)
            dtv = kb.abuf("dtv", [128, NCH, 32], F32)
            sp1 = kb.abuf("sp1", [128, 32], F32)
            sp2 = kb.abuf("sp2", [128, 32], F32)

            if not P:
                hist = kb.abuf("hist", [48, 3072], F32)
                kb.dma(sp, hist.t[:], st_conv, writes=[hist], dsem=msem())
                for cc in range(24):
                    pb = kb.ps()
                    kb.op(pe, lambda: nc.tensor.transpose(out=pb.t[:, 0:48], in_=hist.t[:, cc * 128:(cc + 1) * 128], identity=cstb.t[0:48, C_ID:C_ID + 48]), [hist, cstb], [pb])
                    kb.op(dve, lambda: nc.vector.tensor_copy(out=tail.t[:, cc, :], in_=pb.t[:, 0:48]), [pb], [tail])

            for ch in range(NCH):
                pb = kb.ps()
                for kc in range(8):
                    kb.op(pe, lambda: nc.tensor.matmul(pb.t[:, 0:32], lhsT=hin.t[:, kc, ch * 128:(ch + 1) * 128], rhs=wdt.t[:, kc, :], start=(kc == 0), stop=(kc == 7)), [hin, wdt], [pb], acc=True)
                kb.op(dve, lambda: nc.vector.tensor_tensor(out=sp1.t[:], in0=pb.t[:, 0:32], in1=vec32.t[:, 0, :], op=ALU.add), [pb, vec32], [sp1])
                kb.op(dve, lambda: nc.vector.tensor_scalar(out=sp2.t[:], in0=sp1.t[:], scalar1=-1.0, scalar2=None, op0=ALU.mult), [sp1], [sp2])
                kb.op(dve, lambda: nc.vector.tensor_tensor(out=sp2.t[:], in0=sp2.t[:], in1=sp1.t[:], op=ALU.max), [sp1, sp2], [sp2])
                kb.op(act, lambda: nc.scalar.activation(out=sp2.t[:], in_=sp2.t[:], func=AF.Exp, scale=-1.0), [sp2], [sp2])
                kb.op(act, lambda: nc.scalar.activation(out=sp2.t[:], in_=sp2.t[:], func=AF.Ln, bias=epsb.t[:, 1:2], scale=1.0), [sp2, epsb], [sp2])
                kb.op(dve, lambda: nc.vector.tensor_scalar(out=sp1.t[:], in0=sp1.t[:], scalar1=0.0, scalar2=None, op0=ALU.max), [sp1], [sp1])
                kb.op(dve, lambda: nc.vector.tensor_tensor(out=dtv.t[:, ch, :], in0=sp1.t[:], in1=sp2.t[:], op=ALU.add), [sp1, sp2], [dtv])
                kb.op(dve, lambda: nc.vector.tensor_tensor(out=dA.t[:, ch, :], in0=dtv.t[:, ch, :], in1=vec32.t[:, 1, :], op=ALU.mult), [dtv, vec32], [dA])

            def conv_s1(cc):
                sl = wsm.get(cc // 4)
                wv = sl.t[:, :].rearrange("p (c n) -> p c n", c=8)
                pb = kb.ps()
                for kc in range(8):
                    kb.op(pe, lambda: nc.tensor.matmul(pb.t[:, 0:T], lhsT=wv[:, kc, (cc % 4) * 128:(cc % 4 + 1) * 128], rhs=hin.t[:, kc, 0:T], start=(kc == 0), stop=(kc == 7)), [sl, hin], [pb], acc=True)
                rw = raw[cc % 2]; ca = cacc[cc % 3]
                kb.op(dve, lambda: nc.vector.tensor_copy(out=rw.t[:, :, 0:3], in_=tail.t[:, cc, :].rearrange("p (s j) -> p s j", j=3)), [tail], [rw])
                kb.op(act, lambda: nc.scalar.copy(out=rw.t[:, :, 3:3 + tps], in_=pb.t[:, 0:T].rearrange("p (s t) -> p s t", s=nseq)), [pb, rw], [rw])
                kb.op(dve, lambda: nc.vector.tensor_copy(out=tail.t[:, cc, :].rearrange("p (s j) -> p s j", j=3), in_=rw.t[:, :, tps:tps + 3]), [rw, tail], [tail])
                kb.op(act, lambda: nc.scalar.activation(out=ca.t[:], in_=rw.t[:, :, 0:tps], func=AF.Identity, bias=convb.t[:, cc:cc + 1], scale=convw.t[:, cc, 0:1]), [rw, convw, convb], [ca])
                for k in range(1, 4):
                    kb.op(dve, lambda: nc.vector.scalar_tensor_tensor(out=ca.t[:], in0=rw.t[:, :, k:k + tps], scalar=convw.t[:, cc, k:k + 1], in1=ca.t[:], op0=ALU.mult, op1=ALU.add), [rw, convw, ca], [ca])

            def conv_s2(cc):
                ca = cacc[cc % 3]
                if cc < 16:
                    dstb, dst = xsT, xsT.t[:, cc, :]
                elif cc < 20:
                    dstb, dst = BT, BT.t[:, cc - 16, :]
                else:
                    dstb, dst = CT, CT.t[:, cc - 20, :]
                kb.op(act, lambda: nc.scalar.activation(out=dst.rearrange("p (s t) -> p s t", s=nseq), in_=ca.t[:], func=AF.Silu), [ca], [dstb])

            for cc in range(24):
                conv_s1(cc)
                if cc >= 1:
                    conv_s2(cc - 1)
            conv_s2(23)

            if last and P:
                with nc.allow_non_contiguous_dma(reason="small state out"):
                    for j3 in range(3):
                        out_dma(sp, conv_p[j3].rearrange("(c p) -> p c", p=128), tail.t[:, :, j3], [tail])
            if last and not P:
                nr = nseq * 3
                cso = hist
                for cc in range(24):
                    pb = kb.ps()
                    kb.op(pe, lambda: nc.tensor.transpose(out=pb.t[0:nr, 0:128], in_=tail.t[:, cc, :], identity=ident()), [tail, cstb], [pb])
                    kb.op(dve, lambda: nc.vector.tensor_copy(out=cso.t[:, cc * 128:(cc + 1) * 128], in_=pb.t[0:nr, 0:128]), [pb], [cso])
                out_dma(sp, conv_p if P else conv_s, cso.t[:], [cso])

            for ch in range(NCH):
                for q in range(4):
                    pb = kb.ps()
                    pbv = pb.t[:].bitcast(BF16)
                    for k in range(4):
                        cc = q * 4 + k
                        kb.op(pe, lambda: nc.tensor.transpose(out=pbv[:, k * 128:(k + 1) * 128], in_=xsT.t[:, cc, ch * 128:(ch + 1) * 128], identity=ident_bf()), [xsT, cbf], [pb], acc=True)
                    kb.op(act, lambda: nc.scalar.copy(out=xsv[:, ch, q * 512:(q + 1) * 512], in_=pbv[:, 0:512]), [pb], [xs_tok])
                pb = kb.ps()
                pbv = pb.t[:].bitcast(BF16)
                for g in range(4):
                    kb.op(pe, lambda: nc.tensor.transpose(out=pbv[:, g * 128:(g + 1) * 128], in_=BT.t[:, g, ch * 128:(ch + 1) * 128], identity=ident_bf()), [BT, cbf], [pb], acc=True)
                kb.op(act, lambda: nc.scalar.copy(out=B_tok.t[:, ch, :], in_=pbv[:, 0:512]), [pb], [B_tok])

            for zb in range(4):
                sl = wsm.get(6 + zb)
                wv = sl.t[:, :].rearrange("p (c n) -> p c n", c=8)
                for ch in range(NCH):
                    pb = kb.ps()
                    for kc in range(8):
                        kb.op(pe, lambda: nc.tensor.matmul(pb.t[:, :], lhsT=hin.t[:, kc, ch * 128:(ch + 1) * 128], rhs=wv[:, kc, :], start=(kc == 0), stop=(kc == 7)), [sl, hin], [pb], acc=True)
                    kb.op(act, lambda: nc.scalar.activation(out=sz.t[:, ch, zb * 512:(zb + 1) * 512], in_=pb.t[:, :], func=AF.Silu), [pb], [sz])

            R1q = [kb.abuf(f"R1q{k}", [128, 8, 128], F32) for k in range(2)]
            Lsb = [kb.abuf(f"Lsb{k}", [128, 512], F32) for k in range(2)]
            wT = kb.abuf("wT", [128, 32, 128], BF16)
            cbm = kb.abuf("cbm", [128, 4, 128], F32)
            xdt = kb.abuf("xdt", [128, DIN], BF16)
            xdec = xdt if P else kb.abuf("xdec", [128, DIN], BF16)
            sm = kb.abuf("sm", [128, 4, 32], F32)
            ygb = [kb.abuf(f"yg{k}", [128, 512], F32) for k in range(3)]
            ynb = [kb.abuf(f"yn{k}", [128, 512], BF16) for k in range(2)]
            ssq = kb.abuf("ssq", [128, 8], F32)
            junk = kb.abuf("junk", [128, 512], BF16)
            v32 = lambda ap: ap.rearrange("p (h d) -> p h d", h=32)
            if not P:
                dAexp = wT
                dAv = wT.t[:, :, :].rearrange("p h l -> p (h l)").bitcast(F32)
                decS = kb.abuf("decS", [128, 16, 16], F32)
                CTmj = [kb.abuf(f"CTmj{k}", [128, 4, 128], BF16) for k in range(2)]
                h0b = [kb.abuf(f"h0b{k}", [128, 16, 128], BF16) for k in range(2)]
                h0f = kb.abuf("h0f", [128, 16, 128], F32)
                hnv = hT.t[:, :].rearrange("p (c n) -> p c n", c=16)
                Bm = [kb.abuf(f"Bm{k}", [128, 512], BF16) for k in range(2)]
                mJL = kb.abuf("mJL", [128, 16, 128], BF16)
                kb.op(dve, lambda: nc.vector.memset(mJL.t[:], 0.0), [], [mJL])
                for j in range(16):
                    kb.op(dve, lambda: nc.vector.memset(mJL.t[:, j, j * 8:(j + 1) * 8], 1.0), [mJL], [mJL])

            for ch in range(NCH):
                csl = slice(ch * 128, (ch + 1) * 128)
                pv = kb.ps()
                kb.op(pe, lambda: nc.tensor.matmul(pv.t[:, 0:32], lhsT=tri(), rhs=dA.t[:, ch, :], start=True, stop=True), [dA, cstb], [pv])
                kb.op(pe, lambda: nc.tensor.matmul(pv.t[:, 32:64], lhsT=stri(), rhs=dA.t[:, ch, :], start=True, stop=True), [dA, cstb], [pv], acc=True)
                kb.op(pe, lambda: nc.tensor.matmul(pv.t[:, 64:96], lhsT=ones_f(), rhs=dA.t[:, ch, :], start=True, stop=True), [dA, cstb], [pv], acc=True)
                kb.op(act, lambda: nc.scalar.activation(out=sm.t[:, 0:3, :], in_=pv.t[:, 0:96].rearrange("p (a h) -> p a h", a=3), func=AF.Exp), [pv], [sm])
                pc = kb.ps()
                for g in range(4):
                    kb.op(pe, lambda: nc.tensor.matmul(pc.t[:, g * 128:(g + 1) * 128], lhsT=BT.t[:, g, csl], rhs=CT.t[:, g, csl], start=True, stop=True), [BT, CT], [pc], acc=True)
                kb.op(dve, lambda: nc.vector.tensor_tensor(out=cbm.t[:], in0=pc.t[:, :].rearrange("p (g l) -> p g l", g=4), in1=tri().unsqueeze(1).to_broadcast([128, 4, 128]), op=ALU.mult), [pc, cstb], [cbm])
                kb.op(dve, lambda: nc.vector.tensor_tensor(out=v32(xdt.t[:]), in0=v32(xsv[:, ch, :]), in1=dtv.t[:, ch, :].unsqueeze(2).to_broadcast([128, 32, 64]), op=ALU.mult), [xs_tok, dtv], [xdt])
                if not P:
                    kb.op(dve, lambda: nc.vector.tensor_tensor(out=v32(xdec.t[:]), in0=v32(xdt.t[:]), in1=sm.t[:, 1, :].unsqueeze(2).to_broadcast([128, 32, 64]), op=ALU.mult), [xdt, sm], [xdec])
                    kb.op(dve, lambda: nc.vector.tensor_copy(out=v32(dAv), in_=dA.t[:, 0, :].unsqueeze(2).to_broadcast([128, 32, 64])), [dA], [dAexp])
                    pd = kb.ps()
                    for c in range(16):
                        kb.op(pe, lambda: nc.tensor.matmul(pd.t[:, c * 16:(c + 1) * 16], lhsT=dAv[:, c * 128:(c + 1) * 128], rhs=cstb.t[:, C_SEL:C_SEL + 16], start=True, stop=True), [dAexp, cstb], [pd], acc=True)
                    kb.op(act, lambda: nc.scalar.activation(out=decS.t[:], in_=pd.t[:, 0:256].rearrange("p (c j) -> p c j", c=16), func=AF.Exp), [pd], [decS])
                yo = []
                if not P:
                    yo = [kb.ps_hold() for g in range(4)]
                    for j in range(NSEQ_S):
                        hb = h0b[j % 2]; hf = h0f; ht = hT_bf; hn = hT; bm = Bm[j % 2]; CTm = CTmj[j % 2]
                        kb.op(dve, lambda: nc.vector.tensor_tensor(out=CTm.t[:], in0=CT.t[:, :, :], in1=mJL.t[:, j, :].unsqueeze(1).to_broadcast([128, 4, 128]), op=ALU.mult), [CT, mJL], [CTm])
                        kb.dma(pool, hb.t[:], st_ssm[j].rearrange("(c p) n -> p c n", p=128), writes=[hb], dsem=msem())
                        kb.dma(sp, hf.t[:], st_ssm[j].rearrange("(c p) n -> p c n", p=128), writes=[hf], dsem=msem())
                        for q in range(4):
                            pb = kb.ps()
                            pbv = pb.t[:].bitcast(BF16)
                            for k in range(4):
                                c = q * 4 + k
                                kb.op(pe, lambda: nc.tensor.transpose(out=pbv[:, k * 128:(k + 1) * 128], in_=hb.t[:, c, :], identity=ident_bf()), [hb, cbf], [pb], acc=True)
                            kb.op(act, lambda: nc.scalar.copy(out=ht.t[:, q * 512:(q + 1) * 512], in_=pbv[:, 0:512]), [pb], [ht])
                        for g in range(4):
                            kb.op(pe, lambda: nc.tensor.matmul(yo[g].t[:, :], lhsT=CTm.t[:, g, :], rhs=ht.t[:, g * 512:(g + 1) * 512], start=(j == 0), stop=(j == NSEQ_S - 1)), [CTm, ht], [yo[g]], acc=True)
                        kb.op(dve, lambda: nc.vector.tensor_scalar(out=bm.t[:], in0=B_tok.t[:, 0, :], scalar1=cstb.t[:, C_SEL + j:C_SEL + j + 1], scalar2=None, op0=ALU.mult), [B_tok, cstb], [bm])
                        for q in range(4):
                            pb = kb.ps()
                            for k in range(4):
                                c = q * 4 + k
                                kb.op(pe, lambda: nc.tensor.matmul(pb.t[:, k * 128:(k + 1) * 128], lhsT=xdec.t[:, c * 128:(c + 1) * 128], rhs=bm.t[:, (c // 4) * 128:(c // 4 + 1) * 128], start=True, stop=True), [xdec, bm], [pb], acc=True)
                            for k in range(4):
                                c = q * 4 + k
                                kb.op(dve, lambda: nc.vector.scalar_tensor_tensor(out=hnv[:, c, :], in0=hf.t[:, c, :], scalar=decS.t[:, c, j:j + 1], in1=pb.t[:, k * 128:(k + 1) * 128], op0=ALU.mult, op1=ALU.add), [hf, decS, pb], [hn])
                        out_dma(sp, ssm_s[j].rearrange("(c p) n -> p c n", p=128), hnv, [hn])
                kb.op(dve, lambda: nc.vector.memset(ssq.t[:], 0.0), [ssq], [ssq])
                for g in range(4):
                    R1 = R1q[g % 2]
                    kb.op(dve, lambda: nc.vector.tensor_tensor(out=R1.t[:], in0=dA.t[:, ch, g * 8:(g + 1) * 8].unsqueeze(2).to_broadcast([128, 8, 128]),
                                                               in1=tri().unsqueeze(1).to_broadcast([128, 8, 128]), op=ALU.mult), [dA, cstb], [R1])
                    for b in range(2):
                        pb = kb.ps()
                        kb.op(pe, lambda: nc.tensor.matmul(pb.t[:, :], lhsT=stri(), rhs=R1.t[:, :, :].rearrange("p h l -> p (h l)")[:, b * 512:(b + 1) * 512], start=True, stop=True), [R1, cstb], [pb])
                        L = Lsb[b]
                        kb.op(act, lambda: nc.scalar.activation(out=L.t[:], in_=pb.t[:, :], func=AF.Exp), [pb], [L])
                        h0_ = g * 8 + b * 4
                        kb.op(dve, lambda: nc.vector.tensor_tensor(out=wT.t[:, h0_:h0_ + 4, :], in0=L.t[:].rearrange("p (h l) -> p h l", h=4),
                                                                   in1=cbm.t[:, g, :].unsqueeze(1).to_broadcast([128, 4, 128]), op=ALU.mult), [L, cbm], [wT])
                v3 = lambda ap: ap.rearrange("p (h d) -> p h d", h=8)

                def emitY(g):
                    if P:
                        yo_g = kb.ps()
                        kb.op(pe, lambda: nc.tensor.matmul(yo_g.t[:, :], lhsT=CT.t[:, g, csl], rhs=hT_bf.t[:, g * 512:(g + 1) * 512], start=True, stop=True), [CT, hT_bf], [yo_g])
                    else:
                        yo_g = yo[g]
                    pb = kb.ps()
                    for r in range(8):
                        h = g * 8 + r
                        kb.op(pe, lambda: nc.tensor.matmul(pb.t[:, r * 64:(r + 1) * 64], lhsT=wT.t[:, h, :], rhs=xdt.t[:, h * 64:(h + 1) * 64], start=True, stop=True), [wT, xdt], [pb], acc=True)
                    return yo_g, pb

                def combine(g, yo_g, pb):
                    gs = slice(g * 512, (g + 1) * 512)
                    eab = sm.t[:, 0, g * 8:(g + 1) * 8].unsqueeze(2).to_broadcast([128, 8, 64])
                    Db = vec32.t[:, 2, g * 8:(g + 1) * 8].unsqueeze(2).to_broadcast([128, 8, 64])
                    t1 = tmp(); t2 = tmp()
                    yg = ygb[g % 3]
                    kb.op(dve, lambda: nc.vector.tensor_tensor(out=v3(t1.t[:]), in0=v3(yo_g.t[:, :]), in1=eab, op=ALU.mult), [yo_g, sm], [t1])
                    if not P:
                        kb.ps_release(yo_g)
                    kb.op(dve, lambda: nc.vector.tensor_tensor(out=t1.t[:], in0=t1.t[:], in1=pb.t[:, :], op=ALU.add), [t1, pb], [t1])
                    kb.op(dve, lambda: nc.vector.tensor_tensor(out=v3(t2.t[:]), in0=v3(xsv[:, ch, gs]), in1=Db, op=ALU.mult), [xs_tok, vec32], [t2])
                    kb.op(dve, lambda: nc.vector.tensor_tensor(out=t1.t[:], in0=t1.t[:], in1=t2.t[:], op=ALU.add), [t1, t2], [t1])
                    kb.op(dve, lambda: nc.vector.tensor_tensor(out=yg.t[:], in0=t1.t[:], in1=sz.t[:, ch, gs], op=ALU.mult), [t1, sz], [yg])
                    kb.op(act, lambda: nc.scalar.activation(out=junk.t[:], in_=yg.t[:], func=AF.Square, accum_out=ssq.t[:, g:g + 1]), [yg, ssq], [junk, ssq])
                    kb.op(act, lambda: nc.scalar.activation(out=ssq.t[:, 4 + g:5 + g], in_=ssq.t[:, g:g + 1], func=AF.Ln, bias=epsb.t[:, 0:1], scale=1.0 / 512), [ssq, epsb], [ssq])
                    kb.op(act, lambda: nc.scalar.activation(out=ssq.t[:, 4 + g:5 + g], in_=ssq.t[:, 4 + g:5 + g], func=AF.Exp, scale=-0.5), [ssq], [ssq])

                def finish(g):
                    yg = ygb[g % 3]; yn = ynb[g % 2]
                    kb.op(dve, lambda: nc.vector.tensor_scalar(out=yn.t[:], in0=yg.t[:], scalar1=ssq.t[:, 4 + g:5 + g], scalar2=None, op0=ALU.mult), [yg, ssq], [yn])
                    pq = kb.ps()
                    pqv = pq.t[:].bitcast(BF16)
                    for k in range(4):
                        kb.op(pe, lambda: nc.tensor.transpose(out=pqv[:, k * 128:(k + 1) * 128], in_=yn.t[:, k * 128:(k + 1) * 128], identity=ident_bf()), [yn, cbf], [pq], acc=True)
                    kb.op(dve, lambda: nc.vector.tensor_tensor(out=ynT.t[:, g * 4:(g + 1) * 4, csl], in0=pqv[:, 0:512].rearrange("p (k t) -> p k t", k=4),
                                                               in1=normwT.t[:, g * 4:(g + 1) * 4].unsqueeze(2).to_broadcast([128, 4, 128]), op=ALU.mult), [pq, normwT], [ynT])

                Y = {0: emitY(0), 1: emitY(1)}
                combine(0, *Y[0])
                for g in range(1, 4):
                    if g + 1 < 4:
                        Y[g + 1] = emitY(g + 1)
                    combine(g, *Y[g])
                    finish(g - 1)
                finish(3)

                if P:
                    kb.op(dve, lambda: nc.vector.tensor_tensor(out=v32(xdt.t[:]), in0=v32(xdt.t[:]), in1=sm.t[:, 1, :].unsqueeze(2).to_broadcast([128, 32, 64]), op=ALU.mult), [xdt, sm], [xdt])
                    for g in range(4):
                        gs = slice(g * 512, (g + 1) * 512)
                        pb = kb.ps()
                        kb.op(pe, lambda: nc.tensor.matmul(pb.t[:, :], lhsT=B_tok.t[:, ch, g * 128:(g + 1) * 128], rhs=xdt.t[:, gs], start=True, stop=True), [B_tok, xdt], [pb])
                        v3 = lambda ap: ap.rearrange("p (h d) -> p h d", h=8)
                        kb.op(dve, lambda: nc.vector.tensor_tensor(out=v3(hT.t[:, gs]), in0=v3(hT.t[:, gs]), in1=sm.t[:, 2, g * 8:(g + 1) * 8].unsqueeze(2).to_broadcast([128, 8, 64]), op=ALU.mult), [hT, sm], [hT])
                        kb.op(dve, lambda: nc.vector.tensor_tensor(out=hT.t[:, gs], in0=hT.t[:, gs], in1=pb.t[:, :], op=ALU.add), [hT, pb], [hT])
                    kb.op(act, lambda: nc.scalar.copy(out=hT_bf.t[:], in_=hT.t[:]), [hT], [hT_bf])

            if P and last:
                houtv = sz.t[:, :, :].rearrange("p c n -> p (c n)").bitcast(F32)[:, 0:2048].rearrange("p (c n) -> p c n", c=16)
                hout = sz
                for c in range(16):
                    pb = kb.ps()
                    kb.op(pe, lambda: nc.tensor.transpose(out=pb.t[:, 0:128], in_=hT.t[:, c * 128:(c + 1) * 128], identity=ident()), [hT, cstb], [pb])
                    kb.op(dve, lambda: nc.vector.tensor_copy(out=houtv[:, c, :], in_=pb.t[:, 0:128]), [pb], [hout])
                out_dma(sp, ssm_p.rearrange("(c p) n -> p c n", p=128), houtv, [hout])

            fv = xs_tok.t[:, :].bitcast(F32).rearrange("p (m t) -> p m t", m=8)
            for m in range(8):
                sl = wsm.get(10 + m // 2)
                wv = sl.t[:, :].rearrange("p (c n) -> p c n", c=16)
                pb = kb.ps()
                for kc in range(16):
                    kb.op(pe, lambda: nc.tensor.matmul(pb.t[:, 0:T], lhsT=wv[:, kc, (m % 2) * 128:(m % 2 + 1) * 128], rhs=ynT.t[:, kc, 0:T], start=(kc == 0), stop=(kc == 15)), [sl, ynT], [pb], acc=True)
                kb.op(act, lambda: nc.scalar.copy(out=fv[:, m, :], in_=pb.t[:, 0:T]), [pb], [xs_tok])
            post(0, 1, kind, xs_tok, fv)
            kb.arena_close()

        def swa(kind, first, last):
            T, NCH, nseq, tps = geom(kind)
            P = (kind == "P")
            norm_mod(1, 1, kind)
            qw = qkv_w.rearrange("(c p) n -> p c n", p=128)
            ow = o_w.rearrange("(c p) n -> p c n", p=128)
            loads = []
            for half in range(2):
                loads.append(lambda t, half=half: [(t[:, :].rearrange("p (c n) -> p c n", c=8), qw[:, :, half * 512:(half + 1) * 512])])
            loads.append(lambda t: [(t[:, :].rearrange("p (c n) -> p c n", c=8), qw[:, :, 1024:1536])])
            for mm in range(2):
                loads.append(lambda t, mm=mm: [(t[:, :].rearrange("p (c n) -> p c n", c=8), ow[:, :, mm * 512:(mm + 1) * 512])])
            wsm = WStream(kb, loads)

            kb.arena_open()
            qT = kb.abuf("qT", [128, 8, T], BF16)
            kv_tok = kb.abuf("kv_tok", [128, NCH, 512], F32)
            biasT = kb.abuf("biasT", [128, 16, 256], F32)
            tq = [kb.abuf(f"tq{k}", [128, 256], F32) for k in range(2)]
            sS = [kb.abuf(f"sS{k}", [128, 256], F32) for k in range(3)]
            eS = [kb.abuf(f"eS{k}", [128, 256], BF16) for k in range(3)]
            en = [kb.abuf(f"en{k}", [128, 256], BF16) for k in range(3)]
            pT = [kb.abuf(f"pT{k}", [128, 2, 128], BF16) for k in range(3)]
            st = [kb.abuf(f"st{k}", [128, 8], F32) for k in range(3)]
            o_tok = kb.abuf("o_tok", [128, D], BF16)
            attnT = kb.abuf("attnT", [128, 8, T], BF16)
            f_fm = kb.abuf("f_fm", [128, 8, T], F32)

            with nc.allow_non_contiguous_dma(reason="toeplitz"):
                for h in range(16):
                    t = tq[h % 2]
                    src = bass.AP(uscr, h * 384, [[1, 128], [1, 256]])
                    kb.dma(sp, t.t[:], src, reads=[uev], writes=[t], dsem=msem())
                    pb = kb.ps()
                    jo = C_J if P else C_JREP
                    kb.op(pe, lambda: nc.tensor.matmul(pb.t[:, 0:256], lhsT=cstb.t[:, jo:jo + 128], rhs=t.t[:], start=True, stop=True), [t, cstb], [pb])
                    kb.op(dve, lambda: nc.vector.tensor_copy(out=biasT.t[:, h, :], in_=pb.t[:, 0:256]), [pb], [biasT])

            if flags.get("swa_stop") == 1:
                kb.arena_close(); return
            wqp = [kb.abuf(f"wqp{k}", [128, 8, 4, 2, 64], BF16) for k in range(2)]
            for half in range(2):
                sl = wsm.get(half)
                nat = sl.t[:, :].rearrange("p (c n) -> p c n", c=8)
                for a in range(4):
                    for b in range(2):
                        kb.op(dve, lambda: nc.vector.tensor_copy(out=wqp[half].t[:, :, a, b, :], in_=nat[:, :, b * 256 + a * 64:b * 256 + (a + 1) * 64]), [sl], [wqp[half]])
            for c in range(8):
                wb = wqp[c // 4]
                wv = wb.t[:, :, :, :, :].rearrange("p c a b d -> p c a (b d)")
                pb = kb.ps()
                for kc in range(8):
                    kb.op(pe, lambda: nc.tensor.matmul(pb.t[:, 0:T], lhsT=wv[:, kc, c % 4, :], rhs=hin.t[:, kc, 0:T], start=(kc == 0), stop=(kc == 7)), [wb, hin], [pb], acc=True)
                kb.op(act, lambda: nc.scalar.activation(out=qT.t[:, c, :], in_=pb.t[:, 0:T], func=AF.Identity, bias=qkb.t[:, c:c + 1], scale=0.125), [pb, qkb], [qT])
            sl = wsm.get(2)
            wv = sl.t[:, :].rearrange("p (c n) -> p c n", c=8)
            for c2 in range(2):
                pb = kb.ps()
                for kc in range(8):
                    kb.op(pe, lambda: nc.tensor.matmul(pb.t[:, 0:T], lhsT=wv[:, kc, c2 * 128:(c2 + 1) * 128], rhs=hin.t[:, kc, 0:T], start=(kc == 0), stop=(kc == 7)), [sl, hin], [pb], acc=True)
                kb.op(act, lambda: nc.scalar.activation(out=kT.t[:, c2, 128:128 + T], in_=pb.t[:, 0:T], func=AF.Identity, bias=qkb.t[:, 8 + c2:9 + c2], scale=1.0), [pb, qkb], [kT])
            for ch in range(NCH):
                pb = kb.ps()
                for kc in range(8):
                    kb.op(pe, lambda: nc.tensor.matmul(pb.t[:, :], lhsT=hin.t[:, kc, ch * 128:(ch + 1) * 128], rhs=wv[:, kc, :], start=(kc == 0), stop=(kc == 7)), [sl, hin], [pb], acc=True)
                kb.op(dve, lambda: nc.vector.tensor_tensor(out=kv_tok.t[:, ch, :], in0=pb.t[:, :], in1=kvb.t[:, :], op=ALU.add), [pb, kvb], [kv_tok])
                kb.op(act, lambda: nc.scalar.copy(out=vtok.t[:, 1 + ch, :], in_=kv_tok.t[:, ch, 256:512]), [kv_tok], [vtok])

            if flags.get("swa_stop") == 2:
                kb.arena_close(); return

            LOOK = 2

            def softmax_head(idx, h, lg, nk, bias_ap, selcol):
                k3 = idx % 3
                s_ = sS[k3]; e_ = eS[k3]; n_ = en[k3]; t_ = st[k3]
                kb.op(dve, lambda: nc.vector.memset(t_.t[:], 0.0), [t_], [t_])
                kb.op(dve, lambda: nc.vector.tensor_tensor(out=s_.t[:, 0:nk], in0=lg.t[:, 0:nk], in1=bias_ap, op=ALU.add), [lg, biasT], [s_])
                kb.op(dve, lambda: nc.vector.tensor_reduce(out=t_.t[:, 0:1], in_=s_.t[:, 0:nk], axis=AX.X, op=ALU.max), [s_], [t_])
                kb.op(dve, lambda: nc.vector.tensor_scalar(out=t_.t[:, 1:2], in0=t_.t[:, 0:1], scalar1=sinkb.t[:, h:h + 1], scalar2=-1.0, op0=ALU.max, op1=ALU.mult), [t_, sinkb], [t_])
                kb.op(act, lambda: nc.scalar.activation(out=e_.t[:, 0:nk], in_=s_.t[:, 0:nk], func=AF.Exp, bias=t_.t[:, 1:2], scale=1.0, accum_out=t_.t[:, 2:3]), [s_, t_], [e_, t_])
                kb.op(act, lambda: nc.scalar.activation(out=t_.t[:, 3:4], in_=sinkb.t[:, h:h + 1], func=AF.Exp, bias=t_.t[:, 1:2], scale=1.0), [sinkb, t_], [t_])
                kb.op(dve, lambda: nc.vector.tensor_tensor(out=t_.t[:, 4:5], in0=t_.t[:, 2:3], in1=t_.t[:, 3:4], op=ALU.add), [t_], [t_])
                kb.op(dve, lambda: nc.vector.reciprocal(out=t_.t[:, 5:6], in_=t_.t[:, 4:5]), [t_], [t_])
                if selcol is not None:
                    kb.op(dve, lambda: nc.vector.tensor_tensor(out=t_.t[:, 5:6], in0=t_.t[:, 5:6], in1=selcol, op=ALU.mult), [t_, cstb], [t_])
                kb.op(dve, lambda: nc.vector.tensor_scalar(out=n_.t[:, 0:nk], in0=e_.t[:, 0:nk], scalar1=t_.t[:, 5:6], scalar2=None, op0=ALU.mult), [e_, t_], [n_])
                return n_

            def head_rows(h):
                g = h // 4
                if h < 8:
                    c = h % 4; half = h // 4
                else:
                    c = 4 + (h - 8) % 4; half = (h - 8) // 4
                return c, half, g

            def o_finish(po, csl):
                for k in range(2):
                    kb.op(act, lambda: nc.scalar.copy(out=o_tok.t[:, k * 512:(k + 1) * 512], in_=po[k].t[:, :]), [po[k]], [o_tok])
                    kb.ps_release(po[k])
                for q in range(2):
                    pb = kb.ps()
                    pbv = pb.t[:].bitcast(BF16)
                    for k in range(4):
                        c = q * 4 + k
                        kb.op(pe, lambda: nc.tensor.transpose(out=pbv[:, k * 128:(k + 1) * 128], in_=o_tok.t[:, c * 128:(c + 1) * 128], identity=ident_bf()), [o_tok, cbf], [pb], acc=True)
                    kb.op(act, lambda: nc.scalar.copy(out=attnT.t[:, q * 4:(q + 1) * 4, csl], in_=pbv[:, 0:512].rearrange("p (k t) -> p k t", k=4)), [pb], [attnT])

            if P:
                for ch in range(NCH):
                    csl = slice(ch * 128, (ch + 1) * 128)
                    blk0 = first and ch == 0
                    po = [kb.ps_hold() for _ in range(2)]

                    def logits(h):
                        c, half, g = head_rows(h)
                        rs = slice(half * 64, (half + 1) * 64)
                        lg = kb.ps()
                        if blk0:
                            kb.op(pe, lambda: nc.tensor.matmul(lg.t[:, 0:128], lhsT=qT.t[rs, c, csl], rhs=kT.t[rs, g // 2, 128:256], start=True, stop=True), [qT, kT], [lg])
                            return lg, 128, biasT.t[:, h, 128:256]
                        kb.op(pe, lambda: nc.tensor.matmul(lg.t[:, 0:256], lhsT=qT.t[rs, c, csl], rhs=kT.t[rs, g // 2, ch * 128:ch * 128 + 256], start=True, stop=True), [qT, kT], [lg])
                        return lg, 256, biasT.t[:, h, 0:256]

                    def tailp(h, n_, nk):
                        g = h // 4
                        nb = nk // 128
                        pt = kb.ps()
                        ptv = pt.t[:].bitcast(BF16)
                        for b in range(nb):
                            kb.op(pe, lambda: nc.tensor.transpose(out=ptv[:, b * 128:(b + 1) * 128], in_=n_.t[:, b * 128:(b + 1) * 128], identity=ident_bf()), [n_, cbf], [pt], acc=True)
                        p_ = pT[h % 3]
                        kb.op(act, lambda: nc.scalar.copy(out=p_.t[:, 0:nb, :], in_=ptv[:, 0:nb * 128].rearrange("p (b q) -> p b q", b=nb)), [pt], [p_])
                        ob_ = po[h // 8]
                        oc = slice((h % 8) * 64, (h % 8 + 1) * 64)
                        for b in range(nb):
                            vb = (ch + b) if not blk0 else 1
                            kb.op(pe, lambda: nc.tensor.matmul(ob_.t[:, oc], lhsT=p_.t[:, b, :], rhs=vtok.t[:, vb, g * 64:(g + 1) * 64], start=(b == 0), stop=(b == nb - 1)), [p_, vtok], [ob_], acc=True)

                    pend = {}
                    for h in range(LOOK):
                        pend[h] = logits(h)
                    for h in range(16):
                        lg, nk, bias_ap = pend.pop(h)
                        n_ = softmax_head(h, h, lg, nk, bias_ap, None)
                        if h + LOOK < 16:
                            pend[h + LOOK] = logits(h + LOOK)
                        tailp(h, n_, nk)
                    o_finish(po, csl)
                if last:
                    out_dma(sp, kp, kv_tok.t[:, NCH - 1, 0:256], [kv_tok])
                    out_dma(sp, vp, kv_tok.t[:, NCH - 1, 256:512], [kv_tok])
                kb.op(dve, lambda: nc.vector.tensor_copy(out=kT.t[:, :, 0:128], in_=kT.t[:, :, T:T + 128]), [kT], [kT])
                kb.op(dve, lambda: nc.vector.tensor_copy(out=vtok.t[:, 0, :], in_=vtok.t[:, NCH, :]), [vtok], [vtok])
            else:
                ckf = [kb.abuf(f"ckf{k}", [128, 256], F32) for k in range(2)]
                kTj = [kb.abuf(f"kTj{k}", [128, 2, 136], BF16) for k in range(2)]
                vj = [kb.abuf(f"vj{k}", [128, 256], BF16) for k in range(2)]
                vnew = kb.abuf("vnew", [8, 16, 256], BF16)
                vtb = kb.abuf("vtb", [128, 256], BF16)
                kb.op(dve, lambda: nc.vector.tensor_copy(out=vtb.t[:], in_=vtok.t[:, 1, :]), [vtok], [vtb])
                for j in range(NSEQ_S):
                    kb.dma(sp, vnew.t[:, j, :], vtb.t[j * 8:(j + 1) * 8, :], reads=[vtb], writes=[vnew], dsem=dmisc[2], join=True)
                po = [kb.ps_hold() for _ in range(2)]

                def prep(j):
                    kf = ckf[j % 2]; ktj = kTj[j % 2]; v_ = vj[j % 2]
                    kb.dma(sp, kf.t[:], ck[j], writes=[kf], dsem=msem())
                    kb.dma(pool, v_.t[:], cv[j], writes=[v_], dsem=msem())
                    out_dma(sp, ks[j, 0:120, :], ck[j, 8:128, :], [])
                    out_dma(sp, vs[j, 0:120, :], cv[j, 8:128, :], [])
                    out_dma(sp, ks[j, 120:128, :], kv_tok.t[j * 8:(j + 1) * 8, 0, 0:256], [kv_tok])
                    out_dma(sp, vs[j, 120:128, :], kv_tok.t[j * 8:(j + 1) * 8, 0, 256:512], [kv_tok])
                    for c2 in range(2):
                        pb = kb.ps()
                        kb.op(pe, lambda: nc.tensor.transpose(out=pb.t[:, 0:128], in_=kf.t[:, c2 * 128:(c2 + 1) * 128], identity=ident()), [kf, cstb], [pb])
                        kb.op(act, lambda: nc.scalar.copy(out=ktj.t[:, c2, 0:128], in_=pb.t[:, 0:128]), [pb], [ktj])
                    kb.op(dve, lambda: nc.vector.tensor_copy(out=ktj.t[:, :, 128:136], in_=kT.t[:, :, 128 + j * 8:128 + (j + 1) * 8]), [kT, ktj], [ktj])

                def logits_s(idx):
                    j, h = divmod(idx, 16)
                    if h == 0:
                        prep(j)
                    ktj = kTj[j % 2]
                    c, half, g = head_rows(h)
                    rs = slice(half * 64, (half + 1) * 64)
                    lg = kb.ps()
                    kb.op(pe, lambda: nc.tensor.matmul(lg.t[:, 0:136], lhsT=qT.t[rs, c, :], rhs=ktj.t[rs, g // 2, :], start=True, stop=True), [qT, ktj], [lg])
                    return lg

                def tails(idx, n_):
                    j, h = divmod(idx, 16)
                    g = h // 4
                    v_ = vj[j % 2]
                    pt = kb.ps()
                    ptv = pt.t[:].bitcast(BF16)
                    kb.op(pe, lambda: nc.tensor.transpose(out=ptv[:, 0:128], in_=n_.t[:, 0:128], identity=ident_bf()), [n_, cbf], [pt], acc=True)
                    kb.op(pe, lambda: nc.tensor.transpose(out=ptv[0:8, 128:256], in_=n_.t[:, 128:136], identity=ident_bf()), [n_, cbf], [pt], acc=True)
                    p_ = pT[idx % 3]
                    kb.op(act, lambda: nc.scalar.copy(out=p_.t[:, 0, :], in_=ptv[:, 0:128]), [pt], [p_])
                    kb.op(act, lambda: nc.scalar.copy(out=p_.t[0:8, 1, :], in_=ptv[0:8, 128:256]), [pt, p_], [p_])
                    ob_ = po[h // 8]
                    oc = slice((h % 8) * 64, (h % 8 + 1) * 64)
                    kb.op(pe, lambda: nc.tensor.matmul(ob_.t[:, oc], lhsT=p_.t[:, 0, :], rhs=v_.t[:, g * 64:(g + 1) * 64], start=(j == 0 and h % 8 == 0), stop=False), [p_, v_], [ob_], acc=True)
                    kb.op(pe, lambda: nc.tensor.matmul(ob_.t[:, oc], lhsT=p_.t[0:8, 1, :], rhs=vnew.t[0:8, j, g * 64:(g + 1) * 64], start=False, stop=(j == NSEQ_S - 1)), [p_, vnew], [ob_], acc=True)

                NI = NSEQ_S * 16
                pend = {}
                for idx in range(LOOK):
                    pend[idx] = logits_s(idx)
                for idx in range(NI):
                    j, h = divmod(idx, 16)
                    lg = pend.pop(idx)
                    n_ = softmax_head(idx, h, lg, 136, biasT.t[:, h, 0:136], cstb.t[:, C_SEL + j:C_SEL + j + 1])
                    if idx + LOOK < NI:
                        pend[idx + LOOK] = logits_s(idx + LOOK)
                    tails(idx, n_)
                o_finish(po, slice(0, 128))


            if flags.get("swa_stop") == 3:
                kb.arena_close(); return
            for m in range(8):
                sl = wsm.get(3 + m // 4)
                wv = sl.t[:, :].rearrange("p (c n) -> p c n", c=8)
                pb = kb.ps()
                for kc in range(8):
                    kb.op(pe, lambda: nc.tensor.matmul(pb.t[:, 0:T], lhsT=wv[:, kc, (m % 4) * 128:(m % 4 + 1) * 128], rhs=attnT.t[:, kc, 0:T], start=(kc == 0), stop=(kc == 7)), [sl, attnT], [pb], acc=True)
                kb.op(act, lambda: nc.scalar.activation(out=f_fm.t[:, m, 0:T], in_=pb.t[:, 0:T], func=AF.Identity, bias=ob.t[:, m:m + 1], scale=1.0), [pb, ob], [f_fm])
            post(1, 1, kind, f_fm, f_fm.t[:, :, :])
            kb.arena_close()

        tiles = [("P", t) for t in range(8)] + [("S", 0)]
        if "tiles" in flags:
            tiles = flags["tiles"]
        xi = [0]
        for kind, ti in tiles:
            T, NCH, nseq, tps = geom(kind)
            src = xp if kind == "P" else xs
            dst = yp if kind == "P" else ys
            first = (ti == 0); last = (ti == 7) or kind == "S"
            if kind == "S":
                pass
            kb.arena_open()
            xtk = [kb.abuf(f"xtk{k}", [128, D], F32) for k in range(2)]
            for ch in range(NCH):
                xt = xtk[xi[0] % 2]; xi[0] += 1
                r0 = ti * 512 + ch * 128
                kb.dma(sp, xt.t[:], src[r0:r0 + 128, :], writes=[xt], dsem=msem())
                for q in range(2):
                    pb = kb.ps()
                    for k in range(4):
                        c = q * 4 + k
                        kb.op(pe, lambda: nc.tensor.transpose(out=pb.t[:, k * 128:(k + 1) * 128], in_=xt.t[:, c * 128:(c + 1) * 128], identity=ident()), [xt, cstb], [pb], acc=True)
                    kb.op(act, lambda: nc.scalar.copy(out=x_fm.t[:, q * 4:(q + 1) * 4, ch * 128:(ch + 1) * 128], in_=pb.t[:, :].rearrange("p (k t) -> p k t", k=4)), [pb], [x_fm])
            kb.arena_close()
            for fl in flags.get("phases", ["f00", "ssd", "f01", "f10", "swa", "f11"]):
                if fl == "f00":
                    ffn(0, 0, kind)
                elif fl == "ssd":
                    ssd(kind, last)
                elif fl == "f01":
                    ffn(0, 1, kind)
                elif fl == "f10":
                    ffn(1, 0, kind)
                elif fl == "swa":
                    swa(kind, first, last)
                elif fl == "f11":
                    ffn(1, 1, kind)
            kb.arena_open()
            xtk = [kb.abuf(f"xtk{k}", [128, D], F32) for k in range(2)]
            for ch in range(NCH):
                xt = xtk[xi[0] % 2]; xi[0] += 1
                r0 = ti * 512 + ch * 128
                for q in range(2):
                    pb = kb.ps()
                    for k in range(4):
                        c = q * 4 + k
                        kb.op(pe, lambda: nc.tensor.transpose(out=pb.t[:, k * 128:(k + 1) * 128], in_=x_fm.t[:, c, ch * 128:(ch + 1) * 128], identity=ident()), [x_fm, cstb], [pb], acc=True)
                    kb.op(act, lambda: nc.scalar.copy(out=xt.t[:, q * 512:(q + 1) * 512], in_=pb.t[:, :]), [pb], [xt])
                out_dma(sp, dst[r0:r0 + 128, :], xt.t[:], [xt])
            kb.arena_close()

        for s in dout_sems:
            if kb.cnts[s] > 0:
                nc.sync.wait_ge(kb.sems[s], kb.cnts[s])
        for E in (pe, act, dve, pool):
            if E.cnt > 0:
                nc.sync.wait_ge(kb.sems[E.name], E.cnt)
        for s in dmisc + [b.dsem for b in kb.wslots]:
            if kb.cnts[s] > 0:
                nc.sync.wait_ge(kb.sems[s], kb.cnts[s])
    return nc


_CACHE = {}


def make_in_maps(inp, cores=range(8)):
    f = lambda a: np.ascontiguousarray(np.asarray(a, dtype=np.float32))
    cstv = _make_consts()
    shared = {
        "ada_w": f(inp["ada_w"]), "ada_b": f(inp["ada_b"]), "norm_pre": f(inp["norm_pre"]), "norm_post": f(inp["norm_post"]),
        "ffn_w_in": f(inp["ffn_w_in"]), "ffn_w_out": f(inp["ffn_w_out"]),
        "ssm_in_w": f(inp["ssm_in_w"][0]), "conv_w": f(inp["ssm_conv_w"][0]), "conv_b": f(inp["ssm_conv_b"]),
        "dt_bias": f(inp["ssm_dt_bias"]), "a_log": f(inp["ssm_a_log"]), "ssm_d": f(inp["ssm_d"]),
        "ssm_norm_w": f(inp["ssm_norm_w"]), "ssm_out_w": f(inp["ssm_out_w"][0]),
        "qkv_w": f(inp["attn_qkv_w"][0]), "qkv_b": f(inp["attn_qkv_b"]), "sinks": f(inp["attn_sinks"]),
        "o_w": f(inp["attn_o_w"][0]), "o_b": f(inp["attn_o_b"]), "rel_bias": f(inp["rel_bias"]), "cst": cstv,
    }
    x_prompt = f(inp["x_prompt"]); x_sample = f(inp["x_sample"])
    state_ssm = f(inp["state_ssm"]); state_conv = f(inp["state_conv"])
    cache_k = f(inp["cache_k"]); cache_v = f(inp["cache_v"])
    c_prompt = f(inp["c_prompt"]); c_sample = f(inp["c_sample"])
    in_maps = []
    for b in cores:
        s0, s1 = b * 16, (b + 1) * 16
        m = dict(shared)
        m["xp"] = x_prompt[b]
        m["xs"] = x_sample[s0:s1].reshape(128, D)
        m["st_ssm"] = state_ssm[0, s0:s1].reshape(16, DIN, 128)
        m["st_conv"] = state_conv[0, s0:s1].reshape(48, 3072)
        m["ck"] = cache_k[0, s0:s1].reshape(16, 128, 256)
        m["cv"] = cache_v[0, s0:s1].reshape(16, 128, 256)
        m["cvec"] = np.concatenate([c_prompt[b:b + 1], c_sample[s0:s1]], axis=0)
        in_maps.append(m)
    return in_maps


def assemble(R):
    n = len(R)
    y_prompt = np.stack([R[b]["yp"] for b in range(n)]).reshape(n, SEQ, D)
    y_sample = np.concatenate([R[b]["ys"].reshape(16, 8, D) for b in range(n)], axis=0)
    ssm_p = np.stack([R[b]["ssm_p"].reshape(32, 64, 128) for b in range(n)])[None]
    conv_p = np.stack([R[b]["conv_p"] for b in range(n)])[None]
    k_p = np.stack([R[b]["kp"].reshape(128, 4, 64) for b in range(n)])[None]
    v_p = np.stack([R[b]["vp"].reshape(128, 4, 64) for b in range(n)])[None]
    ssm_s = np.concatenate([R[b]["ssm_s"].reshape(16, 32, 64, 128) for b in range(n)], axis=0)[None]
    conv_s = np.concatenate([R[b]["conv_s"].reshape(16, 3, 3072) for b in range(n)], axis=0)[None]
    k_s = np.concatenate([R[b]["ks"].reshape(16, 128, 4, 64) for b in range(n)], axis=0)[None]
    v_s = np.concatenate([R[b]["vs"].reshape(16, 128, 4, 64) for b in range(n)], axis=0)[None]
    outs = (y_prompt, y_sample, ssm_p, conv_p, k_p, v_p, ssm_s, conv_s, k_s, v_s)
    return tuple(np.ascontiguousarray(o, dtype=np.float32) for o in outs)


def kernel(**inp):
    nc = _CACHE.get("nc")
    if nc is None:
        nc = build_program()
        _CACHE["nc"] = nc
    in_maps = make_in_maps(inp)
    res = run_bass_kernel_spmd(nc, in_maps, core_ids=list(range(8)))
    return assemble(res.results)
```

```python
import contextlib
import math
import numpy as np
import concourse.bass as bass
import concourse.mybir as mybir
from concourse.bass_utils import run_bass_kernel_spmd

F32 = mybir.dt.float32
BF16 = mybir.dt.bfloat16
AF = mybir.ActivationFunctionType
ALU = mybir.AluOpType
AX = mybir.AxisListType

D = 1024
SEQ = 4096
NSEQ_S = 16
TPS_S = 8
DFF = 2816
NJ = DFF // 128
DIN = 2048
NEG = -30000.0
EPS = 1e-6

C_ID, C_TRIP, C_STRIP, C_TRIS, C_STRIS, C_J, C_JREP = 0, 128, 256, 384, 512, 640, 768
C_SEL = 896
C_OH = 912
C_ONES = 1296
NCST = 1424


def _bucket(d):
    d = np.asarray(d)
    df = np.maximum(d, 1).astype(np.float32)
    large = 16 + (np.log(df / np.float32(16)) / np.float32(math.log(128 / 16)) * np.float32(16)).astype(np.int32)
    large = np.minimum(large, 31)
    return np.where(d < 16, d, large)


def _make_consts():
    c = np.zeros((128, NCST), np.float32)
    i = np.arange(128)
    same = (i[:, None] // 8) == (i[None, :] // 8)
    c[:, C_ID:C_ID + 128] = np.eye(128)
    c[:, C_TRIP:C_TRIP + 128] = (i[:, None] <= i[None, :])
    c[:, C_STRIP:C_STRIP + 128] = (i[:, None] > i[None, :])
    c[:, C_TRIS:C_TRIS + 128] = (i[:, None] <= i[None, :]) & same
    c[:, C_STRIS:C_STRIS + 128] = (i[:, None] > i[None, :]) & same
    c[:, C_J:C_J + 128] = (i[:, None] == 127 - i[None, :])
    c[:, C_JREP:C_JREP + 128] = (i[:, None] == 127 - (i[None, :] % 8))
    c[:, C_SEL:C_SEL + 16] = (i[:, None] // 8) == np.arange(16)[None, :]
    oh = np.zeros((33, 384), np.float32)
    for ii in range(384):
        dist = 255 - ii
        if 0 <= dist <= 128:
            oh[int(_bucket(dist)), ii] = 1.0
        else:
            oh[32, ii] = 1.0
    c[0:33, C_OH:C_OH + 384] = oh
    c[:, C_ONES:C_ONES + 128] = 1.0
    return c


class Buf:
    __slots__ = ("t", "w", "r", "dsem")

    def __init__(self, t, pend=None):
        self.t = t
        self.w = None
        self.r = dict(pend) if pend else {}
        self.dsem = None


class Eng:
    def __init__(self, name, h):
        self.name = name
        self.h = h
        self.cnt = 0
        self.seen = {}


class KB:
    def __init__(self, nc, es):
        self.nc = nc
        self.es = es
        self.sems = {}
        self.cnts = {}
        self.pe = self._eng("pe", nc.tensor)
        self.act = self._eng("act", nc.scalar)
        self.dve = self._eng("dve", nc.vector)
        self.pool = self._eng("pool", nc.gpsimd)
        self.sp = self._eng("sp", nc.sync)
        self.engs = [self.pe, self.act, self.dve, self.pool, self.sp]
        self.banks = []
        for i in range(8):
            t = es.enter_context(nc.psum_tensor(f"psb{i}", [128, 512], F32))
            self.banks.append(Buf(t))
        self.bank_i = 0
        self.held = set()
        self.pending = {}
        self.arena_bufs = []
        self.arena_es = None
        self.out_events = {}
        self.wslots = []
        self.wslot_i = 0
        self.skip_self_waw = True
        self.dq = 0

    def _eng(self, name, h):
        self.sems[name] = self.es.enter_context(self.nc.semaphore(name))
        self.cnts[name] = 0
        return Eng(name, h)

    def dsem(self, name):
        self.sems[name] = self.es.enter_context(self.nc.semaphore(name))
        self.cnts[name] = 0
        return name

    def pbuf(self, name, shape, dt):
        t = self.es.enter_context(self.nc.sbuf_tensor(name, list(shape), dt))
        return Buf(t)

    def arena_open(self):
        self.arena_es = contextlib.ExitStack()
        self.arena_bufs = []

    def abuf(self, name, shape, dt):
        self.uid = getattr(self, "uid", 0) + 1
        t = self.arena_es.enter_context(self.nc.sbuf_tensor(f"{name}_{self.uid}", list(shape), dt))
        b = Buf(t, self.pending)
        self.arena_bufs.append(b)
        return b

    def arena_close(self):
        pend = dict(self.pending)
        for b in self.arena_bufs:
            if b.w and pend.get(b.w[0], 0) < b.w[1]:
                pend[b.w[0]] = b.w[1]
            for s, v in b.r.items():
                if pend.get(s, 0) < v:
                    pend[s] = v
        self.pending = pend
        self.arena_es.close()
        self.arena_es = None
        self.arena_bufs = []

    def ps(self):
        for _ in range(16):
            i = self.bank_i
            self.bank_i = (self.bank_i + 1) % 8
            if i not in self.held:
                return self.banks[i]
        raise RuntimeError("no psum bank")

    def ps_hold(self):
        b = self.ps()
        self.held.add(self.banks.index(b))
        return b

    def ps_release(self, b):
        self.held.discard(self.banks.index(b))

    def _need(self, E, reads, writes, acc, skipname):
        need = {}

        def add(s, v):
            if need.get(s, 0) < v:
                need[s] = v
        for b in reads:
            if b.w:
                add(*b.w)
        for b in writes:
            if b.w and not ((acc or self.skip_self_waw) and b.w[0] == skipname):
                add(*b.w)
            for s, v in b.r.items():
                if s != E.name:
                    add(s, v)
        for s, v in need.items():
            if E.seen.get(s, 0) < v:
                E.h.wait_ge(self.sems[s], v)
                E.seen[s] = v

    def op(self, E, fn, reads=(), writes=(), acc=False):
        self._need(E, reads, writes, acc, E.name)
        ins = fn()
        E.cnt += 1
        ins.then_inc(self.sems[E.name], 1)
        for b in reads:
            if b.r.get(E.name, 0) < E.cnt:
                b.r[E.name] = E.cnt
        for b in writes:
            b.w = (E.name, E.cnt)
            b.r = {}
        return ins

    def dma(self, Q, out, in_, reads=(), writes=(), dsem=None, join=False, **kw):
        self._need(Q, reads, writes, join, dsem)
        self.cnts[dsem] += 16
        v = self.cnts[dsem]
        Q.h.dma_start(out=out, in_=in_, **kw).then_inc(self.sems[dsem], 16)
        for b in reads:
            if b.r.get(dsem, 0) < v:
                b.r[dsem] = v
        for b in writes:
            b.w = (dsem, v)
            b.r = {}
        return (dsem, v)

    def wslot(self):
        s = self.wslots[self.wslot_i]
        self.wslot_i = (self.wslot_i + 1) % len(self.wslots)
        return s


class WStream:
    def __init__(self, kb, loads, depth=2):
        self.kb = kb
        self.loads = loads
        self.depth = depth
        self.issued = 0
        self.slots = []

    def get(self, k):
        kb = self.kb
        while self.issued < min(len(self.loads), k + 1 + self.depth):
            sl = kb.wslot()
            for i, (d, s) in enumerate(self.loads[self.issued](sl.t)):
                kb.dma(kb.pool, d, s, writes=[sl], dsem=sl.dsem, join=(i > 0))
            self.slots.append(sl)
            self.issued += 1
        return self.slots[k]


def build_program(flags=None):
    flags = flags or {}
    nc = bass.Bass("TRN2", target_bir_lowering=False)

    def din(name, shape, dt=F32):
        return nc.dram_tensor(name, list(shape), dt, kind="ExternalInput").ap()

    def dout(name, shape):
        return nc.dram_tensor(name, list(shape), F32, kind="ExternalOutput").ap()

    xp = din("xp", [SEQ, D]); xs = din("xs", [128, D])
    st_ssm = din("st_ssm", [NSEQ_S, DIN, 128]); st_conv = din("st_conv", [48, 3072])
    ck = din("ck", [NSEQ_S, 128, 256]); cv = din("cv", [NSEQ_S, 128, 256])
    cvec = din("cvec", [17, D])
    ada_w = din("ada_w", [2, D, 9216]); ada_b = din("ada_b", [2, 9216])
    norm_pre = din("norm_pre", [2, 3, D]); norm_post = din("norm_post", [2, 3, D])
    ffn_w_in = din("ffn_w_in", [2, 2, D, 2 * DFF]); ffn_w_out = din("ffn_w_out", [2, 2, DFF, D])
    ssm_in_w = din("ssm_in_w", [D, 5152]); conv_w = din("conv_w", [4, 3072]); conv_b = din("conv_b", [1, 3072])
    dt_bias = din("dt_bias", [1, 32]); a_log = din("a_log", [1, 32]); ssm_d = din("ssm_d", [1, 32])
    ssm_norm_w = din("ssm_norm_w", [1, DIN]); ssm_out_w = din("ssm_out_w", [DIN, D])
    qkv_w = din("qkv_w", [D, 1536]); qkv_b = din("qkv_b", [1, 1536]); sinks = din("sinks", [1, 16])
    o_w = din("o_w", [D, D]); o_b = din("o_b", [1, D]); rel_bias = din("rel_bias", [32, 16])
    cst = din("cst", [128, NCST])

    yp = dout("yp", [SEQ, D]); ys = dout("ys", [128, D])
    ssm_p = dout("ssm_p", [DIN, 128]); conv_p = dout("conv_p", [3, 3072])
    kp = dout("kp", [128, 256]); vp = dout("vp", [128, 256])
    ssm_s = dout("ssm_s", [NSEQ_S, DIN, 128]); conv_s = dout("conv_s", [48, 3072])
    ks = dout("ks", [NSEQ_S, 128, 256]); vs = dout("vs", [NSEQ_S, 128, 256])
    uscr = nc.dram_tensor("uscr", [16, 384], F32, kind="Internal")

    es = contextlib.ExitStack()
    with es:
        kb = KB(nc, es)
        pe, act, dve, pool, sp = kb.pe, kb.act, kb.dve, kb.pool, kb.sp
        NW = 4
        for i in range(NW):
            b = kb.pbuf(f"wslot{i}", [128, 4096], BF16)
            b.dsem = kb.dsem(f"dw{i}")
            kb.wslots.append(b)
        dmisc = [kb.dsem(f"dm{i}") for i in range(8)]
        dout_sems = [kb.dsem(f"do{i}") for i in range(4)]
        mi = [0]

        def msem():
            mi[0] = (mi[0] + 1) % len(dmisc)
            return dmisc[mi[0]]
        oi = [0]

        def osem():
            oi[0] = (oi[0] + 1) % len(dout_sems)
            return dout_sems[oi[0]]

        def out_dma(Q, out, in_, reads, **kw):
            s = osem()
            kb.dma(Q, out, in_, reads=reads, dsem=s, **kw)

        cstb = kb.pbuf("cstb", [128, NCST], F32)
        cbf = kb.pbuf("cbf", [128, 256], BF16)
        epsb = kb.pbuf("epsb", [128, 2], F32)
        x_fm = kb.pbuf("x_fm", [128, 8, 512], F32)
        hin = kb.pbuf("hin", [128, 8, 512], BF16)
        rstd = kb.pbuf("rstd", [128, 512], F32)
        tmpA = [kb.pbuf(f"tmpA{i}", [128, 512], F32) for i in range(3)]
        PRE = kb.pbuf("PRE", [128, 18, 8, 17], F32)
        hT = kb.pbuf("hT", [128, DIN], F32)
        hT_bf = kb.pbuf("hT_bf", [128, DIN], BF16)
        tailP = kb.pbuf("tailP", [128, 24, 3], F32)
        tailP_cc = [Buf(tailP.t) for _ in range(24)]
        convw = kb.pbuf("convw", [128, 24, 4], F32)
        convb = kb.pbuf("convb", [128, 24], F32)
        vec32 = kb.pbuf("vec32", [128, 4, 32], F32)
        normwT = kb.pbuf("normwT", [128, 16], F32)
        wdt = kb.pbuf("wdt", [128, 8, 32], BF16)
        qkb = kb.pbuf("qkb", [128, 10], F32)
        kvb = kb.pbuf("kvb", [128, 512], F32)
        ob = kb.pbuf("ob", [128, 8], F32)
        sinkb = kb.pbuf("sinkb", [128, 16], F32)
        kT = kb.pbuf("kT", [128, 2, 128 + 512], BF16)
        vtok = kb.pbuf("vtok", [128, 5, 256], BF16)
        tmi = [0]

        def tmp():
            tmi[0] = (tmi[0] + 1) % 3
            return tmpA[tmi[0]]

        ident = lambda: cstb.t[:, C_ID:C_ID + 128]
        ident_bf = lambda: cbf.t[:, 0:128]
        ones_bf = lambda: cbf.t[:, 128:256]
        ones_f = lambda: cstb.t[:, C_ONES:C_ONES + 128]

        kb.dma(sp, cstb.t[:], cst, writes=[cstb], dsem=msem())
        kb.op(dve, lambda: nc.vector.tensor_copy(out=cbf.t[:, 0:128], in_=cstb.t[:, C_ID:C_ID + 128]), [cstb], [cbf])
        kb.op(dve, lambda: nc.vector.tensor_copy(out=cbf.t[:, 128:256], in_=cstb.t[:, C_ONES:C_ONES + 128]), [cbf, cstb], [cbf])
        kb.op(dve, lambda: nc.vector.memset(epsb.t[:, 0:1], EPS), [], [epsb])
        kb.op(dve, lambda: nc.vector.memset(epsb.t[:, 1:2], 1.0), [epsb], [epsb])
        kb.op(dve, lambda: nc.vector.memset(tailP.t[:], 0.0), [], [tailP] + tailP_cc)
        kb.op(dve, lambda: nc.vector.memset(hT.t[:], 0.0), [], [hT])
        kb.op(dve, lambda: nc.vector.memset(hT_bf.t[:], 0.0), [], [hT_bf])
        kb.op(dve, lambda: nc.vector.memset(kT.t[:], 0.0), [], [kT])
        kb.op(dve, lambda: nc.vector.memset(vtok.t[:], 0.0), [], [vtok])

        with nc.allow_non_contiguous_dma(reason="small param loads"):
            for k in range(4):
                kb.dma(sp, convw.t[:, :, k], conv_w[k].rearrange("(c p) -> p c", p=128), writes=[convw], dsem=dmisc[3], join=True)
            kb.dma(sp, convb.t[:], conv_b.rearrange("o (c p) -> p (o c)", p=128), writes=[convb], dsem=msem())
            kb.dma(sp, ob.t[:], o_b.rearrange("o (c p) -> p (o c)", p=128), writes=[ob], dsem=msem())
            for c in range(8):
                A = c if c < 4 else c + 4
                for half, hh in ((0, A), (1, A + 4)):
                    kb.dma(sp, qkb.t[half * 64:(half + 1) * 64, c:c + 1],
                           qkv_b[0:1, hh * 64:(hh + 1) * 64].rearrange("o d -> d o"), writes=[qkb], dsem=dmisc[0], join=True)
            kb.dma(sp, qkb.t[:, 8:10], qkv_b[0:1, 1024:1280].rearrange("o (c p) -> p (o c)", p=128), writes=[qkb], dsem=dmisc[0], join=True)
        kb.dma(sp, vec32.t[:, 0, :], dt_bias.partition_broadcast(128), writes=[vec32], dsem=dmisc[1])
        kb.dma(sp, vec32.t[:, 1, :], a_log.partition_broadcast(128), writes=[vec32], dsem=dmisc[1], join=True)
        kb.dma(sp, vec32.t[:, 2, :], ssm_d.partition_broadcast(128), writes=[vec32], dsem=dmisc[1], join=True)
        with nc.allow_non_contiguous_dma(reason="small param loads"):
            kb.dma(sp, normwT.t[:], ssm_norm_w.rearrange("o (c p) -> p (o c)", p=128), writes=[normwT], dsem=msem())
        kb.dma(sp, sinkb.t[:], sinks.partition_broadcast(128), writes=[sinkb], dsem=msem())
        kb.dma(pool, wdt.t[:], ssm_in_w.rearrange("(c p) n -> p c n", p=128)[:, :, 5120:5152], writes=[wdt], dsem=msem())
        kb.dma(sp, kvb.t[:], qkv_b[0:1, 1024:1536].partition_broadcast(128), writes=[kvb], dsem=msem())
        kb.op(act, lambda: nc.scalar.activation(out=vec32.t[:, 1, :], in_=vec32.t[:, 1, :], func=AF.Exp), [vec32], [vec32])
        kb.op(dve, lambda: nc.vector.tensor_scalar(out=vec32.t[:, 1, :], in0=vec32.t[:, 1, :], scalar1=-1.0, scalar2=None, op0=ALU.mult), [vec32], [vec32])
        kb.op(dve, lambda: nc.vector.tensor_scalar(out=qkb.t[:, 0:8], in0=qkb.t[:, 0:8], scalar1=0.125, scalar2=None, op0=ALU.mult), [qkb], [qkb])

        kb.arena_open()
        cT = kb.abuf("cT", [128, 8, 17], F32)
        csT = kb.abuf("csT", [128, 8, 17], BF16)
        adab = kb.abuf("adab", [128, 2, 72], F32)
        npre = kb.abuf("npre", [128, 6, 8], F32)
        npost = kb.abuf("npost", [128, 6, 8], F32)
        modT = kb.abuf("modT", [128, 2, 72, 17], F32)
        ctok = kb.abuf("ctok", [17, D], F32)
        kb.dma(sp, ctok.t[:], cvec, writes=[ctok], dsem=msem())
        for c in range(8):
            pb = kb.ps()
            kb.op(pe, lambda: nc.tensor.transpose(out=pb.t[:, 0:17], in_=ctok.t[:, c * 128:(c + 1) * 128], identity=cstb.t[0:17, C_ID:C_ID + 17]), [ctok, cstb], [pb])
            kb.op(dve, lambda: nc.vector.tensor_copy(out=cT.t[:, c, :], in_=pb.t[:, 0:17]), [pb], [cT])
        kb.op(act, lambda: nc.scalar.activation(out=csT.t[:], in_=cT.t[:], func=AF.Silu), [cT], [csT])
        with nc.allow_non_contiguous_dma(reason="small param loads"):
            for i in range(2):
                kb.dma(sp, adab.t[:, i, :], ada_b[i].rearrange("(c p) -> p c", p=128), writes=[adab], dsem=dmisc[4], join=True)
                for sub in range(3):
                    kb.dma(sp, npre.t[:, i * 3 + sub, :], norm_pre[i, sub].rearrange("(c p) -> p c", p=128), writes=[npre], dsem=dmisc[5], join=True)
                    kb.dma(sp, npost.t[:, i * 3 + sub, :], norm_post[i, sub].rearrange("(c p) -> p c", p=128), writes=[npost], dsem=dmisc[6], join=True)
        for i in range(2):
            aw = ada_w[i].rearrange("(c p) n -> p c n", p=128)
            loads = []
            for nb in range(18):
                loads.append(lambda t, nb=nb: [(t[:, :].rearrange("p (c n) -> p c n", c=8), aw[:, :, nb * 512:(nb + 1) * 512])])
            wsm = WStream(kb, loads)
            for nb in range(18):
                sl = wsm.get(nb)
                wv = sl.t[:, :].rearrange("p (c n) -> p c n", c=8)
                pb = kb.ps()
                for m in range(4):
                    for kc in range(8):
                        kb.op(pe, lambda: nc.tensor.matmul(pb.t[:, m * 17:(m + 1) * 17], lhsT=wv[:, kc, m * 128:(m + 1) * 128], rhs=csT.t[:, kc, :], start=(kc == 0), stop=(kc == 7)), [sl, csT], [pb], acc=True)
                kb.op(dve, lambda: nc.vector.tensor_tensor(out=modT.t[:, i, nb * 4:(nb + 1) * 4, :], in0=pb.t[:, 0:68].rearrange("p (m s) -> p m s", m=4),
                                                           in1=adab.t[:, i, nb * 4:(nb + 1) * 4].unsqueeze(2).to_broadcast([128, 4, 17]), op=ALU.add), [pb, adab], [modT])
        for i in range(2):
            for sub in range(3):
                base = (i * 3 + sub) * 3
                sh = modT.t[:, i, (sub * 3 + 0) * 8:(sub * 3 + 0) * 8 + 8, :]
                sc = modT.t[:, i, (sub * 3 + 1) * 8:(sub * 3 + 1) * 8 + 8, :]
                gt = modT.t[:, i, (sub * 3 + 2) * 8:(sub * 3 + 2) * 8 + 8, :]
                npb = npre.t[:, i * 3 + sub, :].unsqueeze(2).to_broadcast([128, 8, 17])
                npo = npost.t[:, i * 3 + sub, :].unsqueeze(2).to_broadcast([128, 8, 17])
                kb.op(dve, lambda: nc.vector.scalar_tensor_tensor(out=PRE.t[:, base + 0, :, :], in0=sc, scalar=1.0, in1=npb, op0=ALU.add, op1=ALU.mult), [modT, npre], [PRE])
                kb.op(dve, lambda: nc.vector.tensor_copy(out=PRE.t[:, base + 1, :, :], in_=sh), [modT, PRE], [PRE])
                res = 1.0 if sub == 1 else 0.5
                kb.op(dve, lambda: nc.vector.scalar_tensor_tensor(out=PRE.t[:, base + 2, :, :], in0=gt, scalar=res, in1=npo, op0=ALU.mult, op1=ALU.mult), [modT, npost, PRE], [PRE])

        rb = kb.abuf("rb", [33, 16], F32)
        usb = kb.abuf("usb", [16, 384], F32)
        kb.op(dve, lambda: nc.vector.memset(rb.t[:], NEG), [], [rb])
        kb.dma(sp, rb.t[0:32, :], rel_bias, reads=[], writes=[rb], dsem=msem())
        pb = kb.ps()
        kb.op(pe, lambda: nc.tensor.matmul(pb.t[0:16, 0:384], lhsT=rb.t[:, :], rhs=cstb.t[0:33, C_OH:C_OH + 384], start=True, stop=True), [rb, cstb], [pb])
        kb.op(dve, lambda: nc.vector.tensor_copy(out=usb.t[:], in_=pb.t[0:16, 0:384]), [pb], [usb])
        uev = Buf(None)
        kb.dma(sp, uscr.ap(), usb.t[:], reads=[usb], writes=[uev], dsem=msem())
        kb.arena_close()

        def geom(kind):
            if kind == "P":
                return 512, 4, 1, 512
            return 128, 1, 16, 8

        def sumsq_rstd(src, sv, T):
            kb.op(act, lambda: nc.scalar.activation(out=hin.t[:, :, 0:T], in_=sv, func=AF.Square), [src], [hin])
            pb = kb.ps()
            for c in range(8):
                kb.op(pe, lambda: nc.tensor.matmul(pb.t[:, 0:T], lhsT=ones_bf(), rhs=hin.t[:, c, 0:T], start=(c == 0), stop=(c == 7)), [hin, cbf], [pb], acc=True)
            kb.op(act, lambda: nc.scalar.activation(out=rstd.t[:, 0:T], in_=pb.t[:, 0:T], func=AF.Ln, bias=epsb.t[:, 0:1], scale=1.0 / D), [pb, epsb], [rstd])
            kb.op(act, lambda: nc.scalar.activation(out=rstd.t[:, 0:T], in_=rstd.t[:, 0:T], func=AF.Exp, scale=-0.5), [rstd], [rstd])

        def norm_mod(i, sub, kind):
            T, NCH, nseq, tps = geom(kind)
            base = (i * 3 + sub) * 3
            sumsq_rstd(x_fm, x_fm.t[:, :, 0:T], T)
            for c in range(8):
                t1 = tmp()
                kb.op(dve, lambda: nc.vector.tensor_tensor(out=t1.t[:, 0:T], in0=x_fm.t[:, c, 0:T], in1=rstd.t[:, 0:T], op=ALU.mult), [x_fm, rstd], [t1])
                if kind == "P":
                    kb.op(act, lambda: nc.scalar.activation(out=hin.t[:, c, 0:T], in_=t1.t[:, 0:T], func=AF.Identity,
                                                            bias=PRE.t[:, base + 1, c, 0:1], scale=PRE.t[:, base + 0, c, 0:1]), [t1, PRE], [hin])
                else:
                    v3 = lambda ap: ap.rearrange("p (s t) -> p s t", s=16)
                    kb.op(dve, lambda: nc.vector.tensor_tensor(out=v3(t1.t[:, 0:T]), in0=v3(t1.t[:, 0:T]), in1=PRE.t[:, base + 0, c, 1:17].unsqueeze(2).to_broadcast([128, 16, 8]), op=ALU.mult), [t1, PRE], [t1])
                    kb.op(dve, lambda: nc.vector.tensor_tensor(out=v3(hin.t[:, c, 0:T]), in0=v3(t1.t[:, 0:T]), in1=PRE.t[:, base + 1, c, 1:17].unsqueeze(2).to_broadcast([128, 16, 8]), op=ALU.add), [t1, PRE], [hin])

        def post(i, sub, kind, f_fm, fv):
            T, NCH, nseq, tps = geom(kind)
            base = (i * 3 + sub) * 3
            sumsq_rstd(f_fm, fv, T)
            for c in range(8):
                t1 = tmp()
                kb.op(dve, lambda: nc.vector.tensor_tensor(out=t1.t[:, 0:T], in0=fv[:, c, :], in1=rstd.t[:, 0:T], op=ALU.mult), [f_fm, rstd], [t1])
                if kind == "P":
                    kb.op(dve, lambda: nc.vector.scalar_tensor_tensor(out=x_fm.t[:, c, 0:T], in0=t1.t[:, 0:T], scalar=PRE.t[:, base + 2, c, 0:1], in1=x_fm.t[:, c, 0:T], op0=ALU.mult, op1=ALU.add), [t1, PRE, x_fm], [x_fm])
                else:
                    v3 = lambda ap: ap.rearrange("p (s t) -> p s t", s=16)
                    kb.op(dve, lambda: nc.vector.tensor_tensor(out=v3(t1.t[:, 0:T]), in0=v3(t1.t[:, 0:T]), in1=PRE.t[:, base + 2, c, 1:17].unsqueeze(2).to_broadcast([128, 16, 8]), op=ALU.mult), [t1, PRE], [t1])
                    kb.op(dve, lambda: nc.vector.tensor_tensor(out=x_fm.t[:, c, 0:T], in0=x_fm.t[:, c, 0:T], in1=t1.t[:, 0:T], op=ALU.add), [t1, x_fm], [x_fm])

        def ffn(i, which, kind):
            T, NCH, nseq, tps = geom(kind)
            sub = 0 if which == 0 else 2
            norm_mod(i, sub, kind)
            win = ffn_w_in[i, which].rearrange("(c p) n -> p c n", p=128)
            wout = ffn_w_out[i, which].rearrange("(j p) n -> p j n", p=128)
            loads = []
            NJB = (NJ + 3) // 4
            for jb in range(NJB):
                w_ = min(512, DFF - jb * 512)
                for gu in range(2):
                    loads.append(lambda t, jb=jb, gu=gu, w_=w_: [
                        (t[:, 0:8 * w_].rearrange("p (c n) -> p c n", c=8), win[:, :, gu * DFF + jb * 512:gu * DFF + jb * 512 + w_])])
            for m in range(8):
                loads.append(lambda t, m=m: [(t[:, 0:NJ * 128].rearrange("p (j n) -> p j n", j=NJ), wout[:, :, m * 128:(m + 1) * 128])])
            wsm = WStream(kb, loads)
            kb.arena_open()
            actb = [kb.abuf(f"act{j}", [128, 512], BF16) for j in range(NJ)]
            sg = [kb.abuf(f"sg{j}", [128, 512], F32) for j in range(2)]
            f_fm = kb.abuf("f_fm", [128, 8, T], F32)
            for j in range(NJ):
                jb = j // 4
                w_ = min(512, DFF - jb * 512)
                slg = wsm.get(2 * jb); slu = wsm.get(2 * jb + 1)
                wg = slg.t[:, 0:8 * w_].rearrange("p (c n) -> p c n", c=8)
                wu = slu.t[:, 0:8 * w_].rearrange("p (c n) -> p c n", c=8)
                jo = (j % 4) * 128
                pg = kb.ps(); pu = kb.ps()
                for kc in range(8):
                    kb.op(pe, lambda: nc.tensor.matmul(pg.t[:, 0:T], lhsT=wg[:, kc, jo:jo + 128], rhs=hin.t[:, kc, 0:T], start=(kc == 0), stop=(kc == 7)), [slg, hin], [pg], acc=True)
                for kc in range(8):
                    kb.op(pe, lambda: nc.tensor.matmul(pu.t[:, 0:T], lhsT=wu[:, kc, jo:jo + 128], rhs=hin.t[:, kc, 0:T], start=(kc == 0), stop=(kc == 7)), [slu, hin], [pu], acc=True)
                s = sg[j % 2]
                kb.op(act, lambda: nc.scalar.activation(out=s.t[:, 0:T], in_=pg.t[:, 0:T], func=AF.Silu), [pg], [s])
                kb.op(dve, lambda: nc.vector.tensor_tensor(out=actb[j].t[:, 0:T], in0=s.t[:, 0:T], in1=pu.t[:, 0:T], op=ALU.mult), [s, pu], [actb[j]])
            for m in range(8):
                sl = wsm.get(2 * NJB + m)
                wv = sl.t[:, 0:NJ * 128].rearrange("p (j n) -> p j n", j=NJ)
                pb = kb.ps()
                for j in range(NJ):
                    kb.op(pe, lambda: nc.tensor.matmul(pb.t[:, 0:T], lhsT=wv[:, j, :], rhs=actb[j].t[:, 0:T], start=(j == 0), stop=(j == NJ - 1)), [sl, actb[j]], [pb], acc=True)
                kb.op(act, lambda: nc.scalar.copy(out=f_fm.t[:, m, 0:T], in_=pb.t[:, 0:T]), [pb], [f_fm])
            post(i, sub, kind, f_fm, f_fm.t[:, :, :])
            kb.arena_close()

        def ssd(kind, last):
            T, NCH, nseq, tps = geom(kind)
            P = (kind == "P")
            norm_mod(0, 1, kind)
            inw = ssm_in_w.rearrange("(c p) n -> p c n", p=128)
            outw = ssm_out_w.rearrange("(c p) n -> p c n", p=128)
            loads = []
            for q in range(6):
                loads.append(lambda t, q=q: [(t[:, :].rearrange("p (c n) -> p c n", c=8), inw[:, :, 2048 + q * 512:2048 + (q + 1) * 512])])
            for zb in range(4):
                loads.append(lambda t, zb=zb: [(t[:, :].rearrange("p (c n) -> p c n", c=8), inw[:, :, zb * 512:(zb + 1) * 512])])
            for mm in range(4):
                loads.append(lambda t, mm=mm: [(t[:, :].rearrange("p (c n) -> p c n", c=16), outw[:, :, mm * 256:(mm + 1) * 256])])
            wsm = WStream(kb, loads)
            tri_o, stri_o = (C_TRIP, C_STRIP) if P else (C_TRIS, C_STRIS)
            tri = lambda: cstb.t[:, tri_o:tri_o + 128]
            stri = lambda: cstb.t[:, stri_o:stri_o + 128]

            kb.arena_open()
            xsT = kb.abuf("xsT", [128, 16, T], BF16)
            if P:
                tails = tailP_cc
            BT = kb.abuf("BT", [128, 4, T], BF16)
            CT = kb.abuf("CT", [128, 4, T], BF16)
            raw = [kb.abuf(f"raw{k}", [128, nseq, 3 + tps], F32) for k in range(2)]
            cacc = [kb.abuf(f"cacc{k}", [128, nseq, tps], F32) for k in range(3)]
            xs_tok = kb.abuf("xs_tok", [128, NCH * DIN], BF16)
            xsv = xs_tok.t[:, :].rearrange("p (c n) -> p c n", c=NCH)
            tail = tailP if P else kb.abuf("tailS", [128, 24, 48], F32)
            if not P:
                tails = [tail] * 24
            B_tok = kb.abuf("B_tok", [128, NCH, 512], BF16)
            sz = kb.abuf("sz", [128, NCH, DIN], BF16)
            ynT = xsT
            dA = kb.abuf("dA", [128, NCH, 32], F32)
            dtv = kb.abuf("dtv", [128, NCH, 32], F32)
            sp1 = kb.abuf("sp1", [128, 32], F32)
            sp2 = kb.abuf("sp2", [128, 32], F32)

            if not P:
                hist = kb.abuf("hist", [48, 3072], F32)
                kb.dma(sp, hist.t[:], st_conv, writes=[hist], dsem=msem())
                for cc in range(24):
                    pb = kb.ps()
                    kb.op(pe, lambda: nc.tensor.transpose(out=pb.t[:, 0:48], in_=hist.t[:, cc * 128:(cc + 1) * 128], identity=cstb.t[0:48, C_ID:C_ID + 48]), [hist, cstb], [pb])
                    kb.op(dve, lambda: nc.vector.tensor_copy(out=tail.t[:, cc, :], in_=pb.t[:, 0:48]), [pb], [tail])

            for ch in range(NCH):
                pb = kb.ps()
                for kc in range(8):
                    kb.op(pe, lambda: nc.tensor.matmul(pb.t[:, 0:32], lhsT=hin.t[:, kc, ch * 128:(ch + 1) * 128], rhs=wdt.t[:, kc, :], start=(kc == 0), stop=(kc == 7)), [hin, wdt], [pb], acc=True)
                kb.op(dve, lambda: nc.vector.tensor_tensor(out=sp1.t[:], in0=pb.t[:, 0:32], in1=vec32.t[:, 0, :], op=ALU.add), [pb, vec32], [sp1])
                kb.op(dve, lambda: nc.vector.tensor_scalar(out=sp2.t[:], in0=sp1.t[:], scalar1=-1.0, scalar2=None, op0=ALU.mult), [sp1], [sp2])
                kb.op(dve, lambda: nc.vector.tensor_tensor(out=sp2.t[:], in0=sp2.t[:], in1=sp1.t[:], op=ALU.max), [sp1, sp2], [sp2])
                kb.op(act, lambda: nc.scalar.activation(out=sp2.t[:], in_=sp2.t[:], func=AF.Exp, scale=-1.0), [sp2], [sp2])
                kb.op(act, lambda: nc.scalar.activation(out=sp2.t[:], in_=sp2.t[:], func=AF.Ln, bias=epsb.t[:, 1:2], scale=1.0), [sp2, epsb], [sp2])
                kb.op(dve, lambda: nc.vector.tensor_scalar(out=sp1.t[:], in0=sp1.t[:], scalar1=0.0, scalar2=None, op0=ALU.max), [sp1], [sp1])
                kb.op(dve, lambda: nc.vector.tensor_tensor(out=dtv.t[:, ch, :], in0=sp1.t[:], in1=sp2.t[:], op=ALU.add), [sp1, sp2], [dtv])
                kb.op(dve, lambda: nc.vector.tensor_tensor(out=dA.t[:, ch, :], in0=dtv.t[:, ch, :], in1=vec32.t[:, 1, :], op=ALU.mult), [dtv, vec32], [dA])

            def conv_s1(cc):
                sl = wsm.get(cc // 4)
                wv = sl.t[:, :].rearrange("p (c n) -> p c n", c=8)
                pb = kb.ps()
                for kc in range(8):
                    kb.op(pe, lambda: nc.tensor.matmul(pb.t[:, 0:T], lhsT=wv[:, kc, (cc % 4) * 128:(cc % 4 + 1) * 128], rhs=hin.t[:, kc, 0:T], start=(kc == 0), stop=(kc == 7)), [sl, hin], [pb], acc=True)
                rw = raw[cc % 2]; ca = cacc[cc % 3]
                tl = tails[cc]
                kb.op(act, lambda: nc.scalar.copy(out=rw.t[:, :, 0:3], in_=tail.t[:, cc, :].rearrange("p (s j) -> p s j", j=3)), [tl], [rw])
                kb.op(act, lambda: nc.scalar.copy(out=rw.t[:, :, 3:3 + tps], in_=pb.t[:, 0:T].rearrange("p (s t) -> p s t", s=nseq)), [pb], [rw])
                kb.op(act, lambda: nc.scalar.copy(out=tail.t[:, cc, :].rearrange("p (s j) -> p s j", j=3), in_=rw.t[:, :, tps:tps + 3]), [rw], [tl])
                kb.op(act, lambda: nc.scalar.activation(out=ca.t[:], in_=rw.t[:, :, 0:tps], func=AF.Identity, bias=convb.t[:, cc:cc + 1], scale=convw.t[:, cc, 0:1]), [rw, convw, convb], [ca])
                for k in range(1, 4):
                    kb.op(dve, lambda: nc.vector.scalar_tensor_tensor(out=ca.t[:], in0=rw.t[:, :, k:k + tps], scalar=convw.t[:, cc, k:k + 1], in1=ca.t[:], op0=ALU.mult, op1=ALU.add), [rw, convw, ca], [ca])

            def conv_s2(cc):
                ca = cacc[cc % 3]
                if cc < 16:
                    dstb, dst = xsT, xsT.t[:, cc, :]
                elif cc < 20:
                    dstb, dst = BT, BT.t[:, cc - 16, :]
                else:
                    dstb, dst = CT, CT.t[:, cc - 20, :]
                kb.op(act, lambda: nc.scalar.activation(out=dst.rearrange("p (s t) -> p s t", s=nseq), in_=ca.t[:], func=AF.Silu), [ca], [dstb])

            for cc in range(24):
                conv_s1(cc)
                if cc >= 1:
                    conv_s2(cc - 1)
            conv_s2(23)

            if last and P:
                with nc.allow_non_contiguous_dma(reason="small state out"):
                    for j3 in range(3):
                        out_dma(sp, conv_p[j3].rearrange("(c p) -> p c", p=128), tail.t[:, :, j3], tails)
            if last and not P:
                nr = nseq * 3
                cso = hist
                for cc in range(24):
                    pb = kb.ps()
                    kb.op(pe, lambda: nc.tensor.transpose(out=pb.t[0:nr, 0:128], in_=tail.t[:, cc, :], identity=ident()), [tail, cstb], [pb])
                    kb.op(dve, lambda: nc.vector.tensor_copy(out=cso.t[:, cc * 128:(cc + 1) * 128], in_=pb.t[0:nr, 0:128]), [pb], [cso])
                out_dma(sp, conv_p if P else conv_s, cso.t[:], [cso])

            for ch in range(NCH):
                for q in range(4):
                    pb = kb.ps()
                    pbv = pb.t[:].bitcast(BF16)
                    for k in range(4):
                        cc = q * 4 + k
                        kb.op(pe, lambda: nc.tensor.transpose(out=pbv[:, k * 128:(k + 1) * 128], in_=xsT.t[:, cc, ch * 128:(ch + 1) * 128], identity=ident_bf()), [xsT, cbf], [pb], acc=True)
                    kb.op(act, lambda: nc.scalar.copy(out=xsv[:, ch, q * 512:(q + 1) * 512], in_=pbv[:, 0:512]), [pb], [xs_tok])
                pb = kb.ps()
                pbv = pb.t[:].bitcast(BF16)
                for g in range(4):
                    kb.op(pe, lambda: nc.tensor.transpose(out=pbv[:, g * 128:(g + 1) * 128], in_=BT.t[:, g, ch * 128:(ch + 1) * 128], identity=ident_bf()), [BT, cbf], [pb], acc=True)
                kb.op(act, lambda: nc.scalar.copy(out=B_tok.t[:, ch, :], in_=pbv[:, 0:512]), [pb], [B_tok])

            for zb in range(4):
                sl = wsm.get(6 + zb)
                wv = sl.t[:, :].rearrange("p (c n) -> p c n", c=8)
                for ch in range(NCH):
                    pb = kb.ps()
                    for kc in range(8):
                        kb.op(pe, lambda: nc.tensor.matmul(pb.t[:, :], lhsT=hin.t[:, kc, ch * 128:(ch + 1) * 128], rhs=wv[:, kc, :], start=(kc == 0), stop=(kc == 7)), [sl, hin], [pb], acc=True)
                    kb.op(act, lambda: nc.scalar.activation(out=sz.t[:, ch, zb * 512:(zb + 1) * 512], in_=pb.t[:, :], func=AF.Silu), [pb], [sz])

            R1q = [kb.abuf(f"R1q{k}", [128, 8, 128], F32) for k in range(2)]
            Lsb = [kb.abuf(f"Lsb{k}", [128, 512], F32) for k in range(2)]
            wT = kb.abuf("wT", [128, 32, 128], BF16)
            cbm = kb.abuf("cbm", [128, 4, 128], F32)
            xdt = kb.abuf("xdt", [128, DIN], BF16)
            xdec = xdt if P else kb.abuf("xdec", [128, DIN], BF16)
            sm = kb.abuf("sm", [128, 4, 32], F32)
            ygb = [kb.abuf(f"yg{k}", [128, 512], F32) for k in range(3)]
            ynb = [kb.abuf(f"yn{k}", [128, 512], BF16) for k in range(2)]
            ssqg = [kb.abuf(f"ssq{k}", [128, 2], F32) for k in range(4)]
            junk = kb.abuf("junk", [128, 512], BF16)
            v32 = lambda ap: ap.rearrange("p (h d) -> p h d", h=32)
            if not P:
                dAexp = wT
                dAv = wT.t[:, :, :].rearrange("p h l -> p (h l)").bitcast(F32)
                decS = kb.abuf("decS", [128, 16, 16], F32)
                CTmj = [kb.abuf(f"CTmj{k}", [128, 4, 128], BF16) for k in range(2)]
                h0b = [kb.abuf(f"h0b{k}", [128, 16, 128], BF16) for k in range(2)]
                h0f = kb.abuf("h0f", [128, 16, 128], F32)
                hnv = hT.t[:, :].rearrange("p (c n) -> p c n", c=16)
                Bm = [kb.abuf(f"Bm{k}", [128, 512], BF16) for k in range(2)]
                mJL = kb.abuf("mJL", [128, 16, 128], BF16)
                kb.op(dve, lambda: nc.vector.memset(mJL.t[:], 0.0), [], [mJL])
                for j in range(16):
                    kb.op(dve, lambda: nc.vector.memset(mJL.t[:, j, j * 8:(j + 1) * 8], 1.0), [mJL], [mJL])

            for ch in range(NCH):
                csl = slice(ch * 128, (ch + 1) * 128)
                pv = kb.ps()
                kb.op(pe, lambda: nc.tensor.matmul(pv.t[:, 0:32], lhsT=tri(), rhs=dA.t[:, ch, :], start=True, stop=True), [dA, cstb], [pv])
                kb.op(pe, lambda: nc.tensor.matmul(pv.t[:, 32:64], lhsT=stri(), rhs=dA.t[:, ch, :], start=True, stop=True), [dA, cstb], [pv], acc=True)
                kb.op(pe, lambda: nc.tensor.matmul(pv.t[:, 64:96], lhsT=ones_f(), rhs=dA.t[:, ch, :], start=True, stop=True), [dA, cstb], [pv], acc=True)
                kb.op(act, lambda: nc.scalar.activation(out=sm.t[:, 0:3, :], in_=pv.t[:, 0:96].rearrange("p (a h) -> p a h", a=3), func=AF.Exp), [pv], [sm])
                pc = kb.ps()
                for g in range(4):
                    kb.op(pe, lambda: nc.tensor.matmul(pc.t[:, g * 128:(g + 1) * 128], lhsT=BT.t[:, g, csl], rhs=CT.t[:, g, csl], start=True, stop=True), [BT, CT], [pc], acc=True)
                kb.op(dve, lambda: nc.vector.tensor_tensor(out=cbm.t[:], in0=pc.t[:, :].rearrange("p (g l) -> p g l", g=4), in1=tri().unsqueeze(1).to_broadcast([128, 4, 128]), op=ALU.mult), [pc, cstb], [cbm])
                kb.op(dve, lambda: nc.vector.tensor_tensor(out=v32(xdt.t[:]), in0=v32(xsv[:, ch, :]), in1=dtv.t[:, ch, :].unsqueeze(2).to_broadcast([128, 32, 64]), op=ALU.mult), [xs_tok, dtv], [xdt])
                if not P:
                    kb.op(dve, lambda: nc.vector.tensor_tensor(out=v32(xdec.t[:]), in0=v32(xdt.t[:]), in1=sm.t[:, 1, :].unsqueeze(2).to_broadcast([128, 32, 64]), op=ALU.mult), [xdt, sm], [xdec])
                    kb.op(dve, lambda: nc.vector.tensor_copy(out=v32(dAv), in_=dA.t[:, 0, :].unsqueeze(2).to_broadcast([128, 32, 64])), [dA], [dAexp])
                    pd = kb.ps()
                    for c in range(16):
                        kb.op(pe, lambda: nc.tensor.matmul(pd.t[:, c * 16:(c + 1) * 16], lhsT=dAv[:, c * 128:(c + 1) * 128], rhs=cstb.t[:, C_SEL:C_SEL + 16], start=True, stop=True), [dAexp, cstb], [pd], acc=True)
                    kb.op(act, lambda: nc.scalar.activation(out=decS.t[:], in_=pd.t[:, 0:256].rearrange("p (c j) -> p c j", c=16), func=AF.Exp), [pd], [decS])
                yo = []
                if not P:
                    yo = [kb.ps_hold() for g in range(4)]
                    for j in range(NSEQ_S):
                        hb = h0b[j % 2]; hf = h0f; ht = hT_bf; hn = hT; bm = Bm[j % 2]; CTm = CTmj[j % 2]
                        kb.op(dve, lambda: nc.vector.tensor_tensor(out=CTm.t[:], in0=CT.t[:, :, :], in1=mJL.t[:, j, :].unsqueeze(1).to_broadcast([128, 4, 128]), op=ALU.mult), [CT, mJL], [CTm])
                        kb.dma(pool, hb.t[:], st_ssm[j].rearrange("(c p) n -> p c n", p=128), writes=[hb], dsem=msem())
                        kb.dma(sp, hf.t[:], st_ssm[j].rearrange("(c p) n -> p c n", p=128), writes=[hf], dsem=msem())
                        for q in range(4):
                            pb = kb.ps()
                            pbv = pb.t[:].bitcast(BF16)
                            for k in range(4):
                                c = q * 4 + k
                                kb.op(pe, lambda: nc.tensor.transpose(out=pbv[:, k * 128:(k + 1) * 128], in_=hb.t[:, c, :], identity=ident_bf()), [hb, cbf], [pb], acc=True)
                            kb.op(act, lambda: nc.scalar.copy(out=ht.t[:, q * 512:(q + 1) * 512], in_=pbv[:, 0:512]), [pb], [ht])
                        for g in range(4):
                            kb.op(pe, lambda: nc.tensor.matmul(yo[g].t[:, :], lhsT=CTm.t[:, g, :], rhs=ht.t[:, g * 512:(g + 1) * 512], start=(j == 0), stop=(j == NSEQ_S - 1)), [CTm, ht], [yo[g]], acc=True)
                        kb.op(dve, lambda: nc.vector.tensor_scalar(out=bm.t[:], in0=B_tok.t[:, 0, :], scalar1=cstb.t[:, C_SEL + j:C_SEL + j + 1], scalar2=None, op0=ALU.mult), [B_tok, cstb], [bm])
                        for q in range(4):
                            pb = kb.ps()
                            for k in range(4):
                                c = q * 4 + k
                                kb.op(pe, lambda: nc.tensor.matmul(pb.t[:, k * 128:(k + 1) * 128], lhsT=xdec.t[:, c * 128:(c + 1) * 128], rhs=bm.t[:, (c // 4) * 128:(c // 4 + 1) * 128], start=True, stop=True), [xdec, bm], [pb], acc=True)
                            for k in range(4):
                                c = q * 4 + k
                                kb.op(dve, lambda: nc.vector.scalar_tensor_tensor(out=hnv[:, c, :], in0=hf.t[:, c, :], scalar=decS.t[:, c, j:j + 1], in1=pb.t[:, k * 128:(k + 1) * 128], op0=ALU.mult, op1=ALU.add), [hf, decS, pb], [hn])
                        out_dma(sp, ssm_s[j].rearrange("(c p) n -> p c n", p=128), hnv, [hn])
                for g in range(4):
                    kb.op(dve, lambda: nc.vector.memset(ssqg[g].t[:], 0.0), [ssqg[g]], [ssqg[g]])

                def buildR1(g):
                    R1 = R1q[g % 2]
                    kb.op(dve, lambda: nc.vector.tensor_tensor(out=R1.t[:], in0=dA.t[:, ch, g * 8:(g + 1) * 8].unsqueeze(2).to_broadcast([128, 8, 128]),
                                                               in1=tri().unsqueeze(1).to_broadcast([128, 8, 128]), op=ALU.mult), [dA, cstb], [R1])
                buildR1(0)
                for g in range(4):
                    R1 = R1q[g % 2]
                    pbs = []
                    for b in range(2):
                        pb = kb.ps()
                        kb.op(pe, lambda: nc.tensor.matmul(pb.t[:, :], lhsT=stri(), rhs=R1.t[:, :, :].rearrange("p h l -> p (h l)")[:, b * 512:(b + 1) * 512], start=True, stop=True), [R1, cstb], [pb])
                        pbs.append(pb)
                    if g + 1 < 4:
                        buildR1(g + 1)
                    for b in range(2):
                        pb = pbs[b]
                        L = Lsb[b]
                        kb.op(act, lambda: nc.scalar.activation(out=L.t[:], in_=pb.t[:, :], func=AF.Exp), [pb], [L])
                        h0_ = g * 8 + b * 4
                        kb.op(dve, lambda: nc.vector.tensor_tensor(out=wT.t[:, h0_:h0_ + 4, :], in0=L.t[:].rearrange("p (h l) -> p h l", h=4),
                                                                   in1=cbm.t[:, g, :].unsqueeze(1).to_broadcast([128, 4, 128]), op=ALU.mult), [L, cbm], [wT])
                v3 = lambda ap: ap.rearrange("p (h d) -> p h d", h=8)

                def emitY(g):
                    if P:
                        yo_g = kb.ps()
                        kb.op(pe, lambda: nc.tensor.matmul(yo_g.t[:, :], lhsT=CT.t[:, g, csl], rhs=hT_bf.t[:, g * 512:(g + 1) * 512], start=True, stop=True), [CT, hT_bf], [yo_g])
                    else:
                        yo_g = yo[g]
                    pb = kb.ps()
                    for r in range(8):
                        h = g * 8 + r
                        kb.op(pe, lambda: nc.tensor.matmul(pb.t[:, r * 64:(r + 1) * 64], lhsT=wT.t[:, h, :], rhs=xdt.t[:, h * 64:(h + 1) * 64], start=True, stop=True), [wT, xdt], [pb], acc=True)
                    return yo_g, pb

                def combine(g, yo_g, pb):
                    gs = slice(g * 512, (g + 1) * 512)
                    eab = sm.t[:, 0, g * 8:(g + 1) * 8].unsqueeze(2).to_broadcast([128, 8, 64])
                    Db = vec32.t[:, 2, g * 8:(g + 1) * 8].unsqueeze(2).to_broadcast([128, 8, 64])
                    t1 = tmp(); t2 = tmp()
                    yg = ygb[g % 3]
                    kb.op(dve, lambda: nc.vector.tensor_tensor(out=v3(t1.t[:]), in0=v3(yo_g.t[:, :]), in1=eab, op=ALU.mult), [yo_g, sm], [t1])
                    if not P:
                        kb.ps_release(yo_g)
                    kb.op(dve, lambda: nc.vector.tensor_tensor(out=t1.t[:], in0=t1.t[:], in1=pb.t[:, :], op=ALU.add), [t1, pb], [t1])
                    kb.op(dve, lambda: nc.vector.tensor_tensor(out=v3(t2.t[:]), in0=v3(xsv[:, ch, gs]), in1=Db, op=ALU.mult), [xs_tok, vec32], [t2])
                    kb.op(dve, lambda: nc.vector.tensor_tensor(out=t1.t[:], in0=t1.t[:], in1=t2.t[:], op=ALU.add), [t1, t2], [t1])
                    kb.op(dve, lambda: nc.vector.tensor_tensor(out=yg.t[:], in0=t1.t[:], in1=sz.t[:, ch, gs], op=ALU.mult), [t1, sz], [yg])
                    sq_ = ssqg[g]
                    kb.op(act, lambda: nc.scalar.activation(out=junk.t[:], in_=yg.t[:], func=AF.Square, accum_out=sq_.t[:, 0:1]), [yg, sq_], [junk, sq_])
                    kb.op(act, lambda: nc.scalar.activation(out=sq_.t[:, 1:2], in_=sq_.t[:, 0:1], func=AF.Ln, bias=epsb.t[:, 0:1], scale=1.0 / 512), [sq_, epsb], [sq_])
                    kb.op(act, lambda: nc.scalar.activation(out=sq_.t[:, 1:2], in_=sq_.t[:, 1:2], func=AF.Exp, scale=-0.5), [sq_], [sq_])
                    yn = ynb[g % 2]
                    kb.op(act, lambda: nc.scalar.activation(out=yn.t[:], in_=yg.t[:], func=AF.Identity, scale=sq_.t[:, 1:2]), [yg, sq_], [yn])

                def finish(g):
                    yn = ynb[g % 2]
                    pq = kb.ps()
                    pqv = pq.t[:].bitcast(BF16)
                    for k in range(4):
                        kb.op(pe, lambda: nc.tensor.transpose(out=pqv[:, k * 128:(k + 1) * 128], in_=yn.t[:, k * 128:(k + 1) * 128], identity=ident_bf()), [yn, cbf], [pq], acc=True)
                    kb.op(dve, lambda: nc.vector.tensor_tensor(out=ynT.t[:, g * 4:(g + 1) * 4, csl], in0=pqv[:, 0:512].rearrange("p (k t) -> p k t", k=4),
                                                               in1=normwT.t[:, g * 4:(g + 1) * 4].unsqueeze(2).to_broadcast([128, 4, 128]), op=ALU.mult), [pq, normwT], [ynT])

                Y = {0: emitY(0), 1: emitY(1)}
                combine(0, *Y[0])
                for g in range(1, 4):
                    if g + 1 < 4:
                        Y[g + 1] = emitY(g + 1)
                    combine(g, *Y[g])
                    finish(g - 1)
                finish(3)

                if P:
                    kb.op(dve, lambda: nc.vector.tensor_tensor(out=v32(xdt.t[:]), in0=v32(xdt.t[:]), in1=sm.t[:, 1, :].unsqueeze(2).to_broadcast([128, 32, 64]), op=ALU.mult), [xdt, sm], [xdt])
                    for g in range(4):
                        gs = slice(g * 512, (g + 1) * 512)
                        pb = kb.ps()
                        kb.op(pe, lambda: nc.tensor.matmul(pb.t[:, :], lhsT=B_tok.t[:, ch, g * 128:(g + 1) * 128], rhs=xdt.t[:, gs], start=True, stop=True), [B_tok, xdt], [pb])
                        v3 = lambda ap: ap.rearrange("p (h d) -> p h d", h=8)
                        kb.op(dve, lambda: nc.vector.tensor_tensor(out=v3(hT.t[:, gs]), in0=v3(hT.t[:, gs]), in1=sm.t[:, 2, g * 8:(g + 1) * 8].unsqueeze(2).to_broadcast([128, 8, 64]), op=ALU.mult), [hT, sm], [hT])
                        kb.op(dve, lambda: nc.vector.tensor_tensor(out=hT.t[:, gs], in0=hT.t[:, gs], in1=pb.t[:, :], op=ALU.add), [hT, pb], [hT])
                    kb.op(act, lambda: nc.scalar.copy(out=hT_bf.t[:], in_=hT.t[:]), [hT], [hT_bf])

            if P and last:
                houtv = sz.t[:, :, :].rearrange("p c n -> p (c n)").bitcast(F32)[:, 0:2048].rearrange("p (c n) -> p c n", c=16)
                hout = sz
                for c in range(16):
                    pb = kb.ps()
                    kb.op(pe, lambda: nc.tensor.transpose(out=pb.t[:, 0:128], in_=hT.t[:, c * 128:(c + 1) * 128], identity=ident()), [hT, cstb], [pb])
                    kb.op(dve, lambda: nc.vector.tensor_copy(out=houtv[:, c, :], in_=pb.t[:, 0:128]), [pb], [hout])
                out_dma(sp, ssm_p.rearrange("(c p) n -> p c n", p=128), houtv, [hout])

            fv = xs_tok.t[:, :].bitcast(F32).rearrange("p (m t) -> p m t", m=8)
            for m in range(8):
                sl = wsm.get(10 + m // 2)
                wv = sl.t[:, :].rearrange("p (c n) -> p c n", c=16)
                pb = kb.ps()
                for kc in range(16):
                    kb.op(pe, lambda: nc.tensor.matmul(pb.t[:, 0:T], lhsT=wv[:, kc, (m % 2) * 128:(m % 2 + 1) * 128], rhs=ynT.t[:, kc, 0:T], start=(kc == 0), stop=(kc == 15)), [sl, ynT], [pb], acc=True)
                kb.op(act, lambda: nc.scalar.copy(out=fv[:, m, :], in_=pb.t[:, 0:T]), [pb], [xs_tok])
            post(0, 1, kind, xs_tok, fv)
            kb.arena_close()

        def swa(kind, first, last):
            T, NCH, nseq, tps = geom(kind)
            P = (kind == "P")
            norm_mod(1, 1, kind)
            qw = qkv_w.rearrange("(c p) n -> p c n", p=128)
            ow = o_w.rearrange("(c p) n -> p c n", p=128)
            loads = []
            for half in range(2):
                loads.append(lambda t, half=half: [(t[:, :].rearrange("p (c n) -> p c n", c=8), qw[:, :, half * 512:(half + 1) * 512])])
            loads.append(lambda t: [(t[:, :].rearrange("p (c n) -> p c n", c=8), qw[:, :, 1024:1536])])
            for mm in range(2):
                loads.append(lambda t, mm=mm: [(t[:, :].rearrange("p (c n) -> p c n", c=8), ow[:, :, mm * 512:(mm + 1) * 512])])
            wsm = WStream(kb, loads)

            kb.arena_open()
            qT = kb.abuf("qT", [128, 8, T], BF16)
            kv_tok = kb.abuf("kv_tok", [128, NCH, 512], F32)
            biasT = kb.abuf("biasT", [128, 16, 256], F32)
            tq = [kb.abuf(f"tq{k}", [128, 256], F32) for k in range(2)]
            sS = [kb.abuf(f"sS{k}", [128, 256], F32) for k in range(3)]
            eS = [kb.abuf(f"eS{k}", [128, 256], BF16) for k in range(3)]
            en = [kb.abuf(f"en{k}", [128, 256], BF16) for k in range(3)]
            pT = [kb.abuf(f"pT{k}", [128, 2, 128], BF16) for k in range(3)]
            st = [kb.abuf(f"st{k}", [128, 8], F32) for k in range(3)]
            o_tok = kb.abuf("o_tok", [128, D], BF16)
            attnT = kb.abuf("attnT", [128, 8, T], BF16)
            f_fm = kb.abuf("f_fm", [128, 8, T], F32)

            with nc.allow_non_contiguous_dma(reason="toeplitz"):
                for h in range(16):
                    t = tq[h % 2]
                    src = bass.AP(uscr, h * 384, [[1, 128], [1, 256]])
                    kb.dma(sp, t.t[:], src, reads=[uev], writes=[t], dsem=msem())
                    pb = kb.ps()
                    jo = C_J if P else C_JREP
                    kb.op(pe, lambda: nc.tensor.matmul(pb.t[:, 0:256], lhsT=cstb.t[:, jo:jo + 128], rhs=t.t[:], start=True, stop=True), [t, cstb], [pb])
                    kb.op(dve, lambda: nc.vector.tensor_copy(out=biasT.t[:, h, :], in_=pb.t[:, 0:256]), [pb], [biasT])

            if flags.get("swa_stop") == 1:
                kb.arena_close(); return
            wqp = [kb.abuf(f"wqp{k}", [128, 8, 4, 2, 64], BF16) for k in range(2)]
            for half in range(2):
                sl = wsm.get(half)
                nat = sl.t[:, :].rearrange("p (c n) -> p c n", c=8)
                for a in range(4):
                    for b in range(2):
                        kb.op(dve, lambda: nc.vector.tensor_copy(out=wqp[half].t[:, :, a, b, :], in_=nat[:, :, b * 256 + a * 64:b * 256 + (a + 1) * 64]), [sl], [wqp[half]])
            for c in range(8):
                wb = wqp[c // 4]
                wv = wb.t[:, :, :, :, :].rearrange("p c a b d -> p c a (b d)")
                pb = kb.ps()
                for kc in range(8):
                    kb.op(pe, lambda: nc.tensor.matmul(pb.t[:, 0:T], lhsT=wv[:, kc, c % 4, :], rhs=hin.t[:, kc, 0:T], start=(kc == 0), stop=(kc == 7)), [wb, hin], [pb], acc=True)
                kb.op(act, lambda: nc.scalar.activation(out=qT.t[:, c, :], in_=pb.t[:, 0:T], func=AF.Identity, bias=qkb.t[:, c:c + 1], scale=0.125), [pb, qkb], [qT])
            sl = wsm.get(2)
            wv = sl.t[:, :].rearrange("p (c n) -> p c n", c=8)
            for c2 in range(2):
                pb = kb.ps()
                for kc in range(8):
                    kb.op(pe, lambda: nc.tensor.matmul(pb.t[:, 0:T], lhsT=wv[:, kc, c2 * 128:(c2 + 1) * 128], rhs=hin.t[:, kc, 0:T], start=(kc == 0), stop=(kc == 7)), [sl, hin], [pb], acc=True)
                kb.op(act, lambda: nc.scalar.activation(out=kT.t[:, c2, 128:128 + T], in_=pb.t[:, 0:T], func=AF.Identity, bias=qkb.t[:, 8 + c2:9 + c2], scale=1.0), [pb, qkb], [kT])
            for ch in range(NCH):
                pb = kb.ps()
                for kc in range(8):
                    kb.op(pe, lambda: nc.tensor.matmul(pb.t[:, :], lhsT=hin.t[:, kc, ch * 128:(ch + 1) * 128], rhs=wv[:, kc, :], start=(kc == 0), stop=(kc == 7)), [sl, hin], [pb], acc=True)
                kb.op(dve, lambda: nc.vector.tensor_tensor(out=kv_tok.t[:, ch, :], in0=pb.t[:, :], in1=kvb.t[:, :], op=ALU.add), [pb, kvb], [kv_tok])
                kb.op(act, lambda: nc.scalar.copy(out=vtok.t[:, 1 + ch, :], in_=kv_tok.t[:, ch, 256:512]), [kv_tok], [vtok])

            if flags.get("swa_stop") == 2:
                kb.arena_close(); return

            LOOK = 3

            def softmax_s1(idx, h, lg, nk, bias_ap):
                k3 = idx % 3
                s_ = sS[k3]; e_ = eS[k3]; t_ = st[k3]
                kb.op(dve, lambda: nc.vector.memset(t_.t[:], 0.0), [t_], [t_])
                kb.op(dve, lambda: nc.vector.tensor_tensor(out=s_.t[:, 0:nk], in0=lg.t[:, 0:nk], in1=bias_ap, op=ALU.add), [lg, biasT], [s_])
                kb.op(dve, lambda: nc.vector.tensor_reduce(out=t_.t[:, 0:1], in_=s_.t[:, 0:nk], axis=AX.X, op=ALU.max), [s_], [t_])
                kb.op(dve, lambda: nc.vector.tensor_scalar(out=t_.t[:, 1:2], in0=t_.t[:, 0:1], scalar1=sinkb.t[:, h:h + 1], scalar2=-1.0, op0=ALU.max, op1=ALU.mult), [t_, sinkb], [t_])
                kb.op(act, lambda: nc.scalar.activation(out=e_.t[:, 0:nk], in_=s_.t[:, 0:nk], func=AF.Exp, bias=t_.t[:, 1:2], scale=1.0, accum_out=t_.t[:, 2:3]), [s_, t_], [e_, t_])
                kb.op(act, lambda: nc.scalar.activation(out=t_.t[:, 3:4], in_=sinkb.t[:, h:h + 1], func=AF.Exp, bias=t_.t[:, 1:2], scale=1.0), [sinkb, t_], [t_])

            def softmax_s2(idx, nk, selcol):
                k3 = idx % 3
                e_ = eS[k3]; n_ = en[k3]; t_ = st[k3]
                kb.op(dve, lambda: nc.vector.tensor_tensor(out=t_.t[:, 4:5], in0=t_.t[:, 2:3], in1=t_.t[:, 3:4], op=ALU.add), [t_], [t_])
                kb.op(dve, lambda: nc.vector.reciprocal(out=t_.t[:, 5:6], in_=t_.t[:, 4:5]), [t_], [t_])
                if selcol is not None:
                    kb.op(dve, lambda: nc.vector.tensor_tensor(out=t_.t[:, 5:6], in0=t_.t[:, 5:6], in1=selcol, op=ALU.mult), [t_, cstb], [t_])
                kb.op(dve, lambda: nc.vector.tensor_scalar(out=n_.t[:, 0:nk], in0=e_.t[:, 0:nk], scalar1=t_.t[:, 5:6], scalar2=None, op0=ALU.mult), [e_, t_], [n_])
                return n_

            def head_rows(h):
                g = h // 4
                if h < 8:
                    c = h % 4; half = h // 4
                else:
                    c = 4 + (h - 8) % 4; half = (h - 8) // 4
                return c, half, g

            def o_finish(po, csl):
                for k in range(2):
                    kb.op(act, lambda: nc.scalar.copy(out=o_tok.t[:, k * 512:(k + 1) * 512], in_=po[k].t[:, :]), [po[k]], [o_tok])
                    kb.ps_release(po[k])
                for q in range(2):
                    pb = kb.ps()
                    pbv = pb.t[:].bitcast(BF16)
                    for k in range(4):
                        c = q * 4 + k
                        kb.op(pe, lambda: nc.tensor.transpose(out=pbv[:, k * 128:(k + 1) * 128], in_=o_tok.t[:, c * 128:(c + 1) * 128], identity=ident_bf()), [o_tok, cbf], [pb], acc=True)
                    kb.op(act, lambda: nc.scalar.copy(out=attnT.t[:, q * 4:(q + 1) * 4, csl], in_=pbv[:, 0:512].rearrange("p (k t) -> p k t", k=4)), [pb], [attnT])

            if P:
                for ch in range(NCH):
                    csl = slice(ch * 128, (ch + 1) * 128)
                    blk0 = first and ch == 0
                    po = [kb.ps_hold() for _ in range(2)]

                    def logits(h):
                        c, half, g = head_rows(h)
                        rs = slice(half * 64, (half + 1) * 64)
                        lg = kb.ps()
                        if blk0:
                            kb.op(pe, lambda: nc.tensor.matmul(lg.t[:, 0:128], lhsT=qT.t[rs, c, csl], rhs=kT.t[rs, g // 2, 128:256], start=True, stop=True), [qT, kT], [lg])
                            return lg, 128, biasT.t[:, h, 128:256]
                        kb.op(pe, lambda: nc.tensor.matmul(lg.t[:, 0:256], lhsT=qT.t[rs, c, csl], rhs=kT.t[rs, g // 2, ch * 128:ch * 128 + 256], start=True, stop=True), [qT, kT], [lg])
                        return lg, 256, biasT.t[:, h, 0:256]

                    def tailp(h, n_, nk):
                        g = h // 4
                        nb = nk // 128
                        pt = kb.ps()
                        ptv = pt.t[:].bitcast(BF16)
                        for b in range(nb):
                            kb.op(pe, lambda: nc.tensor.transpose(out=ptv[:, b * 128:(b + 1) * 128], in_=n_.t[:, b * 128:(b + 1) * 128], identity=ident_bf()), [n_, cbf], [pt], acc=True)
                        p_ = pT[h % 3]
                        kb.op(act, lambda: nc.scalar.copy(out=p_.t[:, 0:nb, :], in_=ptv[:, 0:nb * 128].rearrange("p (b q) -> p b q", b=nb)), [pt], [p_])
                        ob_ = po[h // 8]
                        oc = slice((h % 8) * 64, (h % 8 + 1) * 64)
                        for b in range(nb):
                            vb = (ch + b) if not blk0 else 1
                            kb.op(pe, lambda: nc.tensor.matmul(ob_.t[:, oc], lhsT=p_.t[:, b, :], rhs=vtok.t[:, vb, g * 64:(g + 1) * 64], start=(b == 0), stop=(b == nb - 1)), [p_, vtok], [ob_], acc=True)

                    pend = {}
                    for h in range(LOOK):
                        pend[h] = logits(h)
                    nks = {}
                    for h in range(2):
                        lg, nk, bias_ap = pend.pop(h)
                        nks[h] = nk
                        softmax_s1(h, h, lg, nk, bias_ap)
                    for h in range(16):
                        n_ = softmax_s2(h, nks[h], None)
                        if h + LOOK < 16:
                            pend[h + LOOK] = logits(h + LOOK)
                        if h + 2 < 16:
                            lg, nk, bias_ap = pend.pop(h + 2)
                            nks[h + 2] = nk
                            softmax_s1(h + 2, h + 2, lg, nk, bias_ap)
                        tailp(h, n_, nks[h])
                    o_finish(po, csl)
                if last:
                    out_dma(sp, kp, kv_tok.t[:, NCH - 1, 0:256], [kv_tok])
                    out_dma(sp, vp, kv_tok.t[:, NCH - 1, 256:512], [kv_tok])
                kb.op(dve, lambda: nc.vector.tensor_copy(out=kT.t[:, :, 0:128], in_=kT.t[:, :, T:T + 128]), [kT], [kT])
                kb.op(dve, lambda: nc.vector.tensor_copy(out=vtok.t[:, 0, :], in_=vtok.t[:, NCH, :]), [vtok], [vtok])
            else:
                ckf = [kb.abuf(f"ckf{k}", [128, 256], F32) for k in range(2)]
                kTj = [kb.abuf(f"kTj{k}", [128, 2, 136], BF16) for k in range(2)]
                vj = [kb.abuf(f"vj{k}", [128, 256], BF16) for k in range(2)]
                vnew = kb.abuf("vnew", [8, 16, 256], BF16)
                vtb = kb.abuf("vtb", [128, 256], BF16)
                kb.op(dve, lambda: nc.vector.tensor_copy(out=vtb.t[:], in_=vtok.t[:, 1, :]), [vtok], [vtb])
                for j in range(NSEQ_S):
                    kb.dma(sp, vnew.t[:, j, :], vtb.t[j * 8:(j + 1) * 8, :], reads=[vtb], writes=[vnew], dsem=dmisc[2], join=True)
                po = [kb.ps_hold() for _ in range(2)]

                def prep(j):
                    kf = ckf[j % 2]; ktj = kTj[j % 2]; v_ = vj[j % 2]
                    kb.dma(sp, kf.t[:], ck[j], writes=[kf], dsem=msem())
                    kb.dma(pool, v_.t[:], cv[j], writes=[v_], dsem=msem())
                    out_dma(sp, ks[j, 0:120, :], ck[j, 8:128, :], [])
                    out_dma(sp, vs[j, 0:120, :], cv[j, 8:128, :], [])
                    out_dma(sp, ks[j, 120:128, :], kv_tok.t[j * 8:(j + 1) * 8, 0, 0:256], [kv_tok])
                    out_dma(sp, vs[j, 120:128, :], kv_tok.t[j * 8:(j + 1) * 8, 0, 256:512], [kv_tok])
                    for c2 in range(2):
                        pb = kb.ps()
                        kb.op(pe, lambda: nc.tensor.transpose(out=pb.t[:, 0:128], in_=kf.t[:, c2 * 128:(c2 + 1) * 128], identity=ident()), [kf, cstb], [pb])
                        kb.op(act, lambda: nc.scalar.copy(out=ktj.t[:, c2, 0:128], in_=pb.t[:, 0:128]), [pb], [ktj])
                    kb.op(dve, lambda: nc.vector.tensor_copy(out=ktj.t[:, :, 128:136], in_=kT.t[:, :, 128 + j * 8:128 + (j + 1) * 8]), [kT, ktj], [ktj])

                def logits_s(idx):
                    j, h = divmod(idx, 16)
                    if h == 0:
                        prep(j)
                    ktj = kTj[j % 2]
                    c, half, g = head_rows(h)
                    rs = slice(half * 64, (half + 1) * 64)
                    lg = kb.ps()
                    kb.op(pe, lambda: nc.tensor.matmul(lg.t[:, 0:136], lhsT=qT.t[rs, c, :], rhs=ktj.t[rs, g // 2, :], start=True, stop=True), [qT, ktj], [lg])
                    return lg

                def tails(idx, n_):
                    j, h = divmod(idx, 16)
                    g = h // 4
                    v_ = vj[j % 2]
                    pt = kb.ps()
                    ptv = pt.t[:].bitcast(BF16)
                    kb.op(pe, lambda: nc.tensor.transpose(out=ptv[:, 0:128], in_=n_.t[:, 0:128], identity=ident_bf()), [n_, cbf], [pt], acc=True)
                    kb.op(pe, lambda: nc.tensor.transpose(out=ptv[0:8, 128:256], in_=n_.t[:, 128:136], identity=ident_bf()), [n_, cbf], [pt], acc=True)
                    p_ = pT[idx % 3]
                    kb.op(act, lambda: nc.scalar.copy(out=p_.t[:, 0, :], in_=ptv[:, 0:128]), [pt], [p_])
                    kb.op(act, lambda: nc.scalar.copy(out=p_.t[0:8, 1, :], in_=ptv[0:8, 128:256]), [pt, p_], [p_])
                    ob_ = po[h // 8]
                    oc = slice((h % 8) * 64, (h % 8 + 1) * 64)
                    kb.op(pe, lambda: nc.tensor.matmul(ob_.t[:, oc], lhsT=p_.t[:, 0, :], rhs=v_.t[:, g * 64:(g + 1) * 64], start=(j == 0 and h % 8 == 0), stop=False), [p_, v_], [ob_], acc=True)
                    kb.op(pe, lambda: nc.tensor.matmul(ob_.t[:, oc], lhsT=p_.t[0:8, 1, :], rhs=vnew.t[0:8, j, g * 64:(g + 1) * 64], start=False, stop=(j == NSEQ_S - 1)), [p_, vnew], [ob_], acc=True)

                NI = NSEQ_S * 16
                pend = {}
                for idx in range(LOOK):
                    pend[idx] = logits_s(idx)
                for idx in range(2):
                    softmax_s1(idx, idx % 16, pend.pop(idx), 136, biasT.t[:, idx % 16, 0:136])
                for idx in range(NI):
                    j, h = divmod(idx, 16)
                    n_ = softmax_s2(idx, 136, cstb.t[:, C_SEL + j:C_SEL + j + 1])
                    if idx + LOOK < NI:
                        pend[idx + LOOK] = logits_s(idx + LOOK)
                    if idx + 2 < NI:
                        softmax_s1(idx + 2, (idx + 2) % 16, pend.pop(idx + 2), 136, biasT.t[:, (idx + 2) % 16, 0:136])
                    tails(idx, n_)
                o_finish(po, slice(0, 128))


            if flags.get("swa_stop") == 3:
                kb.arena_close(); return
            for m in range(8):
                sl = wsm.get(3 + m // 4)
                wv = sl.t[:, :].rearrange("p (c n) -> p c n", c=8)
                pb = kb.ps()
                for kc in range(8):
                    kb.op(pe, lambda: nc.tensor.matmul(pb.t[:, 0:T], lhsT=wv[:, kc, (m % 4) * 128:(m % 4 + 1) * 128], rhs=attnT.t[:, kc, 0:T], start=(kc == 0), stop=(kc == 7)), [sl, attnT], [pb], acc=True)
                kb.op(act, lambda: nc.scalar.activation(out=f_fm.t[:, m, 0:T], in_=pb.t[:, 0:T], func=AF.Identity, bias=ob.t[:, m:m + 1], scale=1.0), [pb, ob], [f_fm])
            post(1, 1, kind, f_fm, f_fm.t[:, :, :])
            kb.arena_close()

        tiles = [("P", t) for t in range(8)] + [("S", 0)]
        if "tiles" in flags:
            tiles = flags["tiles"]
        xi = [0]
        for kind, ti in tiles:
            T, NCH, nseq, tps = geom(kind)
            src = xp if kind == "P" else xs
            dst = yp if kind == "P" else ys
            first = (ti == 0); last = (ti == 7) or kind == "S"
            if kind == "S":
                pass
            kb.arena_open()
            xtk = [kb.abuf(f"xtk{k}", [128, D], F32) for k in range(2)]
            for ch in range(NCH):
                xt = xtk[xi[0] % 2]; xi[0] += 1
                r0 = ti * 512 + ch * 128
                kb.dma(sp, xt.t[:], src[r0:r0 + 128, :], writes=[xt], dsem=msem())
                for q in range(2):
                    pb = kb.ps()
                    for k in range(4):
                        c = q * 4 + k
                        kb.op(pe, lambda: nc.tensor.transpose(out=pb.t[:, k * 128:(k + 1) * 128], in_=xt.t[:, c * 128:(c + 1) * 128], identity=ident()), [xt, cstb], [pb], acc=True)
                    kb.op(act, lambda: nc.scalar.copy(out=x_fm.t[:, q * 4:(q + 1) * 4, ch * 128:(ch + 1) * 128], in_=pb.t[:, :].rearrange("p (k t) -> p k t", k=4)), [pb], [x_fm])
            kb.arena_close()
            for fl in flags.get("phases", ["f00", "ssd", "f01", "f10", "swa", "f11"]):
                if fl == "f00":
                    ffn(0, 0, kind)
                elif fl == "ssd":
                    ssd(kind, last)
                elif fl == "f01":
                    ffn(0, 1, kind)
                elif fl == "f10":
                    ffn(1, 0, kind)
                elif fl == "swa":
                    swa(kind, first, last)
                elif fl == "f11":
                    ffn(1, 1, kind)
            kb.arena_open()
            xtk = [kb.abuf(f"xtk{k}", [128, D], F32) for k in range(2)]
            for ch in range(NCH):
                xt = xtk[xi[0] % 2]; xi[0] += 1
                r0 = ti * 512 + ch * 128
                for q in range(2):
                    pb = kb.ps()
                    for k in range(4):
                        c = q * 4 + k
                        kb.op(pe, lambda: nc.tensor.transpose(out=pb.t[:, k * 128:(k + 1) * 128], in_=x_fm.t[:, c, ch * 128:(ch + 1) * 128], identity=ident()), [x_fm, cstb], [pb], acc=True)
                    kb.op(act, lambda: nc.scalar.copy(out=xt.t[:, q * 512:(q + 1) * 512], in_=pb.t[:, :]), [pb], [xt])
                out_dma(sp, dst[r0:r0 + 128, :], xt.t[:], [xt])
            kb.arena_close()

        for s in dout_sems:
            if kb.cnts[s] > 0:
                nc.sync.wait_ge(kb.sems[s], kb.cnts[s])
        for E in (pe, act, dve, pool):
            if E.cnt > 0:
                nc.sync.wait_ge(kb.sems[E.name], E.cnt)
        for s in dmisc + [b.dsem for b in kb.wslots]:
            if kb.cnts[s] > 0:
                nc.sync.wait_ge(kb.sems[s], kb.cnts[s])
    return nc


_CACHE = {}


def make_in_maps(inp, cores=range(8)):
    f = lambda a: np.ascontiguousarray(np.asarray(a, dtype=np.float32))
    cstv = _make_consts()
    shared = {
        "ada_w": f(inp["ada_w"]), "ada_b": f(inp["ada_b"]), "norm_pre": f(inp["norm_pre"]), "norm_post": f(inp["norm_post"]),
        "ffn_w_in": f(inp["ffn_w_in"]), "ffn_w_out": f(inp["ffn_w_out"]),
        "ssm_in_w": f(inp["ssm_in_w"][0]), "conv_w": f(inp["ssm_conv_w"][0]), "conv_b": f(inp["ssm_conv_b"]),
        "dt_bias": f(inp["ssm_dt_bias"]), "a_log": f(inp["ssm_a_log"]), "ssm_d": f(inp["ssm_d"]),
        "ssm_norm_w": f(inp["ssm_norm_w"]), "ssm_out_w": f(inp["ssm_out_w"][0]),
        "qkv_w": f(inp["attn_qkv_w"][0]), "qkv_b": f(inp["attn_qkv_b"]), "sinks": f(inp["attn_sinks"]),
        "o_w": f(inp["attn_o_w"][0]), "o_b": f(inp["attn_o_b"]), "rel_bias": f(inp["rel_bias"]), "cst": cstv,
    }
    x_prompt = f(inp["x_prompt"]); x_sample = f(inp["x_sample"])
    state_ssm = f(inp["state_ssm"]); state_conv = f(inp["state_conv"])
    cache_k = f(inp["cache_k"]); cache_v = f(inp["cache_v"])
    c_prompt = f(inp["c_prompt"]); c_sample = f(inp["c_sample"])
    in_maps = []
    for b in cores:
        s0, s1 = b * 16, (b + 1) * 16
        m = dict(shared)
        m["xp"] = x_prompt[b]
        m["xs"] = x_sample[s0:s1].reshape(128, D)
        m["st_ssm"] = state_ssm[0, s0:s1].reshape(16, DIN, 128)
        m["st_conv"] = state_conv[0, s0:s1].reshape(48, 3072)
        m["ck"] = cache_k[0, s0:s1].reshape(16, 128, 256)
        m["cv"] = cache_v[0, s0:s1].reshape(16, 128, 256)
        m["cvec"] = np.concatenate([c_prompt[b:b + 1], c_sample[s0:s1]], axis=0)
        in_maps.append(m)
    return in_maps


def assemble(R):
    n = len(R)
    y_prompt = np.stack([R[b]["yp"] for b in range(n)]).reshape(n, SEQ, D)
    y_sample = np.concatenate([R[b]["ys"].reshape(16, 8, D) for b in range(n)], axis=0)
    ssm_p = np.stack([R[b]["ssm_p"].reshape(32, 64, 128) for b in range(n)])[None]
    conv_p = np.stack([R[b]["conv_p"] for b in range(n)])[None]
    k_p = np.stack([R[b]["kp"].reshape(128, 4, 64) for b in range(n)])[None]
    v_p = np.stack([R[b]["vp"].reshape(128, 4, 64) for b in range(n)])[None]
    ssm_s = np.concatenate([R[b]["ssm_s"].reshape(16, 32, 64, 128) for b in range(n)], axis=0)[None]
    conv_s = np.concatenate([R[b]["conv_s"].reshape(16, 3, 3072) for b in range(n)], axis=0)[None]
    k_s = np.concatenate([R[b]["ks"].reshape(16, 128, 4, 64) for b in range(n)], axis=0)[None]
    v_s = np.concatenate([R[b]["vs"].reshape(16, 128, 4, 64) for b in range(n)], axis=0)[None]
    outs = (y_prompt, y_sample, ssm_p, conv_p, k_p, v_p, ssm_s, conv_s, k_s, v_s)
    return tuple(np.ascontiguousarray(o, dtype=np.float32) for o in outs)


def kernel(**inp):
    nc = _CACHE.get("nc")
    if nc is None:
        nc = build_program()
        _CACHE["nc"] = nc
    in_maps = make_in_maps(inp)
    res = run_bass_kernel_spmd(nc, in_maps, core_ids=list(range(8)))
    return assemble(res.results)
```

```python
import contextlib
import math
import numpy as np
import concourse.bass as bass
import concourse.mybir as mybir
from concourse.bass_utils import run_bass_kernel_spmd

F32 = mybir.dt.float32
BF16 = mybir.dt.bfloat16
AF = mybir.ActivationFunctionType
ALU = mybir.AluOpType
AX = mybir.AxisListType

D = 1024
SEQ = 4096
NSEQ_S = 16
TPS_S = 8
DFF = 2816
NJ = DFF // 128
DIN = 2048
NEG = -30000.0
EPS = 1e-6

C_ID, C_TRIP, C_STRIP, C_TRIS, C_STRIS, C_J, C_JREP = 0, 128, 256, 384, 512, 640, 768
C_SEL = 896
C_OH = 912
C_ONES = 1296
NCST = 1424


def _bucket(d):
    d = np.asarray(d)
    df = np.maximum(d, 1).astype(np.float32)
    large = 16 + (np.log(df / np.float32(16)) / np.float32(math.log(128 / 16)) * np.float32(16)).astype(np.int32)
    large = np.minimum(large, 31)
    return np.where(d < 16, d, large)


def _make_consts():
    c = np.zeros((128, NCST), np.float32)
    i = np.arange(128)
    same = (i[:, None] // 8) == (i[None, :] // 8)
    c[:, C_ID:C_ID + 128] = np.eye(128)
    c[:, C_TRIP:C_TRIP + 128] = (i[:, None] <= i[None, :])
    c[:, C_STRIP:C_STRIP + 128] = (i[:, None] > i[None, :])
    c[:, C_TRIS:C_TRIS + 128] = (i[:, None] <= i[None, :]) & same
    c[:, C_STRIS:C_STRIS + 128] = (i[:, None] > i[None, :]) & same
    c[:, C_J:C_J + 128] = (i[:, None] == 127 - i[None, :])
    c[:, C_JREP:C_JREP + 128] = (i[:, None] == 127 - (i[None, :] % 8))
    c[:, C_SEL:C_SEL + 16] = (i[:, None] // 8) == np.arange(16)[None, :]
    oh = np.zeros((33, 384), np.float32)
    for ii in range(384):
        dist = 255 - ii
        if 0 <= dist <= 128:
            oh[int(_bucket(dist)), ii] = 1.0
        else:
            oh[32, ii] = 1.0
    c[0:33, C_OH:C_OH + 384] = oh
    c[:, C_ONES:C_ONES + 128] = 1.0
    return c


class Buf:
    __slots__ = ("t", "w", "r", "dsem")

    def __init__(self, t, pend=None):
        self.t = t
        self.w = None
        self.r = dict(pend) if pend else {}
        self.dsem = None


class Eng:
    def __init__(self, name, h):
        self.name = name
        self.h = h
        self.cnt = 0
        self.seen = {}


class KB:
    def __init__(self, nc, es):
        self.nc = nc
        self.es = es
        self.sems = {}
        self.cnts = {}
        self.pe = self._eng("pe", nc.tensor)
        self.act = self._eng("act", nc.scalar)
        self.dve = self._eng("dve", nc.vector)
        self.pool = self._eng("pool", nc.gpsimd)
        self.sp = self._eng("sp", nc.sync)
        self.engs = [self.pe, self.act, self.dve, self.pool, self.sp]
        self.banks = []
        for i in range(8):
            t = es.enter_context(nc.psum_tensor(f"psb{i}", [128, 512], F32))
            self.banks.append(Buf(t))
        self.bank_i = 0
        self.held = set()
        self.pending = {}
        self.arena_bufs = []
        self.arena_es = None
        self.out_events = {}
        self.wslots = []
        self.wslot_i = 0
        self.skip_self_waw = True
        self.dq = 0

    def _eng(self, name, h):
        self.sems[name] = self.es.enter_context(self.nc.semaphore(name))
        self.cnts[name] = 0
        return Eng(name, h)

    def dsem(self, name):
        self.sems[name] = self.es.enter_context(self.nc.semaphore(name))
        self.cnts[name] = 0
        return name

    def pbuf(self, name, shape, dt):
        t = self.es.enter_context(self.nc.sbuf_tensor(name, list(shape), dt))
        return Buf(t)

    def arena_open(self):
        self.arena_es = contextlib.ExitStack()
        self.arena_bufs = []

    def abuf(self, name, shape, dt):
        self.uid = getattr(self, "uid", 0) + 1
        t = self.arena_es.enter_context(self.nc.sbuf_tensor(f"{name}_{self.uid}", list(shape), dt))
        b = Buf(t, self.pending)
        self.arena_bufs.append(b)
        return b

    def arena_close(self):
        pend = dict(self.pending)
        for b in self.arena_bufs:
            if b.w and pend.get(b.w[0], 0) < b.w[1]:
                pend[b.w[0]] = b.w[1]
            for s, v in b.r.items():
                if pend.get(s, 0) < v:
                    pend[s] = v
        self.pending = pend
        self.arena_es.close()
        self.arena_es = None
        self.arena_bufs = []

    def ps(self):
        for _ in range(16):
            i = self.bank_i
            self.bank_i = (self.bank_i + 1) % 8
            if i not in self.held:
                return self.banks[i]
        raise RuntimeError("no psum bank")

    def ps_hold(self):
        b = self.ps()
        self.held.add(self.banks.index(b))
        return b

    def ps_release(self, b):
        self.held.discard(self.banks.index(b))

    def _need(self, E, reads, writes, acc, skipname):
        need = {}

        def add(s, v):
            if need.get(s, 0) < v:
                need[s] = v
        for b in reads:
            if b.w:
                add(*b.w)
        for b in writes:
            if b.w and not ((acc or self.skip_self_waw) and b.w[0] == skipname):
                add(*b.w)
            for s, v in b.r.items():
                if s != E.name:
                    add(s, v)
        for s, v in need.items():
            if E.seen.get(s, 0) < v:
                E.h.wait_ge(self.sems[s], v)
                E.seen[s] = v

    def op(self, E, fn, reads=(), writes=(), acc=False):
        self._need(E, reads, writes, acc, E.name)
        ins = fn()
        E.cnt += 1
        ins.then_inc(self.sems[E.name], 1)
        for b in reads:
            if b.r.get(E.name, 0) < E.cnt:
                b.r[E.name] = E.cnt
        for b in writes:
            b.w = (E.name, E.cnt)
            b.r = {}
        return ins

    def dma(self, Q, out, in_, reads=(), writes=(), dsem=None, join=False, **kw):
        self._need(Q, reads, writes, join, dsem)
        self.cnts[dsem] += 16
        v = self.cnts[dsem]
        Q.h.dma_start(out=out, in_=in_, **kw).then_inc(self.sems[dsem], 16)
        for b in reads:
            if b.r.get(dsem, 0) < v:
                b.r[dsem] = v
        for b in writes:
            b.w = (dsem, v)
            b.r = {}
        return (dsem, v)

    def wslot(self):
        s = self.wslots[self.wslot_i]
        self.wslot_i = (self.wslot_i + 1) % len(self.wslots)
        return s


class WStream:
    def __init__(self, kb, loads, depth=2):
        self.kb = kb
        self.loads = loads
        self.depth = depth
        self.issued = 0
        self.slots = []

    def get(self, k):
        kb = self.kb
        while self.issued < min(len(self.loads), k + 1 + self.depth):
            sl = kb.wslot()
            for i, (d, s) in enumerate(self.loads[self.issued](sl.t)):
                kb.dma(kb.pool, d, s, writes=[sl], dsem=sl.dsem, join=(i > 0))
            self.slots.append(sl)
            self.issued += 1
        return self.slots[k]


def build_program(flags=None):
    flags = flags or {}
    nc = bass.Bass("TRN2", target_bir_lowering=False)

    def din(name, shape, dt=F32):
        return nc.dram_tensor(name, list(shape), dt, kind="ExternalInput").ap()

    def dout(name, shape):
        return nc.dram_tensor(name, list(shape), F32, kind="ExternalOutput").ap()

    xp = din("xp", [SEQ, D]); xs = din("xs", [128, D])
    st_ssm = din("st_ssm", [NSEQ_S, DIN, 128]); st_conv = din("st_conv", [48, 3072])
    ck = din("ck", [NSEQ_S, 128, 256]); cv = din("cv", [NSEQ_S, 128, 256])
    cvec = din("cvec", [17, D])
    ada_w = din("ada_w", [2, D, 9216]); ada_b = din("ada_b", [2, 9216])
    norm_pre = din("norm_pre", [2, 3, D]); norm_post = din("norm_post", [2, 3, D])
    ffn_w_in = din("ffn_w_in", [2, 2, D, 2 * DFF]); ffn_w_out = din("ffn_w_out", [2, 2, DFF, D])
    ssm_in_w = din("ssm_in_w", [D, 5152]); conv_w = din("conv_w", [4, 3072]); conv_b = din("conv_b", [1, 3072])
    dt_bias = din("dt_bias", [1, 32]); a_log = din("a_log", [1, 32]); ssm_d = din("ssm_d", [1, 32])
    ssm_norm_w = din("ssm_norm_w", [1, DIN]); ssm_out_w = din("ssm_out_w", [DIN, D])
    qkv_w = din("qkv_w", [D, 1536]); qkv_b = din("qkv_b", [1, 1536]); sinks = din("sinks", [1, 16])
    o_w = din("o_w", [D, D]); o_b = din("o_b", [1, D]); rel_bias = din("rel_bias", [32, 16])
    cst = din("cst", [128, NCST])

    yp = dout("yp", [SEQ, D]); ys = dout("ys", [128, D])
    ssm_p = dout("ssm_p", [DIN, 128]); conv_p = dout("conv_p", [3, 3072])
    kp = dout("kp", [128, 256]); vp = dout("vp", [128, 256])
    ssm_s = dout("ssm_s", [NSEQ_S, DIN, 128]); conv_s = dout("conv_s", [48, 3072])
    ks = dout("ks", [NSEQ_S, 128, 256]); vs = dout("vs", [NSEQ_S, 128, 256])
    uscr = nc.dram_tensor("uscr", [16, 384], F32, kind="Internal")

    es = contextlib.ExitStack()
    with es:
        kb = KB(nc, es)
        pe, act, dve, pool, sp = kb.pe, kb.act, kb.dve, kb.pool, kb.sp
        NW = 4
        for i in range(NW):
            b = kb.pbuf(f"wslot{i}", [128, 4096], BF16)
            b.dsem = kb.dsem(f"dw{i}")
            kb.wslots.append(b)
        dmisc = [kb.dsem(f"dm{i}") for i in range(8)]
        dout_sems = [kb.dsem(f"do{i}") for i in range(4)]
        mi = [0]

        def msem():
            mi[0] = (mi[0] + 1) % len(dmisc)
            return dmisc[mi[0]]
        oi = [0]

        def osem():
            oi[0] = (oi[0] + 1) % len(dout_sems)
            return dout_sems[oi[0]]

        def out_dma(Q, out, in_, reads, **kw):
            s = osem()
            kb.dma(Q, out, in_, reads=reads, dsem=s, **kw)

        cstb = kb.pbuf("cstb", [128, NCST], F32)
        cbf = kb.pbuf("cbf", [128, 256], BF16)
        epsb = kb.pbuf("epsb", [128, 2], F32)
        x_fm = kb.pbuf("x_fm", [128, 8, 512], F32)
        hin = kb.pbuf("hin", [128, 8, 512], BF16)
        rstd = kb.pbuf("rstd", [128, 512], F32)
        tmpA = [kb.pbuf(f"tmpA{i}", [128, 512], F32) for i in range(3)]
        PRE = kb.pbuf("PRE", [128, 18, 8, 17], F32)
        hT = kb.pbuf("hT", [128, DIN], F32)
        hT_bf = kb.pbuf("hT_bf", [128, DIN], BF16)
        tailP = kb.pbuf("tailP", [128, 24, 3], F32)
        tailP_cc = [Buf(tailP.t) for _ in range(24)]
        convw = kb.pbuf("convw", [128, 24, 4], F32)
        convb = kb.pbuf("convb", [128, 24], F32)
        vec32 = kb.pbuf("vec32", [128, 4, 32], F32)
        normwT = kb.pbuf("normwT", [128, 16], F32)
        wdt = kb.pbuf("wdt", [128, 8, 32], BF16)
        qkb = kb.pbuf("qkb", [128, 10], F32)
        kvb = kb.pbuf("kvb", [128, 512], F32)
        ob = kb.pbuf("ob", [128, 8], F32)
        sinkb = kb.pbuf("sinkb", [128, 16], F32)
        kT = kb.pbuf("kT", [128, 2, 128 + 512], BF16)
        vtok = kb.pbuf("vtok", [128, 5, 256], BF16)
        tmi = [0]

        def tmp():
            tmi[0] = (tmi[0] + 1) % 3
            return tmpA[tmi[0]]

        ident = lambda: cstb.t[:, C_ID:C_ID + 128]
        ident_bf = lambda: cbf.t[:, 0:128]
        ones_bf = lambda: cbf.t[:, 128:256]
        ones_f = lambda: cstb.t[:, C_ONES:C_ONES + 128]

        kb.dma(sp, cstb.t[:], cst, writes=[cstb], dsem=msem())
        kb.op(dve, lambda: nc.vector.tensor_copy(out=cbf.t[:, 0:128], in_=cstb.t[:, C_ID:C_ID + 128]), [cstb], [cbf])
        kb.op(dve, lambda: nc.vector.tensor_copy(out=cbf.t[:, 128:256], in_=cstb.t[:, C_ONES:C_ONES + 128]), [cbf, cstb], [cbf])
        kb.op(dve, lambda: nc.vector.memset(epsb.t[:, 0:1], EPS), [], [epsb])
        kb.op(dve, lambda: nc.vector.memset(epsb.t[:, 1:2], 1.0), [epsb], [epsb])
        kb.op(dve, lambda: nc.vector.memset(tailP.t[:], 0.0), [], [tailP] + tailP_cc)
        kb.op(dve, lambda: nc.vector.memset(hT.t[:], 0.0), [], [hT])
        kb.op(dve, lambda: nc.vector.memset(hT_bf.t[:], 0.0), [], [hT_bf])
        kb.op(dve, lambda: nc.vector.memset(kT.t[:], 0.0), [], [kT])
        kb.op(dve, lambda: nc.vector.memset(vtok.t[:], 0.0), [], [vtok])

        with nc.allow_non_contiguous_dma(reason="small param loads"):
            for k in range(4):
                kb.dma(sp, convw.t[:, :, k], conv_w[k].rearrange("(c p) -> p c", p=128), writes=[convw], dsem=dmisc[3], join=True)
            kb.dma(sp, convb.t[:], conv_b.rearrange("o (c p) -> p (o c)", p=128), writes=[convb], dsem=msem())
            kb.dma(sp, ob.t[:], o_b.rearrange("o (c p) -> p (o c)", p=128), writes=[ob], dsem=msem())
            for c in range(8):
                A = c if c < 4 else c + 4
                for half, hh in ((0, A), (1, A + 4)):
                    kb.dma(sp, qkb.t[half * 64:(half + 1) * 64, c:c + 1],
                           qkv_b[0:1, hh * 64:(hh + 1) * 64].rearrange("o d -> d o"), writes=[qkb], dsem=dmisc[0], join=True)
            kb.dma(sp, qkb.t[:, 8:10], qkv_b[0:1, 1024:1280].rearrange("o (c p) -> p (o c)", p=128), writes=[qkb], dsem=dmisc[0], join=True)
        kb.dma(sp, vec32.t[:, 0, :], dt_bias.partition_broadcast(128), writes=[vec32], dsem=dmisc[1])
        kb.dma(sp, vec32.t[:, 1, :], a_log.partition_broadcast(128), writes=[vec32], dsem=dmisc[1], join=True)
        kb.dma(sp, vec32.t[:, 2, :], ssm_d.partition_broadcast(128), writes=[vec32], dsem=dmisc[1], join=True)
        with nc.allow_non_contiguous_dma(reason="small param loads"):
            kb.dma(sp, normwT.t[:], ssm_norm_w.rearrange("o (c p) -> p (o c)", p=128), writes=[normwT], dsem=msem())
        kb.dma(sp, sinkb.t[:], sinks.partition_broadcast(128), writes=[sinkb], dsem=msem())
        kb.dma(pool, wdt.t[:], ssm_in_w.rearrange("(c p) n -> p c n", p=128)[:, :, 5120:5152], writes=[wdt], dsem=msem())
        kb.dma(sp, kvb.t[:], qkv_b[0:1, 1024:1536].partition_broadcast(128), writes=[kvb], dsem=msem())
        kb.op(act, lambda: nc.scalar.activation(out=vec32.t[:, 1, :], in_=vec32.t[:, 1, :], func=AF.Exp), [vec32], [vec32])
        kb.op(dve, lambda: nc.vector.tensor_scalar(out=vec32.t[:, 1, :], in0=vec32.t[:, 1, :], scalar1=-1.0, scalar2=None, op0=ALU.mult), [vec32], [vec32])
        kb.op(dve, lambda: nc.vector.tensor_scalar(out=qkb.t[:, 0:8], in0=qkb.t[:, 0:8], scalar1=0.125, scalar2=None, op0=ALU.mult), [qkb], [qkb])

        kb.arena_open()
        cT = kb.abuf("cT", [128, 8, 17], F32)
        csT = kb.abuf("csT", [128, 8, 17], BF16)
        adab = kb.abuf("adab", [128, 2, 72], F32)
        npre = kb.abuf("npre", [128, 6, 8], F32)
        npost = kb.abuf("npost", [128, 6, 8], F32)
        modT = kb.abuf("modT", [128, 2, 72, 17], F32)
        ctok = kb.abuf("ctok", [17, D], F32)
        kb.dma(sp, ctok.t[:], cvec, writes=[ctok], dsem=msem())
        for c in range(8):
            pb = kb.ps()
            kb.op(pe, lambda: nc.tensor.transpose(out=pb.t[:, 0:17], in_=ctok.t[:, c * 128:(c + 1) * 128], identity=cstb.t[0:17, C_ID:C_ID + 17]), [ctok, cstb], [pb])
            kb.op(dve, lambda: nc.vector.tensor_copy(out=cT.t[:, c, :], in_=pb.t[:, 0:17]), [pb], [cT])
        kb.op(act, lambda: nc.scalar.activation(out=csT.t[:], in_=cT.t[:], func=AF.Silu), [cT], [csT])
        with nc.allow_non_contiguous_dma(reason="small param loads"):
            for i in range(2):
                kb.dma(sp, adab.t[:, i, :], ada_b[i].rearrange("(c p) -> p c", p=128), writes=[adab], dsem=dmisc[4], join=True)
                for sub in range(3):
                    kb.dma(sp, npre.t[:, i * 3 + sub, :], norm_pre[i, sub].rearrange("(c p) -> p c", p=128), writes=[npre], dsem=dmisc[5], join=True)
                    kb.dma(sp, npost.t[:, i * 3 + sub, :], norm_post[i, sub].rearrange("(c p) -> p c", p=128), writes=[npost], dsem=dmisc[6], join=True)
        for i in range(2):
            aw = ada_w[i].rearrange("(c p) n -> p c n", p=128)
            loads = []
            for nb in range(18):
                loads.append(lambda t, nb=nb: [(t[:, :].rearrange("p (c n) -> p c n", c=8), aw[:, :, nb * 512:(nb + 1) * 512])])
            wsm = WStream(kb, loads)
            for nb in range(18):
                sl = wsm.get(nb)
                wv = sl.t[:, :].rearrange("p (c n) -> p c n", c=8)
                pb = kb.ps()
                for m in range(4):
                    for kc in range(8):
                        kb.op(pe, lambda: nc.tensor.matmul(pb.t[:, m * 17:(m + 1) * 17], lhsT=wv[:, kc, m * 128:(m + 1) * 128], rhs=csT.t[:, kc, :], start=(kc == 0), stop=(kc == 7)), [sl, csT], [pb], acc=True)
                kb.op(dve, lambda: nc.vector.tensor_tensor(out=modT.t[:, i, nb * 4:(nb + 1) * 4, :], in0=pb.t[:, 0:68].rearrange("p (m s) -> p m s", m=4),
                                                           in1=adab.t[:, i, nb * 4:(nb + 1) * 4].unsqueeze(2).to_broadcast([128, 4, 17]), op=ALU.add), [pb, adab], [modT])
        for i in range(2):
            for sub in range(3):
                base = (i * 3 + sub) * 3
                sh = modT.t[:, i, (sub * 3 + 0) * 8:(sub * 3 + 0) * 8 + 8, :]
                sc = modT.t[:, i, (sub * 3 + 1) * 8:(sub * 3 + 1) * 8 + 8, :]
                gt = modT.t[:, i, (sub * 3 + 2) * 8:(sub * 3 + 2) * 8 + 8, :]
                npb = npre.t[:, i * 3 + sub, :].unsqueeze(2).to_broadcast([128, 8, 17])
                npo = npost.t[:, i * 3 + sub, :].unsqueeze(2).to_broadcast([128, 8, 17])
                kb.op(dve, lambda: nc.vector.scalar_tensor_tensor(out=PRE.t[:, base + 0, :, :], in0=sc, scalar=1.0, in1=npb, op0=ALU.add, op1=ALU.mult), [modT, npre], [PRE])
                kb.op(dve, lambda: nc.vector.tensor_copy(out=PRE.t[:, base + 1, :, :], in_=sh), [modT, PRE], [PRE])
                res = 1.0 if sub == 1 else 0.5
                kb.op(dve, lambda: nc.vector.scalar_tensor_tensor(out=PRE.t[:, base + 2, :, :], in0=gt, scalar=res, in1=npo, op0=ALU.mult, op1=ALU.mult), [modT, npost, PRE], [PRE])

        rb = kb.abuf("rb", [33, 16], F32)
        usb = kb.abuf("usb", [16, 384], F32)
        kb.op(dve, lambda: nc.vector.memset(rb.t[:], NEG), [], [rb])
        kb.dma(sp, rb.t[0:32, :], rel_bias, reads=[], writes=[rb], dsem=msem())
        pb = kb.ps()
        kb.op(pe, lambda: nc.tensor.matmul(pb.t[0:16, 0:384], lhsT=rb.t[:, :], rhs=cstb.t[0:33, C_OH:C_OH + 384], start=True, stop=True), [rb, cstb], [pb])
        kb.op(dve, lambda: nc.vector.tensor_copy(out=usb.t[:], in_=pb.t[0:16, 0:384]), [pb], [usb])
        uev = Buf(None)
        kb.dma(sp, uscr.ap(), usb.t[:], reads=[usb], writes=[uev], dsem=msem())
        kb.arena_close()

        def geom(kind):
            if kind == "P":
                return 512, 4, 1, 512
            return 128, 1, 16, 8

        def sumsq_rstd(src, sv, T):
            kb.op(act, lambda: nc.scalar.activation(out=hin.t[:, :, 0:T], in_=sv, func=AF.Square), [src], [hin])
            pb = kb.ps()
            for c in range(8):
                kb.op(pe, lambda: nc.tensor.matmul(pb.t[:, 0:T], lhsT=ones_bf(), rhs=hin.t[:, c, 0:T], start=(c == 0), stop=(c == 7)), [hin, cbf], [pb], acc=True)
            kb.op(act, lambda: nc.scalar.activation(out=rstd.t[:, 0:T], in_=pb.t[:, 0:T], func=AF.Ln, bias=epsb.t[:, 0:1], scale=1.0 / D), [pb, epsb], [rstd])
            kb.op(act, lambda: nc.scalar.activation(out=rstd.t[:, 0:T], in_=rstd.t[:, 0:T], func=AF.Exp, scale=-0.5), [rstd], [rstd])

        def norm_mod(i, sub, kind):
            T, NCH, nseq, tps = geom(kind)
            base = (i * 3 + sub) * 3
            sumsq_rstd(x_fm, x_fm.t[:, :, 0:T], T)
            for c in range(8):
                t1 = tmp()
                kb.op(dve, lambda: nc.vector.tensor_tensor(out=t1.t[:, 0:T], in0=x_fm.t[:, c, 0:T], in1=rstd.t[:, 0:T], op=ALU.mult), [x_fm, rstd], [t1])
                if kind == "P":
                    kb.op(act, lambda: nc.scalar.activation(out=hin.t[:, c, 0:T], in_=t1.t[:, 0:T], func=AF.Identity,
                                                            bias=PRE.t[:, base + 1, c, 0:1], scale=PRE.t[:, base + 0, c, 0:1]), [t1, PRE], [hin])
                else:
                    v3 = lambda ap: ap.rearrange("p (s t) -> p s t", s=16)
                    kb.op(dve, lambda: nc.vector.tensor_tensor(out=v3(t1.t[:, 0:T]), in0=v3(t1.t[:, 0:T]), in1=PRE.t[:, base + 0, c, 1:17].unsqueeze(2).to_broadcast([128, 16, 8]), op=ALU.mult), [t1, PRE], [t1])
                    kb.op(dve, lambda: nc.vector.tensor_tensor(out=v3(hin.t[:, c, 0:T]), in0=v3(t1.t[:, 0:T]), in1=PRE.t[:, base + 1, c, 1:17].unsqueeze(2).to_broadcast([128, 16, 8]), op=ALU.add), [t1, PRE], [hin])

        def post(i, sub, kind, f_fm, fv):
            T, NCH, nseq, tps = geom(kind)
            base = (i * 3 + sub) * 3
            sumsq_rstd(f_fm, fv, T)
            for c in range(8):
                t1 = tmp()
                kb.op(dve, lambda: nc.vector.tensor_tensor(out=t1.t[:, 0:T], in0=fv[:, c, :], in1=rstd.t[:, 0:T], op=ALU.mult), [f_fm, rstd], [t1])
                if kind == "P":
                    kb.op(dve, lambda: nc.vector.scalar_tensor_tensor(out=x_fm.t[:, c, 0:T], in0=t1.t[:, 0:T], scalar=PRE.t[:, base + 2, c, 0:1], in1=x_fm.t[:, c, 0:T], op0=ALU.mult, op1=ALU.add), [t1, PRE, x_fm], [x_fm])
                else:
                    v3 = lambda ap: ap.rearrange("p (s t) -> p s t", s=16)
                    kb.op(dve, lambda: nc.vector.tensor_tensor(out=v3(t1.t[:, 0:T]), in0=v3(t1.t[:, 0:T]), in1=PRE.t[:, base + 2, c, 1:17].unsqueeze(2).to_broadcast([128, 16, 8]), op=ALU.mult), [t1, PRE], [t1])
                    kb.op(dve, lambda: nc.vector.tensor_tensor(out=x_fm.t[:, c, 0:T], in0=x_fm.t[:, c, 0:T], in1=t1.t[:, 0:T], op=ALU.add), [t1, x_fm], [x_fm])

        def ffn(i, which, kind):
            T, NCH, nseq, tps = geom(kind)
            sub = 0 if which == 0 else 2
            norm_mod(i, sub, kind)
            win = ffn_w_in[i, which].rearrange("(c p) n -> p c n", p=128)
            wout = ffn_w_out[i, which].rearrange("(j p) n -> p j n", p=128)
            loads = []
            NJB = (NJ + 3) // 4
            for jb in range(NJB):
                w_ = min(512, DFF - jb * 512)
                for gu in range(2):
                    loads.append(lambda t, jb=jb, gu=gu, w_=w_: [
                        (t[:, 0:8 * w_].rearrange("p (c n) -> p c n", c=8), win[:, :, gu * DFF + jb * 512:gu * DFF + jb * 512 + w_])])
            for jb in range(NJB):
                nj_ = min(4, NJ - jb * 4)
                loads.append(lambda t, jb=jb, nj_=nj_: [(t[:, 0:nj_ * 1024].rearrange("p (j n) -> p j n", j=nj_), wout[:, jb * 4:jb * 4 + nj_, :])])
            wsm = WStream(kb, loads)
            kb.arena_open()
            actb = [kb.abuf(f"act{j}", [128, 512], BF16) for j in range(NJ)]
            sg = [kb.abuf(f"sg{j}", [128, 512], F32) for j in range(2)]
            f_fm = kb.abuf("f_fm", [128, 8, T], F32)
            for j in range(NJ):
                jb = j // 4
                w_ = min(512, DFF - jb * 512)
                slg = wsm.get(2 * jb); slu = wsm.get(2 * jb + 1)
                wg = slg.t[:, 0:8 * w_].rearrange("p (c n) -> p c n", c=8)
                wu = slu.t[:, 0:8 * w_].rearrange("p (c n) -> p c n", c=8)
                jo = (j % 4) * 128
                pg = kb.ps(); pu = kb.ps()
                for kc in range(8):
                    kb.op(pe, lambda: nc.tensor.matmul(pg.t[:, 0:T], lhsT=wg[:, kc, jo:jo + 128], rhs=hin.t[:, kc, 0:T], start=(kc == 0), stop=(kc == 7)), [slg, hin], [pg], acc=True)
                for kc in range(8):
                    kb.op(pe, lambda: nc.tensor.matmul(pu.t[:, 0:T], lhsT=wu[:, kc, jo:jo + 128], rhs=hin.t[:, kc, 0:T], start=(kc == 0), stop=(kc == 7)), [slu, hin], [pu], acc=True)
                s = sg[j % 2]
                kb.op(act, lambda: nc.scalar.activation(out=s.t[:, 0:T], in_=pg.t[:, 0:T], func=AF.Silu), [pg], [s])
                kb.op(dve, lambda: nc.vector.tensor_tensor(out=actb[j].t[:, 0:T], in0=s.t[:, 0:T], in1=pu.t[:, 0:T], op=ALU.mult), [s, pu], [actb[j]])
            pbs = [kb.ps_hold() for _ in range(8)]
            for jb in range(NJB):
                nj_ = min(4, NJ - jb * 4)
                sl = wsm.get(2 * NJB + jb)
                wv = sl.t[:, 0:nj_ * 1024].rearrange("p (j n) -> p j n", j=nj_)
                for m in range(8):
                    for jj in range(nj_):
                        j = jb * 4 + jj
                        kb.op(pe, lambda: nc.tensor.matmul(pbs[m].t[:, 0:T], lhsT=wv[:, jj, m * 128:(m + 1) * 128], rhs=actb[j].t[:, 0:T], start=(j == 0), stop=(j == NJ - 1)), [sl, actb[j]], [pbs[m]], acc=True)
            for m in range(8):
                if m % 2 == 0:
                    kb.op(act, lambda: nc.scalar.copy(out=f_fm.t[:, m, 0:T], in_=pbs[m].t[:, 0:T]), [pbs[m]], [f_fm])
                else:
                    kb.op(dve, lambda: nc.vector.tensor_copy(out=f_fm.t[:, m, 0:T], in_=pbs[m].t[:, 0:T]), [pbs[m]], [f_fm])
                kb.ps_release(pbs[m])
            post(i, sub, kind, f_fm, f_fm.t[:, :, :])
            kb.arena_close()

        def ssd(kind, last):
            T, NCH, nseq, tps = geom(kind)
            P = (kind == "P")
            norm_mod(0, 1, kind)
            inw = ssm_in_w.rearrange("(c p) n -> p c n", p=128)
            outw = ssm_out_w.rearrange("(c p) n -> p c n", p=128)
            loads = []
            for q in range(6):
                loads.append(lambda t, q=q: [(t[:, :].rearrange("p (c n) -> p c n", c=8), inw[:, :, 2048 + q * 512:2048 + (q + 1) * 512])])
            for zb in range(4):
                loads.append(lambda t, zb=zb: [(t[:, :].rearrange("p (c n) -> p c n", c=8), inw[:, :, zb * 512:(zb + 1) * 512])])
            for mm in range(4):
                loads.append(lambda t, mm=mm: [(t[:, :].rearrange("p (c n) -> p c n", c=16), outw[:, :, mm * 256:(mm + 1) * 256])])
            wsm = WStream(kb, loads)
            tri_o, stri_o = (C_TRIP, C_STRIP) if P else (C_TRIS, C_STRIS)
            tri = lambda: cstb.t[:, tri_o:tri_o + 128]
            stri = lambda: cstb.t[:, stri_o:stri_o + 128]

            kb.arena_open()
            xsT = kb.abuf("xsT", [128, 16, T], BF16)
            if P:
                tails = tailP_cc
            BT = kb.abuf("BT", [128, 4, T], BF16)
            CT = kb.abuf("CT", [128, 4, T], BF16)
            raw = [kb.abuf(f"raw{k}", [128, nseq, 3 + tps], F32) for k in range(2)]
            cacc = [kb.abuf(f"cacc{k}", [128, nseq, tps], F32) for k in range(3)]
            xs_tok = kb.abuf("xs_tok", [128, NCH * DIN], BF16)
            xsv = xs_tok.t[:, :].rearrange("p (c n) -> p c n", c=NCH)
            tail = tailP if P else kb.abuf("tailS", [128, 24, 48], F32)
            if not P:
                tails = [tail] * 24
            B_tok = kb.abuf("B_tok", [128, NCH, 512], BF16)
            sz = kb.abuf("sz", [128, NCH, DIN], BF16)
            ynT = xsT
            dA = kb.abuf("dA", [128, NCH, 32], F32)
            dtv = kb.abuf("dtv", [128, NCH, 32], F32)
            sp1 = kb.abuf("sp1", [128, 32], F32)
            sp2 = kb.abuf("sp2", [128, 32], F32)

            if not P:
                hist = kb.abuf("hist", [48, 3072], F32)
                kb.dma(sp, hist.t[:], st_conv, writes=[hist], dsem=msem())
                for cc in range(24):
                    pb = kb.ps()
                    kb.op(pe, lambda: nc.tensor.transpose(out=pb.t[:, 0:48], in_=hist.t[:, cc * 128:(cc + 1) * 128], identity=cstb.t[0:48, C_ID:C_ID + 48]), [hist, cstb], [pb])
                    kb.op(dve, lambda: nc.vector.tensor_copy(out=tail.t[:, cc, :], in_=pb.t[:, 0:48]), [pb], [tail])

            for ch in range(NCH):
                pb = kb.ps()
                for kc in range(8):
                    kb.op(pe, lambda: nc.tensor.matmul(pb.t[:, 0:32], lhsT=hin.t[:, kc, ch * 128:(ch + 1) * 128], rhs=wdt.t[:, kc, :], start=(kc == 0), stop=(kc == 7)), [hin, wdt], [pb], acc=True)
                kb.op(dve, lambda: nc.vector.tensor_tensor(out=sp1.t[:], in0=pb.t[:, 0:32], in1=vec32.t[:, 0, :], op=ALU.add), [pb, vec32], [sp1])
                kb.op(dve, lambda: nc.vector.tensor_scalar(out=sp2.t[:], in0=sp1.t[:], scalar1=-1.0, scalar2=None, op0=ALU.mult), [sp1], [sp2])
                kb.op(dve, lambda: nc.vector.tensor_tensor(out=sp2.t[:], in0=sp2.t[:], in1=sp1.t[:], op=ALU.max), [sp1, sp2], [sp2])
                kb.op(act, lambda: nc.scalar.activation(out=sp2.t[:], in_=sp2.t[:], func=AF.Exp, scale=-1.0), [sp2], [sp2])
                kb.op(act, lambda: nc.scalar.activation(out=sp2.t[:], in_=sp2.t[:], func=AF.Ln, bias=epsb.t[:, 1:2], scale=1.0), [sp2, epsb], [sp2])
                kb.op(dve, lambda: nc.vector.tensor_scalar(out=sp1.t[:], in0=sp1.t[:], scalar1=0.0, scalar2=None, op0=ALU.max), [sp1], [sp1])
                kb.op(dve, lambda: nc.vector.tensor_tensor(out=dtv.t[:, ch, :], in0=sp1.t[:], in1=sp2.t[:], op=ALU.add), [sp1, sp2], [dtv])
                kb.op(dve, lambda: nc.vector.tensor_tensor(out=dA.t[:, ch, :], in0=dtv.t[:, ch, :], in1=vec32.t[:, 1, :], op=ALU.mult), [dtv, vec32], [dA])

            def conv_s1(cc):
                sl = wsm.get(cc // 4)
                wv = sl.t[:, :].rearrange("p (c n) -> p c n", c=8)
                pb = kb.ps()
                for kc in range(8):
                    kb.op(pe, lambda: nc.tensor.matmul(pb.t[:, 0:T], lhsT=wv[:, kc, (cc % 4) * 128:(cc % 4 + 1) * 128], rhs=hin.t[:, kc, 0:T], start=(kc == 0), stop=(kc == 7)), [sl, hin], [pb], acc=True)
                rw = raw[cc % 2]; ca = cacc[cc % 3]
                tl = tails[cc]
                kb.op(act, lambda: nc.scalar.copy(out=rw.t[:, :, 0:3], in_=tail.t[:, cc, :].rearrange("p (s j) -> p s j", j=3)), [tl], [rw])
                kb.op(act, lambda: nc.scalar.copy(out=rw.t[:, :, 3:3 + tps], in_=pb.t[:, 0:T].rearrange("p (s t) -> p s t", s=nseq)), [pb], [rw])
                kb.op(act, lambda: nc.scalar.copy(out=tail.t[:, cc, :].rearrange("p (s j) -> p s j", j=3), in_=rw.t[:, :, tps:tps + 3]), [rw], [tl])
                kb.op(act, lambda: nc.scalar.activation(out=ca.t[:], in_=rw.t[:, :, 0:tps], func=AF.Identity, bias=convb.t[:, cc:cc + 1], scale=convw.t[:, cc, 0:1]), [rw, convw, convb], [ca])
                for k in range(1, 4):
                    kb.op(dve, lambda: nc.vector.scalar_tensor_tensor(out=ca.t[:], in0=rw.t[:, :, k:k + tps], scalar=convw.t[:, cc, k:k + 1], in1=ca.t[:], op0=ALU.mult, op1=ALU.add), [rw, convw, ca], [ca])

            def conv_s2(cc):
                ca = cacc[cc % 3]
                if cc < 16:
                    dstb, dst = xsT, xsT.t[:, cc, :]
                elif cc < 20:
                    dstb, dst = BT, BT.t[:, cc - 16, :]
                else:
                    dstb, dst = CT, CT.t[:, cc - 20, :]
                kb.op(act, lambda: nc.scalar.activation(out=dst.rearrange("p (s t) -> p s t", s=nseq), in_=ca.t[:], func=AF.Silu), [ca], [dstb])

            for cc in range(24):
                conv_s1(cc)
                if cc >= 1:
                    conv_s2(cc - 1)
            conv_s2(23)

            if last and P:
                with nc.allow_non_contiguous_dma(reason="small state out"):
                    for j3 in range(3):
                        out_dma(sp, conv_p[j3].rearrange("(c p) -> p c", p=128), tail.t[:, :, j3], tails)
            if last and not P:
                nr = nseq * 3
                cso = hist
                for cc in range(24):
                    pb = kb.ps()
                    kb.op(pe, lambda: nc.tensor.transpose(out=pb.t[0:nr, 0:128], in_=tail.t[:, cc, :], identity=ident()), [tail, cstb], [pb])
                    kb.op(dve, lambda: nc.vector.tensor_copy(out=cso.t[:, cc * 128:(cc + 1) * 128], in_=pb.t[0:nr, 0:128]), [pb], [cso])
                out_dma(sp, conv_p if P else conv_s, cso.t[:], [cso])

            for ch in range(NCH):
                for q in range(4):
                    pb = kb.ps()
                    pbv = pb.t[:].bitcast(BF16)
                    for k in range(4):
                        cc = q * 4 + k
                        kb.op(pe, lambda: nc.tensor.transpose(out=pbv[:, k * 128:(k + 1) * 128], in_=xsT.t[:, cc, ch * 128:(ch + 1) * 128], identity=ident_bf()), [xsT, cbf], [pb], acc=True)
                    kb.op(act, lambda: nc.scalar.copy(out=xsv[:, ch, q * 512:(q + 1) * 512], in_=pbv[:, 0:512]), [pb], [xs_tok])
                pb = kb.ps()
                pbv = pb.t[:].bitcast(BF16)
                for g in range(4):
                    kb.op(pe, lambda: nc.tensor.transpose(out=pbv[:, g * 128:(g + 1) * 128], in_=BT.t[:, g, ch * 128:(ch + 1) * 128], identity=ident_bf()), [BT, cbf], [pb], acc=True)
                kb.op(act, lambda: nc.scalar.copy(out=B_tok.t[:, ch, :], in_=pbv[:, 0:512]), [pb], [B_tok])

            for zb in range(4):
                sl = wsm.get(6 + zb)
                wv = sl.t[:, :].rearrange("p (c n) -> p c n", c=8)
                for ch in range(NCH):
                    pb = kb.ps()
                    for kc in range(8):
                        kb.op(pe, lambda: nc.tensor.matmul(pb.t[:, :], lhsT=hin.t[:, kc, ch * 128:(ch + 1) * 128], rhs=wv[:, kc, :], start=(kc == 0), stop=(kc == 7)), [sl, hin], [pb], acc=True)
                    kb.op(act, lambda: nc.scalar.activation(out=sz.t[:, ch, zb * 512:(zb + 1) * 512], in_=pb.t[:, :], func=AF.Silu), [pb], [sz])

            R1q = [kb.abuf(f"R1q{k}", [128, 8, 128], F32) for k in range(2)]
            Lsb = [kb.abuf(f"Lsb{k}", [128, 512], F32) for k in range(2)]
            wT = kb.abuf("wT", [128, 32, 128], BF16)
            cbm = kb.abuf("cbm", [128, 4, 128], F32)
            xdt = kb.abuf("xdt", [128, DIN], BF16)
            xdec = xdt if P else kb.abuf("xdec", [128, DIN], BF16)
            sm = kb.abuf("sm", [128, 4, 32], F32)
            ygb = [kb.abuf(f"yg{k}", [128, 512], F32) for k in range(3)]
            ynb = [kb.abuf(f"yn{k}", [128, 512], BF16) for k in range(2)]
            ssqg = [kb.abuf(f"ssq{k}", [128, 2], F32) for k in range(4)]
            junk = kb.abuf("junk", [128, 512], BF16)
            v32 = lambda ap: ap.rearrange("p (h d) -> p h d", h=32)
            if not P:
                dAexp = wT
                dAv = wT.t[:, :, :].rearrange("p h l -> p (h l)").bitcast(F32)
                decS = kb.abuf("decS", [128, 16, 16], F32)
                CTmj = [kb.abuf(f"CTmj{k}", [128, 4, 128], BF16) for k in range(2)]
                h0b = [kb.abuf(f"h0b{k}", [128, 16, 128], BF16) for k in range(2)]
                h0f = kb.abuf("h0f", [128, 16, 128], F32)
                hnv = hT.t[:, :].rearrange("p (c n) -> p c n", c=16)
                Bm = [kb.abuf(f"Bm{k}", [128, 512], BF16) for k in range(2)]
                mJL = kb.abuf("mJL", [128, 16, 128], BF16)
                kb.op(dve, lambda: nc.vector.memset(mJL.t[:], 0.0), [], [mJL])
                for j in range(16):
                    kb.op(dve, lambda: nc.vector.memset(mJL.t[:, j, j * 8:(j + 1) * 8], 1.0), [mJL], [mJL])

            for ch in range(NCH):
                csl = slice(ch * 128, (ch + 1) * 128)
                pv = kb.ps()
                kb.op(pe, lambda: nc.tensor.matmul(pv.t[:, 0:32], lhsT=tri(), rhs=dA.t[:, ch, :], start=True, stop=True), [dA, cstb], [pv])
                kb.op(pe, lambda: nc.tensor.matmul(pv.t[:, 32:64], lhsT=stri(), rhs=dA.t[:, ch, :], start=True, stop=True), [dA, cstb], [pv], acc=True)
                kb.op(pe, lambda: nc.tensor.matmul(pv.t[:, 64:96], lhsT=ones_f(), rhs=dA.t[:, ch, :], start=True, stop=True), [dA, cstb], [pv], acc=True)
                kb.op(act, lambda: nc.scalar.activation(out=sm.t[:, 0:3, :], in_=pv.t[:, 0:96].rearrange("p (a h) -> p a h", a=3), func=AF.Exp), [pv], [sm])
                pc = kb.ps()
                for g in range(4):
                    kb.op(pe, lambda: nc.tensor.matmul(pc.t[:, g * 128:(g + 1) * 128], lhsT=BT.t[:, g, csl], rhs=CT.t[:, g, csl], start=True, stop=True), [BT, CT], [pc], acc=True)
                kb.op(dve, lambda: nc.vector.tensor_tensor(out=cbm.t[:], in0=pc.t[:, :].rearrange("p (g l) -> p g l", g=4), in1=tri().unsqueeze(1).to_broadcast([128, 4, 128]), op=ALU.mult), [pc, cstb], [cbm])
                kb.op(pool, lambda: nc.gpsimd.tensor_tensor(out=v32(xdt.t[:]), in0=v32(xsv[:, ch, :]), in1=dtv.t[:, ch, :].unsqueeze(2).to_broadcast([128, 32, 64]), op=ALU.mult), [xs_tok, dtv], [xdt])
                if not P:
                    kb.op(dve, lambda: nc.vector.tensor_tensor(out=v32(xdec.t[:]), in0=v32(xdt.t[:]), in1=sm.t[:, 1, :].unsqueeze(2).to_broadcast([128, 32, 64]), op=ALU.mult), [xdt, sm], [xdec])
                    kb.op(dve, lambda: nc.vector.tensor_copy(out=v32(dAv), in_=dA.t[:, 0, :].unsqueeze(2).to_broadcast([128, 32, 64])), [dA], [dAexp])
                    pd = kb.ps()
                    for c in range(16):
                        kb.op(pe, lambda: nc.tensor.matmul(pd.t[:, c * 16:(c + 1) * 16], lhsT=dAv[:, c * 128:(c + 1) * 128], rhs=cstb.t[:, C_SEL:C_SEL + 16], start=True, stop=True), [dAexp, cstb], [pd], acc=True)
                    kb.op(act, lambda: nc.scalar.activation(out=decS.t[:], in_=pd.t[:, 0:256].rearrange("p (c j) -> p c j", c=16), func=AF.Exp), [pd], [decS])
                yo = []
                if not P:
                    yo = [kb.ps_hold() for g in range(4)]
                    for j in range(NSEQ_S):
                        hb = h0b[j % 2]; hf = h0f; ht = hT_bf; hn = hT; bm = Bm[j % 2]; CTm = CTmj[j % 2]
                        kb.op(dve, lambda: nc.vector.tensor_tensor(out=CTm.t[:], in0=CT.t[:, :, :], in1=mJL.t[:, j, :].unsqueeze(1).to_broadcast([128, 4, 128]), op=ALU.mult), [CT, mJL], [CTm])
                        kb.dma(pool, hb.t[:], st_ssm[j].rearrange("(c p) n -> p c n", p=128), writes=[hb], dsem=msem())
                        kb.dma(sp, hf.t[:], st_ssm[j].rearrange("(c p) n -> p c n", p=128), writes=[hf], dsem=msem())
                        for q in range(4):
                            pb = kb.ps()
                            pbv = pb.t[:].bitcast(BF16)
                            for k in range(4):
                                c = q * 4 + k
                                kb.op(pe, lambda: nc.tensor.transpose(out=pbv[:, k * 128:(k + 1) * 128], in_=hb.t[:, c, :], identity=ident_bf()), [hb, cbf], [pb], acc=True)
                            kb.op(act, lambda: nc.scalar.copy(out=ht.t[:, q * 512:(q + 1) * 512], in_=pbv[:, 0:512]), [pb], [ht])
                        for g in range(4):
                            kb.op(pe, lambda: nc.tensor.matmul(yo[g].t[:, :], lhsT=CTm.t[:, g, :], rhs=ht.t[:, g * 512:(g + 1) * 512], start=(j == 0), stop=(j == NSEQ_S - 1)), [CTm, ht], [yo[g]], acc=True)
                        kb.op(dve, lambda: nc.vector.tensor_scalar(out=bm.t[:], in0=B_tok.t[:, 0, :], scalar1=cstb.t[:, C_SEL + j:C_SEL + j + 1], scalar2=None, op0=ALU.mult), [B_tok, cstb], [bm])
                        for q in range(4):
                            pb = kb.ps()
                            for k in range(4):
                                c = q * 4 + k
                                kb.op(pe, lambda: nc.tensor.matmul(pb.t[:, k * 128:(k + 1) * 128], lhsT=xdec.t[:, c * 128:(c + 1) * 128], rhs=bm.t[:, (c // 4) * 128:(c // 4 + 1) * 128], start=True, stop=True), [xdec, bm], [pb], acc=True)
                            for k in range(4):
                                c = q * 4 + k
                                kb.op(dve, lambda: nc.vector.scalar_tensor_tensor(out=hnv[:, c, :], in0=hf.t[:, c, :], scalar=decS.t[:, c, j:j + 1], in1=pb.t[:, k * 128:(k + 1) * 128], op0=ALU.mult, op1=ALU.add), [hf, decS, pb], [hn])
                        out_dma(sp, ssm_s[j].rearrange("(c p) n -> p c n", p=128), hnv, [hn])
                for g in range(4):
                    kb.op(dve, lambda: nc.vector.memset(ssqg[g].t[:], 0.0), [ssqg[g]], [ssqg[g]])

                def buildR1(g):
                    R1 = R1q[g % 2]
                    kb.op(pool, lambda: nc.gpsimd.tensor_tensor(out=R1.t[:], in0=dA.t[:, ch, g * 8:(g + 1) * 8].unsqueeze(2).to_broadcast([128, 8, 128]),
                                                                in1=tri().unsqueeze(1).to_broadcast([128, 8, 128]), op=ALU.mult), [dA, cstb], [R1])
                buildR1(0)
                for g in range(4):
                    R1 = R1q[g % 2]
                    pbs = []
                    for b in range(2):
                        pb = kb.ps()
                        kb.op(pe, lambda: nc.tensor.matmul(pb.t[:, :], lhsT=stri(), rhs=R1.t[:, :, :].rearrange("p h l -> p (h l)")[:, b * 512:(b + 1) * 512], start=True, stop=True), [R1, cstb], [pb])
                        pbs.append(pb)
                    if g + 1 < 4:
                        buildR1(g + 1)
                    for b in range(2):
                        pb = pbs[b]
                        L = Lsb[b]
                        kb.op(act, lambda: nc.scalar.activation(out=L.t[:], in_=pb.t[:, :], func=AF.Exp), [pb], [L])
                        h0_ = g * 8 + b * 4
                        kb.op(dve, lambda: nc.vector.tensor_tensor(out=wT.t[:, h0_:h0_ + 4, :], in0=L.t[:].rearrange("p (h l) -> p h l", h=4),
                                                                   in1=cbm.t[:, g, :].unsqueeze(1).to_broadcast([128, 4, 128]), op=ALU.mult), [L, cbm], [wT])
                v3 = lambda ap: ap.rearrange("p (h d) -> p h d", h=8)

                def emitY(g):
                    if P:
                        yo_g = kb.ps()
                        kb.op(pe, lambda: nc.tensor.matmul(yo_g.t[:, :], lhsT=CT.t[:, g, csl], rhs=hT_bf.t[:, g * 512:(g + 1) * 512], start=True, stop=True), [CT, hT_bf], [yo_g])
                    else:
                        yo_g = yo[g]
                    pb = kb.ps()
                    for r in range(8):
                        h = g * 8 + r
                        kb.op(pe, lambda: nc.tensor.matmul(pb.t[:, r * 64:(r + 1) * 64], lhsT=wT.t[:, h, :], rhs=xdt.t[:, h * 64:(h + 1) * 64], start=True, stop=True), [wT, xdt], [pb], acc=True)
                    return yo_g, pb

                def combine(g, yo_g, pb):
                    gs = slice(g * 512, (g + 1) * 512)
                    eab = sm.t[:, 0, g * 8:(g + 1) * 8].unsqueeze(2).to_broadcast([128, 8, 64])
                    Db = vec32.t[:, 2, g * 8:(g + 1) * 8].unsqueeze(2).to_broadcast([128, 8, 64])
                    t1 = tmp(); t2 = tmp()
                    yg = ygb[g % 3]
                    kb.op(dve, lambda: nc.vector.tensor_tensor(out=v3(t1.t[:]), in0=v3(yo_g.t[:, :]), in1=eab, op=ALU.mult), [yo_g, sm], [t1])
                    if not P:
                        kb.ps_release(yo_g)
                    kb.op(dve, lambda: nc.vector.tensor_tensor(out=t1.t[:], in0=t1.t[:], in1=pb.t[:, :], op=ALU.add), [t1, pb], [t1])
                    kb.op(pool, lambda: nc.gpsimd.tensor_tensor(out=v3(t2.t[:]), in0=v3(xsv[:, ch, gs]), in1=Db, op=ALU.mult), [xs_tok, vec32], [t2])
                    kb.op(dve, lambda: nc.vector.tensor_tensor(out=t1.t[:], in0=t1.t[:], in1=t2.t[:], op=ALU.add), [t1, t2], [t1])
                    kb.op(dve, lambda: nc.vector.tensor_tensor(out=yg.t[:], in0=t1.t[:], in1=sz.t[:, ch, gs], op=ALU.mult), [t1, sz], [yg])
                    sq_ = ssqg[g]
                    kb.op(act, lambda: nc.scalar.activation(out=junk.t[:], in_=yg.t[:], func=AF.Square, accum_out=sq_.t[:, 0:1]), [yg, sq_], [junk, sq_])
                    kb.op(act, lambda: nc.scalar.activation(out=sq_.t[:, 1:2], in_=sq_.t[:, 0:1], func=AF.Ln, bias=epsb.t[:, 0:1], scale=1.0 / 512), [sq_, epsb], [sq_])
                    kb.op(act, lambda: nc.scalar.activation(out=sq_.t[:, 1:2], in_=sq_.t[:, 1:2], func=AF.Exp, scale=-0.5), [sq_], [sq_])
                    yn = ynb[g % 2]
                    kb.op(act, lambda: nc.scalar.activation(out=yn.t[:], in_=yg.t[:], func=AF.Identity, scale=sq_.t[:, 1:2]), [yg, sq_], [yn])

                def finish(g):
                    yn = ynb[g % 2]
                    pq = kb.ps()
                    pqv = pq.t[:].bitcast(BF16)
                    for k in range(4):
                        kb.op(pe, lambda: nc.tensor.transpose(out=pqv[:, k * 128:(k + 1) * 128], in_=yn.t[:, k * 128:(k + 1) * 128], identity=ident_bf()), [yn, cbf], [pq], acc=True)
                    kb.op(dve, lambda: nc.vector.tensor_tensor(out=ynT.t[:, g * 4:(g + 1) * 4, csl], in0=pqv[:, 0:512].rearrange("p (k t) -> p k t", k=4),
                                                               in1=normwT.t[:, g * 4:(g + 1) * 4].unsqueeze(2).to_broadcast([128, 4, 128]), op=ALU.mult), [pq, normwT], [ynT])

                Y = {0: emitY(0), 1: emitY(1)}
                combine(0, *Y[0])
                for g in range(1, 4):
                    if g + 1 < 4:
                        Y[g + 1] = emitY(g + 1)
                    combine(g, *Y[g])
                    finish(g - 1)
                finish(3)

                if P:
                    kb.op(pool, lambda: nc.gpsimd.tensor_tensor(out=v32(xdt.t[:]), in0=v32(xdt.t[:]), in1=sm.t[:, 1, :].unsqueeze(2).to_broadcast([128, 32, 64]), op=ALU.mult), [xdt, sm], [xdt])
                    for g in range(4):
                        gs = slice(g * 512, (g + 1) * 512)
                        pb = kb.ps()
                        kb.op(pe, lambda: nc.tensor.matmul(pb.t[:, :], lhsT=B_tok.t[:, ch, g * 128:(g + 1) * 128], rhs=xdt.t[:, gs], start=True, stop=True), [B_tok, xdt], [pb])
                        v3 = lambda ap: ap.rearrange("p (h d) -> p h d", h=8)
                        kb.op(pool, lambda: nc.gpsimd.tensor_tensor(out=v3(hT.t[:, gs]), in0=v3(hT.t[:, gs]), in1=sm.t[:, 2, g * 8:(g + 1) * 8].unsqueeze(2).to_broadcast([128, 8, 64]), op=ALU.mult), [hT, sm], [hT])
                        kb.op(dve, lambda: nc.vector.tensor_tensor(out=hT.t[:, gs], in0=hT.t[:, gs], in1=pb.t[:, :], op=ALU.add), [hT, pb], [hT])
                    kb.op(act, lambda: nc.scalar.copy(out=hT_bf.t[:], in_=hT.t[:]), [hT], [hT_bf])

            if P and last:
                houtv = sz.t[:, :, :].rearrange("p c n -> p (c n)").bitcast(F32)[:, 0:2048].rearrange("p (c n) -> p c n", c=16)
                hout = sz
                for c in range(16):
                    pb = kb.ps()
                    kb.op(pe, lambda: nc.tensor.transpose(out=pb.t[:, 0:128], in_=hT.t[:, c * 128:(c + 1) * 128], identity=ident()), [hT, cstb], [pb])
                    kb.op(dve, lambda: nc.vector.tensor_copy(out=houtv[:, c, :], in_=pb.t[:, 0:128]), [pb], [hout])
                out_dma(sp, ssm_p.rearrange("(c p) n -> p c n", p=128), houtv, [hout])

            fv = xs_tok.t[:, :].bitcast(F32).rearrange("p (m t) -> p m t", m=8)
            for m in range(8):
                sl = wsm.get(10 + m // 2)
                wv = sl.t[:, :].rearrange("p (c n) -> p c n", c=16)
                pb = kb.ps()
                for kc in range(16):
                    kb.op(pe, lambda: nc.tensor.matmul(pb.t[:, 0:T], lhsT=wv[:, kc, (m % 2) * 128:(m % 2 + 1) * 128], rhs=ynT.t[:, kc, 0:T], start=(kc == 0), stop=(kc == 15)), [sl, ynT], [pb], acc=True)
                kb.op(act, lambda: nc.scalar.copy(out=fv[:, m, :], in_=pb.t[:, 0:T]), [pb], [xs_tok])
            post(0, 1, kind, xs_tok, fv)
            kb.arena_close()

        def swa(kind, first, last):
            T, NCH, nseq, tps = geom(kind)
            P = (kind == "P")
            norm_mod(1, 1, kind)
            qw = qkv_w.rearrange("(c p) n -> p c n", p=128)
            ow = o_w.rearrange("(c p) n -> p c n", p=128)
            loads = []
            for half in range(2):
                loads.append(lambda t, half=half: [(t[:, :].rearrange("p (c n) -> p c n", c=8), qw[:, :, half * 512:(half + 1) * 512])])
            loads.append(lambda t: [(t[:, :].rearrange("p (c n) -> p c n", c=8), qw[:, :, 1024:1536])])
            for mm in range(2):
                loads.append(lambda t, mm=mm: [(t[:, :].rearrange("p (c n) -> p c n", c=8), ow[:, :, mm * 512:(mm + 1) * 512])])
            wsm = WStream(kb, loads)

            kb.arena_open()
            qT = kb.abuf("qT", [128, 8, T], BF16)
            kv_tok = kb.abuf("kv_tok", [128, NCH, 512], F32)
            biasT = kb.abuf("biasT", [128, 16, 256], BF16)
            tq = [kb.abuf(f"tq{k}", [128, 256], F32) for k in range(2)]
            eS = [kb.abuf(f"eS{k}", [128, 256], BF16) for k in range(3)]
            en = [kb.abuf(f"en{k}", [128, 256], BF16) for k in range(3)]
            pT = [kb.abuf(f"pT{k}", [128, 2, 128], BF16) for k in range(3)]
            st = [kb.abuf(f"st{k}", [128, 8], F32) for k in range(3)]
            o_tok = kb.abuf("o_tok", [128, D], BF16)
            attnT = kb.abuf("attnT", [128, 8, T], BF16)
            f_fm = kb.abuf("f_fm", [128, 8, T], F32)

            with nc.allow_non_contiguous_dma(reason="toeplitz"):
                for h in range(16):
                    t = tq[h % 2]
                    src = bass.AP(uscr, h * 384, [[1, 128], [1, 256]])
                    kb.dma(sp, t.t[:], src, reads=[uev], writes=[t], dsem=msem())
                    pb = kb.ps()
                    jo = C_J if P else C_JREP
                    kb.op(pe, lambda: nc.tensor.matmul(pb.t[:, 0:256], lhsT=cstb.t[:, jo:jo + 128], rhs=t.t[:], start=True, stop=True), [t, cstb], [pb])
                    kb.op(dve, lambda: nc.vector.tensor_copy(out=biasT.t[:, h, :], in_=pb.t[:, 0:256]), [pb], [biasT])

            if flags.get("swa_stop") == 1:
                kb.arena_close(); return
            wqp = [kb.abuf(f"wqp{k}", [128, 8, 4, 2, 64], BF16) for k in range(2)]
            for half in range(2):
                sl = wsm.get(half)
                nat = sl.t[:, :].rearrange("p (c n) -> p c n", c=8)
                for a in range(4):
                    for b in range(2):
                        kb.op(dve, lambda: nc.vector.tensor_copy(out=wqp[half].t[:, :, a, b, :], in_=nat[:, :, b * 256 + a * 64:b * 256 + (a + 1) * 64]), [sl], [wqp[half]])
            for c in range(8):
                wb = wqp[c // 4]
                wv = wb.t[:, :, :, :, :].rearrange("p c a b d -> p c a (b d)")
                pb = kb.ps()
                for kc in range(8):
                    kb.op(pe, lambda: nc.tensor.matmul(pb.t[:, 0:T], lhsT=wv[:, kc, c % 4, :], rhs=hin.t[:, kc, 0:T], start=(kc == 0), stop=(kc == 7)), [wb, hin], [pb], acc=True)
                kb.op(act, lambda: nc.scalar.activation(out=qT.t[:, c, :], in_=pb.t[:, 0:T], func=AF.Identity, bias=qkb.t[:, c:c + 1], scale=0.125), [pb, qkb], [qT])
            sl = wsm.get(2)
            wv = sl.t[:, :].rearrange("p (c n) -> p c n", c=8)
            for c2 in range(2):
                pb = kb.ps()
                for kc in range(8):
                    kb.op(pe, lambda: nc.tensor.matmul(pb.t[:, 0:T], lhsT=wv[:, kc, c2 * 128:(c2 + 1) * 128], rhs=hin.t[:, kc, 0:T], start=(kc == 0), stop=(kc == 7)), [sl, hin], [pb], acc=True)
                kb.op(act, lambda: nc.scalar.activation(out=kT.t[:, c2, 128:128 + T], in_=pb.t[:, 0:T], func=AF.Identity, bias=qkb.t[:, 8 + c2:9 + c2], scale=1.0), [pb, qkb], [kT])
            for ch in range(NCH):
                pb = kb.ps()
                for kc in range(8):
                    kb.op(pe, lambda: nc.tensor.matmul(pb.t[:, :], lhsT=hin.t[:, kc, ch * 128:(ch + 1) * 128], rhs=wv[:, kc, :], start=(kc == 0), stop=(kc == 7)), [sl, hin], [pb], acc=True)
                kb.op(dve, lambda: nc.vector.tensor_tensor(out=kv_tok.t[:, ch, :], in0=pb.t[:, :], in1=kvb.t[:, :], op=ALU.add), [pb, kvb], [kv_tok])
                kb.op(act, lambda: nc.scalar.copy(out=vtok.t[:, 1 + ch, :], in_=kv_tok.t[:, ch, 256:512]), [kv_tok], [vtok])

            if flags.get("swa_stop") == 2:
                kb.arena_close(); return

            LOOK = 3

            def softmax_s1(idx, h, lg, nk, bias_ap):
                k3 = idx % 3
                s_ = lg; e_ = eS[k3]; t_ = st[k3]
                kb.op(dve, lambda: nc.vector.memset(t_.t[:], 0.0), [t_], [t_])
                kb.op(dve, lambda: nc.vector.tensor_reduce(out=t_.t[:, 0:1], in_=s_.t[:, 0:nk], axis=AX.X, op=ALU.max), [s_], [t_])
                kb.op(dve, lambda: nc.vector.tensor_scalar(out=t_.t[:, 1:2], in0=t_.t[:, 0:1], scalar1=sinkb.t[:, h:h + 1], scalar2=-1.0, op0=ALU.max, op1=ALU.mult), [t_, sinkb], [t_])
                kb.op(act, lambda: nc.scalar.activation(out=e_.t[:, 0:nk], in_=s_.t[:, 0:nk], func=AF.Exp, bias=t_.t[:, 1:2], scale=1.0, accum_out=t_.t[:, 2:3]), [s_, t_], [e_, t_])
                kb.op(act, lambda: nc.scalar.activation(out=t_.t[:, 3:4], in_=sinkb.t[:, h:h + 1], func=AF.Exp, bias=t_.t[:, 1:2], scale=1.0), [sinkb, t_], [t_])

            def softmax_s2(idx, nk, selcol):
                k3 = idx % 3
                e_ = eS[k3]; n_ = en[k3]; t_ = st[k3]
                kb.op(dve, lambda: nc.vector.tensor_tensor(out=t_.t[:, 4:5], in0=t_.t[:, 2:3], in1=t_.t[:, 3:4], op=ALU.add), [t_], [t_])
                kb.op(dve, lambda: nc.vector.reciprocal(out=t_.t[:, 5:6], in_=t_.t[:, 4:5]), [t_], [t_])
                if selcol is not None:
                    kb.op(dve, lambda: nc.vector.tensor_tensor(out=t_.t[:, 5:6], in0=t_.t[:, 5:6], in1=selcol, op=ALU.mult), [t_, cstb], [t_])
                kb.op(pool, lambda: nc.gpsimd.tensor_tensor(out=n_.t[:, 0:nk], in0=e_.t[:, 0:nk], in1=t_.t[:, 5:6].to_broadcast([128, nk]), op=ALU.mult), [e_, t_], [n_])
                return n_

            def head_rows(h):
                g = h // 4
                if h < 8:
                    c = h % 4; half = h // 4
                else:
                    c = 4 + (h - 8) % 4; half = (h - 8) // 4
                return c, half, g

            def o_finish(po, csl):
                for k in range(2):
                    kb.op(act, lambda: nc.scalar.copy(out=o_tok.t[:, k * 512:(k + 1) * 512], in_=po[k].t[:, :]), [po[k]], [o_tok])
                    kb.ps_release(po[k])
                for q in range(2):
                    pb = kb.ps()
                    pbv = pb.t[:].bitcast(BF16)
                    for k in range(4):
                        c = q * 4 + k
                        kb.op(pe, lambda: nc.tensor.transpose(out=pbv[:, k * 128:(k + 1) * 128], in_=o_tok.t[:, c * 128:(c + 1) * 128], identity=ident_bf()), [o_tok, cbf], [pb], acc=True)
                    kb.op(act, lambda: nc.scalar.copy(out=attnT.t[:, q * 4:(q + 1) * 4, csl], in_=pbv[:, 0:512].rearrange("p (k t) -> p k t", k=4)), [pb], [attnT])

            if P:
                for ch in range(NCH):
                    csl = slice(ch * 128, (ch + 1) * 128)
                    blk0 = first and ch == 0
                    po = [kb.ps_hold() for _ in range(2)]

                    def logits(h):
                        c, half, g = head_rows(h)
                        rs = slice(half * 64, (half + 1) * 64)
                        lg = kb.ps()
                        if blk0:
                            kb.op(pe, lambda: nc.tensor.matmul(lg.t[:, 0:128], lhsT=qT.t[rs, c, csl], rhs=kT.t[rs, g // 2, 128:256], start=True, stop=False), [qT, kT], [lg])
                            kb.op(pe, lambda: nc.tensor.matmul(lg.t[:, 0:128], lhsT=ident_bf(), rhs=biasT.t[:, h, 128:256], start=False, stop=True), [cbf, biasT], [lg], acc=True)
                            return lg, 128, None
                        kb.op(pe, lambda: nc.tensor.matmul(lg.t[:, 0:256], lhsT=qT.t[rs, c, csl], rhs=kT.t[rs, g // 2, ch * 128:ch * 128 + 256], start=True, stop=False), [qT, kT], [lg])
                        kb.op(pe, lambda: nc.tensor.matmul(lg.t[:, 0:256], lhsT=ident_bf(), rhs=biasT.t[:, h, 0:256], start=False, stop=True), [cbf, biasT], [lg], acc=True)
                        return lg, 256, None

                    def tailp(h, n_, nk):
                        g = h // 4
                        nb = nk // 128
                        pt = kb.ps()
                        ptv = pt.t[:].bitcast(BF16)
                        for b in range(nb):
                            kb.op(pe, lambda: nc.tensor.transpose(out=ptv[:, b * 128:(b + 1) * 128], in_=n_.t[:, b * 128:(b + 1) * 128], identity=ident_bf()), [n_, cbf], [pt], acc=True)
                        p_ = pT[h % 3]
                        kb.op(act, lambda: nc.scalar.copy(out=p_.t[:, 0:nb, :], in_=ptv[:, 0:nb * 128].rearrange("p (b q) -> p b q", b=nb)), [pt], [p_])
                        ob_ = po[h // 8]
                        oc = slice((h % 8) * 64, (h % 8 + 1) * 64)
                        for b in range(nb):
                            vb = (ch + b) if not blk0 else 1
                            kb.op(pe, lambda: nc.tensor.matmul(ob_.t[:, oc], lhsT=p_.t[:, b, :], rhs=vtok.t[:, vb, g * 64:(g + 1) * 64], start=(b == 0), stop=(b == nb - 1)), [p_, vtok], [ob_], acc=True)

                    pend = {}
                    for h in range(LOOK):
                        pend[h] = logits(h)
                    nks = {}
                    for h in range(2):
                        lg, nk, bias_ap = pend.pop(h)
                        nks[h] = nk
                        softmax_s1(h, h, lg, nk, bias_ap)
                    for h in range(16):
                        n_ = softmax_s2(h, nks[h], None)
                        if h + LOOK < 16:
                            pend[h + LOOK] = logits(h + LOOK)
                        if h + 2 < 16:
                            lg, nk, bias_ap = pend.pop(h + 2)
                            nks[h + 2] = nk
                            softmax_s1(h + 2, h + 2, lg, nk, bias_ap)
                        tailp(h, n_, nks[h])
                    o_finish(po, csl)
                if last:
                    out_dma(sp, kp, kv_tok.t[:, NCH - 1, 0:256], [kv_tok])
                    out_dma(sp, vp, kv_tok.t[:, NCH - 1, 256:512], [kv_tok])
                kb.op(dve, lambda: nc.vector.tensor_copy(out=kT.t[:, :, 0:128], in_=kT.t[:, :, T:T + 128]), [kT], [kT])
                kb.op(dve, lambda: nc.vector.tensor_copy(out=vtok.t[:, 0, :], in_=vtok.t[:, NCH, :]), [vtok], [vtok])
            else:
                ckf = [kb.abuf(f"ckf{k}", [128, 256], F32) for k in range(2)]
                kTj = [kb.abuf(f"kTj{k}", [128, 2, 136], BF16) for k in range(2)]
                vj = [kb.abuf(f"vj{k}", [128, 256], BF16) for k in range(2)]
                vnew = kb.abuf("vnew", [8, 16, 256], BF16)
                vtb = kb.abuf("vtb", [128, 256], BF16)
                kb.op(dve, lambda: nc.vector.tensor_copy(out=vtb.t[:], in_=vtok.t[:, 1, :]), [vtok], [vtb])
                for j in range(NSEQ_S):
                    kb.dma(sp, vnew.t[:, j, :], vtb.t[j * 8:(j + 1) * 8, :], reads=[vtb], writes=[vnew], dsem=dmisc[2], join=True)
                po = [kb.ps_hold() for _ in range(2)]

                def prep(j):
                    kf = ckf[j % 2]; ktj = kTj[j % 2]; v_ = vj[j % 2]
                    kb.dma(sp, kf.t[:], ck[j], writes=[kf], dsem=msem())
                    kb.dma(pool, v_.t[:], cv[j], writes=[v_], dsem=msem())
                    out_dma(sp, ks[j, 0:120, :], ck[j, 8:128, :], [])
                    out_dma(sp, vs[j, 0:120, :], cv[j, 8:128, :], [])
                    out_dma(sp, ks[j, 120:128, :], kv_tok.t[j * 8:(j + 1) * 8, 0, 0:256], [kv_tok])
                    out_dma(sp, vs[j, 120:128, :], kv_tok.t[j * 8:(j + 1) * 8, 0, 256:512], [kv_tok])
                    for c2 in range(2):
                        pb = kb.ps()
                        kb.op(pe, lambda: nc.tensor.transpose(out=pb.t[:, 0:128], in_=kf.t[:, c2 * 128:(c2 + 1) * 128], identity=ident()), [kf, cstb], [pb])
                        kb.op(act, lambda: nc.scalar.copy(out=ktj.t[:, c2, 0:128], in_=pb.t[:, 0:128]), [pb], [ktj])
                    kb.op(dve, lambda: nc.vector.tensor_copy(out=ktj.t[:, :, 128:136], in_=kT.t[:, :, 128 + j * 8:128 + (j + 1) * 8]), [kT, ktj], [ktj])

                def logits_s(idx):
                    j, h = divmod(idx, 16)
                    if h == 0:
                        prep(j)
                    ktj = kTj[j % 2]
                    c, half, g = head_rows(h)
                    rs = slice(half * 64, (half + 1) * 64)
                    lg = kb.ps()
                    kb.op(pe, lambda: nc.tensor.matmul(lg.t[:, 0:136], lhsT=qT.t[rs, c, :], rhs=ktj.t[rs, g // 2, :], start=True, stop=False), [qT, ktj], [lg])
                    kb.op(pe, lambda: nc.tensor.matmul(lg.t[:, 0:136], lhsT=ident_bf(), rhs=biasT.t[:, h, 0:136], start=False, stop=True), [cbf, biasT], [lg], acc=True)
                    return lg

                def tails(idx, n_):
                    j, h = divmod(idx, 16)
                    g = h // 4
                    v_ = vj[j % 2]
                    pt = kb.ps()
                    ptv = pt.t[:].bitcast(BF16)
                    kb.op(pe, lambda: nc.tensor.transpose(out=ptv[:, 0:128], in_=n_.t[:, 0:128], identity=ident_bf()), [n_, cbf], [pt], acc=True)
                    kb.op(pe, lambda: nc.tensor.transpose(out=ptv[0:8, 128:256], in_=n_.t[:, 128:136], identity=ident_bf()), [n_, cbf], [pt], acc=True)
                    p_ = pT[idx % 3]
                    kb.op(act, lambda: nc.scalar.copy(out=p_.t[:, 0, :], in_=ptv[:, 0:128]), [pt], [p_])
                    kb.op(act, lambda: nc.scalar.copy(out=p_.t[0:8, 1, :], in_=ptv[0:8, 128:256]), [pt, p_], [p_])
                    ob_ = po[h // 8]
                    oc = slice((h % 8) * 64, (h % 8 + 1) * 64)
                    kb.op(pe, lambda: nc.tensor.matmul(ob_.t[:, oc], lhsT=p_.t[:, 0, :], rhs=v_.t[:, g * 64:(g + 1) * 64], start=(j == 0 and h % 8 == 0), stop=False), [p_, v_], [ob_], acc=True)
                    kb.op(pe, lambda: nc.tensor.matmul(ob_.t[:, oc], lhsT=p_.t[0:8, 1, :], rhs=vnew.t[0:8, j, g * 64:(g + 1) * 64], start=False, stop=(j == NSEQ_S - 1)), [p_, vnew], [ob_], acc=True)

                NI = NSEQ_S * 16
                pend = {}
                for idx in range(LOOK):
                    pend[idx] = logits_s(idx)
                for idx in range(2):
                    softmax_s1(idx, idx % 16, pend.pop(idx), 136, biasT.t[:, idx % 16, 0:136])
                for idx in range(NI):
                    j, h = divmod(idx, 16)
                    n_ = softmax_s2(idx, 136, cstb.t[:, C_SEL + j:C_SEL + j + 1])
                    if idx + LOOK < NI:
                        pend[idx + LOOK] = logits_s(idx + LOOK)
                    if idx + 2 < NI:
                        softmax_s1(idx + 2, (idx + 2) % 16, pend.pop(idx + 2), 136, biasT.t[:, (idx + 2) % 16, 0:136])
                    tails(idx, n_)
                o_finish(po, slice(0, 128))


            if flags.get("swa_stop") == 3:
                kb.arena_close(); return
            for m in range(8):
                sl = wsm.get(3 + m // 4)
                wv = sl.t[:, :].rearrange("p (c n) -> p c n", c=8)
                pb = kb.ps()
                for kc in range(8):
                    kb.op(pe, lambda: nc.tensor.matmul(pb.t[:, 0:T], lhsT=wv[:, kc, (m % 4) * 128:(m % 4 + 1) * 128], rhs=attnT.t[:, kc, 0:T], start=(kc == 0), stop=(kc == 7)), [sl, attnT], [pb], acc=True)
                kb.op(act, lambda: nc.scalar.activation(out=f_fm.t[:, m, 0:T], in_=pb.t[:, 0:T], func=AF.Identity, bias=ob.t[:, m:m + 1], scale=1.0), [pb, ob], [f_fm])
            post(1, 1, kind, f_fm, f_fm.t[:, :, :])
            kb.arena_close()

        tiles = [("P", t) for t in range(8)] + [("S", 0)]
        if "tiles" in flags:
            tiles = flags["tiles"]
        xi = [0]
        for kind, ti in tiles:
            T, NCH, nseq, tps = geom(kind)
            src = xp if kind == "P" else xs
            dst = yp if kind == "P" else ys
            first = (ti == 0); last = (ti == 7) or kind == "S"
            if kind == "S":
                pass
            kb.arena_open()
            xtk = [kb.abuf(f"xtk{k}", [128, D], F32) for k in range(2)]
            for ch in range(NCH):
                xt = xtk[xi[0] % 2]; xi[0] += 1
                r0 = ti * 512 + ch * 128
                kb.dma(sp, xt.t[:], src[r0:r0 + 128, :], writes=[xt], dsem=msem())
                for q in range(2):
                    pb = kb.ps()
                    for k in range(4):
                        c = q * 4 + k
                        kb.op(pe, lambda: nc.tensor.transpose(out=pb.t[:, k * 128:(k + 1) * 128], in_=xt.t[:, c * 128:(c + 1) * 128], identity=ident()), [xt, cstb], [pb], acc=True)
                    kb.op(act, lambda: nc.scalar.copy(out=x_fm.t[:, q * 4:(q + 1) * 4, ch * 128:(ch + 1) * 128], in_=pb.t[:, :].rearrange("p (k t) -> p k t", k=4)), [pb], [x_fm])
            kb.arena_close()
            for fl in flags.get("phases", ["f00", "ssd", "f01", "f10", "swa", "f11"]):
                if fl == "f00":
                    ffn(0, 0, kind)
                elif fl == "ssd":
                    ssd(kind, last)
                elif fl == "f01":
                    ffn(0, 1, kind)
                elif fl == "f10":
                    ffn(1, 0, kind)
                elif fl == "swa":
                    swa(kind, first, last)
                elif fl == "f11":
                    ffn(1, 1, kind)
            kb.arena_open()
            xtk = [kb.abuf(f"xtk{k}", [128, D], F32) for k in range(2)]
            for ch in range(NCH):
                xt = xtk[xi[0] % 2]; xi[0] += 1
                r0 = ti * 512 + ch * 128
                for q in range(2):
                    pb = kb.ps()
                    for k in range(4):
                        c = q * 4 + k
                        kb.op(pe, lambda: nc.tensor.transpose(out=pb.t[:, k * 128:(k + 1) * 128], in_=x_fm.t[:, c, ch * 128:(ch + 1) * 128], identity=ident()), [x_fm, cstb], [pb], acc=True)
                    kb.op(act, lambda: nc.scalar.copy(out=xt.t[:, q * 512:(q + 1) * 512], in_=pb.t[:, :]), [pb], [xt])
                out_dma(sp, dst[r0:r0 + 128, :], xt.t[:], [xt])
            kb.arena_close()

        for s in dout_sems:
            if kb.cnts[s] > 0:
                nc.sync.wait_ge(kb.sems[s], kb.cnts[s])
        for E in (pe, act, dve, pool):
            if E.cnt > 0:
                nc.sync.wait_ge(kb.sems[E.name], E.cnt)
        for s in dmisc + [b.dsem for b in kb.wslots]:
            if kb.cnts[s] > 0:
                nc.sync.wait_ge(kb.sems[s], kb.cnts[s])
    return nc


_CACHE = {}


def make_in_maps(inp, cores=range(8)):
    f = lambda a: np.ascontiguousarray(np.asarray(a, dtype=np.float32))
    cstv = _make_consts()
    shared = {
        "ada_w": f(inp["ada_w"]), "ada_b": f(inp["ada_b"]), "norm_pre": f(inp["norm_pre"]), "norm_post": f(inp["norm_post"]),
        "ffn_w_in": f(inp["ffn_w_in"]), "ffn_w_out": f(inp["ffn_w_out"]),
        "ssm_in_w": f(inp["ssm_in_w"][0]), "conv_w": f(inp["ssm_conv_w"][0]), "conv_b": f(inp["ssm_conv_b"]),
        "dt_bias": f(inp["ssm_dt_bias"]), "a_log": f(inp["ssm_a_log"]), "ssm_d": f(inp["ssm_d"]),
        "ssm_norm_w": f(inp["ssm_norm_w"]), "ssm_out_w": f(inp["ssm_out_w"][0]),
        "qkv_w": f(inp["attn_qkv_w"][0]), "qkv_b": f(inp["attn_qkv_b"]), "sinks": f(inp["attn_sinks"]),
        "o_w": f(inp["attn_o_w"][0]), "o_b": f(inp["attn_o_b"]), "rel_bias": f(inp["rel_bias"]), "cst": cstv,
    }
    x_prompt = f(inp["x_prompt"]); x_sample = f(inp["x_sample"])
    state_ssm = f(inp["state_ssm"]); state_conv = f(inp["state_conv"])
    cache_k = f(inp["cache_k"]); cache_v = f(inp["cache_v"])
    c_prompt = f(inp["c_prompt"]); c_sample = f(inp["c_sample"])
    in_maps = []
    for b in cores:
        s0, s1 = b * 16, (b + 1) * 16
        m = dict(shared)
        m["xp"] = x_prompt[b]
        m["xs"] = x_sample[s0:s1].reshape(128, D)
        m["st_ssm"] = state_ssm[0, s0:s1].reshape(16, DIN, 128)
        m["st_conv"] = state_conv[0, s0:s1].reshape(48, 3072)
        m["ck"] = cache_k[0, s0:s1].reshape(16, 128, 256)
        m["cv"] = cache_v[0, s0:s1].reshape(16, 128, 256)
        m["cvec"] = np.concatenate([c_prompt[b:b + 1], c_sample[s0:s1]], axis=0)
        in_maps.append(m)
    return in_maps


def assemble(R):
    n = len(R)
    y_prompt = np.stack([R[b]["yp"] for b in range(n)]).reshape(n, SEQ, D)
    y_sample = np.concatenate([R[b]["ys"].reshape(16, 8, D) for b in range(n)], axis=0)
    ssm_p = np.stack([R[b]["ssm_p"].reshape(32, 64, 128) for b in range(n)])[None]
    conv_p = np.stack([R[b]["conv_p"] for b in range(n)])[None]
    k_p = np.stack([R[b]["kp"].reshape(128, 4, 64) for b in range(n)])[None]
    v_p = np.stack([R[b]["vp"].reshape(128, 4, 64) for b in range(n)])[None]
    ssm_s = np.concatenate([R[b]["ssm_s"].reshape(16, 32, 64, 128) for b in range(n)], axis=0)[None]
    conv_s = np.concatenate([R[b]["conv_s"].reshape(16, 3, 3072) for b in range(n)], axis=0)[None]
    k_s = np.concatenate([R[b]["ks"].reshape(16, 128, 4, 64) for b in range(n)], axis=0)[None]
    v_s = np.concatenate([R[b]["vs"].reshape(16, 128, 4, 64) for b in range(n)], axis=0)[None]
    outs = (y_prompt, y_sample, ssm_p, conv_p, k_p, v_p, ssm_s, conv_s, k_s, v_s)
    return tuple(np.ascontiguousarray(o, dtype=np.float32) for o in outs)


def kernel(**inp):
    nc = _CACHE.get("nc")
    if nc is None:
        nc = build_program()
        _CACHE["nc"] = nc
    in_maps = make_in_maps(inp)
    res = run_bass_kernel_spmd(nc, in_maps, core_ids=list(range(8)))
    return assemble(res.results)
```

```python
import contextlib
import math
import numpy as np
import concourse.bass as bass
import concourse.mybir as mybir
from concourse.bass_utils import run_bass_kernel_spmd

F32 = mybir.dt.float32
BF16 = mybir.dt.bfloat16
AF = mybir.ActivationFunctionType
ALU = mybir.AluOpType
AX = mybir.AxisListType

D = 1024
SEQ = 4096
NSEQ_S = 16
TPS_S = 8
DFF = 2816
NJ = DFF // 128
DIN = 2048
NEG = -30000.0
EPS = 1e-6

C_ID, C_TRIP, C_STRIP, C_TRIS, C_STRIS, C_J, C_JREP = 0, 128, 256, 384, 512, 640, 768
C_SEL = 896
C_OH = 912
C_ONES = 1296
NCST = 1424


def _bucket(d):
    d = np.asarray(d)
    df = np.maximum(d, 1).astype(np.float32)
    large = 16 + (np.log(df / np.float32(16)) / np.float32(math.log(128 / 16)) * np.float32(16)).astype(np.int32)
    large = np.minimum(large, 31)
    return np.where(d < 16, d, large)


def _make_consts():
    c = np.zeros((128, NCST), np.float32)
    i = np.arange(128)
    same = (i[:, None] // 8) == (i[None, :] // 8)
    c[:, C_ID:C_ID + 128] = np.eye(128)
    c[:, C_TRIP:C_TRIP + 128] = (i[:, None] <= i[None, :])
    c[:, C_STRIP:C_STRIP + 128] = (i[:, None] > i[None, :])
    c[:, C_TRIS:C_TRIS + 128] = (i[:, None] <= i[None, :]) & same
    c[:, C_STRIS:C_STRIS + 128] = (i[:, None] > i[None, :]) & same
    c[:, C_J:C_J + 128] = (i[:, None] == 127 - i[None, :])
    c[:, C_JREP:C_JREP + 128] = (i[:, None] == 127 - (i[None, :] % 8))
    c[:, C_SEL:C_SEL + 16] = (i[:, None] // 8) == np.arange(16)[None, :]
    oh = np.zeros((33, 384), np.float32)
    for ii in range(384):
        dist = 255 - ii
        if 0 <= dist <= 128:
            oh[int(_bucket(dist)), ii] = 1.0
        else:
            oh[32, ii] = 1.0
    c[0:33, C_OH:C_OH + 384] = oh
    c[:, C_ONES:C_ONES + 128] = 1.0
    return c


class Buf:
    __slots__ = ("t", "w", "r", "dsem")

    def __init__(self, t, pend=None):
        self.t = t
        self.w = None
        self.r = dict(pend) if pend else {}
        self.dsem = None


class Eng:
    def __init__(self, name, h):
        self.name = name
        self.h = h
        self.cnt = 0
        self.seen = {}


class KB:
    def __init__(self, nc, es):
        self.nc = nc
        self.es = es
        self.sems = {}
        self.cnts = {}
        self.pe = self._eng("pe", nc.tensor)
        self.act = self._eng("act", nc.scalar)
        self.dve = self._eng("dve", nc.vector)
        self.pool = self._eng("pool", nc.gpsimd)
        self.sp = self._eng("sp", nc.sync)
        self.engs = [self.pe, self.act, self.dve, self.pool, self.sp]
        self.banks = []
        for i in range(8):
            t = es.enter_context(nc.psum_tensor(f"psb{i}", [128, 512], F32))
            self.banks.append(Buf(t))
        self.bank_i = 0
        self.held = set()
        self.pending = {}
        self.arena_bufs = []
        self.arena_es = None
        self.out_events = {}
        self.wslots = []
        self.wslot_i = 0
        self.skip_self_waw = True
        self.cache_mode = None
        self.cache_bufs = {}
        self.dq = 0

    def _eng(self, name, h):
        self.sems[name] = self.es.enter_context(self.nc.semaphore(name))
        self.cnts[name] = 0
        return Eng(name, h)

    def dsem(self, name):
        self.sems[name] = self.es.enter_context(self.nc.semaphore(name))
        self.cnts[name] = 0
        return name

    def pbuf(self, name, shape, dt):
        t = self.es.enter_context(self.nc.sbuf_tensor(name, list(shape), dt))
        return Buf(t)

    def arena_open(self):
        self.arena_es = contextlib.ExitStack()
        self.arena_bufs = []

    def abuf(self, name, shape, dt):
        self.uid = getattr(self, "uid", 0) + 1
        t = self.arena_es.enter_context(self.nc.sbuf_tensor(f"{name}_{self.uid}", list(shape), dt))
        b = Buf(t, self.pending)
        self.arena_bufs.append(b)
        return b

    def arena_close(self):
        pend = dict(self.pending)
        for b in self.arena_bufs:
            if b.w and pend.get(b.w[0], 0) < b.w[1]:
                pend[b.w[0]] = b.w[1]
            for s, v in b.r.items():
                if pend.get(s, 0) < v:
                    pend[s] = v
        self.pending = pend
        self.arena_es.close()
        self.arena_es = None
        self.arena_bufs = []

    def ps(self):
        for _ in range(16):
            i = self.bank_i
            self.bank_i = (self.bank_i + 1) % 8
            if i not in self.held:
                return self.banks[i]
        raise RuntimeError("no psum bank")

    def ps_hold(self):
        b = self.ps()
        self.held.add(self.banks.index(b))
        return b

    def ps_release(self, b):
        self.held.discard(self.banks.index(b))

    def _need(self, E, reads, writes, acc, skipname):
        need = {}

        def add(s, v):
            if need.get(s, 0) < v:
                need[s] = v
        for b in reads:
            if b.w:
                add(*b.w)
        for b in writes:
            if b.w and not ((acc or self.skip_self_waw) and b.w[0] == skipname):
                add(*b.w)
            for s, v in b.r.items():
                if s != E.name:
                    add(s, v)
        for s, v in need.items():
            if E.seen.get(s, 0) < v:
                E.h.wait_ge(self.sems[s], v)
                E.seen[s] = v

    def op(self, E, fn, reads=(), writes=(), acc=False):
        self._need(E, reads, writes, acc, E.name)
        ins = fn()
        E.cnt += 1
        ins.then_inc(self.sems[E.name], 1)
        for b in reads:
            if b.r.get(E.name, 0) < E.cnt:
                b.r[E.name] = E.cnt
        for b in writes:
            b.w = (E.name, E.cnt)
            b.r = {}
        return ins

    def dma(self, Q, out, in_, reads=(), writes=(), dsem=None, join=False, **kw):
        self._need(Q, reads, writes, join, dsem)
        self.cnts[dsem] += 16
        v = self.cnts[dsem]
        Q.h.dma_start(out=out, in_=in_, **kw).then_inc(self.sems[dsem], 16)
        for b in reads:
            if b.r.get(dsem, 0) < v:
                b.r[dsem] = v
        for b in writes:
            b.w = (dsem, v)
            b.r = {}
        return (dsem, v)

    def wslot(self):
        s = self.wslots[self.wslot_i]
        self.wslot_i = (self.wslot_i + 1) % len(self.wslots)
        return s


class WStream:
    def __init__(self, kb, loads, depth=2, key=None):
        self.kb = kb
        self.loads = loads
        self.depth = depth
        self.issued = 0
        self.slots = []
        self.key = key

    def get(self, k):
        kb = self.kb
        while self.issued < min(len(self.loads), k + 1 + self.depth):
            sl = kb.wslot()
            ck = (self.key, self.issued)
            if kb.cache_mode == "read" and self.key is not None and ck in kb.cache_bufs:
                idx, cb = kb.cache_bufs[ck]
                kb.dma(kb.pool, sl.t[:, :], kb.wcache[idx], reads=[cb], writes=[sl], dsem=sl.dsem)
            else:
                for i, (d, s) in enumerate(self.loads[self.issued](sl.t)):
                    kb.dma(kb.pool, d, s, writes=[sl], dsem=sl.dsem, join=(i > 0))
                if kb.cache_mode == "write" and self.key is not None:
                    idx = len(kb.cache_bufs)
                    cb = Buf(None)
                    kb.cache_bufs[ck] = (idx, cb)
                    kb.dma(kb.sp, kb.wcache[idx], sl.t[:, :], reads=[sl], writes=[cb], dsem=kb.cache_sem)
            self.slots.append(sl)
            self.issued += 1
        return self.slots[k]


def build_program(flags=None):
    flags = flags or {}
    nc = bass.Bass("TRN2", target_bir_lowering=False)

    def din(name, shape, dt=F32):
        return nc.dram_tensor(name, list(shape), dt, kind="ExternalInput").ap()

    def dout(name, shape):
        return nc.dram_tensor(name, list(shape), F32, kind="ExternalOutput").ap()

    xp = din("xp", [SEQ, D]); xs = din("xs", [128, D])
    st_ssm = din("st_ssm", [NSEQ_S, DIN, 128]); st_conv = din("st_conv", [48, 3072])
    ck = din("ck", [NSEQ_S, 128, 256]); cv = din("cv", [NSEQ_S, 128, 256])
    cvec = din("cvec", [17, D])
    ada_w = din("ada_w", [2, D, 9216]); ada_b = din("ada_b", [2, 9216])
    norm_pre = din("norm_pre", [2, 3, D]); norm_post = din("norm_post", [2, 3, D])
    ffn_w_in = din("ffn_w_in", [2, 2, D, 2 * DFF]); ffn_w_out = din("ffn_w_out", [2, 2, DFF, D])
    ssm_in_w = din("ssm_in_w", [D, 5152]); conv_w = din("conv_w", [4, 3072]); conv_b = din("conv_b", [1, 3072])
    dt_bias = din("dt_bias", [1, 32]); a_log = din("a_log", [1, 32]); ssm_d = din("ssm_d", [1, 32])
    ssm_norm_w = din("ssm_norm_w", [1, DIN]); ssm_out_w = din("ssm_out_w", [DIN, D])
    qkv_w = din("qkv_w", [D, 1536]); qkv_b = din("qkv_b", [1, 1536]); sinks = din("sinks", [1, 16])
    o_w = din("o_w", [D, D]); o_b = din("o_b", [1, D]); rel_bias = din("rel_bias", [32, 16])
    cst = din("cst", [128, NCST])

    yp = dout("yp", [SEQ, D]); ys = dout("ys", [128, D])
    ssm_p = dout("ssm_p", [DIN, 128]); conv_p = dout("conv_p", [3, 3072])
    kp = dout("kp", [128, 256]); vp = dout("vp", [128, 256])
    ssm_s = dout("ssm_s", [NSEQ_S, DIN, 128]); conv_s = dout("conv_s", [48, 3072])
    ks = dout("ks", [NSEQ_S, 128, 256]); vs = dout("vs", [NSEQ_S, 128, 256])
    uscr = nc.dram_tensor("uscr", [16, 384], F32, kind="Internal")
    wcache_t = nc.dram_tensor("wcache", [96, 128, 4096], BF16, kind="Internal")

    es = contextlib.ExitStack()
    with es:
        kb = KB(nc, es)
        pe, act, dve, pool, sp = kb.pe, kb.act, kb.dve, kb.pool, kb.sp
        NW = 4
        for i in range(NW):
            b = kb.pbuf(f"wslot{i}", [128, 4096], BF16)
            b.dsem = kb.dsem(f"dw{i}")
            kb.wslots.append(b)
        dmisc = [kb.dsem(f"dm{i}") for i in range(8)]
        kb.cache_sem = kb.dsem("dcache")
        kb.wcache = wcache_t.ap()
        dout_sems = [kb.dsem(f"do{i}") for i in range(4)]
        mi = [0]

        def msem():
            mi[0] = (mi[0] + 1) % len(dmisc)
            return dmisc[mi[0]]
        oi = [0]

        def osem():
            oi[0] = (oi[0] + 1) % len(dout_sems)
            return dout_sems[oi[0]]

        def out_dma(Q, out, in_, reads, **kw):
            s = osem()
            kb.dma(Q, out, in_, reads=reads, dsem=s, **kw)

        cstb = kb.pbuf("cstb", [128, NCST], F32)
        cbf = kb.pbuf("cbf", [128, 256], BF16)
        epsb = kb.pbuf("epsb", [128, 2], F32)
        x_fm = kb.pbuf("x_fm", [128, 8, 512], F32)
        hin = kb.pbuf("hin", [128, 8, 512], BF16)
        rstd = kb.pbuf("rstd", [128, 512], F32)
        tmpA = [kb.pbuf(f"tmpA{i}", [128, 512], F32) for i in range(3)]
        PRE = kb.pbuf("PRE", [128, 18, 8, 17], F32)
        hT = kb.pbuf("hT", [128, DIN], F32)
        hT_bf = kb.pbuf("hT_bf", [128, DIN], BF16)
        tailP = kb.pbuf("tailP", [128, 24, 3], F32)
        tailP_cc = [Buf(tailP.t) for _ in range(24)]
        convw = kb.pbuf("convw", [128, 24, 4], F32)
        convb = kb.pbuf("convb", [128, 24], F32)
        vec32 = kb.pbuf("vec32", [128, 4, 32], F32)
        normwT = kb.pbuf("normwT", [128, 16], F32)
        wdt = kb.pbuf("wdt", [128, 8, 32], BF16)
        qkb = kb.pbuf("qkb", [128, 10], F32)
        kvb = kb.pbuf("kvb", [128, 512], F32)
        ob = kb.pbuf("ob", [128, 8], F32)
        sinkb = kb.pbuf("sinkb", [128, 16], F32)
        kT = kb.pbuf("kT", [128, 2, 128 + 512], BF16)
        vtok = kb.pbuf("vtok", [128, 5, 256], BF16)
        tmi = [0]

        def tmp():
            tmi[0] = (tmi[0] + 1) % 3
            return tmpA[tmi[0]]

        ident = lambda: cstb.t[:, C_ID:C_ID + 128]
        ident_bf = lambda: cbf.t[:, 0:128]
        ones_bf = lambda: cbf.t[:, 128:256]
        ones_f = lambda: cstb.t[:, C_ONES:C_ONES + 128]

        kb.dma(sp, cstb.t[:], cst, writes=[cstb], dsem=msem())
        kb.op(dve, lambda: nc.vector.tensor_copy(out=cbf.t[:, 0:128], in_=cstb.t[:, C_ID:C_ID + 128]), [cstb], [cbf])
        kb.op(dve, lambda: nc.vector.tensor_copy(out=cbf.t[:, 128:256], in_=cstb.t[:, C_ONES:C_ONES + 128]), [cbf, cstb], [cbf])
        kb.op(dve, lambda: nc.vector.memset(epsb.t[:, 0:1], EPS), [], [epsb])
        kb.op(dve, lambda: nc.vector.memset(epsb.t[:, 1:2], 1.0), [epsb], [epsb])
        kb.op(dve, lambda: nc.vector.memset(tailP.t[:], 0.0), [], [tailP] + tailP_cc)
        kb.op(dve, lambda: nc.vector.memset(hT.t[:], 0.0), [], [hT])
        kb.op(dve, lambda: nc.vector.memset(hT_bf.t[:], 0.0), [], [hT_bf])
        kb.op(dve, lambda: nc.vector.memset(kT.t[:], 0.0), [], [kT])
        kb.op(dve, lambda: nc.vector.memset(vtok.t[:], 0.0), [], [vtok])

        with nc.allow_non_contiguous_dma(reason="small param loads"):
            for k in range(4):
                kb.dma(sp, convw.t[:, :, k], conv_w[k].rearrange("(c p) -> p c", p=128), writes=[convw], dsem=dmisc[3], join=True)
            kb.dma(sp, convb.t[:], conv_b.rearrange("o (c p) -> p (o c)", p=128), writes=[convb], dsem=msem())
            kb.dma(sp, ob.t[:], o_b.rearrange("o (c p) -> p (o c)", p=128), writes=[ob], dsem=msem())
            for c in range(8):
                A = c if c < 4 else c + 4
                for half, hh in ((0, A), (1, A + 4)):
                    kb.dma(sp, qkb.t[half * 64:(half + 1) * 64, c:c + 1],
                           qkv_b[0:1, hh * 64:(hh + 1) * 64].rearrange("o d -> d o"), writes=[qkb], dsem=dmisc[0], join=True)
            kb.dma(sp, qkb.t[:, 8:10], qkv_b[0:1, 1024:1280].rearrange("o (c p) -> p (o c)", p=128), writes=[qkb], dsem=dmisc[0], join=True)
        kb.dma(sp, vec32.t[:, 0, :], dt_bias.partition_broadcast(128), writes=[vec32], dsem=dmisc[1])
        kb.dma(sp, vec32.t[:, 1, :], a_log.partition_broadcast(128), writes=[vec32], dsem=dmisc[1], join=True)
        kb.dma(sp, vec32.t[:, 2, :], ssm_d.partition_broadcast(128), writes=[vec32], dsem=dmisc[1], join=True)
        with nc.allow_non_contiguous_dma(reason="small param loads"):
            kb.dma(sp, normwT.t[:], ssm_norm_w.rearrange("o (c p) -> p (o c)", p=128), writes=[normwT], dsem=msem())
        kb.dma(sp, sinkb.t[:], sinks.partition_broadcast(128), writes=[sinkb], dsem=msem())
        kb.dma(pool, wdt.t[:], ssm_in_w.rearrange("(c p) n -> p c n", p=128)[:, :, 5120:5152], writes=[wdt], dsem=msem())
        kb.dma(sp, kvb.t[:], qkv_b[0:1, 1024:1536].partition_broadcast(128), writes=[kvb], dsem=msem())
        kb.op(act, lambda: nc.scalar.activation(out=vec32.t[:, 1, :], in_=vec32.t[:, 1, :], func=AF.Exp), [vec32], [vec32])
        kb.op(dve, lambda: nc.vector.tensor_scalar(out=vec32.t[:, 1, :], in0=vec32.t[:, 1, :], scalar1=-1.0, scalar2=None, op0=ALU.mult), [vec32], [vec32])
        kb.op(dve, lambda: nc.vector.tensor_scalar(out=qkb.t[:, 0:8], in0=qkb.t[:, 0:8], scalar1=0.125, scalar2=None, op0=ALU.mult), [qkb], [qkb])

        kb.arena_open()
        cT = kb.abuf("cT", [128, 8, 17], F32)
        csT = kb.abuf("csT", [128, 8, 17], BF16)
        adab = kb.abuf("adab", [128, 2, 72], F32)
        npre = kb.abuf("npre", [128, 6, 8], F32)
        npost = kb.abuf("npost", [128, 6, 8], F32)
        modT = kb.abuf("modT", [128, 2, 72, 17], F32)
        ctok = kb.abuf("ctok", [17, D], F32)
        kb.dma(sp, ctok.t[:], cvec, writes=[ctok], dsem=msem())
        for c in range(8):
            pb = kb.ps()
            kb.op(pe, lambda: nc.tensor.transpose(out=pb.t[:, 0:17], in_=ctok.t[:, c * 128:(c + 1) * 128], identity=cstb.t[0:17, C_ID:C_ID + 17]), [ctok, cstb], [pb])
            kb.op(dve, lambda: nc.vector.tensor_copy(out=cT.t[:, c, :], in_=pb.t[:, 0:17]), [pb], [cT])
        kb.op(act, lambda: nc.scalar.activation(out=csT.t[:], in_=cT.t[:], func=AF.Silu), [cT], [csT])
        with nc.allow_non_contiguous_dma(reason="small param loads"):
            for i in range(2):
                kb.dma(sp, adab.t[:, i, :], ada_b[i].rearrange("(c p) -> p c", p=128), writes=[adab], dsem=dmisc[4], join=True)
                for sub in range(3):
                    kb.dma(sp, npre.t[:, i * 3 + sub, :], norm_pre[i, sub].rearrange("(c p) -> p c", p=128), writes=[npre], dsem=dmisc[5], join=True)
                    kb.dma(sp, npost.t[:, i * 3 + sub, :], norm_post[i, sub].rearrange("(c p) -> p c", p=128), writes=[npost], dsem=dmisc[6], join=True)
        for i in range(2):
            aw = ada_w[i].rearrange("(c p) n -> p c n", p=128)
            loads = []
            for nb in range(18):
                loads.append(lambda t, nb=nb: [(t[:, :].rearrange("p (c n) -> p c n", c=8), aw[:, :, nb * 512:(nb + 1) * 512])])
            wsm = WStream(kb, loads)
            for nb in range(18):
                sl = wsm.get(nb)
                wv = sl.t[:, :].rearrange("p (c n) -> p c n", c=8)
                pb = kb.ps()
                for m in range(4):
                    for kc in range(8):
                        kb.op(pe, lambda: nc.tensor.matmul(pb.t[:, m * 17:(m + 1) * 17], lhsT=wv[:, kc, m * 128:(m + 1) * 128], rhs=csT.t[:, kc, :], start=(kc == 0), stop=(kc == 7)), [sl, csT], [pb], acc=True)
                kb.op(dve, lambda: nc.vector.tensor_tensor(out=modT.t[:, i, nb * 4:(nb + 1) * 4, :], in0=pb.t[:, 0:68].rearrange("p (m s) -> p m s", m=4),
                                                           in1=adab.t[:, i, nb * 4:(nb + 1) * 4].unsqueeze(2).to_broadcast([128, 4, 17]), op=ALU.add), [pb, adab], [modT])
        for i in range(2):
            for sub in range(3):
                base = (i * 3 + sub) * 3
                sh = modT.t[:, i, (sub * 3 + 0) * 8:(sub * 3 + 0) * 8 + 8, :]
                sc = modT.t[:, i, (sub * 3 + 1) * 8:(sub * 3 + 1) * 8 + 8, :]
                gt = modT.t[:, i, (sub * 3 + 2) * 8:(sub * 3 + 2) * 8 + 8, :]
                npb = npre.t[:, i * 3 + sub, :].unsqueeze(2).to_broadcast([128, 8, 17])
                npo = npost.t[:, i * 3 + sub, :].unsqueeze(2).to_broadcast([128, 8, 17])
                kb.op(dve, lambda: nc.vector.scalar_tensor_tensor(out=PRE.t[:, base + 0, :, :], in0=sc, scalar=1.0, in1=npb, op0=ALU.add, op1=ALU.mult), [modT, npre], [PRE])
                kb.op(dve, lambda: nc.vector.tensor_copy(out=PRE.t[:, base + 1, :, :], in_=sh), [modT, PRE], [PRE])
                res = 1.0 if sub == 1 else 0.5
                kb.op(dve, lambda: nc.vector.scalar_tensor_tensor(out=PRE.t[:, base + 2, :, :], in0=gt, scalar=res, in1=npo, op0=ALU.mult, op1=ALU.mult), [modT, npost, PRE], [PRE])

        rb = kb.abuf("rb", [33, 16], F32)
        usb = kb.abuf("usb", [16, 384], F32)
        kb.op(dve, lambda: nc.vector.memset(rb.t[:], NEG), [], [rb])
        kb.dma(sp, rb.t[0:32, :], rel_bias, reads=[], writes=[rb], dsem=msem())
        pb = kb.ps()
        kb.op(pe, lambda: nc.tensor.matmul(pb.t[0:16, 0:384], lhsT=rb.t[:, :], rhs=cstb.t[0:33, C_OH:C_OH + 384], start=True, stop=True), [rb, cstb], [pb])
        kb.op(dve, lambda: nc.vector.tensor_copy(out=usb.t[:], in_=pb.t[0:16, 0:384]), [pb], [usb])
        uev = Buf(None)
        kb.dma(sp, uscr.ap(), usb.t[:], reads=[usb], writes=[uev], dsem=msem())
        kb.arena_close()

        def geom(kind):
            if kind == "P":
                return 512, 4, 1, 512
            return 128, 1, 16, 8

        def sumsq_rstd(src, sv, T):
            kb.op(act, lambda: nc.scalar.activation(out=hin.t[:, :, 0:T], in_=sv, func=AF.Square), [src], [hin])
            pb = kb.ps()
            for c in range(8):
                kb.op(pe, lambda: nc.tensor.matmul(pb.t[:, 0:T], lhsT=ones_bf(), rhs=hin.t[:, c, 0:T], start=(c == 0), stop=(c == 7)), [hin, cbf], [pb], acc=True)
            kb.op(act, lambda: nc.scalar.activation(out=rstd.t[:, 0:T], in_=pb.t[:, 0:T], func=AF.Ln, bias=epsb.t[:, 0:1], scale=1.0 / D), [pb, epsb], [rstd])
            kb.op(act, lambda: nc.scalar.activation(out=rstd.t[:, 0:T], in_=rstd.t[:, 0:T], func=AF.Exp, scale=-0.5), [rstd], [rstd])

        def norm_mod(i, sub, kind):
            T, NCH, nseq, tps = geom(kind)
            base = (i * 3 + sub) * 3
            sumsq_rstd(x_fm, x_fm.t[:, :, 0:T], T)
            for c in range(8):
                t1 = tmp()
                kb.op(dve, lambda: nc.vector.tensor_tensor(out=t1.t[:, 0:T], in0=x_fm.t[:, c, 0:T], in1=rstd.t[:, 0:T], op=ALU.mult), [x_fm, rstd], [t1])
                if kind == "P":
                    kb.op(act, lambda: nc.scalar.activation(out=hin.t[:, c, 0:T], in_=t1.t[:, 0:T], func=AF.Identity,
                                                            bias=PRE.t[:, base + 1, c, 0:1], scale=PRE.t[:, base + 0, c, 0:1]), [t1, PRE], [hin])
                else:
                    v3 = lambda ap: ap.rearrange("p (s t) -> p s t", s=16)
                    kb.op(dve, lambda: nc.vector.tensor_tensor(out=v3(t1.t[:, 0:T]), in0=v3(t1.t[:, 0:T]), in1=PRE.t[:, base + 0, c, 1:17].unsqueeze(2).to_broadcast([128, 16, 8]), op=ALU.mult), [t1, PRE], [t1])
                    kb.op(dve, lambda: nc.vector.tensor_tensor(out=v3(hin.t[:, c, 0:T]), in0=v3(t1.t[:, 0:T]), in1=PRE.t[:, base + 1, c, 1:17].unsqueeze(2).to_broadcast([128, 16, 8]), op=ALU.add), [t1, PRE], [hin])

        def post(i, sub, kind, f_fm, fv):
            T, NCH, nseq, tps = geom(kind)
            base = (i * 3 + sub) * 3
            sumsq_rstd(f_fm, fv, T)
            for c in range(8):
                t1 = tmp()
                kb.op(dve, lambda: nc.vector.tensor_tensor(out=t1.t[:, 0:T], in0=fv[:, c, :], in1=rstd.t[:, 0:T], op=ALU.mult), [f_fm, rstd], [t1])
                if kind == "P":
                    kb.op(dve, lambda: nc.vector.scalar_tensor_tensor(out=x_fm.t[:, c, 0:T], in0=t1.t[:, 0:T], scalar=PRE.t[:, base + 2, c, 0:1], in1=x_fm.t[:, c, 0:T], op0=ALU.mult, op1=ALU.add), [t1, PRE, x_fm], [x_fm])
                else:
                    v3 = lambda ap: ap.rearrange("p (s t) -> p s t", s=16)
                    kb.op(dve, lambda: nc.vector.tensor_tensor(out=v3(t1.t[:, 0:T]), in0=v3(t1.t[:, 0:T]), in1=PRE.t[:, base + 2, c, 1:17].unsqueeze(2).to_broadcast([128, 16, 8]), op=ALU.mult), [t1, PRE], [t1])
                    kb.op(dve, lambda: nc.vector.tensor_tensor(out=x_fm.t[:, c, 0:T], in0=x_fm.t[:, c, 0:T], in1=t1.t[:, 0:T], op=ALU.add), [t1, x_fm], [x_fm])

        def ffn(i, which, kind):
            T, NCH, nseq, tps = geom(kind)
            sub = 0 if which == 0 else 2
            norm_mod(i, sub, kind)
            win = ffn_w_in[i, which].rearrange("(c p) n -> p c n", p=128)
            wout = ffn_w_out[i, which].rearrange("(j p) n -> p j n", p=128)
            loads = []
            NJB = (NJ + 3) // 4
            for jb in range(NJB):
                w_ = min(512, DFF - jb * 512)
                for gu in range(2):
                    loads.append(lambda t, jb=jb, gu=gu, w_=w_: [
                        (t[:, 0:8 * w_].rearrange("p (c n) -> p c n", c=8), win[:, :, gu * DFF + jb * 512:gu * DFF + jb * 512 + w_])])
            for jb in range(NJB):
                nj_ = min(4, NJ - jb * 4)
                loads.append(lambda t, jb=jb, nj_=nj_: [(t[:, 0:nj_ * 1024].rearrange("p (j n) -> p j n", j=nj_), wout[:, jb * 4:jb * 4 + nj_, :])])
            wsm = WStream(kb, loads, key=("ffn", i, which))
            kb.arena_open()
            actb = [kb.abuf(f"act{j}", [128, 512], BF16) for j in range(NJ)]
            sg = [kb.abuf(f"sg{j}", [128, 512], F32) for j in range(2)]
            f_fm = kb.abuf("f_fm", [128, 8, T], F32)
            for j in range(NJ):
                jb = j // 4
                w_ = min(512, DFF - jb * 512)
                slg = wsm.get(2 * jb); slu = wsm.get(2 * jb + 1)
                wg = slg.t[:, 0:8 * w_].rearrange("p (c n) -> p c n", c=8)
                wu = slu.t[:, 0:8 * w_].rearrange("p (c n) -> p c n", c=8)
                jo = (j % 4) * 128
                pg = kb.ps(); pu = kb.ps()
                for kc in range(8):
                    kb.op(pe, lambda: nc.tensor.matmul(pg.t[:, 0:T], lhsT=wg[:, kc, jo:jo + 128], rhs=hin.t[:, kc, 0:T], start=(kc == 0), stop=(kc == 7)), [slg, hin], [pg], acc=True)
                for kc in range(8):
                    kb.op(pe, lambda: nc.tensor.matmul(pu.t[:, 0:T], lhsT=wu[:, kc, jo:jo + 128], rhs=hin.t[:, kc, 0:T], start=(kc == 0), stop=(kc == 7)), [slu, hin], [pu], acc=True)
                s = sg[j % 2]
                kb.op(act, lambda: nc.scalar.activation(out=s.t[:, 0:T], in_=pg.t[:, 0:T], func=AF.Silu), [pg], [s])
                kb.op(dve, lambda: nc.vector.tensor_tensor(out=actb[j].t[:, 0:T], in0=s.t[:, 0:T], in1=pu.t[:, 0:T], op=ALU.mult), [s, pu], [actb[j]])
            pbs = [kb.ps_hold() for _ in range(8)]
            for jb in range(NJB):
                nj_ = min(4, NJ - jb * 4)
                sl = wsm.get(2 * NJB + jb)
                wv = sl.t[:, 0:nj_ * 1024].rearrange("p (j n) -> p j n", j=nj_)
                for m in range(8):
                    for jj in range(nj_):
                        j = jb * 4 + jj
                        kb.op(pe, lambda: nc.tensor.matmul(pbs[m].t[:, 0:T], lhsT=wv[:, jj, m * 128:(m + 1) * 128], rhs=actb[j].t[:, 0:T], start=(j == 0), stop=(j == NJ - 1)), [sl, actb[j]], [pbs[m]], acc=True)
            for m in range(8):
                if m % 2 == 0:
                    kb.op(act, lambda: nc.scalar.copy(out=f_fm.t[:, m, 0:T], in_=pbs[m].t[:, 0:T]), [pbs[m]], [f_fm])
                else:
                    kb.op(dve, lambda: nc.vector.tensor_copy(out=f_fm.t[:, m, 0:T], in_=pbs[m].t[:, 0:T]), [pbs[m]], [f_fm])
                kb.ps_release(pbs[m])
            post(i, sub, kind, f_fm, f_fm.t[:, :, :])
            kb.arena_close()

        def ssd(kind, last):
            T, NCH, nseq, tps = geom(kind)
            P = (kind == "P")
            norm_mod(0, 1, kind)
            inw = ssm_in_w.rearrange("(c p) n -> p c n", p=128)
            outw = ssm_out_w.rearrange("(c p) n -> p c n", p=128)
            loads = []
            for q in range(6):
                loads.append(lambda t, q=q: [(t[:, :].rearrange("p (c n) -> p c n", c=8), inw[:, :, 2048 + q * 512:2048 + (q + 1) * 512])])
            for zb in range(4):
                loads.append(lambda t, zb=zb: [(t[:, :].rearrange("p (c n) -> p c n", c=8), inw[:, :, zb * 512:(zb + 1) * 512])])
            for mm in range(4):
                loads.append(lambda t, mm=mm: [(t[:, :].rearrange("p (c n) -> p c n", c=16), outw[:, :, mm * 256:(mm + 1) * 256])])
            wsm = WStream(kb, loads, key="ssd")
            tri_o, stri_o = (C_TRIP, C_STRIP) if P else (C_TRIS, C_STRIS)
            tri = lambda: cstb.t[:, tri_o:tri_o + 128]
            stri = lambda: cstb.t[:, stri_o:stri_o + 128]

            kb.arena_open()
            xsT = kb.abuf("xsT", [128, 16, T], BF16)
            if P:
                tails = tailP_cc
            BT = kb.abuf("BT", [128, 4, T], BF16)
            CT = kb.abuf("CT", [128, 4, T], BF16)
            raw = [kb.abuf(f"raw{k}", [128, nseq, 3 + tps], F32) for k in range(2)]
            cacc = [kb.abuf(f"cacc{k}", [128, nseq, tps], F32) for k in range(3)]
            xs_tok = kb.abuf("xs_tok", [128, NCH * DIN], BF16)
            xsv = xs_tok.t[:, :].rearrange("p (c n) -> p c n", c=NCH)
            tail = tailP if P else kb.abuf("tailS", [128, 24, 48], F32)
            if not P:
                tails = [tail] * 24
            B_tok = kb.abuf("B_tok", [128, NCH, 512], BF16)
            sz = kb.abuf("sz", [128, NCH, DIN], BF16)
            ynT = xsT
            dA = kb.abuf("dA", [128, NCH, 32], F32)
            dtv = kb.abuf("dtv", [128, NCH, 32], F32)
            sp1 = kb.abuf("sp1", [128, 32], F32)
            sp2 = kb.abuf("sp2", [128, 32], F32)

            if not P:
                hist = kb.abuf("hist", [48, 3072], F32)
                kb.dma(sp, hist.t[:], st_conv, writes=[hist], dsem=msem())
                for cc in range(24):
                    pb = kb.ps()
                    kb.op(pe, lambda: nc.tensor.transpose(out=pb.t[:, 0:48], in_=hist.t[:, cc * 128:(cc + 1) * 128], identity=cstb.t[0:48, C_ID:C_ID + 48]), [hist, cstb], [pb])
                    kb.op(dve, lambda: nc.vector.tensor_copy(out=tail.t[:, cc, :], in_=pb.t[:, 0:48]), [pb], [tail])

            for ch in range(NCH):
                pb = kb.ps()
                for kc in range(8):
                    kb.op(pe, lambda: nc.tensor.matmul(pb.t[:, 0:32], lhsT=hin.t[:, kc, ch * 128:(ch + 1) * 128], rhs=wdt.t[:, kc, :], start=(kc == 0), stop=(kc == 7)), [hin, wdt], [pb], acc=True)
                kb.op(dve, lambda: nc.vector.tensor_tensor(out=sp1.t[:], in0=pb.t[:, 0:32], in1=vec32.t[:, 0, :], op=ALU.add), [pb, vec32], [sp1])
                kb.op(dve, lambda: nc.vector.tensor_scalar(out=sp2.t[:], in0=sp1.t[:], scalar1=-1.0, scalar2=None, op0=ALU.mult), [sp1], [sp2])
                kb.op(dve, lambda: nc.vector.tensor_tensor(out=sp2.t[:], in0=sp2.t[:], in1=sp1.t[:], op=ALU.max), [sp1, sp2], [sp2])
                kb.op(act, lambda: nc.scalar.activation(out=sp2.t[:], in_=sp2.t[:], func=AF.Exp, scale=-1.0), [sp2], [sp2])
                kb.op(act, lambda: nc.scalar.activation(out=sp2.t[:], in_=sp2.t[:], func=AF.Ln, bias=epsb.t[:, 1:2], scale=1.0), [sp2, epsb], [sp2])
                kb.op(dve, lambda: nc.vector.tensor_scalar(out=sp1.t[:], in0=sp1.t[:], scalar1=0.0, scalar2=None, op0=ALU.max), [sp1], [sp1])
                kb.op(dve, lambda: nc.vector.tensor_tensor(out=dtv.t[:, ch, :], in0=sp1.t[:], in1=sp2.t[:], op=ALU.add), [sp1, sp2], [dtv])
                kb.op(dve, lambda: nc.vector.tensor_tensor(out=dA.t[:, ch, :], in0=dtv.t[:, ch, :], in1=vec32.t[:, 1, :], op=ALU.mult), [dtv, vec32], [dA])

            def conv_s1(cc):
                sl = wsm.get(cc // 4)
                wv = sl.t[:, :].rearrange("p (c n) -> p c n", c=8)
                pb = kb.ps()
                for kc in range(8):
                    kb.op(pe, lambda: nc.tensor.matmul(pb.t[:, 0:T], lhsT=wv[:, kc, (cc % 4) * 128:(cc % 4 + 1) * 128], rhs=hin.t[:, kc, 0:T], start=(kc == 0), stop=(kc == 7)), [sl, hin], [pb], acc=True)
                rw = raw[cc % 2]; ca = cacc[cc % 3]
                tl = tails[cc]
                kb.op(act, lambda: nc.scalar.copy(out=rw.t[:, :, 0:3], in_=tail.t[:, cc, :].rearrange("p (s j) -> p s j", j=3)), [tl], [rw])
                kb.op(act, lambda: nc.scalar.copy(out=rw.t[:, :, 3:3 + tps], in_=pb.t[:, 0:T].rearrange("p (s t) -> p s t", s=nseq)), [pb], [rw])
                kb.op(act, lambda: nc.scalar.copy(out=tail.t[:, cc, :].rearrange("p (s j) -> p s j", j=3), in_=rw.t[:, :, tps:tps + 3]), [rw], [tl])
                kb.op(act, lambda: nc.scalar.activation(out=ca.t[:], in_=rw.t[:, :, 0:tps], func=AF.Identity, bias=convb.t[:, cc:cc + 1], scale=convw.t[:, cc, 0:1]), [rw, convw, convb], [ca])
                for k in range(1, 4):
                    kb.op(dve, lambda: nc.vector.scalar_tensor_tensor(out=ca.t[:], in0=rw.t[:, :, k:k + tps], scalar=convw.t[:, cc, k:k + 1], in1=ca.t[:], op0=ALU.mult, op1=ALU.add), [rw, convw, ca], [ca])

            def conv_s2(cc):
                ca = cacc[cc % 3]
                if cc < 16:
                    dstb, dst = xsT, xsT.t[:, cc, :]
                elif cc < 20:
                    dstb, dst = BT, BT.t[:, cc - 16, :]
                else:
                    dstb, dst = CT, CT.t[:, cc - 20, :]
                kb.op(act, lambda: nc.scalar.activation(out=dst.rearrange("p (s t) -> p s t", s=nseq), in_=ca.t[:], func=AF.Silu), [ca], [dstb])

            for cc in range(24):
                conv_s1(cc)
                if cc >= 1:
                    conv_s2(cc - 1)
            conv_s2(23)

            if last and P:
                with nc.allow_non_contiguous_dma(reason="small state out"):
                    for j3 in range(3):
                        out_dma(sp, conv_p[j3].rearrange("(c p) -> p c", p=128), tail.t[:, :, j3], tails)
            if last and not P:
                nr = nseq * 3
                cso = hist
                for cc in range(24):
                    pb = kb.ps()
                    kb.op(pe, lambda: nc.tensor.transpose(out=pb.t[0:nr, 0:128], in_=tail.t[:, cc, :], identity=ident()), [tail, cstb], [pb])
                    kb.op(dve, lambda: nc.vector.tensor_copy(out=cso.t[:, cc * 128:(cc + 1) * 128], in_=pb.t[0:nr, 0:128]), [pb], [cso])
                out_dma(sp, conv_p if P else conv_s, cso.t[:], [cso])

            for ch in range(NCH):
                for q in range(4):
                    pb = kb.ps()
                    pbv = pb.t[:].bitcast(BF16)
                    for k in range(4):
                        cc = q * 4 + k
                        kb.op(pe, lambda: nc.tensor.transpose(out=pbv[:, k * 128:(k + 1) * 128], in_=xsT.t[:, cc, ch * 128:(ch + 1) * 128], identity=ident_bf()), [xsT, cbf], [pb], acc=True)
                    kb.op(act, lambda: nc.scalar.copy(out=xsv[:, ch, q * 512:(q + 1) * 512], in_=pbv[:, 0:512]), [pb], [xs_tok])
                pb = kb.ps()
                pbv = pb.t[:].bitcast(BF16)
                for g in range(4):
                    kb.op(pe, lambda: nc.tensor.transpose(out=pbv[:, g * 128:(g + 1) * 128], in_=BT.t[:, g, ch * 128:(ch + 1) * 128], identity=ident_bf()), [BT, cbf], [pb], acc=True)
                kb.op(act, lambda: nc.scalar.copy(out=B_tok.t[:, ch, :], in_=pbv[:, 0:512]), [pb], [B_tok])

            for zb in range(4):
                sl = wsm.get(6 + zb)
                wv = sl.t[:, :].rearrange("p (c n) -> p c n", c=8)
                for ch in range(NCH):
                    pb = kb.ps()
                    for kc in range(8):
                        kb.op(pe, lambda: nc.tensor.matmul(pb.t[:, :], lhsT=hin.t[:, kc, ch * 128:(ch + 1) * 128], rhs=wv[:, kc, :], start=(kc == 0), stop=(kc == 7)), [sl, hin], [pb], acc=True)
                    kb.op(act, lambda: nc.scalar.activation(out=sz.t[:, ch, zb * 512:(zb + 1) * 512], in_=pb.t[:, :], func=AF.Silu), [pb], [sz])

            R1q = [kb.abuf(f"R1q{k}", [128, 8, 128], F32) for k in range(2)]
            Lsb = [kb.abuf(f"Lsb{k}", [128, 512], F32) for k in range(2)]
            wT = kb.abuf("wT", [128, 32, 128], BF16)
            cbm = kb.abuf("cbm", [128, 4, 128], F32)
            xdt = kb.abuf("xdt", [128, DIN], BF16)
            xdec = xdt if P else kb.abuf("xdec", [128, DIN], BF16)
            sm = kb.abuf("sm", [128, 4, 32], F32)
            ygb = [kb.abuf(f"yg{k}", [128, 512], F32) for k in range(3)]
            ynb = [kb.abuf(f"yn{k}", [128, 512], BF16) for k in range(2)]
            ssqg = [kb.abuf(f"ssq{k}", [128, 2], F32) for k in range(4)]
            junk = kb.abuf("junk", [128, 512], BF16)
            v32 = lambda ap: ap.rearrange("p (h d) -> p h d", h=32)
            if not P:
                dAexp = wT
                dAv = wT.t[:, :, :].rearrange("p h l -> p (h l)").bitcast(F32)
                decS = kb.abuf("decS", [128, 16, 16], F32)
                CTmj = [kb.abuf(f"CTmj{k}", [128, 4, 128], BF16) for k in range(2)]
                h0b = [kb.abuf(f"h0b{k}", [128, 16, 128], BF16) for k in range(2)]
                h0f = kb.abuf("h0f", [128, 16, 128], F32)
                hnv = hT.t[:, :].rearrange("p (c n) -> p c n", c=16)
                Bm = [kb.abuf(f"Bm{k}", [128, 512], BF16) for k in range(2)]
                mJL = kb.abuf("mJL", [128, 16, 128], BF16)
                kb.op(dve, lambda: nc.vector.memset(mJL.t[:], 0.0), [], [mJL])
                for j in range(16):
                    kb.op(dve, lambda: nc.vector.memset(mJL.t[:, j, j * 8:(j + 1) * 8], 1.0), [mJL], [mJL])

            for ch in range(NCH):
                csl = slice(ch * 128, (ch + 1) * 128)
                pv = kb.ps()
                kb.op(pe, lambda: nc.tensor.matmul(pv.t[:, 0:32], lhsT=tri(), rhs=dA.t[:, ch, :], start=True, stop=True), [dA, cstb], [pv])
                kb.op(pe, lambda: nc.tensor.matmul(pv.t[:, 32:64], lhsT=stri(), rhs=dA.t[:, ch, :], start=True, stop=True), [dA, cstb], [pv], acc=True)
                kb.op(pe, lambda: nc.tensor.matmul(pv.t[:, 64:96], lhsT=ones_f(), rhs=dA.t[:, ch, :], start=True, stop=True), [dA, cstb], [pv], acc=True)
                kb.op(act, lambda: nc.scalar.activation(out=sm.t[:, 0:3, :], in_=pv.t[:, 0:96].rearrange("p (a h) -> p a h", a=3), func=AF.Exp), [pv], [sm])
                pc = kb.ps()
                for g in range(4):
                    kb.op(pe, lambda: nc.tensor.matmul(pc.t[:, g * 128:(g + 1) * 128], lhsT=BT.t[:, g, csl], rhs=CT.t[:, g, csl], start=True, stop=True), [BT, CT], [pc], acc=True)
                kb.op(dve, lambda: nc.vector.tensor_tensor(out=cbm.t[:], in0=pc.t[:, :].rearrange("p (g l) -> p g l", g=4), in1=tri().unsqueeze(1).to_broadcast([128, 4, 128]), op=ALU.mult), [pc, cstb], [cbm])
                kb.op(pool, lambda: nc.gpsimd.tensor_tensor(out=v32(xdt.t[:]), in0=v32(xsv[:, ch, :]), in1=dtv.t[:, ch, :].unsqueeze(2).to_broadcast([128, 32, 64]), op=ALU.mult), [xs_tok, dtv], [xdt])
                if not P:
                    kb.op(dve, lambda: nc.vector.tensor_tensor(out=v32(xdec.t[:]), in0=v32(xdt.t[:]), in1=sm.t[:, 1, :].unsqueeze(2).to_broadcast([128, 32, 64]), op=ALU.mult), [xdt, sm], [xdec])
                    kb.op(dve, lambda: nc.vector.tensor_copy(out=v32(dAv), in_=dA.t[:, 0, :].unsqueeze(2).to_broadcast([128, 32, 64])), [dA], [dAexp])
                    pd = kb.ps()
                    for c in range(16):
                        kb.op(pe, lambda: nc.tensor.matmul(pd.t[:, c * 16:(c + 1) * 16], lhsT=dAv[:, c * 128:(c + 1) * 128], rhs=cstb.t[:, C_SEL:C_SEL + 16], start=True, stop=True), [dAexp, cstb], [pd], acc=True)
                    kb.op(act, lambda: nc.scalar.activation(out=decS.t[:], in_=pd.t[:, 0:256].rearrange("p (c j) -> p c j", c=16), func=AF.Exp), [pd], [decS])
                yo = []
                if not P:
                    yo = [kb.ps_hold() for g in range(4)]
                    for j in range(NSEQ_S):
                        hb = h0b[j % 2]; hf = h0f; ht = hT_bf; hn = hT; bm = Bm[j % 2]; CTm = CTmj[j % 2]
                        kb.op(dve, lambda: nc.vector.tensor_tensor(out=CTm.t[:], in0=CT.t[:, :, :], in1=mJL.t[:, j, :].unsqueeze(1).to_broadcast([128, 4, 128]), op=ALU.mult), [CT, mJL], [CTm])
                        kb.dma(pool, hb.t[:], st_ssm[j].rearrange("(c p) n -> p c n", p=128), writes=[hb], dsem=msem())
                        kb.dma(sp, hf.t[:], st_ssm[j].rearrange("(c p) n -> p c n", p=128), writes=[hf], dsem=msem())
                        for q in range(4):
                            pb = kb.ps()
                            pbv = pb.t[:].bitcast(BF16)
                            for k in range(4):
                                c = q * 4 + k
                                kb.op(pe, lambda: nc.tensor.transpose(out=pbv[:, k * 128:(k + 1) * 128], in_=hb.t[:, c, :], identity=ident_bf()), [hb, cbf], [pb], acc=True)
                            kb.op(act, lambda: nc.scalar.copy(out=ht.t[:, q * 512:(q + 1) * 512], in_=pbv[:, 0:512]), [pb], [ht])
                        for g in range(4):
                            kb.op(pe, lambda: nc.tensor.matmul(yo[g].t[:, :], lhsT=CTm.t[:, g, :], rhs=ht.t[:, g * 512:(g + 1) * 512], start=(j == 0), stop=(j == NSEQ_S - 1)), [CTm, ht], [yo[g]], acc=True)
                        kb.op(dve, lambda: nc.vector.tensor_scalar(out=bm.t[:], in0=B_tok.t[:, 0, :], scalar1=cstb.t[:, C_SEL + j:C_SEL + j + 1], scalar2=None, op0=ALU.mult), [B_tok, cstb], [bm])
                        for q in range(4):
                            pb = kb.ps()
                            for k in range(4):
                                c = q * 4 + k
                                kb.op(pe, lambda: nc.tensor.matmul(pb.t[:, k * 128:(k + 1) * 128], lhsT=xdec.t[:, c * 128:(c + 1) * 128], rhs=bm.t[:, (c // 4) * 128:(c // 4 + 1) * 128], start=True, stop=True), [xdec, bm], [pb], acc=True)
                            for k in range(4):
                                c = q * 4 + k
                                kb.op(dve, lambda: nc.vector.scalar_tensor_tensor(out=hnv[:, c, :], in0=hf.t[:, c, :], scalar=decS.t[:, c, j:j + 1], in1=pb.t[:, k * 128:(k + 1) * 128], op0=ALU.mult, op1=ALU.add), [hf, decS, pb], [hn])
                        out_dma(sp, ssm_s[j].rearrange("(c p) n -> p c n", p=128), hnv, [hn])
                for g in range(4):
                    kb.op(dve, lambda: nc.vector.memset(ssqg[g].t[:], 0.0), [ssqg[g]], [ssqg[g]])

                def buildR1(g):
                    R1 = R1q[g % 2]
                    kb.op(pool, lambda: nc.gpsimd.tensor_tensor(out=R1.t[:], in0=dA.t[:, ch, g * 8:(g + 1) * 8].unsqueeze(2).to_broadcast([128, 8, 128]),
                                                                in1=tri().unsqueeze(1).to_broadcast([128, 8, 128]), op=ALU.mult), [dA, cstb], [R1])
                buildR1(0)
                for g in range(4):
                    R1 = R1q[g % 2]
                    pbs = []
                    for b in range(2):
                        pb = kb.ps()
                        kb.op(pe, lambda: nc.tensor.matmul(pb.t[:, :], lhsT=stri(), rhs=R1.t[:, :, :].rearrange("p h l -> p (h l)")[:, b * 512:(b + 1) * 512], start=True, stop=True), [R1, cstb], [pb])
                        pbs.append(pb)
                    if g + 1 < 4:
                        buildR1(g + 1)
                    for b in range(2):
                        pb = pbs[b]
                        L = Lsb[b]
                        kb.op(act, lambda: nc.scalar.activation(out=L.t[:], in_=pb.t[:, :], func=AF.Exp), [pb], [L])
                        h0_ = g * 8 + b * 4
                        kb.op(dve, lambda: nc.vector.tensor_tensor(out=wT.t[:, h0_:h0_ + 4, :], in0=L.t[:].rearrange("p (h l) -> p h l", h=4),
                                                                   in1=cbm.t[:, g, :].unsqueeze(1).to_broadcast([128, 4, 128]), op=ALU.mult), [L, cbm], [wT])
                v3 = lambda ap: ap.rearrange("p (h d) -> p h d", h=8)

                def emitY(g):
                    if P:
                        yo_g = kb.ps()
                        kb.op(pe, lambda: nc.tensor.matmul(yo_g.t[:, :], lhsT=CT.t[:, g, csl], rhs=hT_bf.t[:, g * 512:(g + 1) * 512], start=True, stop=True), [CT, hT_bf], [yo_g])
                    else:
                        yo_g = yo[g]
                    pb = kb.ps()
                    for r in range(8):
                        h = g * 8 + r
                        kb.op(pe, lambda: nc.tensor.matmul(pb.t[:, r * 64:(r + 1) * 64], lhsT=wT.t[:, h, :], rhs=xdt.t[:, h * 64:(h + 1) * 64], start=True, stop=True), [wT, xdt], [pb], acc=True)
                    return yo_g, pb

                def combine(g, yo_g, pb):
                    gs = slice(g * 512, (g + 1) * 512)
                    eab = sm.t[:, 0, g * 8:(g + 1) * 8].unsqueeze(2).to_broadcast([128, 8, 64])
                    Db = vec32.t[:, 2, g * 8:(g + 1) * 8].unsqueeze(2).to_broadcast([128, 8, 64])
                    t1 = tmp(); t2 = tmp()
                    yg = ygb[g % 3]
                    kb.op(dve, lambda: nc.vector.tensor_tensor(out=v3(t1.t[:]), in0=v3(yo_g.t[:, :]), in1=eab, op=ALU.mult), [yo_g, sm], [t1])
                    if not P:
                        kb.ps_release(yo_g)
                    kb.op(dve, lambda: nc.vector.tensor_tensor(out=t1.t[:], in0=t1.t[:], in1=pb.t[:, :], op=ALU.add), [t1, pb], [t1])
                    kb.op(pool, lambda: nc.gpsimd.tensor_tensor(out=v3(t2.t[:]), in0=v3(xsv[:, ch, gs]), in1=Db, op=ALU.mult), [xs_tok, vec32], [t2])
                    kb.op(dve, lambda: nc.vector.tensor_tensor(out=t1.t[:], in0=t1.t[:], in1=t2.t[:], op=ALU.add), [t1, t2], [t1])
                    kb.op(dve, lambda: nc.vector.tensor_tensor(out=yg.t[:], in0=t1.t[:], in1=sz.t[:, ch, gs], op=ALU.mult), [t1, sz], [yg])
                    sq_ = ssqg[g]
                    kb.op(act, lambda: nc.scalar.activation(out=junk.t[:], in_=yg.t[:], func=AF.Square, accum_out=sq_.t[:, 0:1]), [yg, sq_], [junk, sq_])
                    kb.op(act, lambda: nc.scalar.activation(out=sq_.t[:, 1:2], in_=sq_.t[:, 0:1], func=AF.Ln, bias=epsb.t[:, 0:1], scale=1.0 / 512), [sq_, epsb], [sq_])
                    kb.op(act, lambda: nc.scalar.activation(out=sq_.t[:, 1:2], in_=sq_.t[:, 1:2], func=AF.Exp, scale=-0.5), [sq_], [sq_])
                    yn = ynb[g % 2]
                    kb.op(act, lambda: nc.scalar.activation(out=yn.t[:], in_=yg.t[:], func=AF.Identity, scale=sq_.t[:, 1:2]), [yg, sq_], [yn])

                def finish(g):
                    yn = ynb[g % 2]
                    pq = kb.ps()
                    pqv = pq.t[:].bitcast(BF16)
                    for k in range(4):
                        kb.op(pe, lambda: nc.tensor.transpose(out=pqv[:, k * 128:(k + 1) * 128], in_=yn.t[:, k * 128:(k + 1) * 128], identity=ident_bf()), [yn, cbf], [pq], acc=True)
                    kb.op(dve, lambda: nc.vector.tensor_tensor(out=ynT.t[:, g * 4:(g + 1) * 4, csl], in0=pqv[:, 0:512].rearrange("p (k t) -> p k t", k=4),
                                                               in1=normwT.t[:, g * 4:(g + 1) * 4].unsqueeze(2).to_broadcast([128, 4, 128]), op=ALU.mult), [pq, normwT], [ynT])

                Y = {0: emitY(0), 1: emitY(1)}
                combine(0, *Y[0])
                for g in range(1, 4):
                    if g + 1 < 4:
                        Y[g + 1] = emitY(g + 1)
                    combine(g, *Y[g])
                    finish(g - 1)
                finish(3)

                if P:
                    kb.op(pool, lambda: nc.gpsimd.tensor_tensor(out=v32(xdt.t[:]), in0=v32(xdt.t[:]), in1=sm.t[:, 1, :].unsqueeze(2).to_broadcast([128, 32, 64]), op=ALU.mult), [xdt, sm], [xdt])
                    for g in range(4):
                        gs = slice(g * 512, (g + 1) * 512)
                        pb = kb.ps()
                        kb.op(pe, lambda: nc.tensor.matmul(pb.t[:, :], lhsT=B_tok.t[:, ch, g * 128:(g + 1) * 128], rhs=xdt.t[:, gs], start=True, stop=True), [B_tok, xdt], [pb])
                        v3 = lambda ap: ap.rearrange("p (h d) -> p h d", h=8)
                        kb.op(pool, lambda: nc.gpsimd.tensor_tensor(out=v3(hT.t[:, gs]), in0=v3(hT.t[:, gs]), in1=sm.t[:, 2, g * 8:(g + 1) * 8].unsqueeze(2).to_broadcast([128, 8, 64]), op=ALU.mult), [hT, sm], [hT])
                        kb.op(dve, lambda: nc.vector.tensor_tensor(out=hT.t[:, gs], in0=hT.t[:, gs], in1=pb.t[:, :], op=ALU.add), [hT, pb], [hT])
                    kb.op(act, lambda: nc.scalar.copy(out=hT_bf.t[:], in_=hT.t[:]), [hT], [hT_bf])

            if P and last:
                houtv = sz.t[:, :, :].rearrange("p c n -> p (c n)").bitcast(F32)[:, 0:2048].rearrange("p (c n) -> p c n", c=16)
                hout = sz
                for c in range(16):
                    pb = kb.ps()
                    kb.op(pe, lambda: nc.tensor.transpose(out=pb.t[:, 0:128], in_=hT.t[:, c * 128:(c + 1) * 128], identity=ident()), [hT, cstb], [pb])
                    kb.op(dve, lambda: nc.vector.tensor_copy(out=houtv[:, c, :], in_=pb.t[:, 0:128]), [pb], [hout])
                out_dma(sp, ssm_p.rearrange("(c p) n -> p c n", p=128), houtv, [hout])

            fv = xs_tok.t[:, :].bitcast(F32).rearrange("p (m t) -> p m t", m=8)
            for m in range(8):
                sl = wsm.get(10 + m // 2)
                wv = sl.t[:, :].rearrange("p (c n) -> p c n", c=16)
                pb = kb.ps()
                for kc in range(16):
                    kb.op(pe, lambda: nc.tensor.matmul(pb.t[:, 0:T], lhsT=wv[:, kc, (m % 2) * 128:(m % 2 + 1) * 128], rhs=ynT.t[:, kc, 0:T], start=(kc == 0), stop=(kc == 15)), [sl, ynT], [pb], acc=True)
                kb.op(act, lambda: nc.scalar.copy(out=fv[:, m, :], in_=pb.t[:, 0:T]), [pb], [xs_tok])
            post(0, 1, kind, xs_tok, fv)
            kb.arena_close()

        def swa(kind, first, last):
            T, NCH, nseq, tps = geom(kind)
            P = (kind == "P")
            norm_mod(1, 1, kind)
            qw = qkv_w.rearrange("(c p) n -> p c n", p=128)
            ow = o_w.rearrange("(c p) n -> p c n", p=128)
            loads = []
            for half in range(2):
                loads.append(lambda t, half=half: [(t[:, :].rearrange("p (c n) -> p c n", c=8), qw[:, :, half * 512:(half + 1) * 512])])
            loads.append(lambda t: [(t[:, :].rearrange("p (c n) -> p c n", c=8), qw[:, :, 1024:1536])])
            for mm in range(2):
                loads.append(lambda t, mm=mm: [(t[:, :].rearrange("p (c n) -> p c n", c=8), ow[:, :, mm * 512:(mm + 1) * 512])])
            wsm = WStream(kb, loads, key="swa")

            kb.arena_open()
            qT = kb.abuf("qT", [128, 8, T], BF16)
            kv_tok = kb.abuf("kv_tok", [128, NCH, 512], F32)
            biasT = kb.abuf("biasT", [128, 16, 256], BF16)
            tq = [kb.abuf(f"tq{k}", [128, 256], F32) for k in range(2)]
            eS = [kb.abuf(f"eS{k}", [128, 256], BF16) for k in range(3)]
            en = [kb.abuf(f"en{k}", [128, 256], BF16) for k in range(3)]
            pT = [kb.abuf(f"pT{k}", [128, 2, 128], BF16) for k in range(3)]
            st = [kb.abuf(f"st{k}", [128, 8], F32) for k in range(3)]
            o_tok = kb.abuf("o_tok", [128, D], BF16)
            attnT = kb.abuf("attnT", [128, 8, T], BF16)
            f_fm = kb.abuf("f_fm", [128, 8, T], F32)

            with nc.allow_non_contiguous_dma(reason="toeplitz"):
                for h in range(16):
                    t = tq[h % 2]
                    src = bass.AP(uscr, h * 384, [[1, 128], [1, 256]])
                    kb.dma(sp, t.t[:], src, reads=[uev], writes=[t], dsem=msem())
                    pb = kb.ps()
                    jo = C_J if P else C_JREP
                    kb.op(pe, lambda: nc.tensor.matmul(pb.t[:, 0:256], lhsT=cstb.t[:, jo:jo + 128], rhs=t.t[:], start=True, stop=True), [t, cstb], [pb])
                    kb.op(dve, lambda: nc.vector.tensor_copy(out=biasT.t[:, h, :], in_=pb.t[:, 0:256]), [pb], [biasT])

            if flags.get("swa_stop") == 1:
                kb.arena_close(); return
            wqp = [kb.abuf(f"wqp{k}", [128, 8, 4, 2, 64], BF16) for k in range(2)]
            for half in range(2):
                sl = wsm.get(half)
                nat = sl.t[:, :].rearrange("p (c n) -> p c n", c=8)
                for a in range(4):
                    for b in range(2):
                        kb.op(dve, lambda: nc.vector.tensor_copy(out=wqp[half].t[:, :, a, b, :], in_=nat[:, :, b * 256 + a * 64:b * 256 + (a + 1) * 64]), [sl], [wqp[half]])
            for c in range(8):
                wb = wqp[c // 4]
                wv = wb.t[:, :, :, :, :].rearrange("p c a b d -> p c a (b d)")
                pb = kb.ps()
                for kc in range(8):
                    kb.op(pe, lambda: nc.tensor.matmul(pb.t[:, 0:T], lhsT=wv[:, kc, c % 4, :], rhs=hin.t[:, kc, 0:T], start=(kc == 0), stop=(kc == 7)), [wb, hin], [pb], acc=True)
                kb.op(act, lambda: nc.scalar.activation(out=qT.t[:, c, :], in_=pb.t[:, 0:T], func=AF.Identity, bias=qkb.t[:, c:c + 1], scale=0.125), [pb, qkb], [qT])
            sl = wsm.get(2)
            wv = sl.t[:, :].rearrange("p (c n) -> p c n", c=8)
            for c2 in range(2):
                pb = kb.ps()
                for kc in range(8):
                    kb.op(pe, lambda: nc.tensor.matmul(pb.t[:, 0:T], lhsT=wv[:, kc, c2 * 128:(c2 + 1) * 128], rhs=hin.t[:, kc, 0:T], start=(kc == 0), stop=(kc == 7)), [sl, hin], [pb], acc=True)
                kb.op(act, lambda: nc.scalar.activation(out=kT.t[:, c2, 128:128 + T], in_=pb.t[:, 0:T], func=AF.Identity, bias=qkb.t[:, 8 + c2:9 + c2], scale=1.0), [pb, qkb], [kT])
            for ch in range(NCH):
                pb = kb.ps()
                for kc in range(8):
                    kb.op(pe, lambda: nc.tensor.matmul(pb.t[:, :], lhsT=hin.t[:, kc, ch * 128:(ch + 1) * 128], rhs=wv[:, kc, :], start=(kc == 0), stop=(kc == 7)), [sl, hin], [pb], acc=True)
                kb.op(dve, lambda: nc.vector.tensor_tensor(out=kv_tok.t[:, ch, :], in0=pb.t[:, :], in1=kvb.t[:, :], op=ALU.add), [pb, kvb], [kv_tok])
                kb.op(act, lambda: nc.scalar.copy(out=vtok.t[:, 1 + ch, :], in_=kv_tok.t[:, ch, 256:512]), [kv_tok], [vtok])

            if flags.get("swa_stop") == 2:
                kb.arena_close(); return

            LOOK = 3

            def softmax_s1(idx, h, lg, nk, bias_ap):
                k3 = idx % 3
                s_ = lg; e_ = eS[k3]; t_ = st[k3]
                kb.op(dve, lambda: nc.vector.memset(t_.t[:], 0.0), [t_], [t_])
                kb.op(dve, lambda: nc.vector.tensor_reduce(out=t_.t[:, 0:1], in_=s_.t[:, 0:nk], axis=AX.X, op=ALU.max), [s_], [t_])
                kb.op(dve, lambda: nc.vector.tensor_scalar(out=t_.t[:, 1:2], in0=t_.t[:, 0:1], scalar1=sinkb.t[:, h:h + 1], scalar2=-1.0, op0=ALU.max, op1=ALU.mult), [t_, sinkb], [t_])
                kb.op(act, lambda: nc.scalar.activation(out=e_.t[:, 0:nk], in_=s_.t[:, 0:nk], func=AF.Exp, bias=t_.t[:, 1:2], scale=1.0, accum_out=t_.t[:, 2:3]), [s_, t_], [e_, t_])
                kb.op(act, lambda: nc.scalar.activation(out=t_.t[:, 3:4], in_=sinkb.t[:, h:h + 1], func=AF.Exp, bias=t_.t[:, 1:2], scale=1.0), [sinkb, t_], [t_])

            def softmax_s2(idx, nk, selcol):
                k3 = idx % 3
                e_ = eS[k3]; n_ = en[k3]; t_ = st[k3]
                kb.op(dve, lambda: nc.vector.tensor_tensor(out=t_.t[:, 4:5], in0=t_.t[:, 2:3], in1=t_.t[:, 3:4], op=ALU.add), [t_], [t_])
                kb.op(dve, lambda: nc.vector.reciprocal(out=t_.t[:, 5:6], in_=t_.t[:, 4:5]), [t_], [t_])
                if selcol is not None:
                    kb.op(dve, lambda: nc.vector.tensor_tensor(out=t_.t[:, 5:6], in0=t_.t[:, 5:6], in1=selcol, op=ALU.mult), [t_, cstb], [t_])
                kb.op(pool, lambda: nc.gpsimd.tensor_tensor(out=n_.t[:, 0:nk], in0=e_.t[:, 0:nk], in1=t_.t[:, 5:6].to_broadcast([128, nk]), op=ALU.mult), [e_, t_], [n_])
                return n_

            def head_rows(h):
                g = h // 4
                if h < 8:
                    c = h % 4; half = h // 4
                else:
                    c = 4 + (h - 8) % 4; half = (h - 8) // 4
                return c, half, g

            def o_finish(po, csl):
                for k in range(2):
                    kb.op(act, lambda: nc.scalar.copy(out=o_tok.t[:, k * 512:(k + 1) * 512], in_=po[k].t[:, :]), [po[k]], [o_tok])
                    kb.ps_release(po[k])
                for q in range(2):
                    pb = kb.ps()
                    pbv = pb.t[:].bitcast(BF16)
                    for k in range(4):
                        c = q * 4 + k
                        kb.op(pe, lambda: nc.tensor.transpose(out=pbv[:, k * 128:(k + 1) * 128], in_=o_tok.t[:, c * 128:(c + 1) * 128], identity=ident_bf()), [o_tok, cbf], [pb], acc=True)
                    kb.op(act, lambda: nc.scalar.copy(out=attnT.t[:, q * 4:(q + 1) * 4, csl], in_=pbv[:, 0:512].rearrange("p (k t) -> p k t", k=4)), [pb], [attnT])

            if P:
                for ch in range(NCH):
                    csl = slice(ch * 128, (ch + 1) * 128)
                    blk0 = first and ch == 0
                    po = [kb.ps_hold() for _ in range(2)]

                    def logits(h):
                        c, half, g = head_rows(h)
                        rs = slice(half * 64, (half + 1) * 64)
                        lg = kb.ps()
                        if blk0:
                            kb.op(pe, lambda: nc.tensor.matmul(lg.t[:, 0:128], lhsT=qT.t[rs, c, csl], rhs=kT.t[rs, g // 2, 128:256], start=True, stop=False), [qT, kT], [lg])
                            kb.op(pe, lambda: nc.tensor.matmul(lg.t[:, 0:128], lhsT=ident_bf(), rhs=biasT.t[:, h, 128:256], start=False, stop=True), [cbf, biasT], [lg], acc=True)
                            return lg, 128, None
                        kb.op(pe, lambda: nc.tensor.matmul(lg.t[:, 0:256], lhsT=qT.t[rs, c, csl], rhs=kT.t[rs, g // 2, ch * 128:ch * 128 + 256], start=True, stop=False), [qT, kT], [lg])
                        kb.op(pe, lambda: nc.tensor.matmul(lg.t[:, 0:256], lhsT=ident_bf(), rhs=biasT.t[:, h, 0:256], start=False, stop=True), [cbf, biasT], [lg], acc=True)
                        return lg, 256, None

                    def tailp(h, n_, nk):
                        g = h // 4
                        nb = nk // 128
                        pt = kb.ps()
                        ptv = pt.t[:].bitcast(BF16)
                        for b in range(nb):
                            kb.op(pe, lambda: nc.tensor.transpose(out=ptv[:, b * 128:(b + 1) * 128], in_=n_.t[:, b * 128:(b + 1) * 128], identity=ident_bf()), [n_, cbf], [pt], acc=True)
                        p_ = pT[h % 3]
                        kb.op(act, lambda: nc.scalar.copy(out=p_.t[:, 0:nb, :], in_=ptv[:, 0:nb * 128].rearrange("p (b q) -> p b q", b=nb)), [pt], [p_])
                        ob_ = po[h // 8]
                        oc = slice((h % 8) * 64, (h % 8 + 1) * 64)
                        for b in range(nb):
                            vb = (ch + b) if not blk0 else 1
                            kb.op(pe, lambda: nc.tensor.matmul(ob_.t[:, oc], lhsT=p_.t[:, b, :], rhs=vtok.t[:, vb, g * 64:(g + 1) * 64], start=(b == 0), stop=(b == nb - 1)), [p_, vtok], [ob_], acc=True)

                    pend = {}
                    for h in range(LOOK):
                        pend[h] = logits(h)
                    nks = {}
                    for h in range(2):
                        lg, nk, bias_ap = pend.pop(h)
                        nks[h] = nk
                        softmax_s1(h, h, lg, nk, bias_ap)
                    for h in range(16):
                        n_ = softmax_s2(h, nks[h], None)
                        if h + LOOK < 16:
                            pend[h + LOOK] = logits(h + LOOK)
                        if h + 2 < 16:
                            lg, nk, bias_ap = pend.pop(h + 2)
                            nks[h + 2] = nk
                            softmax_s1(h + 2, h + 2, lg, nk, bias_ap)
                        tailp(h, n_, nks[h])
                    o_finish(po, csl)
                if last:
                    out_dma(sp, kp, kv_tok.t[:, NCH - 1, 0:256], [kv_tok])
                    out_dma(sp, vp, kv_tok.t[:, NCH - 1, 256:512], [kv_tok])
                kb.op(dve, lambda: nc.vector.tensor_copy(out=kT.t[:, :, 0:128], in_=kT.t[:, :, T:T + 128]), [kT], [kT])
                kb.op(dve, lambda: nc.vector.tensor_copy(out=vtok.t[:, 0, :], in_=vtok.t[:, NCH, :]), [vtok], [vtok])
            else:
                ckf = [kb.abuf(f"ckf{k}", [128, 256], F32) for k in range(2)]
                kTj = [kb.abuf(f"kTj{k}", [128, 2, 136], BF16) for k in range(2)]
                vj = [kb.abuf(f"vj{k}", [128, 256], BF16) for k in range(2)]
                vnew = kb.abuf("vnew", [8, 16, 256], BF16)
                vtb = kb.abuf("vtb", [128, 256], BF16)
                kb.op(dve, lambda: nc.vector.tensor_copy(out=vtb.t[:], in_=vtok.t[:, 1, :]), [vtok], [vtb])
                for j in range(NSEQ_S):
                    kb.dma(sp, vnew.t[:, j, :], vtb.t[j * 8:(j + 1) * 8, :], reads=[vtb], writes=[vnew], dsem=dmisc[2], join=True)
                po = [kb.ps_hold() for _ in range(2)]

                def prep(j):
                    kf = ckf[j % 2]; ktj = kTj[j % 2]; v_ = vj[j % 2]
                    kb.dma(sp, kf.t[:], ck[j], writes=[kf], dsem=msem())
                    kb.dma(pool, v_.t[:], cv[j], writes=[v_], dsem=msem())
                    out_dma(sp, ks[j, 0:120, :], ck[j, 8:128, :], [])
                    out_dma(sp, vs[j, 0:120, :], cv[j, 8:128, :], [])
                    out_dma(sp, ks[j, 120:128, :], kv_tok.t[j * 8:(j + 1) * 8, 0, 0:256], [kv_tok])
                    out_dma(sp, vs[j, 120:128, :], kv_tok.t[j * 8:(j + 1) * 8, 0, 256:512], [kv_tok])
                    for c2 in range(2):
                        pb = kb.ps()
                        kb.op(pe, lambda: nc.tensor.transpose(out=pb.t[:, 0:128], in_=kf.t[:, c2 * 128:(c2 + 1) * 128], identity=ident()), [kf, cstb], [pb])
                        kb.op(act, lambda: nc.scalar.copy(out=ktj.t[:, c2, 0:128], in_=pb.t[:, 0:128]), [pb], [ktj])
                    kb.op(dve, lambda: nc.vector.tensor_copy(out=ktj.t[:, :, 128:136], in_=kT.t[:, :, 128 + j * 8:128 + (j + 1) * 8]), [kT, ktj], [ktj])

                def logits_s(idx):
                    j, h = divmod(idx, 16)
                    if h == 0:
                        prep(j)
                    ktj = kTj[j % 2]
                    c, half, g = head_rows(h)
                    rs = slice(half * 64, (half + 1) * 64)
                    lg = kb.ps()
                    kb.op(pe, lambda: nc.tensor.matmul(lg.t[:, 0:136], lhsT=qT.t[rs, c, :], rhs=ktj.t[rs, g // 2, :], start=True, stop=False), [qT, ktj], [lg])
                    kb.op(pe, lambda: nc.tensor.matmul(lg.t[:, 0:136], lhsT=ident_bf(), rhs=biasT.t[:, h, 0:136], start=False, stop=True), [cbf, biasT], [lg], acc=True)
                    return lg

                def tails(idx, n_):
                    j, h = divmod(idx, 16)
                    g = h // 4
                    v_ = vj[j % 2]
                    pt = kb.ps()
                    ptv = pt.t[:].bitcast(BF16)
                    kb.op(pe, lambda: nc.tensor.transpose(out=ptv[:, 0:128], in_=n_.t[:, 0:128], identity=ident_bf()), [n_, cbf], [pt], acc=True)
                    kb.op(pe, lambda: nc.tensor.transpose(out=ptv[0:8, 128:256], in_=n_.t[:, 128:136], identity=ident_bf()), [n_, cbf], [pt], acc=True)
                    p_ = pT[idx % 3]
                    kb.op(act, lambda: nc.scalar.copy(out=p_.t[:, 0, :], in_=ptv[:, 0:128]), [pt], [p_])
                    kb.op(act, lambda: nc.scalar.copy(out=p_.t[0:8, 1, :], in_=ptv[0:8, 128:256]), [pt, p_], [p_])
                    ob_ = po[h // 8]
                    oc = slice((h % 8) * 64, (h % 8 + 1) * 64)
                    kb.op(pe, lambda: nc.tensor.matmul(ob_.t[:, oc], lhsT=p_.t[:, 0, :], rhs=v_.t[:, g * 64:(g + 1) * 64], start=(j == 0 and h % 8 == 0), stop=False), [p_, v_], [ob_], acc=True)
                    kb.op(pe, lambda: nc.tensor.matmul(ob_.t[:, oc], lhsT=p_.t[0:8, 1, :], rhs=vnew.t[0:8, j, g * 64:(g + 1) * 64], start=False, stop=(j == NSEQ_S - 1)), [p_, vnew], [ob_], acc=True)

                NI = NSEQ_S * 16
                pend = {}
                for idx in range(LOOK):
                    pend[idx] = logits_s(idx)
                for idx in range(2):
                    softmax_s1(idx, idx % 16, pend.pop(idx), 136, biasT.t[:, idx % 16, 0:136])
                for idx in range(NI):
                    j, h = divmod(idx, 16)
                    n_ = softmax_s2(idx, 136, cstb.t[:, C_SEL + j:C_SEL + j + 1])
                    if idx + LOOK < NI:
                        pend[idx + LOOK] = logits_s(idx + LOOK)
                    if idx + 2 < NI:
                        softmax_s1(idx + 2, (idx + 2) % 16, pend.pop(idx + 2), 136, biasT.t[:, (idx + 2) % 16, 0:136])
                    tails(idx, n_)
                o_finish(po, slice(0, 128))


            if flags.get("swa_stop") == 3:
                kb.arena_close(); return
            for m in range(8):
                sl = wsm.get(3 + m // 4)
                wv = sl.t[:, :].rearrange("p (c n) -> p c n", c=8)
                pb = kb.ps()
                for kc in range(8):
                    kb.op(pe, lambda: nc.tensor.matmul(pb.t[:, 0:T], lhsT=wv[:, kc, (m % 4) * 128:(m % 4 + 1) * 128], rhs=attnT.t[:, kc, 0:T], start=(kc == 0), stop=(kc == 7)), [sl, attnT], [pb], acc=True)
                kb.op(act, lambda: nc.scalar.activation(out=f_fm.t[:, m, 0:T], in_=pb.t[:, 0:T], func=AF.Identity, bias=ob.t[:, m:m + 1], scale=1.0), [pb, ob], [f_fm])
            post(1, 1, kind, f_fm, f_fm.t[:, :, :])
            kb.arena_close()

        tiles = [("P", t) for t in range(8)] + [("S", 0)]
        if "tiles" in flags:
            tiles = flags["tiles"]
        xi = [0]
        for tix, (kind, ti) in enumerate(tiles):
            T, NCH, nseq, tps = geom(kind)
            nxt = tiles[tix + 1][0] if tix + 1 < len(tiles) else None
            kb.cache_mode = "write" if (kind == "P" and nxt == "S") else ("read" if kind == "S" else None)
            src = xp if kind == "P" else xs
            dst = yp if kind == "P" else ys
            first = (ti == 0); last = (ti == 7) or kind == "S"
            if kind == "S":
                pass
            kb.arena_open()
            xtk = [kb.abuf(f"xtk{k}", [128, D], F32) for k in range(2)]
            for ch in range(NCH):
                xt = xtk[xi[0] % 2]; xi[0] += 1
                r0 = ti * 512 + ch * 128
                kb.dma(sp, xt.t[:], src[r0:r0 + 128, :], writes=[xt], dsem=msem())
                for q in range(2):
                    pb = kb.ps()
                    for k in range(4):
                        c = q * 4 + k
                        kb.op(pe, lambda: nc.tensor.transpose(out=pb.t[:, k * 128:(k + 1) * 128], in_=xt.t[:, c * 128:(c + 1) * 128], identity=ident()), [xt, cstb], [pb], acc=True)
                    kb.op(act, lambda: nc.scalar.copy(out=x_fm.t[:, q * 4:(q + 1) * 4, ch * 128:(ch + 1) * 128], in_=pb.t[:, :].rearrange("p (k t) -> p k t", k=4)), [pb], [x_fm])
            kb.arena_close()
            for fl in flags.get("phases", ["f00", "ssd", "f01", "f10", "swa", "f11"]):
                if fl == "f00":
                    ffn(0, 0, kind)
                elif fl == "ssd":
                    ssd(kind, last)
                elif fl == "f01":
                    ffn(0, 1, kind)
                elif fl == "f10":
                    ffn(1, 0, kind)
                elif fl == "swa":
                    swa(kind, first, last)
                elif fl == "f11":
                    ffn(1, 1, kind)
            kb.arena_open()
            xtk = [kb.abuf(f"xtk{k}", [128, D], F32) for k in range(2)]
            for ch in range(NCH):
                xt = xtk[xi[0] % 2]; xi[0] += 1
                r0 = ti * 512 + ch * 128
                for q in range(2):
                    pb = kb.ps()
                    for k in range(4):
                        c = q * 4 + k
                        kb.op(pe, lambda: nc.tensor.transpose(out=pb.t[:, k * 128:(k + 1) * 128], in_=x_fm.t[:, c, ch * 128:(ch + 1) * 128], identity=ident()), [x_fm, cstb], [pb], acc=True)
                    kb.op(act, lambda: nc.scalar.copy(out=xt.t[:, q * 512:(q + 1) * 512], in_=pb.t[:, :]), [pb], [xt])
                out_dma(sp, dst[r0:r0 + 128, :], xt.t[:], [xt])
            kb.arena_close()

        for s in dout_sems:
            if kb.cnts[s] > 0:
                nc.sync.wait_ge(kb.sems[s], kb.cnts[s])
        for E in (pe, act, dve, pool):
            if E.cnt > 0:
                nc.sync.wait_ge(kb.sems[E.name], E.cnt)
        for s in dmisc + [kb.cache_sem] + [b.dsem for b in kb.wslots]:
            if kb.cnts[s] > 0:
                nc.sync.wait_ge(kb.sems[s], kb.cnts[s])
    return nc


_CACHE = {}


def make_in_maps(inp, cores=range(8)):
    f = lambda a: np.ascontiguousarray(np.asarray(a, dtype=np.float32))
    cstv = _make_consts()
    shared = {
        "ada_w": f(inp["ada_w"]), "ada_b": f(inp["ada_b"]), "norm_pre": f(inp["norm_pre"]), "norm_post": f(inp["norm_post"]),
        "ffn_w_in": f(inp["ffn_w_in"]), "ffn_w_out": f(inp["ffn_w_out"]),
        "ssm_in_w": f(inp["ssm_in_w"][0]), "conv_w": f(inp["ssm_conv_w"][0]), "conv_b": f(inp["ssm_conv_b"]),
        "dt_bias": f(inp["ssm_dt_bias"]), "a_log": f(inp["ssm_a_log"]), "ssm_d": f(inp["ssm_d"]),
        "ssm_norm_w": f(inp["ssm_norm_w"]), "ssm_out_w": f(inp["ssm_out_w"][0]),
        "qkv_w": f(inp["attn_qkv_w"][0]), "qkv_b": f(inp["attn_qkv_b"]), "sinks": f(inp["attn_sinks"]),
        "o_w": f(inp["attn_o_w"][0]), "o_b": f(inp["attn_o_b"]), "rel_bias": f(inp["rel_bias"]), "cst": cstv,
    }
    x_prompt = f(inp["x_prompt"]); x_sample = f(inp["x_sample"])
    state_ssm = f(inp["state_ssm"]); state_conv = f(inp["state_conv"])
    cache_k = f(inp["cache_k"]); cache_v = f(inp["cache_v"])
    c_prompt = f(inp["c_prompt"]); c_sample = f(inp["c_sample"])
    in_maps = []
    for b in cores:
        s0, s1 = b * 16, (b + 1) * 16
        m = dict(shared)
        m["xp"] = x_prompt[b]
        m["xs"] = x_sample[s0:s1].reshape(128, D)
        m["st_ssm"] = state_ssm[0, s0:s1].reshape(16, DIN, 128)
        m["st_conv"] = state_conv[0, s0:s1].reshape(48, 3072)
        m["ck"] = cache_k[0, s0:s1].reshape(16, 128, 256)
        m["cv"] = cache_v[0, s0:s1].reshape(16, 128, 256)
        m["cvec"] = np.concatenate([c_prompt[b:b + 1], c_sample[s0:s1]], axis=0)
        in_maps.append(m)
    return in_maps


def assemble(R):
    n = len(R)
    y_prompt = np.stack([R[b]["yp"] for b in range(n)]).reshape(n, SEQ, D)
    y_sample = np.concatenate([R[b]["ys"].reshape(16, 8, D) for b in range(n)], axis=0)
    ssm_p = np.stack([R[b]["ssm_p"].reshape(32, 64, 128) for b in range(n)])[None]
    conv_p = np.stack([R[b]["conv_p"] for b in range(n)])[None]
    k_p = np.stack([R[b]["kp"].reshape(128, 4, 64) for b in range(n)])[None]
    v_p = np.stack([R[b]["vp"].reshape(128, 4, 64) for b in range(n)])[None]
    ssm_s = np.concatenate([R[b]["ssm_s"].reshape(16, 32, 64, 128) for b in range(n)], axis=0)[None]
    conv_s = np.concatenate([R[b]["conv_s"].reshape(16, 3, 3072) for b in range(n)], axis=0)[None]
    k_s = np.concatenate([R[b]["ks"].reshape(16, 128, 4, 64) for b in range(n)], axis=0)[None]
    v_s = np.concatenate([R[b]["vs"].reshape(16, 128, 4, 64) for b in range(n)], axis=0)[None]
    outs = (y_prompt, y_sample, ssm_p, conv_p, k_p, v_p, ssm_s, conv_s, k_s, v_s)
    return tuple(np.ascontiguousarray(o, dtype=np.float32) for o in outs)


def kernel(**inp):
    nc = _CACHE.get("nc")
    if nc is None:
        nc = build_program()
        _CACHE["nc"] = nc
    in_maps = make_in_maps(inp)
    res = run_bass_kernel_spmd(nc, in_maps, core_ids=list(range(8)))
    return assemble(res.results)
```
